# Optimizing a Trainium2 kernel written in Bass

```python
import math
import jax, jax.numpy as jnp
from jax import lax
import numpy as np

D_MODEL = 1024
BATCH = 8
SEQ = 2048
DEPTH = 2
DEC_BATCH = 128
DEC_SEQ = 4
PAST_LEN = 16384
PAGE_SIZE = 128

N_MIXERS = 2
N_RWKV_LAYERS = (DEPTH + 1) // 2
N_MLSTM_LAYERS = DEPTH // 2

RW_HEAD = 64
RW_HEADS = D_MODEL // RW_HEAD
RW_DECAY_LORA = 64
RW_AAA_LORA = 64
RW_GATE_LORA = 160
RW_GN_EPS = 64e-5

ML_HEADS = 8
ML_DV = D_MODEL // ML_HEADS
ML_DK = ML_DV // 2
ML_QK = ML_HEADS * ML_DK
ML_CONV = 4
ML_CHUNK = 64
ML_IN = 2 * ML_QK + ML_HEADS * ML_DV + D_MODEL + 2 * ML_HEADS

D_FF = 2816
NORM_EPS = 1e-6

RW_NAMES = ('rw_mu', 'rw_wr', 'rw_wk', 'rw_wv', 'rw_wo', 'rw_w0', 'rw_w1', 'rw_w2', 'rw_a0', 'rw_a1', 'rw_a2',
            'rw_g1', 'rw_g2', 'rw_k_k', 'rw_k_a', 'rw_r_k', 'rw_gn_w', 'rw_gn_b')
ML_NAMES = ('ml_w_in', 'ml_b_if', 'ml_conv_w', 'ml_conv_b', 'ml_norm_w', 'ml_w_out')

kernel_name = 'rwkv7_mlstm_macaron_step'


def rmsnorm(x, g):
    xf = x.astype(jnp.float32)
    y = xf * lax.rsqrt(jnp.mean(xf * xf, -1, keepdims=True) + NORM_EPS)
    return (y * g.astype(jnp.float32)).astype(x.dtype)


def swiglu(x, wg, wu, wd):
    return (jax.nn.silu(x @ wg) * (x @ wu)) @ wd


def rwkv7_time_mix(xn, shift0, S0, mu, wr, wk, wv, wo, w0, w1, w2, a0, a1, a2, g1, g2, k_k, k_a, r_k, gn_w, gn_b):
    B, T, D = xn.shape
    H, N = RW_HEADS, RW_HEAD
    xf = xn.astype(jnp.float32)
    xprev = jnp.concatenate([shift0.astype(jnp.float32)[:, None], xf[:, :-1]], axis=1)
    xx = xprev - xf
    mu = mu.astype(jnp.float32)
    xr, xw, xk, xv, xa, xg = [xf + xx * mu[c] for c in range(6)]
    r = xr @ wr
    w = -jax.nn.softplus(-(w0 + jnp.tanh(xw @ w1) @ w2)) - 0.5
    k = xk @ wk
    v = xv @ wv
    a = jax.nn.sigmoid(a0 + (xa @ a1) @ a2)
    g = jax.nn.sigmoid(xg @ g1) @ g2
    heads = lambda z: z.reshape(B, T, H, N).astype(jnp.float32)
    kk = heads(k * k_k)
    kk = kk / jnp.maximum(jnp.linalg.norm(kk, axis=-1, keepdims=True), 1e-12)
    k = k * (1.0 + (a - 1.0) * k_a)
    r_h, k_h, v_h, a_h = heads(r), heads(k), heads(v), heads(a)
    decay = jnp.exp(-jnp.exp(heads(w)))
    b_h = kk * a_h

    def step(S, inp):
        r_t, d_t, k_t, v_t, kk_t, b_t = inp
        sa = jnp.einsum('bhvk,bhk->bhv', S, -kk_t)
        S = S * d_t[:, :, None, :] + sa[..., None] * b_t[:, :, None, :] + v_t[..., None] * k_t[:, :, None, :]
        return S, jnp.einsum('bhvk,bhk->bhv', S, r_t)

    tm = lambda z: jnp.swapaxes(z, 0, 1)
    S_T, y = lax.scan(step, S0.astype(jnp.float32), (tm(r_h), tm(decay), tm(k_h), tm(v_h), tm(kk), tm(b_h)))
    y = tm(y)
    mean = jnp.mean(y, -1, keepdims=True)
    var = jnp.mean(jnp.square(y - mean), -1, keepdims=True)
    y = ((y - mean) * lax.rsqrt(var + RW_GN_EPS)).reshape(B, T, D) * gn_w + gn_b
    bonus = jnp.sum(r_h * k_h * r_k, -1, keepdims=True) * v_h
    y = y + bonus.reshape(B, T, D)
    out = (y * g) @ wo
    return out.astype(xn.dtype), S_T, xf[:, -1]


def mlstm_mix(xn, conv0, C0, n0, m0, w_in, b_if, conv_w, conv_b, norm_w, w_out):
    B, T, D = xn.shape
    H, DK, DV = ML_HEADS, ML_DK, ML_DV
    proj = (xn @ w_in).astype(jnp.float32)
    qk_raw, v, o_pre, if_pre = jnp.split(proj, [2 * ML_QK, 2 * ML_QK + H * DV, 2 * ML_QK + H * DV + D], axis=-1)
    xpad = jnp.concatenate([conv0.astype(jnp.float32), qk_raw], axis=1)
    qk = conv_b.astype(jnp.float32) + sum(conv_w[j].astype(jnp.float32) * xpad[:, j:j + T] for j in range(ML_CONV))
    qk = jax.nn.silu(qk)
    new_conv = xpad[:, T:]
    q, k = jnp.split(qk, 2, axis=-1)
    q = q.reshape(B, T, H, DK) * (DK ** -0.5)
    k = k.reshape(B, T, H, DK)
    v = v.reshape(B, T, H, DV)
    o = jax.nn.sigmoid(o_pre)
    if_pre = if_pre + b_if.astype(jnp.float32)
    logi = if_pre[..., :H]
    logf = jax.nn.log_sigmoid(if_pre[..., H:])

    L = math.gcd(T, ML_CHUNK)
    NC = T // L

    def chunks(z):
        z = z.reshape(B, NC, L, H, *z.shape[3:])
        return jnp.moveaxis(z, (1, 3), (0, 2))

    causal = jnp.tril(jnp.ones((L, L), bool))

    def chunk_step(carry, inp):
        C, n, m = carry
        q_c, k_c, v_c, li, lf = inp
        b = jnp.cumsum(lf, axis=-1)
        g_inter = b + m[..., None]
        Dlog = jnp.where(causal, b[..., :, None] - b[..., None, :] + li[..., None, :], -jnp.inf)
        m_t = jnp.maximum(g_inter, Dlog.max(-1))
        Dw = jnp.exp(Dlog - m_t[..., None])
        w_inter = jnp.exp(g_inter - m_t)
        S = jnp.einsum('bhtk,bhsk->bhts', q_c, k_c) * Dw
        num = w_inter[..., None] * jnp.einsum('bhvk,bhtk->bhtv', C, q_c) + jnp.einsum('bhts,bhsv->bhtv', S, v_c)
        den = w_inter * jnp.einsum('bhk,bhtk->bht', n, q_c) + S.sum(-1)
        h = num / jnp.maximum(jnp.abs(den), jnp.exp(-m_t))[..., None]
        m_new = m_t[..., -1]
        w_state = jnp.exp(b[..., -1] + m - m_new)
        w_s = jnp.exp(b[..., -1:] - b + li - m_new[..., None])
        C = w_state[..., None, None] * C + jnp.einsum('bhs,bhsv,bhsk->bhvk', w_s, v_c, k_c)
        n = w_state[..., None] * n + jnp.einsum('bhs,bhsk->bhk', w_s, k_c)
        return (C, n, m_new), h

    carry0 = (C0.astype(jnp.float32), n0.astype(jnp.float32), m0.astype(jnp.float32))
    (C_T, n_T, m_T), h = lax.scan(chunk_step, carry0, (chunks(q), chunks(k), chunks(v), chunks(logi), chunks(logf)))
    h = jnp.moveaxis(h, (0, 2), (1, 3)).reshape(B, T, H, DV)
    h = h * lax.rsqrt(jnp.mean(h * h, -1, keepdims=True) + NORM_EPS)
    h = h.reshape(B, T, H * DV) * norm_w
    out = (h * o) @ w_out
    return out.astype(xn.dtype), C_T, n_T, m_T, new_conv


def trunk(x, rw_S, rw_shift, ml_C, ml_n, ml_m, ml_conv, p):
    new_rw_S, new_rw_shift = [], []
    new_C, new_n, new_m, new_conv = [], [], [], []
    for i in range(DEPTH):
        x = x + 0.5 * swiglu(rmsnorm(x, p['norm_ffa'][i]), p['ffa_wg'][i], p['ffa_wu'][i], p['ffa_wd'][i])
        h = rmsnorm(x, p['norm_mix'][i])
        j = i // N_MIXERS
        if i % N_MIXERS == 0:
            out, S, sh = rwkv7_time_mix(h, rw_shift[j], rw_S[j], *[p[nm][j] for nm in RW_NAMES])
            new_rw_S.append(S)
            new_rw_shift.append(sh)
        else:
            out, C, n, m, cb = mlstm_mix(h, ml_conv[j], ml_C[j], ml_n[j], ml_m[j], *[p[nm][j] for nm in ML_NAMES])
            new_C.append(C)
            new_n.append(n)
            new_m.append(m)
            new_conv.append(cb)
        x = x + out
        x = x + 0.5 * swiglu(rmsnorm(x, p['norm_ffb'][i]), p['ffb_wg'][i], p['ffb_wu'][i], p['ffb_wd'][i])
    y = rmsnorm(x, p['norm_final'])
    return (y, jnp.stack(new_rw_S), jnp.stack(new_rw_shift), jnp.stack(new_C), jnp.stack(new_n),
            jnp.stack(new_m), jnp.stack(new_conv))


def setup_inputs(seed: int = 0) -> dict:
    key = jax.random.key(seed)
    keys = iter(jax.random.split(key, 64))

    def nrm(shape, scale):
        return scale * jax.random.normal(next(keys), shape, jnp.float32)

    def uni(shape, lo, hi):
        return jax.random.uniform(next(keys), shape, jnp.float32, lo, hi)

    def gain(shape):
        return 1.0 + nrm(shape, 0.02)

    NR, NM, D = N_RWKV_LAYERS, N_MLSTM_LAYERS, D_MODEL
    return {
        'x_prompt': nrm((BATCH, SEQ, D), 1.0),
        'x_sample': nrm((DEC_BATCH, DEC_SEQ, D), 1.0),
        'state_rwkv_S': nrm((NR, DEC_BATCH, RW_HEADS, RW_HEAD, RW_HEAD), 0.2),
        'state_rwkv_shift': nrm((NR, DEC_BATCH, D), 1.0),
        'state_mlstm_C': nrm((NM, DEC_BATCH, ML_HEADS, ML_DV, ML_DK), 0.1),
        'state_mlstm_n': nrm((NM, DEC_BATCH, ML_HEADS, ML_DK), 0.1),
        'state_mlstm_m': nrm((NM, DEC_BATCH, ML_HEADS), 1.0),
        'state_mlstm_conv': nrm((NM, DEC_BATCH, ML_CONV - 1, 2 * ML_QK), 1.0),
        'norm_ffa': gain((DEPTH, D)),
        'ffa_wg': nrm((DEPTH, D, D_FF), D ** -0.5),
        'ffa_wu': nrm((DEPTH, D, D_FF), D ** -0.5),
        'ffa_wd': nrm((DEPTH, D_FF, D), D_FF ** -0.5),
        'norm_mix': gain((DEPTH, D)),
        'norm_ffb': gain((DEPTH, D)),
        'ffb_wg': nrm((DEPTH, D, D_FF), D ** -0.5),
        'ffb_wu': nrm((DEPTH, D, D_FF), D ** -0.5),
        'ffb_wd': nrm((DEPTH, D_FF, D), D_FF ** -0.5),
        'rw_mu': uni((NR, 6, D), 0.0, 1.0),
        'rw_wr': nrm((NR, D, D), D ** -0.5),
        'rw_wk': nrm((NR, D, D), D ** -0.5),
        'rw_wv': nrm((NR, D, D), D ** -0.5),
        'rw_wo': nrm((NR, D, D), D ** -0.5),
        'rw_w0': uni((NR, D), -6.0, -1.0),
        'rw_w1': nrm((NR, D, RW_DECAY_LORA), D ** -0.5),
        'rw_w2': nrm((NR, RW_DECAY_LORA, D), 0.1 * RW_DECAY_LORA ** -0.5),
        'rw_a0': nrm((NR, D), 0.1),
        'rw_a1': nrm((NR, D, RW_AAA_LORA), D ** -0.5),
        'rw_a2': nrm((NR, RW_AAA_LORA, D), 0.1 * RW_AAA_LORA ** -0.5),
        'rw_g1': nrm((NR, D, RW_GATE_LORA), D ** -0.5),
        'rw_g2': nrm((NR, RW_GATE_LORA, D), RW_GATE_LORA ** -0.5),
        'rw_k_k': 0.85 + nrm((NR, D), 0.05),
        'rw_k_a': 1.0 + nrm((NR, D), 0.05),
        'rw_r_k': nrm((NR, RW_HEADS, RW_HEAD), 0.1),
        'rw_gn_w': gain((NR, D)),
        'rw_gn_b': nrm((NR, D), 0.02),
        'ml_w_in': nrm((NM, D, ML_IN), D ** -0.5),
        'ml_b_if': jnp.concatenate([nrm((NM, ML_HEADS), 0.1), 3.0 + nrm((NM, ML_HEADS), 0.5)], axis=-1),
        'ml_conv_w': nrm((NM, ML_CONV, 2 * ML_QK), 0.5),
        'ml_conv_b': nrm((NM, 2 * ML_QK), 0.02),
        'ml_norm_w': gain((NM, ML_HEADS * ML_DV)),
        'ml_w_out': nrm((NM, ML_HEADS * ML_DV, D), (ML_HEADS * ML_DV) ** -0.5),
        'norm_final': gain((D,)),
    }


def reference(x_prompt, x_sample, state_rwkv_S, state_rwkv_shift, state_mlstm_C, state_mlstm_n, state_mlstm_m,
              state_mlstm_conv, norm_ffa, ffa_wg, ffa_wu, ffa_wd, norm_mix, norm_ffb, ffb_wg, ffb_wu, ffb_wd,
              rw_mu, rw_wr, rw_wk, rw_wv, rw_wo, rw_w0, rw_w1, rw_w2, rw_a0, rw_a1, rw_a2, rw_g1, rw_g2,
              rw_k_k, rw_k_a, rw_r_k, rw_gn_w, rw_gn_b, ml_w_in, ml_b_if, ml_conv_w, ml_conv_b, ml_norm_w,
              ml_w_out, norm_final):
    p = dict(norm_ffa=norm_ffa, ffa_wg=ffa_wg, ffa_wu=ffa_wu, ffa_wd=ffa_wd, norm_mix=norm_mix,
             norm_ffb=norm_ffb, ffb_wg=ffb_wg, ffb_wu=ffb_wu, ffb_wd=ffb_wd,
             rw_mu=rw_mu, rw_wr=rw_wr, rw_wk=rw_wk, rw_wv=rw_wv, rw_wo=rw_wo, rw_w0=rw_w0, rw_w1=rw_w1,
             rw_w2=rw_w2, rw_a0=rw_a0, rw_a1=rw_a1, rw_a2=rw_a2, rw_g1=rw_g1, rw_g2=rw_g2, rw_k_k=rw_k_k,
             rw_k_a=rw_k_a, rw_r_k=rw_r_k, rw_gn_w=rw_gn_w, rw_gn_b=rw_gn_b,
             ml_w_in=ml_w_in, ml_b_if=ml_b_if, ml_conv_w=ml_conv_w, ml_conv_b=ml_conv_b,
             ml_norm_w=ml_norm_w, ml_w_out=ml_w_out, norm_final=norm_final)
    B = x_prompt.shape[0]
    f32 = jnp.float32
    z_rw_S = jnp.zeros((N_RWKV_LAYERS, B, RW_HEADS, RW_HEAD, RW_HEAD), f32)
    z_rw_shift = jnp.zeros((N_RWKV_LAYERS, B, D_MODEL), f32)
    z_C = jnp.zeros((N_MLSTM_LAYERS, B, ML_HEADS, ML_DV, ML_DK), f32)
    z_n = jnp.zeros((N_MLSTM_LAYERS, B, ML_HEADS, ML_DK), f32)
    z_m = jnp.zeros((N_MLSTM_LAYERS, B, ML_HEADS), f32)
    z_conv = jnp.zeros((N_MLSTM_LAYERS, B, ML_CONV - 1, 2 * ML_QK), f32)
    y_prompt, p_rw_S, p_rw_shift, p_C, p_n, p_m, p_conv = trunk(
        x_prompt, z_rw_S, z_rw_shift, z_C, z_n, z_m, z_conv, p)
    y_sample, s_rw_S, s_rw_shift, s_C, s_n, s_m, s_conv = trunk(
        x_sample, state_rwkv_S, state_rwkv_shift, state_mlstm_C, state_mlstm_n, state_mlstm_m,
        state_mlstm_conv, p)
    return (y_prompt, y_sample, p_rw_S, p_rw_shift, p_C, p_n, p_m, p_conv,
            s_rw_S, s_rw_shift, s_C, s_n, s_m, s_conv)
```

```python
import numpy as np
from contextlib import ExitStack
import concourse.bass as bass
import concourse.mybir as mybir
from concourse.bass_utils import run_bass_kernel_spmd

F32 = mybir.dt.float32
BF16 = mybir.dt.bfloat16
AF = mybir.ActivationFunctionType
ALU = mybir.AluOpType
AX = mybir.AxisListType

NCORES = 8
D = 1024
NCH = 8
SEQ = 2048
NS = 16
TS = 4
NT = SEQ + NS * TS
DFF = 2816
NFF = DFF // 128
EPS = 1e-6

ENGS = ("pe", "act", "dve", "pool", "sp")
SAME_ENGINE_SYNC = True

VID = {}
_v = 0
for _nm in ("norm_ffa0", "norm_ffa1", "norm_mix0", "norm_mix1", "norm_ffb0", "norm_ffb1", "norm_final",
            "mu0", "mu1", "mu2", "mu3", "mu4", "mu5", "w0", "a0", "k_k", "k_a", "r_k", "gn_w", "gn_b",
            "cw0", "cw1", "cw2", "cw3", "cb", "ml_norm_w"):
    VID[_nm] = _v
    _v += 1
NVEC = _v


class Tok:
    __slots__ = ("name", "w", "r", "excl")

    def __init__(self, name="", excl=False):
        self.name = name
        self.w = []
        self.r = []
        self.excl = excl


class TokMap(dict):
    def __missing__(self, key):
        t = Tok(str(key))
        self[key] = t
        return t


class KB:
    def __init__(self, n_dma_sems=12):
        self.nc = bass.Bass("TRN2", target_bir_lowering=False, dynamic_dma_scratch_size=8192)
        self.es = ExitStack()
        nc = self.nc
        self.sem = {}
        self.count = {}
        self.prog = {e: [] for e in ENGS}
        self.waited = {e: {} for e in ENGS}
        for e in ENGS:
            self.sem[e] = self.es.enter_context(nc.semaphore("s_" + e))
            self.count[e] = 0
        self.dsem = {}
        self.dval = {}
        self.dnext = {}
        for q in ("sp", "pool", "act"):
            self.dsem[q] = []
            for j in range(n_dma_sems):
                key = "d_%s_%d" % (q, j)
                self.sem[key] = self.es.enter_context(nc.semaphore(key))
                self.dsem[q].append(key)
                self.dval[key] = 0
            self.dnext[q] = 0
        self.ninstr = 0

    def _wait(self, eng, semkey, value):
        if value <= 0:
            return
        if self.waited[eng].get(semkey, 0) >= value:
            return
        self.waited[eng][semkey] = value
        sem = self.sem[semkey]
        self.prog[eng].append(lambda e, sem=sem, value=value: e.wait_ge(sem, value))

    def _deps(self, eng, reads, writes):
        deps = set()
        for t in reads:
            deps.update(t.w)
            if t.excl:
                deps.update(x for x in t.r if x[0] != eng)
        for t in writes:
            deps.update(t.w)
            deps.update(t.r)
        for (sk, v) in deps:
            if sk == eng and (eng == "pe" or not SAME_ENGINE_SYNC):
                continue
            self._wait(eng, sk, v)

    def emit(self, eng, fn, reads=(), writes=(), signal=True):
        self._deps(eng, reads, writes)
        self.ninstr += 1
        if signal:
            self.count[eng] += 1
            cid = (eng, self.count[eng])
            sem = self.sem[eng]
            self.prog[eng].append(lambda e, fn=fn, sem=sem: fn(e).then_inc(sem, 1))
        else:
            cid = (eng, self.count[eng] + 1)
            self.prog[eng].append(lambda e, fn=fn: fn(e))
        for t in reads:
            t.r.append(cid)
        for t in writes:
            t.w = [cid]
            t.r = []
        return cid

    def dma(self, q, out, in_, reads=(), writes=()):
        self._deps(q, reads, writes)
        j = self.dnext[q]
        self.dnext[q] = (j + 1) % len(self.dsem[q])
        key = self.dsem[q][j]
        self._wait(q, key, self.dval[key])
        self.dval[key] += 16
        cid = (key, self.dval[key])
        sem = self.sem[key]
        self.ninstr += 1
        self.prog[q].append(lambda e, out=out, in_=in_, sem=sem: e.dma_start(out=out, in_=in_).then_inc(sem, 16))
        for t in reads:
            t.r.append(cid)
        for t in writes:
            t.w = [cid]
            t.r = []
        return cid

    def barrier(self):
        for e in ENGS:
            for e2 in ENGS:
                if e2 != e:
                    self._wait(e, e2, self.count[e2])
            for q in self.dsem:
                for key in self.dsem[q]:
                    self._wait(e, key, self.dval[key])

    def flush(self):
        self.barrier()
        nc = self.nc
        prog = self.prog
        with nc.Block() as block:
            @block.tensor
            def _(e):
                for f in prog["pe"]:
                    f(e)

            @block.scalar
            def _(e):
                for f in prog["act"]:
                    f(e)

            @block.vector
            def _(e):
                for f in prog["dve"]:
                    f(e)

            @block.gpsimd
            def _(e):
                for f in prog["pool"]:
                    f(e)

            @block.sync
            def _(e):
                for f in prog["sp"]:
                    f(e)
        self.prog = {e: [] for e in ENGS}


class Arena:
    def __init__(self, ap, nwords):
        self.ap = ap
        self.n = nwords
        self.top = 0

    def mark(self):
        return self.top

    def release(self, m):
        self.top = m

    def f32(self, nwords):
        assert self.top + nwords <= self.n, ("arena overflow", self.top, nwords, self.n)
        a = self.ap[:, self.top:self.top + nwords]
        self.top += nwords
        return a

    def bf16(self, nelem):
        nwords = (nelem + 1) // 2
        a = self.f32(nwords).bitcast(BF16)
        return a[:, 0:nelem]


ARENA_WORDS = 46800
XNW = 1 + SEQ + NS * (TS + 1)
SOFF = 1 + SEQ
TILES = [(0, 512), (512, 512), (1024, 512), (1536, 512), (2048, 64)]


class Builder:
    def __init__(self, stages):
        self.stages = stages
        self.kb = KB()
        kb = self.kb
        nc = kb.nc
        self.nc = nc
        es = kb.es
        d = {}

        def din(name, shape):
            d[name] = nc.dram_tensor(name, list(shape), F32, kind="ExternalInput").ap()

        def dout(name, shape):
            d[name] = nc.dram_tensor(name, list(shape), F32, kind="ExternalOutput").ap()

        din("xT", (D, NT))
        din("vecs", (128, NVEC * 8))
        for nm in ("ffa_wg", "ffa_wu", "ffb_wg", "ffb_wu"):
            din(nm, (2, D, DFF))
        for nm in ("ffa_wd", "ffb_wd"):
            din(nm, (2, DFF, D))
        for nm in ("rw_wr", "rw_wk", "rw_wv", "rw_wo"):
            din(nm, (1, D, D))
        din("rw_w1", (1, D, 64)); din("rw_w2", (1, 64, D))
        din("rw_a1", (1, D, 64)); din("rw_a2", (1, 64, D))
        din("rw_g1", (1, D, 160)); din("rw_g2", (1, 160, D))
        din("ml_w_in", (1, D, 3088)); din("ml_w_out", (1, D, D))
        din("mlv", (64, 8, 10)); din("bif", (8, 2)); din("ml_m0T", (8, NS))
        din("ml_C0T", (NS, 8, 64, 129)); din("ml_convT", (8, 2, 64, NS, 3))
        din("shiftT", (128, NCH, NS))
        din("rw_S0T", (NS, 16, 64, 64))
        dout("yT", (D, NT))
        dout("o_ml_m", (8, 17)); dout("o_ml_Cp", (8, 64, 129)); dout("o_ml_Cs", (NS, 8, 64, 129))
        dout("o_ml_conv", (64, 8, 2, 17, 3))
        dout("o_shift", (128, NCH, 17))
        dout("o_rw_Sp", (16, 64, 64))
        dout("o_rw_Ss", (NS, 16, 64, 64))
        self.d = d
        self.out_names = [k for k in d if k == "yT" or k.startswith("o_")]

        arena_t = es.enter_context(nc.sbuf_tensor("arena", [128, ARENA_WORDS], F32))
        self.ar = Arena(arena_t, ARENA_WORDS)
        self.psall = es.enter_context(nc.psum_tensor("psall", [128, 8, 512], F32))
        self.ps = [self.psall[:, i, :] for i in range(8)]
        self.tps = [Tok("ps%d" % i, excl=True) for i in range(8)]
        self.bank_rr = 0
        self.bank_pe = {}

        ar = self.ar
        self.X = ar.f32(NCH * NT).rearrange("p (c n) -> p c n", c=NCH)
        self.tX = TokMap()
        self.VEC = ar.f32(NVEC * 8)
        self.tVEC = Tok("vec")
        self.ONES = ar.bf16(128)
        self.tONES = Tok("ones")
        self.XNraw = ar.bf16(NCH * XNW)
        self.XN = self.XNraw[:, 0:NCH * NT].rearrange("p (c n) -> p c n", c=NCH)
        self.XNS = self.XNraw.rearrange("p (c n) -> p c n", c=NCH)
        self.tXN = TokMap()


    def bank(self):
        b = self.bank_rr % 8
        self.bank_rr += 1
        return b

    def act(self, out, in_, func, reads, writes, **kw):
        return self.kb.emit("act", lambda e: e.activation(out=out, in_=in_, func=func, **kw), reads, writes)

    def cp(self, eng, out, in_, reads, writes):
        if eng == "act":
            return self.kb.emit("act", lambda e: e.activation(out=out, in_=in_, func=AF.Copy), reads, writes)
        return self.kb.emit(eng, lambda e: e.tensor_copy(out=out, in_=in_), reads, writes)

    def tt(self, eng, out, in0, in1, op, reads, writes):
        return self.kb.emit(eng, lambda e: e.tensor_tensor(out=out, in0=in0, in1=in1, op=op), reads, writes)

    def ts(self, eng, out, in0, s1, s2, op0, op1, reads, writes):
        if s2 is None:
            return self.kb.emit(eng, lambda e: e.tensor_scalar(out=out, in0=in0, scalar1=s1, scalar2=None, op0=op0), reads, writes)
        return self.kb.emit(eng, lambda e: e.tensor_scalar(out=out, in0=in0, scalar1=s1, scalar2=s2, op0=op0, op1=op1), reads, writes)

    def stt(self, out, in0, scalar, in1, op0, op1, reads, writes):
        return self.kb.emit("dve", lambda e: e.scalar_tensor_tensor(out=out, in0=in0, scalar=scalar, in1=in1, op0=op0, op1=op1),
                            reads, writes)

    def _pe_rows(self, lhsT, writes):
        K = lhsT.partition_size()
        base = lhsT.base_partition()
        tile = 32 if K <= 32 else (64 if K <= 64 else 128)
        lo, hi = (base // tile) * tile, (base // tile) * tile + tile
        if tile == 128:
            lo, hi = 0, 128
        for t in writes:
            for b in range(8):
                if t is self.tps[b]:
                    prev = self.bank_pe.get(b)
                    if prev is not None and (prev[1] <= lo or hi <= prev[0]):
                        self.kb._wait("pe", "pe", prev[2][1])
                    self.bank_pe[b] = [lo, hi, None]
        return tile < 128

    def _pe_done(self, writes, cid):
        for t in writes:
            for b in range(8):
                if t is self.tps[b] and self.bank_pe.get(b) is not None:
                    self.bank_pe[b][2] = cid

    def mm(self, out, lhsT, rhs, start, stop, reads, writes, signal=None):
        if signal is None:
            signal = stop
        if self._pe_rows(lhsT, writes):
            signal = True
        cid = self.kb.emit("pe", lambda e: e.matmul(out, lhsT, rhs, start=start, stop=stop), reads, writes, signal=signal)
        self._pe_done(writes, cid)
        return cid

    def tr(self, out, in_, ident, reads, writes):
        self._pe_rows(in_, writes)
        cid = self.kb.emit("pe", lambda e: e.transpose(out, in_, ident), reads, writes)
        self._pe_done(writes, cid)
        return cid

    def memset(self, eng, ap, val, writes):
        return self.kb.emit(eng, lambda e: e.memset(ap, val), (), writes)

    def scan(self, out, d0, d1, init, op0, op1, reads, writes):
        return self.kb.emit("dve", lambda e: e.tensor_tensor_scan(out=out, data0=d0, data1=d1, initial=init, op0=op0, op1=op1), reads, writes)

    def recip(self, out, in_, reads, writes):
        return self.kb.emit("dve", lambda e: e.reciprocal(out=out, in_=in_), reads, writes)

    def reduce(self, out, in_, op, reads, writes, axis=None):
        axis = AX.X if axis is None else axis
        return self.kb.emit("dve", lambda e: e.tensor_reduce(out=out, in_=in_, axis=axis, op=op), reads, writes)

    def vcol(self, name, c):
        j = VID[name] * 8 + c
        return self.VEC[:, j:j + 1]

    def load_inputs(self):
        kb, d = self.kb, self.d
        kb.dma("sp", self.VEC, d["vecs"][:, :], writes=[self.tVEC])
        for c in range(NCH):
            for ti, (t0, n) in enumerate(TILES):
                kb.dma("sp", self.X[:, c, t0:t0 + n], d["xT"][c * 128:(c + 1) * 128, t0:t0 + n],
                       writes=[self.tX[c, ti]])
        kb.emit("dve", lambda e: e.memset(self.ONES, 1.0), writes=[self.tONES])

    def _full_bank(self, b):
        self.bank_pe[b] = [0, 128, ("pe", 0)]

    def rmsnorm_tile(self, ti, gname, out_fn, scratch):
        kb = self.kb
        t0, n = TILES[ti]
        SQ, tSQ, LN, tLN, RS, tRS, bank = scratch
        ps = self.ps[bank][:, :n]
        self._full_bank(bank)
        for c in range(NCH):
            s = c % 2
            kb.emit("act", lambda e, c=c, s=s: e.activation(out=SQ[s][:, :n], in_=self.X[:, c, t0:t0 + n], func=AF.Square),
                    reads=[self.tX[c, ti]], writes=[tSQ[s]])
            kb.emit("pe", lambda e, c=c, s=s: e.matmul(ps, self.ONES, SQ[s][:, :n], start=(c == 0), stop=(c == NCH - 1)),
                    reads=[tSQ[s], self.tONES], writes=[self.tps[bank]], signal=True)
        kb.emit("act", lambda e: e.activation(out=LN[:, :n], in_=ps, func=AF.Ln, scale=1.0 / D, bias=self.EPSC),
                reads=[self.tps[bank], self.tCONST], writes=[tLN])
        kb.emit("act", lambda e: e.activation(out=RS[:, :n], in_=LN[:, :n], func=AF.Exp, scale=-0.5),
                reads=[tLN], writes=[tRS])
        for c in range(NCH):
            out_fn(c, RS[:, :n], tRS)

    def consts(self):
        kb, ar = self.kb, self.ar
        self.CONST = ar.f32(8)
        self.tCONST = Tok("const")
        self.EPSC = self.CONST[:, 0:1]
        self.ONEC = self.CONST[:, 1:2]
        self.NHALFC = self.CONST[:, 2:3]
        self.GNEPSC = self.CONST[:, 3:4]
        for col, val in ((0, EPS), (1, 1.0), (2, -0.5), (3, 64e-5)):
            kb.emit("dve", lambda e, col=col, val=val: e.memset(self.CONST[:, col:col + 1], val), (), [self.tCONST])
        ONESF = ar.f32(128)
        tO = Tok("onesf")
        self.memset("pool", ONESF, 1.0, [tO])
        self.IDENTF = ar.f32(128)
        self.IDENTB = ar.bf16(128)
        self.BONES = ar.bf16(128)
        self.tMASK = Tok("masks")
        kb.emit("pool", lambda e: e.affine_select(out=self.IDENTF, in_=ONESF, pattern=[[-1, 128]], compare_op=ALU.is_equal,
                                                  fill=0.0, base=0, channel_multiplier=1), [tO], [self.tMASK])
        self.cp("pool", self.IDENTB, self.IDENTF, [self.tMASK], [self.tMASK])
        self.memset("pool", self.BONES, 0.0, [self.tMASK])
        self.memset("pool", self.BONES[0:64, 0:64], 1.0, [self.tMASK])
        self.memset("pool", self.BONES[64:128, 64:128], 1.0, [self.tMASK])
        MSU = ar.f32(64)
        MIU = ar.f32(64)
        self.MASKXT = ar.f32(64)
        kb.emit("pool", lambda e: e.affine_select(out=MSU[0:64, :], in_=ONESF[0:64, 0:64], pattern=[[1, 64]], compare_op=ALU.is_gt,
                                                  fill=0.0, base=0, channel_multiplier=-1), [tO], [self.tMASK])
        kb.emit("pool", lambda e: e.affine_select(out=MIU[0:64, :], in_=ONESF[0:64, 0:64], pattern=[[1, 64]], compare_op=ALU.is_ge,
                                                  fill=0.0, base=0, channel_multiplier=-1), [tO], [self.tMASK])
        kb.emit("pool", lambda e: e.affine_select(out=self.MASKXT[0:64, :], in_=ONESF[0:64, 0:64], pattern=[[-1, 64]], compare_op=ALU.is_gt,
                                                  fill=0.0, base=0, channel_multiplier=1), [tO], [self.tMASK])
        self.ts("pool", self.MASKXT[0:64, :], self.MASKXT[0:64, :], -1.0, None, ALU.mult, None, [self.tMASK], [self.tMASK])
        self.MIU = MIU
        self.MASKLL = ar.f32(2 * 4 * 64).rearrange("p (h b t) -> p h b t", h=2, b=4)
        for h in range(2):
            self.cp("pool", self.MASKLL[0:64, h, 0, :], MSU[0:64, :], [self.tMASK], [self.tMASK])
            self.cp("pool", self.MASKLL[0:64, h, 1, :], MIU[0:64, :], [self.tMASK], [self.tMASK])
            self.ts("pool", self.MASKLL[0:64, h, 2, :], MSU[0:64, :], -1.0, None, ALU.mult, None, [self.tMASK], [self.tMASK])
            self.cp("pool", self.MASKLL[0:64, h, 3, :], MIU[0:64, :], [self.tMASK], [self.tMASK])
        self.SM64 = ar.f32(256)
        self.SM4 = ar.f32(16)
        self.memset("pool", self.SM64, 1.0, [self.tMASK])
        self.memset("pool", self.SM64.rearrange("p (j l) -> p j l", l=64)[:, :, 0:1], 0.0, [self.tMASK])
        self.memset("pool", self.SM4, 1.0, [self.tMASK])
        self.memset("pool", self.SM4.rearrange("p (j l) -> p j l", l=4)[:, :, 0:1], 0.0, [self.tMASK])
        self.NEGW0 = ar.f32(8)
        j = VID["w0"] * 8
        self.ts("dve", self.NEGW0, self.VEC[:, j:j + 8], -1.0, None, ALU.mult, None, [self.tVEC], [self.tMASK])

    def rwkv(self):
        kb, d, ar = self.kb, self.d, self.ar
        m0 = ar.mark()
        XNS = self.XNS
        tXNS = TokMap()
        gname = "norm_mix0"
        mu = lambda i, K: self.vcol("mu%d" % i, K)

        HWA = ar.bf16(NT)
        HG1 = ar.bf16(NT)
        HG2 = ar.bf16(NT)
        tHWA, tHG = TokMap(), TokMap()
        W2A2 = ar.bf16(D)
        G2A = ar.bf16(D)
        G2B = ar.bf16(D)
        tW2 = Tok("w2a2g2")
        SHO = ar.f32(NCH * 17).rearrange("p (c j) -> p c j", c=NCH)
        tSHO = Tok("sho")
        SHI = ar.f32(NCH * NS).rearrange("p (c j) -> p c j", c=NCH)
        tSHI = Tok("shi")

        kb.dma("pool", W2A2[0:64, :], d["rw_w2"][0], writes=[tW2])
        kb.dma("pool", W2A2[64:128, :], d["rw_a2"][0], writes=[tW2])
        kb.dma("pool", G2A, d["rw_g2"][0, 0:128, :], writes=[tW2])
        self.memset("pool", G2B, 0.0, [tW2])
        self.memset("pool", HG2, 0.0, [tHG[0]])
        kb.dma("pool", G2B[0:32, :], d["rw_g2"][0, 128:160, :], writes=[tW2])
        kb.dma("sp", SHI, d["shiftT"], writes=[tSHI])

        for c in range(NCH):
            self.memset("pool", XNS[:, c, 0:1], 0.0, [tXNS[c, "init"]])
            sv = XNS[:, c, SOFF:SOFF + NS * 5].rearrange("p (j u) -> p j u", u=5)
            self.cp("pool", sv[:, :, 0], SHI[:, c, :], [tSHI], [tXNS[c, "init"]])

        def xn_aps(K, t0, n):
            if t0 < SEQ:
                return XNS[:, K, 1 + t0:1 + t0 + n], XNS[:, K, t0:t0 + n]
            j0 = (t0 - SEQ) // TS
            nj = n // TS
            sv = XNS[:, K, SOFF + 5 * j0:SOFF + 5 * (j0 + nj)].rearrange("p (j u) -> p j u", u=5)
            return sv[:, :, 1:5], sv[:, :, 0:4]

        def xn_toks(K, t0):
            ti = min(t0 // 512, 4)
            return [tXNS[K, ti], tXNS[K, max(ti - 1, 0)], tXNS[K, "init"]]

        def pview(ps_ap, t0, n):
            if t0 < SEQ:
                return ps_ap
            return ps_ap.rearrange("p (j t) -> p j t", t=TS)

        def mixproj(out, wa, wb, cols, t0, n, wtok, ptok):
            o = pview(out, t0, n)
            for K in range(NCH):
                xa, xb = xn_aps(K, t0, n)
                self.mm(o, wa[:, K, cols], xa, K == 0, False, [wtok] + xn_toks(K, t0), [ptok])
                self.mm(o, wb[:, K, cols], xb, False, K == NCH - 1, [wtok] + xn_toks(K, t0), [ptok])

        def scale_w(raw, wb, Mcols, mu_list, tok):
            for K in range(NCH):
                for (cs, mi) in mu_list:
                    self.ts("pool", wb[:, K, cs], raw[:, K, cs], mu(mi, K), None, ALU.mult, None, [tok, self.tVEC], [tok])
            self.tt("pool", raw, raw, wb, ALU.subtract, [tok], [tok])

        m1 = ar.mark()
        W1A = ar.bf16(NCH * 128).rearrange("p (k m) -> p k m", k=NCH)
        W1B = ar.bf16(NCH * 128).rearrange("p (k m) -> p k m", k=NCH)
        G1A = ar.bf16(NCH * 160).rearrange("p (k m) -> p k m", k=NCH)
        G1B = ar.bf16(NCH * 160).rearrange("p (k m) -> p k m", k=NCH)
        tW1, tG1 = Tok("w1a1"), Tok("g1")
        for K in range(NCH):
            kb.dma("pool", W1A[:, K, 0:64], d["rw_w1"][0, K * 128:(K + 1) * 128, :], writes=[tW1])
            kb.dma("pool", W1A[:, K, 64:128], d["rw_a1"][0, K * 128:(K + 1) * 128, :], writes=[tW1])
            kb.dma("pool", G1A[:, K, :], d["rw_g1"][0, K * 128:(K + 1) * 128, :], writes=[tG1])
        scale_w(W1A, W1B, 128, [(slice(0, 64), 1), (slice(64, 128), 4)], tW1)
        scale_w(G1A, G1B, 160, [(slice(0, 160), 5)], tG1)
        XNF = [ar.f32(512) for _ in range(2)]
        SQ = [ar.bf16(512) for _ in range(2)]
        LN = ar.f32(512)
        RS = ar.f32(512)
        tXNF, tSQ = TokMap(), TokMap()
        tLN, tRS = Tok("ln"), Tok("rs")
        rr = [0]
        for ti, (t0, n) in enumerate(TILES):
            def out_fn(c, rs, trs, ti=ti, t0=t0, n=n):
                s = rr[0] % 2
                rr[0] += 1
                xf = XNF[s][:, :n]
                self.stt(xf, self.X[:, c, t0:t0 + n], self.vcol(gname, c), rs, ALU.mult, ALU.mult,
                         [self.tX[c, ti], trs, self.tVEC], [tXNF[s]])
                if t0 < SEQ:
                    self.cp("act", XNS[:, c, 1 + t0:1 + t0 + n], xf, [tXNF[s]], [tXNS[c, ti]])
                    if t0 + n == SEQ:
                        self.cp("pool", SHO[:, c, 0:1], xf[:, n - 1:n], [tXNF[s]], [tSHO])
                else:
                    sv = XNS[:, c, SOFF:SOFF + NS * 5].rearrange("p (j u) -> p j u", u=5)
                    xv = xf.rearrange("p (j t) -> p j t", t=TS)
                    self.cp("act", sv[:, :, 1:5], xv, [tXNF[s]], [tXNS[c, ti]])
                    self.cp("pool", SHO[:, c, 1:17], xv[:, :, 3], [tXNF[s]], [tSHO])
            self.rmsnorm_tile(ti, gname, out_fn, (SQ, tSQ, LN, tLN, RS, tRS, self.bank()))
            b1, b2, b3 = self.bank(), self.bank(), self.bank()
            mixproj(self.ps[b1][:, :n], W1A, W1B, slice(0, 128), t0, n, tW1, self.tps[b1])
            mixproj(self.ps[b2][:, :n], G1A, G1B, slice(0, 128), t0, n, tG1, self.tps[b2])
            mixproj(self.ps[b3][0:32, :n], G1A, G1B, slice(128, 160), t0, n, tG1, self.tps[b3])
            self.act(HWA[0:64, t0:t0 + n], self.ps[b1][0:64, :n], AF.Tanh, [self.tps[b1]], [tHWA[ti]])
            self.cp("act", HWA[64:128, t0:t0 + n], self.ps[b1][64:128, :n], [self.tps[b1]], [tHWA[ti]])
            self.act(HG1[:, t0:t0 + n], self.ps[b2][:, :n], AF.Sigmoid, [self.tps[b2]], [tHG[ti]])
            self.act(HG2[0:32, t0:t0 + n], self.ps[b3][0:32, :n], AF.Sigmoid, [self.tps[b3]], [tHG[ti]])
        ar.release(m1)
        kb.barrier()
        kb.dma("sp", d["o_shift"], SHO, reads=[tSHO])

        WN = 256

        def f32t():
            return ar.f32(WN)

        def bf16t():
            return ar.bf16(WN)
        WA = {nm: ar.bf16(NCH * 128).rearrange("p (k m) -> p k m", k=NCH) for nm in "rkv"}
        WB = {nm: ar.bf16(NCH * 128).rearrange("p (k m) -> p k m", k=NCH) for nm in "rkv"}
        WO = ar.bf16(D)
        tWc = {nm: Tok("w" + nm) for nm in "rkvo"}
        Rf, Kf, Vf, Gf, A_, EW, CUM, EP, EM, EQ, KK, KF, Bv, BONUS, T1, T2, YF = [f32t() for _ in range(17)]
        Vb, SQb, RKR, KTb, BTb, YG = [bf16t() for _ in range(6)]
        KR = ar.bf16(2 * WN).rearrange("p (a n) -> p a n", a=2)
        KTt = ar.bf16(4 * 128).rearrange("p (j m) -> p j m", j=4)
        BTt = ar.bf16(4 * 128).rearrange("p (j m) -> p j m", j=4)
        VTt = ar.bf16(4 * 128).rearrange("p (j m) -> p j m", j=4)
        LLs = ar.bf16(4 * 2 * 4 * 64)
        XTs = ar.bf16(4 * 2 * 64)
        PW = [ar.bf16(8 * 2 * 64) for _ in range(2)]
        PT = [ar.bf16(8 * 64) for _ in range(2)]
        Gs = ar.bf16(128)
        NU = ar.bf16(128)
        YT = ar.f32(512)
        SQ2 = ar.f32(512)
        STAT = ar.f32(32)
        H = ar.f32(64)
        H0d = ar.f32(64)
        Hb = ar.bf16(64)
        HS = ar.f32(4 * 64).rearrange("p (j v) -> p j v", j=4)
        HSd = ar.f32(4 * 64).rearrange("p (j v) -> p j v", j=4)
        HSb = ar.bf16(4 * 64).rearrange("p (j v) -> p j v", j=4)
        T = TokMap()

        main_tiles = [(t0, 256, 64) for t0 in range(0, SEQ, 256)] + [(SEQ + 16 * q, 16, 4) for q in range(4)]
        import os as _os
        LVL = int(_os.environ.get("RW_LEVEL", "9"))
        if "RW_TILES" in _os.environ:
            main_tiles = [main_tiles[int(i)] for i in _os.environ["RW_TILES"].split(",")]
        if LVL < 1:
            main_tiles = []

        for c in range(int(_os.environ.get("RW_NC", NCH))):
            ccols = slice(c * 128, (c + 1) * 128)
            for nm, key, mi in (("r", "rw_wr", 0), ("k", "rw_wk", 2), ("v", "rw_wv", 3)):
                for K in range(NCH):
                    kb.dma("pool", WA[nm][:, K, :], d[key][0, K * 128:(K + 1) * 128, ccols], writes=[tWc[nm]])
                scale_w(WA[nm], WB[nm], 128, [(slice(0, 128), mi)], tWc[nm])
            kb.dma("pool", WO, d["rw_wo"][0, c * 128:(c + 1) * 128, :], writes=[tWc["o"]])
            self.memset("pool", H, 0.0, [T["H"]])
            self.memset("pool", Hb, 0.0, [T["Hb"]])

            for (t0, n, L) in main_tiles:
                sample = t0 >= SEQ
                NCk = 4
                ti5 = min(t0 // 512, 4)
                tsl = slice(t0, t0 + n)
                cs = lambda j: slice(j * L, (j + 1) * L)
                if sample:
                    q = (t0 - SEQ) // 16
                    for jj in range(4):
                        kb.dma("sp", HS[:, jj, :], d["rw_S0T"][4 * q + jj, 2 * c:2 * c + 2].rearrange("h k v -> (h k) v"),
                               writes=[T["HS", jj]])
                        self.cp("act", HSb[:, jj, :], HS[:, jj, :], [T["HS", jj]], [T["HSb", jj]])
                bA, bB, bC, bD = self.bank(), self.bank(), self.bank(), self.bank()
                PR, PK = self.ps[bA][:, 0:n], self.ps[bA][:, 256:256 + n]
                PV, PGt = self.ps[bB][:, 0:n], self.ps[bB][:, 256:256 + n]
                PWL, PAL = self.ps[bC][:, 0:n], self.ps[bC][:, 256:256 + n]
                PKK, PSm = self.ps[bD][:, 0:n], self.ps[bD][:, 256:256 + n]
                mixproj(PR, WA["r"], WB["r"], slice(0, 128), t0, n, tWc["r"], self.tps[bA])
                mixproj(PK, WA["k"], WB["k"], slice(0, 128), t0, n, tWc["k"], self.tps[bA])
                mixproj(PV, WA["v"], WB["v"], slice(0, 128), t0, n, tWc["v"], self.tps[bB])
                self.mm(PGt, G2A[:, ccols], HG1[:, tsl], True, False, [tW2, tHG[ti5]], [self.tps[bB]])
                self.mm(PGt, G2B[:, ccols], HG2[:, tsl], False, True, [tW2, tHG[ti5], tHG[0]], [self.tps[bB]])
                self.mm(PWL, W2A2[0:64, ccols], HWA[0:64, tsl], True, True, [tW2, tHWA[ti5]], [self.tps[bC]])
                self.mm(PAL, W2A2[64:128, ccols], HWA[64:128, tsl], True, True, [tW2, tHWA[ti5]], [self.tps[bC]])
                w = lambda a: a[:, 0:n]
                tV = self.tVEC
                self.cp("act", w(Rf), PR, [self.tps[bA]], [T["Rf"]])
                self.cp("act", w(Kf), PK, [self.tps[bA]], [T["Kf"]])
                self.cp("act", w(Vf), PV, [self.tps[bB]], [T["Vf"]])
                self.cp("dve", w(Vb), PV, [self.tps[bB]], [T["Vb"]])
                self.cp("act", w(Gf), PGt, [self.tps[bB]], [T["Gf"]])
                self.act(w(A_), PAL, AF.Sigmoid, [self.tps[bC], tV], [T["A"]], bias=self.vcol("a0", c))
                self.act(w(T1), PWL, AF.Exp, [self.tps[bC], self.tMASK], [T["T1"]], scale=-1.0, bias=self.NEGW0[:, c:c + 1])
                self.act(w(T1), w(T1), AF.Ln, [T["T1"], self.tCONST], [T["T1"]], bias=self.ONEC)
                self.act(w(EW), w(T1), AF.Exp, [T["T1"], self.tCONST], [T["EW"]], scale=-1.0, bias=self.NHALFC)
                SM = self.SM4[:, 0:n] if sample else self.SM64[:, 0:n]
                self.scan(w(CUM), SM, w(EW), 0.0, ALU.mult, ALU.subtract, [T["EW"], self.tMASK], [T["CUM"]])
                self.act(w(EP), w(CUM), AF.Exp, [T["CUM"]], [T["EP"]])
                self.act(w(EM), w(CUM), AF.Exp, [T["CUM"]], [T["EM"]], scale=-1.0)
                self.tt("dve", w(T2), w(CUM), w(EW), ALU.add, [T["CUM"], T["EW"]], [T["T2"]])
                self.act(w(EQ), w(T2), AF.Exp, [T["T2"]], [T["EQ"]])
                self.ts("dve", w(KK), w(Kf), self.vcol("k_k", c), None, ALU.mult, None, [T["Kf"], tV], [T["KK"]])
                self.act(w(SQb), w(KK), AF.Square, [T["KK"]], [T["SQb"]])
                self.mm(PKK, self.BONES, w(SQb), True, True, [T["SQb"], self.tMASK], [self.tps[bD]], signal=True)
                self.act(w(T2), PKK, AF.Sqrt, [self.tps[bD]], [T["T2"]])
                self.ts("dve", w(T2), w(T2), 1e-12, None, ALU.max, None, [T["T2"]], [T["T2"]])
                self.recip(w(T2), w(T2), [T["T2"]], [T["T2"]])
                self.tt("dve", w(KK), w(KK), w(T2), ALU.mult, [T["KK"], T["T2"]], [T["KK"]])
                self.ts("dve", w(T1), w(A_), -1.0, self.vcol("k_a", c), ALU.add, ALU.mult, [T["A"], tV], [T["T1"]])
                self.stt(w(KF), w(T1), 1.0, w(Kf), ALU.add, ALU.mult, [T["T1"], T["Kf"]], [T["KF"]])
                self.tt("pool", w(Bv), w(KK), w(A_), ALU.mult, [T["KK"], T["A"]], [T["Bv"]])
                self.tt("dve", KR[:, 0, 0:n], w(KK), w(EQ), ALU.mult, [T["KK"], T["EQ"]], [T["KR"]])
                self.tt("dve", KR[:, 1, 0:n], w(Rf), w(EP), ALU.mult, [T["Rf"], T["EP"]], [T["KR"]])
                self.tt("pool", w(KTb), w(KF), w(EM), ALU.mult, [T["KF"], T["EM"]], [T["KTb"]])
                self.tt("pool", w(BTb), w(Bv), w(EM), ALU.mult, [T["Bv"], T["EM"]], [T["BTb"]])
                self.stt(w(RKR), w(Rf), self.vcol("r_k", c), w(KF), ALU.mult, ALU.mult, [T["Rf"], T["KF"], tV], [T["RKR"]])
                self.mm(PSm, self.BONES, w(RKR), True, True, [T["RKR"], self.tMASK], [self.tps[bD]], signal=True)
                self.tt("dve", w(BONUS), PSm, w(Vf), ALU.mult, [self.tps[bD], T["Vf"]], [T["BONUS"]])
                if LVL < 2:
                    continue
                bT = self.bank()
                PTr = self.ps[bT].bitcast(BF16)
                for (src, ts_, off) in ((KTb, "KTb", 0), (BTb, "BTb", 1)):
                    for j in range(NCk):
                        self.tr(PTr[0:L, off * 512 + j * 128:off * 512 + (j + 1) * 128], src[:, cs(j)], self.IDENTB,
                                [T[ts_], self.tMASK], [self.tps[bT]])
                self.cp("act", KTt[0:L, :, :], PTr[0:L, 0:512].rearrange("p (j m) -> p j m", j=4), [self.tps[bT]], [T["KTt"]])
                self.cp("dve", BTt[0:L, :, :], PTr[0:L, 512:1024].rearrange("p (j m) -> p j m", j=4), [self.tps[bT]], [T["BTt"]])
                bT2 = self.bank()
                PTr2 = self.ps[bT2].bitcast(BF16)
                for j in range(NCk):
                    self.tr(PTr2[0:L, j * 128:(j + 1) * 128], Vb[:, cs(j)], self.IDENTB, [T["Vb"], self.tMASK], [self.tps[bT2]])
                self.cp("act", VTt[0:L, :, :], PTr2[0:L, 0:512].rearrange("p (j m) -> p j m", j=4), [self.tps[bT2]], [T["VTt"]])
                if LVL < 3:
                    continue
                LLv = LLs[:, 0:4 * 2 * 4 * L].rearrange("p (j h b t) -> p j h b t", j=4, h=2, b=4)
                XTv = XTs[:, 0:4 * 2 * L].rearrange("p (j h t) -> p j h t", j=4, h=2)
                mk = self.MASKLL[0:L, :, :, 0:L]
                for h in range(2):
                    hs = slice(64 * h, 64 * h + 64)
                    bX = self.bank()
                    PXT = self.ps[bX][:, 0:4 * L].rearrange("p (j t) -> p j t", j=4)
                    for g0 in (0, 2):
                        bL = self.bank()
                        PLL = self.ps[bL][:, 0:2 * 4 * L].rearrange("p (j b t) -> p j b t", j=2, b=4)
                        for jj in range(2):
                            j = g0 + jj
                            self.mm(PLL[0:L, jj, 0:2, :], KTb[hs, cs(j)], KR[hs, :, cs(j)], True, True,
                                    [T["KTb"], T["KR"]], [self.tps[bL]])
                            self.mm(PLL[0:L, jj, 2:4, :], BTb[hs, cs(j)], KR[hs, :, cs(j)], True, True,
                                    [T["BTb"], T["KR"]], [self.tps[bL]])
                            self.mm(PXT[0:L, j, :], KR[hs, 0, cs(j)], BTb[hs, cs(j)], True, True,
                                    [T["BTb"], T["KR"]], [self.tps[bX]])
                        self.tt("dve", LLv[0:L, g0:g0 + 2, h], PLL[0:L], mk, ALU.mult, [self.tps[bL], self.tMASK], [T["LLs"]])
                    self.tt("dve", XTv[0:L, :, h, :], PXT[0:L], self.MASKXT[0:L, 0:L].unsqueeze(1).to_broadcast([L, 4, L]), ALU.mult,
                            [self.tps[bX], self.tMASK], [T["XTs"]])
                if LVL < 4:
                    continue
                NM = 8
                PWv = [p[:, 0:NM * 2 * L].rearrange("p (i a t) -> p i a t", i=NM, a=2) for p in PW]
                PTv = [p[:, 0:NM * L].rearrange("p (i t) -> p i t", i=NM) for p in PT]
                for j in range(NCk):
                    for h in range(2):
                        i = 2 * j + h
                        self.cp("pool", PWv[0][0:L, i, 0, :], LLv[0:L, j, h, 2, :], [T["LLs"]], [T["PW", 0]])
                        self.cp("pool", PWv[0][0:L, i, 1, :], self.IDENTB[0:L, 0:L], [self.tMASK], [T["PW", 0]])
                        self.cp("pool", PTv[0][0:L, i, :], XTv[0:L, j, h, :], [T["XTs"]], [T["PT", 0]])
                nlev = 6 if L == 64 else 2
                cur = 0
                mpb = 4 if L == 64 else 8
                for lev in range(nlev):
                    nxt = 1 - cur
                    last = lev == nlev - 1
                    for i0 in range(0, NM, mpb):
                        bI = self.bank()
                        PA = self.ps[bI][:, 0:mpb * 2 * L].rearrange("p (i a t) -> p i a t", i=mpb, a=2)
                        for ii in range(mpb):
                            i = i0 + ii
                            self.mm(PA[0:L, ii], PTv[cur][0:L, i, :], PWv[cur][0:L, i], True, True,
                                    [T["PT", cur], T["PW", cur]], [self.tps[bI]], signal=(ii == mpb - 1))
                        if not last:
                            self.cp("act", PWv[nxt][0:L, i0:i0 + mpb, 0, :], PA[0:L, :, 0, :], [self.tps[bI]], [T["PW", nxt]])
                        self.tt("dve", PWv[nxt][0:L, i0:i0 + mpb, 1, :], PA[0:L, :, 1, :], PWv[cur][0:L, i0:i0 + mpb, 1, :], ALU.add,
                                [self.tps[bI], T["PW", cur]], [T["PW", nxt]])
                    if not last:
                        bJ = self.bank()
                        PB = self.ps[bJ][:, 0:NM * L].rearrange("p (i t) -> p i t", i=NM)
                        for i in range(NM):
                            self.mm(PB[0:L, i, :], PWv[cur][0:L, i, 0, :], PTv[cur][0:L, i, :], True, True,
                                    [T["PT", cur], T["PW", cur]], [self.tps[bJ]], signal=(i == NM - 1))
                        self.cp("act", PTv[nxt][0:L], PB[0:L], [self.tps[bJ]], [T["PT", nxt]])
                    cur = nxt
                Wv = PWv[cur]
                tWv = T["PW", cur]
                if LVL < 5:
                    continue
                YTv = YT[:, 0:4 * 128].rearrange("p (j m) -> p j m", j=4)
                for j in range(NCk):
                    if sample:
                        Hc, Hbc, Hdc = HS[:, j, :], HSb[:, j, :], HSd[:, j, :]
                        tH, tHb, tHd = T["HS", j], T["HSb", j], T["HSd", j]
                    else:
                        Hc, Hbc, Hdc = H, Hb, H0d
                        tH, tHb, tHd = T["H"], T["Hb"], T["H0d"]
                    DL = EP[:, (j + 1) * L - 1:(j + 1) * L]
                    bS = self.bank()
                    PG = self.ps[bS][0:L, 0:128]
                    PU = self.ps[bS][0:L, 128:256]
                    PY = self.ps[bS][0:L, 256:384]
                    bH = self.bank()
                    PH = self.ps[bH][:, 0:64]
                    tS = self.tps[bS]
                    tSH = self.tps[bH]
                    for h in range(2):
                        hs = slice(64 * h, 64 * h + 64)
                        self.mm(PG[:, hs], LLv[0:L, j, h, 0, :], VTt[0:L, j, hs], True, False, [T["LLs"], T["VTt"]], [tS])
                        self.mm(PG[:, hs], KR[hs, 0, cs(j)], Hbc[hs, :], False, True, [T["KR"], tHb], [tS], signal=True)
                    self.cp("act", Gs[0:L, :], PG, [tS], [T["Gs"]])
                    for h in range(2):
                        hs = slice(64 * h, 64 * h + 64)
                        self.mm(PU[:, hs], Wv[0:L, 2 * j + h, 1, :], Gs[0:L, hs], True, True, [tWv, T["Gs"]], [tS], signal=True)
                    self.act(NU[0:L, :], PU, AF.Identity, [tS], [T["NU"]], scale=-1.0)
                    for h in range(2):
                        hs = slice(64 * h, 64 * h + 64)
                        self.mm(PY[:, hs], LLv[0:L, j, h, 1, :], VTt[0:L, j, hs], True, False, [T["LLs"], T["VTt"]], [tS])
                        self.mm(PY[:, hs], LLv[0:L, j, h, 3, :], NU[0:L, hs], False, False, [T["LLs"], T["NU"]], [tS])
                        self.mm(PY[:, hs], KR[hs, 1, cs(j)], Hbc[hs, :], False, True, [T["KR"], tHb], [tS], signal=True)
                    self.cp("act", YTv[0:L, j, :], PY, [tS], [T["YT"]])
                    for h in range(2):
                        hs = slice(64 * h, 64 * h + 64)
                        self.mm(PH[hs, :], KTt[0:L, j, hs], VTt[0:L, j, hs], True, False, [T["KTt"], T["VTt"]], [tSH])
                        self.mm(PH[hs, :], BTt[0:L, j, hs], NU[0:L, hs], False, True, [T["BTt"], T["NU"]], [tSH], signal=True)
                    self.ts("pool", Hdc, Hc, DL, None, ALU.mult, None, [tH, T["EP"]], [tHd])
                    self.stt(Hc, PH, DL, Hdc, ALU.mult, ALU.add, [tSH, T["EP"], tHd], [tH])
                    self.cp("act", Hbc, Hc, [tH], [tHb])
                if sample:
                    for jj in range(4):
                        kb.dma("sp", d["o_rw_Ss"][4 * q + jj, 2 * c:2 * c + 2].rearrange("h k v -> (h k) v"), HS[:, jj, :],
                               reads=[T["HS", jj]])
                if LVL < 6:
                    continue
                G8 = 8
                YT3 = YT[:, 0:512].rearrange("p (g v) -> p g v", g=G8)
                SQ3 = SQ2[:, 0:512].rearrange("p (g v) -> p g v", g=G8)
                SUMv, VARv, RSTv = STAT[:, 0:8], STAT[:, 8:16], STAT[:, 16:24]
                self.reduce(SUMv[0:L, :], YT3[0:L], ALU.add, [T["YT"]], [T["SUM"]])
                self.ts("dve", SUMv[0:L, :], SUMv[0:L, :], 1.0 / 64, None, ALU.mult, None, [T["SUM"]], [T["SUM"]])
                self.tt("dve", YT3[0:L], YT3[0:L], SUMv[0:L, :].unsqueeze(2).to_broadcast([L, G8, 64]), ALU.subtract,
                        [T["YT"], T["SUM"]], [T["YT"]])
                self.act(SQ2[0:L, 0:512], YT[0:L, 0:512], AF.Square, [T["YT"]], [T["SQ2"]])
                self.reduce(VARv[0:L, :], SQ3[0:L], ALU.add, [T["SQ2"]], [T["VAR"]])
                self.act(RSTv[0:L, :], VARv[0:L, :], AF.Ln, [T["VAR"], self.tCONST], [T["RST"]], scale=1.0 / 64, bias=self.GNEPSC[0:L, :])
                self.act(RSTv[0:L, :], RSTv[0:L, :], AF.Exp, [T["RST"]], [T["RST"]], scale=-0.5)
                self.tt("dve", YT3[0:L], YT3[0:L], RSTv[0:L, :].unsqueeze(2).to_broadcast([L, G8, 64]), ALU.mult,
                        [T["YT"], T["RST"]], [T["YT"]])
                bY = self.bank()
                PYF = self.ps[bY][:, 0:n]
                for j in range(NCk):
                    self.tr(PYF[:, cs(j)], YTv[0:L, j, :], self.IDENTF[0:L, 0:L], [T["YT"], self.tMASK], [self.tps[bY]])
                self.act(w(YF), PYF, AF.Identity, [self.tps[bY], tV], [T["YF"]], scale=self.vcol("gn_w", c), bias=self.vcol("gn_b", c))
                self.tt("pool", w(YF), w(YF), w(BONUS), ALU.add, [T["YF"], T["BONUS"]], [T["YF"]])
                self.tt("dve", w(YG), w(YF), w(Gf), ALU.mult, [T["YF"], T["Gf"]], [T["YG"]])
                for dc0 in range(0, NCH, 2):
                    bO = self.bank()
                    for k2 in range(2):
                        dc = dc0 + k2
                        PO = self.ps[bO][:, 256 * k2:256 * k2 + n]
                        self.mm(PO, WO[:, dc * 128:(dc + 1) * 128], w(YG), True, True, [tWc["o"], T["YG"]], [self.tps[bO]], signal=True)
                        self.tt("dve", self.X[:, dc, tsl], PO, self.X[:, dc, tsl], ALU.add, [self.tps[bO], self.tX[dc, ti5]], [self.tX[dc, ti5]])
            kb.dma("sp", d["o_rw_Sp"][2 * c:2 * c + 2].rearrange("h k v -> (h k) v"), H, reads=[T["H"]])
        ar.release(m0)
        kb.barrier()

    def mlstm(self):
        kb, d, ar = self.kb, self.d, self.ar
        m0 = ar.mark()
        gname = "norm_mix1"
        XN, tXN = self.XN, self.tXN
        T = TokMap()
        NEG = -1.0e30
        EKA = ar.f32(NT)
        EQA = ar.f32(NT)
        EMTT = ar.f32(48 * 8).rearrange("p (j h) -> p j h", h=8)
        self.EMTP = EMTT[:, 0:32, :]
        EMTS = EMTT[:, 32:48, :]
        self.BBP = ar.f32(2)
        self.ABP = ar.f32(2)
        MLV = ar.f32(8 * 10).rearrange("p (h k) -> p h k", h=8)
        BIF = ar.f32(4)
        M0T = ar.f32(NS)
        MOUT = ar.f32(17)
        SEL = ar.f32(8 * 64).rearrange("p (h m) -> p h m", h=8)
        CONVO = ar.f32(8 * 2 * 17 * 3).rearrange("p (h w s k) -> p h w s k", h=8, w=2, s=17)
        tEK, tEQ = TokMap(), TokMap()
        kb.dma("sp", MLV[0:64], d["mlv"], writes=[T["MLV"]])
        kb.dma("sp", BIF[0:8, 0:2], d["bif"], writes=[T["BIF"]])
        kb.dma("sp", M0T[0:8, :], d["ml_m0T"], writes=[T["M0T"]])
        self.ts("dve", BIF[0:8, 2:3], BIF[0:8, 1:2], -1.0, None, ALU.mult, None, [T["BIF"]], [T["BIF"]])
        self.cp("pool", SEL[0:8], self.IDENTF[0:8, 0:8].unsqueeze(2).to_broadcast([8, 8, 64]), [self.tMASK], [T["SEL"]])

        m1 = ar.mark()
        WIF = ar.bf16(NCH * 16).rearrange("p (k m) -> p k m", k=NCH)
        for K in range(NCH):
            kb.dma("pool", WIF[:, K, :], d["ml_w_in"][0, K * 128:(K + 1) * 128, 3072:3088], writes=[T["WIF"]])
        SQ = [ar.bf16(512) for _ in range(2)]
        LN = ar.f32(512)
        RS = ar.f32(512)
        tSQ = TokMap()
        tLN, tRS = Tok("ln"), Tok("rs")
        LI, LF, BB, AA = [ar.f32(512) for _ in range(4)]
        ABX = ar.f32(513)
        D0, D1, TMPg, MTg, EMg = [ar.f32(512) for _ in range(5)]
        for ti, (t0, n) in enumerate(TILES):
            sample = t0 >= SEQ
            Lc = 4 if sample else 64
            nck = n // Lc

            def out_fn(c, rs, trs, ti=ti, t0=t0, n=n):
                self.stt(XN[:, c, t0:t0 + n], self.X[:, c, t0:t0 + n], self.vcol(gname, c), rs, ALU.mult, ALU.mult,
                         [self.tX[c, ti], trs, self.tVEC], [tXN[c, ti]])
            self.rmsnorm_tile(ti, gname, out_fn, (SQ, tSQ, LN, tLN, RS, tRS, self.bank()))
            bI, bF = self.bank(), self.bank()
            PI, PF = self.ps[bI][0:8, :n], self.ps[bF][0:8, :n]
            for K in range(NCH):
                self.mm(PI, WIF[:, K, 0:8], XN[:, K, t0:t0 + n], K == 0, K == NCH - 1, [T["WIF"], tXN[K, ti]], [self.tps[bI]])
            for K in range(NCH):
                self.mm(PF, WIF[:, K, 8:16], XN[:, K, t0:t0 + n], K == 0, K == NCH - 1, [T["WIF"], tXN[K, ti]], [self.tps[bF]])
            g = lambda a: a[0:8, 0:n]
            self.act(g(LI), PI, AF.Identity, [self.tps[bI], T["BIF"]], [T["LI"]], bias=BIF[0:8, 0:1])
            self.act(g(TMPg), PF, AF.Exp, [self.tps[bF], T["BIF"]], [T["TMP"]], scale=-1.0, bias=BIF[0:8, 2:3])
            self.act(g(LF), g(TMPg), AF.Ln, [T["TMP"], self.tCONST], [T["LF"]], bias=self.ONEC[0:8, :])
            self.memset("dve", g(D0), 1.0, [T["D0"]])
            init = 0.0
            rd = []
            if sample:
                self.memset("dve", g(D0).rearrange("p (s t) -> p s t", t=TS)[:, :, 0:1], 0.0, [T["D0"]])
            elif ti > 0:
                init = self.BBP[0:8, 0:1]
                rd = [T["BBprev"]]
            self.scan(g(BB), g(D0), g(LF), init, ALU.mult, ALU.subtract, [T["D0"], T["LF"]] + rd, [T["BB"]])
            self.tt("dve", g(AA), g(LI), g(BB), ALU.subtract, [T["LI"], T["BB"]], [T["AA"]])
            ab = ABX[0:8, 1:1 + n]
            if sample:
                self.memset("dve", g(D1), 0.0, [T["D1"]])
                self.memset("dve", g(D1).rearrange("p (s t) -> p s t", t=TS)[:, :, 0:1], NEG, [T["D1"]])
                a3 = g(AA).rearrange("p (s t) -> p s t", t=TS)
                self.tt("dve", a3[:, :, 0], a3[:, :, 0], M0T[0:8, :], ALU.max, [T["AA"], T["M0T"]], [T["AA"]])
                self.scan(ab, g(D1), g(AA), 0.0, ALU.add, ALU.max, [T["D1"], T["AA"]], [T["ABX"]])
                rho = M0T[0:8, :].unsqueeze(2).to_broadcast([8, NS, TS])
                rtok = [T["M0T"]]
                a_v = g(AA).rearrange("p (s t) -> p s t", t=TS)
                ab_v = ab.rearrange("p (s t) -> p s t", t=TS)
                ek_v = g(TMPg).rearrange("p (s t) -> p s t", t=TS)
                eq_v = g(D0).rearrange("p (s t) -> p s t", t=TS)
                self.tt("dve", g(AA), g(LI), g(BB), ALU.subtract, [T["LI"], T["BB"], T["ABX"]], [T["AA"]])
            else:
                self.memset("dve", g(D1), 0.0, [T["D1"]])
                if ti == 0:
                    self.memset("dve", ABX[0:8, 0:1], 0.0, [T["ABX"]])
                    ainit = 0.0
                else:
                    self.cp("dve", ABX[0:8, 0:1], self.ABP[0:8, 0:1], [T["ABprev"]], [T["ABX"]])
                    ainit = self.ABP[0:8, 0:1]
                self.scan(ab, g(D1), g(AA), ainit, ALU.add, ALU.max, [T["D1"], T["AA"], T["ABX"]] + ([T["ABprev"]] if ti else []), [T["ABX"]])
                rho = ABX[0:8, 0:n].rearrange("p (j l) -> p j l", l=64)[:, :, 0:1].to_broadcast([8, nck, 64])
                rtok = [T["ABX"]]
                a_v = g(AA).rearrange("p (j l) -> p j l", l=64)
                ab_v = ab.rearrange("p (j l) -> p j l", l=64)
                ek_v = g(TMPg).rearrange("p (j l) -> p j l", l=64)
                eq_v = g(D0).rearrange("p (j l) -> p j l", l=64)
            self.tt("dve", ek_v, a_v, rho, ALU.subtract, [T["AA"]] + rtok, [T["TMP"]])
            self.act(EKA[0:8, t0:t0 + n], g(TMPg), AF.Exp, [T["TMP"]], [tEK[ti]])
            self.tt("dve", eq_v, rho, ab_v, ALU.subtract, [T["ABX"], T["D0"]] + rtok, [T["D0"]])
            self.act(EQA[0:8, t0:t0 + n], g(D0), AF.Exp, [T["D0"]], [tEQ[ti]])
            self.tt("dve", g(MTg), g(BB), ab, ALU.add, [T["BB"], T["ABX"]], [T["MT"]])
            self.act(g(EMg), g(MTg), AF.Exp, [T["MT"]], [T["EM"]], scale=-1.0)
            bT = self.bank()
            for j in range(nck):
                self.tr(self.ps[bT][0:Lc, j * 8:(j + 1) * 8], EMg[0:8, j * Lc:(j + 1) * Lc], self.IDENTF[0:8, 0:8],
                        [T["EM"], self.tMASK], [self.tps[bT]])
            cb0 = t0 // 64 if not sample else 32
            if sample:
                self.cp("act", EMTS[0:Lc, 0:nck, :], self.ps[bT][0:Lc, 0:nck * 8].rearrange("p (j h) -> p j h", h=8),
                        [self.tps[bT]], [T["EMTS"]])
                self.cp("pool", MOUT[0:8, 1:17], g(MTg).rearrange("p (s t) -> p s t", t=TS)[:, :, 3], [T["MT"]], [T["MOUT"]])
            else:
                self.cp("act", self.EMTP[0:Lc, cb0:cb0 + nck, :], self.ps[bT][0:Lc, 0:nck * 8].rearrange("p (j h) -> p j h", h=8),
                        [self.tps[bT]], [T["EMTP"]])
                if ti == 3:
                    self.cp("pool", MOUT[0:8, 0:1], MTg[0:8, n - 1:n], [T["MT"]], [T["MOUT"]])
                self.cp("pool", self.BBP[0:8, 0:1], BB[0:8, n - 1:n], [T["BB"]], [T["BBprev"]])
                self.cp("pool", self.ABP[0:8, 0:1], ABX[0:8, n:n + 1], [T["ABX"]], [T["ABprev"]])
        ar.release(m1)
        kb.barrier()
        kb.dma("sp", d["o_ml_m"], MOUT[0:8, :], reads=[T["MOUT"]])

        WIN = ar.bf16(NCH * 384).rearrange("p (k m) -> p k m", k=NCH)
        WOh = ar.bf16(D)
        RAW = [ar.f32(520) for _ in range(2)]
        ACC = [ar.f32(512) for _ in range(2)]
        SIL = [ar.f32(512) for _ in range(2)]
        QP = ar.bf16(512)
        KP = ar.bf16(512)
        Osig = ar.f32(512)
        LAMB = ar.f32(512)
        VA = ar.bf16(8 * 130).rearrange("p (j m) -> p j m", j=8)
        STs = [ar.bf16(64) for _ in range(2)]
        KTt = ar.bf16(8 * 64).rearrange("p (j m) -> p j m", j=8)
        HT = ar.f32(8 * 128).rearrange("p (j m) -> p j m", j=8)
        SQH = ar.f32(8 * 128)
        STATH = ar.f32(32)
        DEN = ar.f32(16)
        HG = ar.bf16(512)
        C = ar.f32(130)
        Cd = ar.f32(130)
        Cb = ar.bf16(130)
        CS = ar.f32(4 * 130).rearrange("p (s m) -> p s m", s=4)
        CSd = ar.f32(4 * 130).rearrange("p (s m) -> p s m", s=4)
        CSb = ar.bf16(4 * 130).rearrange("p (s m) -> p s m", s=4)
        main_tiles = [(t0, 512, 64, 8) for t0 in range(0, SEQ, 512)] + [(SEQ + 16 * q, 16, 4, 4) for q in range(4)]
        import os as _os
        if "ML_TILES" in _os.environ:
            main_tiles = [main_tiles[int(i)] for i in _os.environ["ML_TILES"].split(",")]
        for h in range(int(_os.environ.get("ML_NH", 8))):
            for K in range(NCH):
                rows = slice(K * 128, (K + 1) * 128)
                kb.dma("pool", WIN[:, K, 0:64], d["ml_w_in"][0, rows, h * 64:(h + 1) * 64], writes=[T["WIN"]])
                kb.dma("pool", WIN[:, K, 64:128], d["ml_w_in"][0, rows, 512 + h * 64:512 + (h + 1) * 64], writes=[T["WIN"]])
                kb.dma("pool", WIN[:, K, 128:256], d["ml_w_in"][0, rows, 1024 + h * 128:1024 + (h + 1) * 128], writes=[T["WIN"]])
                kb.dma("pool", WIN[:, K, 256:384], d["ml_w_in"][0, rows, 2048 + h * 128:2048 + (h + 1) * 128], writes=[T["WIN"]])
            kb.dma("pool", WOh, d["ml_w_out"][0, h * 128:(h + 1) * 128, :], writes=[T["WO"]])
            self.memset("pool", C[0:64], 0.0, [T["C"]])
            self.memset("pool", Cb[0:64], 0.0, [T["Cb"]])
            for w_ in range(2):
                self.memset("pool", RAW[w_][0:64, 0:3], 0.0, [T["RAW", w_]])
            self.memset("pool", VA[0:64, :, 128:129], 1.0, [T["VA"]])
            for (t0, n, L, NCk) in main_tiles:
                sample = t0 >= SEQ
                ti5 = min(t0 // 512, 4)
                tsl = slice(t0, t0 + n)
                cs = lambda j: slice(j * L, (j + 1) * L)
                xt = [tXN[K, ti5] for K in range(NCH)]
                if sample:
                    q4 = (t0 - SEQ) // 16
                    for s in range(4):
                        kb.dma("sp", CS[0:64, s, 0:129], d["ml_C0T"][4 * q4 + s, h], writes=[T["CS", s]])
                        self.cp("act", CSb[0:64, s, 0:129], CS[0:64, s, 0:129], [T["CS", s]], [T["CSb", s]])
                    for w_ in range(2):
                        rv = RAW[w_][0:64, 0:28].rearrange("p (s u) -> p s u", u=7)
                        kb.dma("sp", rv[:, :, 0:3], d["ml_convT"][h, w_, :, 4 * q4:4 * q4 + 4, :], writes=[T["RAW", w_]])
                bQ, bK, bO = self.bank(), self.bank(), self.bank()
                PQ, PK, PO_ = self.ps[bQ][0:64, :n], self.ps[bK][0:64, :n], self.ps[bO][:, :n]
                for (P_, cols, bb) in ((PQ, slice(0, 64), bQ), (PK, slice(64, 128), bK), (PO_, slice(256, 384), bO)):
                    for K in range(NCH):
                        self.mm(P_, WIN[:, K, cols], XN[:, K, tsl], K == 0, K == NCH - 1, [T["WIN"], xt[K]], [self.tps[bb]])
                self.act(Osig[:, :n], PO_, AF.Sigmoid, [self.tps[bO]], [T["O"]])
                for j0 in range(0, NCk, 2):
                    bV = self.bank()
                    for jj in range(min(2, NCk - j0)):
                        j = j0 + jj
                        PVt = self.ps[bV][0:L, jj * 128:(jj + 1) * 128]
                        for K in range(NCH):
                            self.mm(PVt, XN[:, K, t0 + j * L:t0 + (j + 1) * L], WIN[:, K, 128:256], K == 0, K == NCH - 1,
                                    [T["WIN"], xt[K]], [self.tps[bV]])
                    nj = min(2, NCk - j0)
                    self.cp("act", VA[0:L, j0:j0 + nj, 0:128], self.ps[bV][0:L, 0:nj * 128].rearrange("p (j m) -> p j m", m=128),
                            [self.tps[bV]], [T["VA"]])
                for w_, (P_, bb) in enumerate(((PQ, bQ), (PK, bK))):
                    mv = lambda k_: MLV[0:64, h, 5 * w_ + k_:5 * w_ + k_ + 1]
                    R_ = RAW[w_]
                    if sample:
                        rv = R_[0:64, 0:28].rearrange("p (s u) -> p s u", u=7)
                        self.cp("act", rv[:, :, 3:7], P_.rearrange("p (s t) -> p s t", t=TS), [self.tps[bb]], [T["RAW", w_]])
                        taps = [rv[:, :, k_:k_ + 4] for k_ in range(4)]
                        acc = ACC[w_][0:64, 0:n].rearrange("p (s t) -> p s t", t=TS)
                        self.cp("pool", CONVO[0:64, h, w_, 1 + 4 * q4:5 + 4 * q4, :], rv[:, :, 4:7], [T["RAW", w_]], [T["CONVO"]])
                    else:
                        self.cp("act", R_[0:64, 3:3 + n], P_, [self.tps[bb]], [T["RAW", w_]])
                        taps = [R_[0:64, k_:k_ + n] for k_ in range(4)]
                        acc = ACC[w_][0:64, 0:n]
                    self.ts("dve", acc, taps[0], mv(0), mv(4), ALU.mult, ALU.add, [T["RAW", w_], T["MLV"]], [T["ACC", w_]])
                    for k_ in range(1, 4):
                        self.stt(acc, taps[k_], mv(k_), acc, ALU.mult, ALU.add, [T["RAW", w_], T["MLV"], T["ACC", w_]], [T["ACC", w_]])
                    self.act(SIL[w_][0:64, 0:n], ACC[w_][0:64, 0:n], AF.Silu, [T["ACC", w_]], [T["SIL", w_]])
                    if not sample:
                        if t0 + n == SEQ:
                            self.cp("pool", CONVO[0:64, h, w_, 0, :], R_[0:64, n:n + 3], [T["RAW", w_]], [T["CONVO"]])
                        self.cp("pool", R_[0:64, 0:3], R_[0:64, n:n + 3], [T["RAW", w_], T["ACC", w_]], [T["RAW", w_]])
                bM, bM2 = self.bank(), self.bank()
                PBK, PBQ = self.ps[bM][0:64, 0:n], self.ps[bM2][0:64, 0:n]
                tBQ = self.tps[bM2]
                self.mm(PBK, SEL[0:8, h, :], EKA[0:8, tsl], True, True, [T["SEL"], tEK[ti5]], [self.tps[bM]])
                self.mm(PBQ, SEL[0:8, h, :], EQA[0:8, tsl], True, True, [T["SEL"], tEQ[ti5]], [tBQ])
                self.tt("dve", KP[0:64, 0:n], SIL[1][0:64, 0:n], PBK, ALU.mult, [T["SIL", 1], self.tps[bM]], [T["KP"]])
                self.stt(QP[0:64, 0:n], SIL[0][0:64, 0:n], 0.125, PBQ, ALU.mult, ALU.mult, [T["SIL", 0], tBQ], [T["QP"]])
                self.cp("act", LAMB[0:64, 0:n], PBQ, [tBQ], [T["LAM"]])
                LAM = LAMB
                bT = self.bank()
                PTr = self.ps[bT].bitcast(BF16)
                for j in range(NCk):
                    self.tr(PTr[0:L, j * 64:(j + 1) * 64], KP[0:64, cs(j)], self.IDENTB[0:64, 0:64], [T["KP"], self.tMASK], [self.tps[bT]])
                self.cp("act", KTt[0:L, 0:NCk, :], PTr[0:L, 0:NCk * 64].rearrange("p (j m) -> p j m", m=64), [self.tps[bT]], [T["KTt"]])
                for j in range(NCk):
                    if sample:
                        Cc, Cbc, Cdc = CS[0:64, j, 0:129], CSb[0:64, j, 0:129], CSd[0:64, j, 0:129]
                        tC, tCb, tCd = T["CS", j], T["CSb", j], T["CSd", j]
                        em = EMTS[0:L, j + 4 * q4, h:h + 1]
                        tem = T["EMTS"]
                    else:
                        Cc, Cbc, Cdc = C[0:64, 0:129], Cb[0:64, 0:129], Cd[0:64, 0:129]
                        tC, tCb, tCd = T["C"], T["Cb"], T["Cd"]
                        em = self.EMTP[0:L, t0 // 64 + j, h:h + 1]
                        tem = T["EMTP"]
                    bS = self.bank()
                    PST = self.ps[bS][0:L, 0:L]
                    PND = self.ps[bS][0:L, 64:64 + 129]
                    PCS = self.ps[bS][0:64, 256:256 + 129]
                    tS = self.tps[bS]
                    st = STs[j % 2]
                    self.mm(PST, KP[0:64, cs(j)], QP[0:64, cs(j)], True, True, [T["KP"], T["QP"]], [tS])
                    self.tt("dve", st[0:L, 0:L], PST, self.MIU[0:L, 0:L], ALU.mult, [tS, self.tMASK], [T["ST", j % 2]])
                    self.mm(PND, QP[0:64, cs(j)], Cbc, True, False, [T["QP"], tCb], [tS])
                    self.mm(PND, st[0:L, 0:L], VA[0:L, j, 0:129], False, True, [T["ST", j % 2], T["VA"]], [tS])
                    self.mm(PCS, KTt[0:L, j, :], VA[0:L, j, 0:129], True, True, [T["KTt"], T["VA"]], [tS])
                    lam = LAM[0:64, (j + 1) * L - 1:(j + 1) * L]
                    self.ts("pool", Cdc, Cc, lam, None, ALU.mult, None, [tC, T["LAM"]], [tCd])
                    self.stt(Cc, PCS, lam, Cdc, ALU.mult, ALU.add, [tS, T["LAM"], tCd], [tC])
                    self.cp("act", Cbc, Cc, [tC], [tCb])
                    self.act(DEN[0:L, j:j + 1], PND[:, 128:129], AF.Abs, [tS], [T["DEN"]])
                    self.ts("dve", DEN[0:L, j:j + 1], DEN[0:L, j:j + 1], em, None, ALU.max, None, [T["DEN"], tem], [T["DEN"]])
                    self.recip(DEN[0:L, j:j + 1], DEN[0:L, j:j + 1], [T["DEN"]], [T["DEN"]])
                    self.act(HT[0:L, j, :], PND[:, 0:128], AF.Identity, [tS, T["DEN"]], [T["HT"]], scale=DEN[0:L, j:j + 1])
                if sample:
                    for s in range(4):
                        kb.dma("sp", d["o_ml_Cs"][4 * q4 + s, h], CS[0:64, s, 0:129], reads=[T["CS", s]])
                G = NCk
                HTf = HT[0:L, 0:G, :]
                SQv = SQH[0:L, 0:G * 128].rearrange("p (j m) -> p j m", m=128)
                self.act(SQv, HTf, AF.Square, [T["HT"]], [T["SQH"]])
                self.reduce(STATH[0:L, 0:G], SQv, ALU.add, [T["SQH"]], [T["STATH"]])
                self.act(STATH[0:L, 0:G], STATH[0:L, 0:G], AF.Ln, [T["STATH"], self.tCONST], [T["STATH"]], scale=1.0 / 128, bias=self.EPSC[0:L, :])
                self.act(STATH[0:L, 0:G], STATH[0:L, 0:G], AF.Exp, [T["STATH"]], [T["STATH"]], scale=-0.5)
                self.tt("dve", HTf, HTf, STATH[0:L, 0:G].unsqueeze(2).to_broadcast([L, G, 128]), ALU.mult, [T["HT"], T["STATH"]], [T["HT"]])
                bY = self.bank()
                PYF = self.ps[bY][:, 0:n]
                for j in range(NCk):
                    self.tr(PYF[:, cs(j)], HT[0:L, j, :], self.IDENTF[0:L, 0:L], [T["HT"], self.tMASK], [self.tps[bY]])
                self.stt(HG[:, 0:n], PYF, self.vcol("ml_norm_w", h), Osig[:, 0:n], ALU.mult, ALU.mult, [self.tps[bY], T["O"], self.tVEC], [T["HG"]])
                for dc in range(NCH):
                    bO2 = self.bank()
                    PO2 = self.ps[bO2][:, 0:n]
                    self.mm(PO2, WOh[:, dc * 128:(dc + 1) * 128], HG[:, 0:n], True, True, [T["WO"], T["HG"]], [self.tps[bO2]])
                    self.tt("dve", self.X[:, dc, tsl], PO2, self.X[:, dc, tsl], ALU.add, [self.tps[bO2], self.tX[dc, ti5]], [self.tX[dc, ti5]])
            kb.dma("sp", d["o_ml_Cp"][h], C[0:64, 0:129], reads=[T["C"]])
        kb.dma("sp", d["o_ml_conv"], CONVO[0:64], reads=[T["CONVO"]])
        ar.release(m0)
        kb.barrier()

    def ffn(self, L, which):
        kb, d, ar = self.kb, self.d, self.ar
        m = ar.mark()
        wg = d["ff%s_wg" % which][L]
        wu = d["ff%s_wu" % which][L]
        wd = d["ff%s_wd" % which][L]
        gname = "norm_ff%s%d" % (which, L)
        G = 4
        groups = [(f0, min(G, NFF - f0)) for f0 in range(0, NFF, G)]
        WG = [ar.bf16(NCH * 512).rearrange("p (c f) -> p c f", c=NCH) for _ in range(2)]
        WU = [ar.bf16(NCH * 512).rearrange("p (c f) -> p c f", c=NCH) for _ in range(2)]
        WD = [ar.bf16(G * D).rearrange("p (f o) -> p f o", f=G) for _ in range(2)]
        H = [ar.bf16(G * 512).rearrange("p (f n) -> p f n", f=G) for _ in range(2)]
        SG = [ar.f32(512) for _ in range(2)]
        SQ = [ar.bf16(512) for _ in range(2)]
        LN = ar.f32(512)
        RS = ar.f32(512)
        tW = TokMap()
        tH = TokMap()
        tSG = TokMap()
        tSQ = TokMap()
        tLN, tRS = Tok("ln"), Tok("rs")

        def load_group(gi):
            f0, nf = groups[gi]
            s = gi % 2
            for c in range(NCH):
                kb.dma("pool", WG[s][:, c, 0:nf * 128], wg[c * 128:(c + 1) * 128, f0 * 128:(f0 + nf) * 128],
                       writes=[tW["g", s, c]])
                kb.dma("pool", WU[s][:, c, 0:nf * 128], wu[c * 128:(c + 1) * 128, f0 * 128:(f0 + nf) * 128],
                       writes=[tW["u", s, c]])
            for fi in range(nf):
                kb.dma("pool", WD[s][:, fi, :], wd[(f0 + fi) * 128:(f0 + fi + 1) * 128, :], writes=[tW["d", s, fi]])

        load_group(0)
        load_group(1)

        for ti, (t0, n) in enumerate(TILES):
            def out_fn(c, rs, trs, ti=ti, t0=t0, n=n):
                kb.emit("dve", lambda e: e.scalar_tensor_tensor(out=self.XN[:, c, t0:t0 + n], in0=self.X[:, c, t0:t0 + n],
                                                                scalar=self.vcol(gname, c), in1=rs,
                                                                op0=ALU.mult, op1=ALU.mult),
                        reads=[self.tX[c, ti], trs, self.tVEC], writes=[self.tXN[c, ti]])
            self.rmsnorm_tile(ti, gname, out_fn, (SQ, tSQ, LN, tLN, RS, tRS, 4 + ti % 4))

        items = [(gi, ti) for gi in range(len(groups)) for ti in range(len(TILES))]
        po_rr = [0]

        def up(idx):
            gi, ti = items[idx]
            f0, nf = groups[gi]
            s = gi % 2
            hs = idx % 2
            t0, n = TILES[ti]
            for fi in range(nf):
                b = fi % 2
                pg = self.ps[b][:, :n]
                pu = self.ps[2 + b][:, :n]
                for c in range(NCH):
                    kb.emit("pe", lambda e, c=c, fi=fi, pg=pg: e.matmul(pg, WG[s][:, c, fi * 128:(fi + 1) * 128],
                                                                     self.XN[:, c, t0:t0 + n], start=(c == 0), stop=(c == NCH - 1)),
                            reads=[tW["g", s, c], self.tXN[c, ti]], writes=[self.tps[b]], signal=(c == NCH - 1))
                for c in range(NCH):
                    kb.emit("pe", lambda e, c=c, fi=fi, pu=pu: e.matmul(pu, WU[s][:, c, fi * 128:(fi + 1) * 128],
                                                                     self.XN[:, c, t0:t0 + n], start=(c == 0), stop=(c == NCH - 1)),
                            reads=[tW["u", s, c], self.tXN[c, ti]], writes=[self.tps[2 + b]], signal=(c == NCH - 1))
                kb.emit("act", lambda e, b=b, pg=pg: e.activation(out=SG[b][:, :n], in_=pg, func=AF.Silu),
                        reads=[self.tps[b]], writes=[tSG[b]])
                kb.emit("dve", lambda e, b=b, fi=fi, pu=pu: e.tensor_tensor(out=H[hs][:, fi, :n], in0=SG[b][:, :n], in1=pu, op=ALU.mult),
                        reads=[tSG[b], self.tps[2 + b]], writes=[tH[hs, fi]])

        def down(idx):
            gi, ti = items[idx]
            f0, nf = groups[gi]
            s = gi % 2
            hs = idx % 2
            t0, n = TILES[ti]
            for dc in range(NCH):
                bank = 4 + po_rr[0] % 4
                po_rr[0] += 1
                po = self.ps[bank][:, :n]
                for fi in range(nf):
                    kb.emit("pe", lambda e, fi=fi, dc=dc, po=po: e.matmul(po, WD[s][:, fi, dc * 128:(dc + 1) * 128], H[hs][:, fi, :n],
                                                                       start=(fi == 0), stop=(fi == nf - 1)),
                            reads=[tW["d", s, fi], tH[hs, fi]], writes=[self.tps[bank]], signal=(fi == nf - 1))
                kb.emit("dve", lambda e, dc=dc, po=po: e.scalar_tensor_tensor(out=self.X[:, dc, t0:t0 + n], in0=po, scalar=0.5,
                                                                           in1=self.X[:, dc, t0:t0 + n], op0=ALU.mult, op1=ALU.add),
                        reads=[self.tps[bank], self.tX[dc, ti]], writes=[self.tX[dc, ti]])

        ntile = len(TILES)
        for idx in range(len(items)):
            up(idx)
            if idx > 0:
                down(idx - 1)
                gi_prev, ti_prev = items[idx - 1]
                if ti_prev == ntile - 1 and gi_prev + 2 < len(groups):
                    load_group(gi_prev + 2)
        down(len(items) - 1)
        ar.release(m)
        kb.barrier()

    def final_norm(self):
        kb, d, ar = self.kb, self.d, self.ar
        m = ar.mark()
        SQ = [ar.bf16(512) for _ in range(2)]
        LN = ar.f32(512)
        RS = ar.f32(512)
        Y = [ar.f32(512) for _ in range(4)]
        tSQ, tY = TokMap(), TokMap()
        tLN, tRS = Tok("ln"), Tok("rs")
        rr = [0]
        for ti, (t0, n) in enumerate(TILES):
            def out_fn(c, rs, trs, ti=ti, t0=t0, n=n):
                s = rr[0] % 4
                rr[0] += 1
                kb.emit("dve", lambda e: e.scalar_tensor_tensor(out=Y[s][:, :n], in0=self.X[:, c, t0:t0 + n],
                                                                scalar=self.vcol("norm_final", c), in1=rs,
                                                                op0=ALU.mult, op1=ALU.mult),
                        reads=[self.tX[c, ti], trs, self.tVEC], writes=[tY[s]])
                kb.dma("sp", d["yT"][c * 128:(c + 1) * 128, t0:t0 + n], Y[s][:, :n], reads=[tY[s]])
            self.rmsnorm_tile(ti, "norm_final", out_fn, (SQ, tSQ, LN, tLN, RS, tRS, 4 + ti % 4))
        ar.release(m)
        kb.barrier()

    def dump_x(self):
        kb, d = self.kb, self.d
        for c in range(NCH):
            for ti, (t0, n) in enumerate(TILES):
                kb.dma("sp", d["yT"][c * 128:(c + 1) * 128, t0:t0 + n], self.X[:, c, t0:t0 + n], reads=[self.tX[c, ti]])

    def build(self):
        st = self.stages
        self.load_inputs()
        self.consts()
        for L in range(2):
            if "ffa%d" % L in st:
                self.ffn(L, "a")
            if "mix%d" % L in st and L == 0:
                self.rwkv()
            if "mix%d" % L in st and L == 1:
                self.mlstm()
            if "ffb%d" % L in st:
                self.ffn(L, "b")
        if "final" in st:
            self.final_norm()
        else:
            self.dump_x()
        self.kb.flush()
        return self.nc


ALL_STAGES = ("ffa0", "mix0", "ffb0", "ffa1", "mix1", "ffb1", "final")


def pack_vecs(inp):
    vecs = np.zeros((NVEC, D), np.float32)

    def put(name, v):
        vecs[VID[name]] = np.asarray(v, np.float32).reshape(D)

    for L in range(2):
        put("norm_ffa%d" % L, inp["norm_ffa"][L])
        put("norm_mix%d" % L, inp["norm_mix"][L])
        put("norm_ffb%d" % L, inp["norm_ffb"][L])
    put("norm_final", inp["norm_final"])
    for i in range(6):
        put("mu%d" % i, inp["rw_mu"][0, i])
    put("w0", inp["rw_w0"][0])
    put("a0", inp["rw_a0"][0])
    put("k_k", inp["rw_k_k"][0])
    put("k_a", inp["rw_k_a"][0])
    put("r_k", inp["rw_r_k"][0])
    put("gn_w", inp["rw_gn_w"][0])
    put("gn_b", inp["rw_gn_b"][0])
    for j in range(4):
        put("cw%d" % j, inp["ml_conv_w"][0, j])
    put("cb", inp["ml_conv_b"][0])
    put("ml_norm_w", inp["ml_norm_w"][0])
    return np.ascontiguousarray(vecs.reshape(NVEC, NCH, 128).transpose(2, 0, 1).reshape(128, NVEC * NCH))


def make_in_maps(inp):
    vecs = pack_vecs(inp)
    shared = {"vecs": vecs}
    for nm in ("ffa_wg", "ffa_wu", "ffb_wg", "ffb_wu", "ffa_wd", "ffb_wd"):
        shared[nm] = np.ascontiguousarray(inp[nm], dtype=np.float32)
    cw = np.asarray(inp["ml_conv_w"][0], np.float32)
    cbv = np.asarray(inp["ml_conv_b"][0], np.float32)
    mlv = np.zeros((64, 8, 10), np.float32)
    for w in range(2):
        for k in range(4):
            mlv[:, :, 5 * w + k] = cw[k, w * 512:(w + 1) * 512].reshape(8, 64).T
        mlv[:, :, 5 * w + 4] = cbv[w * 512:(w + 1) * 512].reshape(8, 64).T
    shared["mlv"] = mlv
    shared["bif"] = np.ascontiguousarray(np.asarray(inp["ml_b_if"][0], np.float32).reshape(2, 8).T)
    for nm in ("rw_wr", "rw_wk", "rw_wv", "rw_wo", "rw_w1", "rw_w2", "rw_a1", "rw_a2", "rw_g1", "rw_g2", "ml_w_in", "ml_w_out"):
        shared[nm] = np.ascontiguousarray(inp[nm], dtype=np.float32)
    maps = []
    for core in range(NCORES):
        xs = np.concatenate([inp["x_prompt"][core], inp["x_sample"][core * NS:(core + 1) * NS].reshape(NS * TS, D)], axis=0)
        m = dict(shared)
        m["xT"] = np.ascontiguousarray(xs.T.astype(np.float32))
        sq = slice(core * NS, (core + 1) * NS)
        sh = inp["state_rwkv_shift"][0, sq]
        m["shiftT"] = np.ascontiguousarray(sh.reshape(NS, NCH, 128).transpose(2, 1, 0).astype(np.float32))
        m["rw_S0T"] = np.ascontiguousarray(inp["state_rwkv_S"][0, sq].transpose(0, 1, 3, 2).astype(np.float32))
        m["ml_m0T"] = np.ascontiguousarray(inp["state_mlstm_m"][0, sq].T.astype(np.float32))
        c0 = np.concatenate([inp["state_mlstm_C"][0, sq].transpose(0, 1, 3, 2), inp["state_mlstm_n"][0, sq][..., None]], axis=-1)
        m["ml_C0T"] = np.ascontiguousarray(c0.astype(np.float32))
        cv = inp["state_mlstm_conv"][0, sq]
        m["ml_convT"] = np.ascontiguousarray(cv.reshape(NS, 3, 2, 8, 64).transpose(3, 2, 4, 0, 1).astype(np.float32))
        maps.append(m)
    return maps


def run(inp, stages=ALL_STAGES, trace=False):
    b = Builder(stages)
    nc = b.build()
    maps = make_in_maps(inp)
    res = run_bass_kernel_spmd(nc, maps, core_ids=list(range(NCORES)), trace=trace)
    return b, res


def assemble(results):
    f = np.float32
    yp = np.zeros((8, SEQ, D), f); ys = np.zeros((128, TS, D), f)
    p_S = np.zeros((1, 8, 16, 64, 64), f); p_sh = np.zeros((1, 8, D), f)
    p_C = np.zeros((1, 8, 8, 128, 64), f); p_n = np.zeros((1, 8, 8, 64), f); p_m = np.zeros((1, 8, 8), f)
    p_cv = np.zeros((1, 8, 3, D), f)
    s_S = np.zeros((1, 128, 16, 64, 64), f); s_sh = np.zeros((1, 128, D), f)
    s_C = np.zeros((1, 128, 8, 128, 64), f); s_n = np.zeros((1, 128, 8, 64), f); s_m = np.zeros((1, 128, 8), f)
    s_cv = np.zeros((1, 128, 3, D), f)
    for core in range(NCORES):
        r = results[core]
        sq = slice(core * NS, (core + 1) * NS)
        y = r["yT"].T
        yp[core] = y[:SEQ]
        ys[sq] = y[SEQ:].reshape(NS, TS, D)
        sho = r["o_shift"]
        p_sh[0, core] = sho[:, :, 0].T.reshape(D)
        s_sh[0, sq] = sho[:, :, 1:].transpose(2, 1, 0).reshape(NS, D)
        p_S[0, core] = r["o_rw_Sp"].transpose(0, 2, 1)
        s_S[0, sq] = r["o_rw_Ss"].transpose(0, 1, 3, 2)
        cp = r["o_ml_Cp"]
        p_C[0, core] = cp[:, :, 0:128].transpose(0, 2, 1)
        p_n[0, core] = cp[:, :, 128]
        cs_ = r["o_ml_Cs"]
        s_C[0, sq] = cs_[:, :, :, 0:128].transpose(0, 1, 3, 2)
        s_n[0, sq] = cs_[:, :, :, 128]
        mo = r["o_ml_m"]
        p_m[0, core] = mo[:, 0]
        s_m[0, sq] = mo[:, 1:].T
        cv = r["o_ml_conv"]
        cvt = cv.transpose(3, 4, 2, 1, 0).reshape(17, 3, D)
        p_cv[0, core] = cvt[0]
        s_cv[0, sq] = cvt[1:]
    return (yp, ys, p_S, p_sh, p_C, p_n, p_m, p_cv, s_S, s_sh, s_C, s_n, s_m, s_cv)


def kernel(**inp):
    b, res = run(inp)
    return assemble(res.results)
```

```python
import numpy as np
from contextlib import ExitStack
import concourse.bass as bass
import concourse.mybir as mybir
from concourse.bass_utils import run_bass_kernel_spmd

F32 = mybir.dt.float32
BF16 = mybir.dt.bfloat16
AF = mybir.ActivationFunctionType
ALU = mybir.AluOpType
AX = mybir.AxisListType

NCORES = 8
D = 1024
NCH = 8
SEQ = 2048
NS = 16
TS = 4
NT = SEQ + NS * TS
DFF = 2816
NFF = DFF // 128
EPS = 1e-6

ENGS = ("pe", "act", "dve", "pool", "sp")
SAME_ENGINE_SYNC = True

VID = {}
_v = 0
for _nm in ("norm_ffa0", "norm_ffa1", "norm_mix0", "norm_mix1", "norm_ffb0", "norm_ffb1", "norm_final",
            "mu0", "mu1", "mu2", "mu3", "mu4", "mu5", "w0", "a0", "k_k", "k_a", "r_k", "gn_w", "gn_b",
            "cw0", "cw1", "cw2", "cw3", "cb", "ml_norm_w"):
    VID[_nm] = _v
    _v += 1
NVEC = _v


class Tok:
    __slots__ = ("name", "w", "r", "excl")

    def __init__(self, name="", excl=False):
        self.name = name
        self.w = []
        self.r = []
        self.excl = excl


class TokMap(dict):
    def __missing__(self, key):
        t = Tok(str(key))
        self[key] = t
        return t


class KB:
    def __init__(self, n_dma_sems=12):
        self.nc = bass.Bass("TRN2", target_bir_lowering=False, dynamic_dma_scratch_size=8192)
        self.es = ExitStack()
        nc = self.nc
        self.sem = {}
        self.count = {}
        self.prog = {e: [] for e in ENGS}
        self.waited = {e: {} for e in ENGS}
        for e in ENGS:
            self.sem[e] = self.es.enter_context(nc.semaphore("s_" + e))
            self.count[e] = 0
        self.dsem = {}
        self.dval = {}
        self.dnext = {}
        for q in ("sp", "pool", "act"):
            self.dsem[q] = []
            for j in range(n_dma_sems):
                key = "d_%s_%d" % (q, j)
                self.sem[key] = self.es.enter_context(nc.semaphore(key))
                self.dsem[q].append(key)
                self.dval[key] = 0
            self.dnext[q] = 0
        self.ninstr = 0

    def _wait(self, eng, semkey, value):
        if value <= 0:
            return
        if self.waited[eng].get(semkey, 0) >= value:
            return
        self.waited[eng][semkey] = value
        sem = self.sem[semkey]
        self.prog[eng].append(lambda e, sem=sem, value=value: e.wait_ge(sem, value))

    def _deps(self, eng, reads, writes):
        deps = set()
        for t in reads:
            deps.update(t.w)
            if t.excl:
                deps.update(x for x in t.r if x[0] != eng)
        for t in writes:
            deps.update(t.w)
            deps.update(t.r)
        for (sk, v) in deps:
            if sk == eng and (eng == "pe" or not SAME_ENGINE_SYNC):
                continue
            self._wait(eng, sk, v)

    def emit(self, eng, fn, reads=(), writes=(), signal=True):
        self._deps(eng, reads, writes)
        self.ninstr += 1
        if signal:
            self.count[eng] += 1
            cid = (eng, self.count[eng])
            sem = self.sem[eng]
            self.prog[eng].append(lambda e, fn=fn, sem=sem: fn(e).then_inc(sem, 1))
        else:
            cid = (eng, self.count[eng] + 1)
            self.prog[eng].append(lambda e, fn=fn: fn(e))
        for t in reads:
            t.r.append(cid)
        for t in writes:
            t.w = [cid]
            t.r = []
        return cid

    def dma(self, q, out, in_, reads=(), writes=()):
        self._deps(q, reads, writes)
        j = self.dnext[q]
        self.dnext[q] = (j + 1) % len(self.dsem[q])
        key = self.dsem[q][j]
        self._wait(q, key, self.dval[key])
        self.dval[key] += 16
        cid = (key, self.dval[key])
        sem = self.sem[key]
        self.ninstr += 1
        self.prog[q].append(lambda e, out=out, in_=in_, sem=sem: e.dma_start(out=out, in_=in_).then_inc(sem, 16))
        for t in reads:
            t.r.append(cid)
        for t in writes:
            t.w = [cid]
            t.r = []
        return cid

    def barrier(self):
        for e in ENGS:
            for e2 in ENGS:
                if e2 != e:
                    self._wait(e, e2, self.count[e2])
            for q in self.dsem:
                for key in self.dsem[q]:
                    self._wait(e, key, self.dval[key])

    def flush(self):
        self.barrier()
        nc = self.nc
        prog = self.prog
        with nc.Block() as block:
            @block.tensor
            def _(e):
                for f in prog["pe"]:
                    f(e)

            @block.scalar
            def _(e):
                for f in prog["act"]:
                    f(e)

            @block.vector
            def _(e):
                for f in prog["dve"]:
                    f(e)

            @block.gpsimd
            def _(e):
                for f in prog["pool"]:
                    f(e)

            @block.sync
            def _(e):
                for f in prog["sp"]:
                    f(e)
        self.prog = {e: [] for e in ENGS}


class Arena:
    def __init__(self, ap, nwords):
        self.ap = ap
        self.n = nwords
        self.top = 0

    def mark(self):
        return self.top

    def release(self, m):
        self.top = m

    def f32(self, nwords):
        assert self.top + nwords <= self.n, ("arena overflow", self.top, nwords, self.n)
        a = self.ap[:, self.top:self.top + nwords]
        self.top += nwords
        return a

    def bf16(self, nelem):
        nwords = (nelem + 1) // 2
        a = self.f32(nwords).bitcast(BF16)
        return a[:, 0:nelem]


ARENA_WORDS = 55000
XNW = 1 + SEQ + NS * (TS + 1)
SOFF = 1 + SEQ
TILES = [(0, 512), (512, 512), (1024, 512), (1536, 512), (2048, 64)]


class Builder:
    def __init__(self, stages):
        self.stages = stages
        self.kb = KB()
        kb = self.kb
        nc = kb.nc
        self.nc = nc
        es = kb.es
        d = {}

        def din(name, shape):
            d[name] = nc.dram_tensor(name, list(shape), F32, kind="ExternalInput").ap()

        def dout(name, shape):
            d[name] = nc.dram_tensor(name, list(shape), F32, kind="ExternalOutput").ap()

        din("xT", (D, NT))
        din("vecs", (128, NVEC * 8))
        for nm in ("ffa_wg", "ffa_wu", "ffb_wg", "ffb_wu"):
            din(nm, (2, D, DFF))
        for nm in ("ffa_wd", "ffb_wd"):
            din(nm, (2, DFF, D))
        for nm in ("rw_wr", "rw_wk", "rw_wv", "rw_wo"):
            din(nm, (1, D, D))
        din("rw_w1", (1, D, 64)); din("rw_w2", (1, 64, D))
        din("rw_a1", (1, D, 64)); din("rw_a2", (1, 64, D))
        din("rw_g1", (1, D, 160)); din("rw_g2", (1, 160, D))
        din("ml_w_in", (1, D, 3088)); din("ml_w_out", (1, D, D))
        din("mlv", (64, 8, 10)); din("bif", (8, 2)); din("ml_m0T", (8, NS))
        din("ml_C0T", (NS, 8, 64, 129)); din("ml_convT", (8, 2, 64, NS, 3))
        din("shiftT", (128, NCH, NS))
        din("rw_S0T", (NS, 16, 64, 64))
        dout("yT", (D, NT))
        dout("o_ml_m", (8, 17)); dout("o_ml_Cp", (8, 64, 129)); dout("o_ml_Cs", (NS, 8, 64, 129))
        dout("o_ml_conv", (64, 8, 2, 17, 3))
        dout("o_shift", (128, NCH, 17))
        dout("o_rw_Sp", (16, 64, 64))
        dout("o_rw_Ss", (NS, 16, 64, 64))
        self.d = d
        self.out_names = [k for k in d if k == "yT" or k.startswith("o_")]

        arena_t = es.enter_context(nc.sbuf_tensor("arena", [128, ARENA_WORDS], F32))
        self.ar = Arena(arena_t, ARENA_WORDS)
        self.psall = es.enter_context(nc.psum_tensor("psall", [128, 8, 512], F32))
        self.ps = [self.psall[:, i, :] for i in range(8)]
        self.tps = [Tok("ps%d" % i, excl=True) for i in range(8)]
        self.bank_rr = 0
        self.bank_pe = {}

        ar = self.ar
        self.X = ar.f32(NCH * NT).rearrange("p (c n) -> p c n", c=NCH)
        self.tX = TokMap()
        self.VEC = ar.f32(NVEC * 8)
        self.tVEC = Tok("vec")
        self.ONES = ar.bf16(128)
        self.tONES = Tok("ones")
        self.XNraw = ar.bf16(NCH * XNW)
        self.XN = self.XNraw[:, 0:NCH * NT].rearrange("p (c n) -> p c n", c=NCH)
        self.XNS = self.XNraw.rearrange("p (c n) -> p c n", c=NCH)
        self.tXN = TokMap()


    BANK_GROUPS = {"A": (0, 1, 2, 3), "B": (4, 5), "C": (6, 7)}

    def bank(self, group=None):
        if group is None:
            b = self.bank_rr % 8
            self.bank_rr += 1
            return b
        if not hasattr(self, "_grr"):
            self._grr = {}
        k = self._grr.get(group, 0)
        self._grr[group] = k + 1
        g = self.BANK_GROUPS[group]
        return g[k % len(g)]

    def act(self, out, in_, func, reads, writes, **kw):
        return self.kb.emit("act", lambda e: e.activation(out=out, in_=in_, func=func, **kw), reads, writes)

    def cp(self, eng, out, in_, reads, writes):
        if eng == "act":
            return self.kb.emit("act", lambda e: e.activation(out=out, in_=in_, func=AF.Copy), reads, writes)
        return self.kb.emit(eng, lambda e: e.tensor_copy(out=out, in_=in_), reads, writes)

    def tt(self, eng, out, in0, in1, op, reads, writes):
        return self.kb.emit(eng, lambda e: e.tensor_tensor(out=out, in0=in0, in1=in1, op=op), reads, writes)

    def ts(self, eng, out, in0, s1, s2, op0, op1, reads, writes):
        if s2 is None:
            return self.kb.emit(eng, lambda e: e.tensor_scalar(out=out, in0=in0, scalar1=s1, scalar2=None, op0=op0), reads, writes)
        return self.kb.emit(eng, lambda e: e.tensor_scalar(out=out, in0=in0, scalar1=s1, scalar2=s2, op0=op0, op1=op1), reads, writes)

    def stt(self, out, in0, scalar, in1, op0, op1, reads, writes):
        return self.kb.emit("dve", lambda e: e.scalar_tensor_tensor(out=out, in0=in0, scalar=scalar, in1=in1, op0=op0, op1=op1),
                            reads, writes)

    def _pe_rows(self, lhsT, writes):
        K = lhsT.partition_size()
        base = lhsT.base_partition()
        tile = 32 if K <= 32 else (64 if K <= 64 else 128)
        lo, hi = (base // tile) * tile, (base // tile) * tile + tile
        if tile == 128:
            lo, hi = 0, 128
        for t in writes:
            for b in range(8):
                if t is self.tps[b]:
                    prev = self.bank_pe.get(b)
                    if prev is not None and (prev[1] <= lo or hi <= prev[0]):
                        self.kb._wait("pe", "pe", prev[2][1])
                    self.bank_pe[b] = [lo, hi, None]
        return tile < 128

    def _pe_done(self, writes, cid):
        for t in writes:
            for b in range(8):
                if t is self.tps[b] and self.bank_pe.get(b) is not None:
                    self.bank_pe[b][2] = cid

    def mm(self, out, lhsT, rhs, start, stop, reads, writes, signal=None):
        if signal is None:
            signal = stop
        if self._pe_rows(lhsT, writes):
            signal = True
        cid = self.kb.emit("pe", lambda e: e.matmul(out, lhsT, rhs, start=start, stop=stop), reads, writes, signal=signal)
        self._pe_done(writes, cid)
        return cid

    def tr(self, out, in_, ident, reads, writes):
        self._pe_rows(in_, writes)
        cid = self.kb.emit("pe", lambda e: e.transpose(out, in_, ident), reads, writes)
        self._pe_done(writes, cid)
        return cid

    def memset(self, eng, ap, val, writes):
        return self.kb.emit(eng, lambda e: e.memset(ap, val), (), writes)

    def scan(self, out, d0, d1, init, op0, op1, reads, writes):
        return self.kb.emit("dve", lambda e: e.tensor_tensor_scan(out=out, data0=d0, data1=d1, initial=init, op0=op0, op1=op1), reads, writes)

    def recip(self, out, in_, reads, writes):
        return self.kb.emit("dve", lambda e: e.reciprocal(out=out, in_=in_), reads, writes)

    def reduce(self, out, in_, op, reads, writes, axis=None):
        axis = AX.X if axis is None else axis
        return self.kb.emit("dve", lambda e: e.tensor_reduce(out=out, in_=in_, axis=axis, op=op), reads, writes)

    def vcol(self, name, c):
        j = VID[name] * 8 + c
        return self.VEC[:, j:j + 1]

    def load_inputs(self):
        kb, d = self.kb, self.d
        kb.dma("sp", self.VEC, d["vecs"][:, :], writes=[self.tVEC])
        for c in range(NCH):
            for ti, (t0, n) in enumerate(TILES):
                kb.dma("sp", self.X[:, c, t0:t0 + n], d["xT"][c * 128:(c + 1) * 128, t0:t0 + n],
                       writes=[self.tX[c, ti]])
        kb.emit("dve", lambda e: e.memset(self.ONES, 1.0), writes=[self.tONES])

    def _full_bank(self, b):
        self.bank_pe[b] = [0, 128, ("pe", 0)]

    def rmsnorm_tile(self, ti, gname, out_fn, scratch):
        kb = self.kb
        t0, n = TILES[ti]
        SQ, tSQ, LN, tLN, RS, tRS, bank = scratch
        ps = self.ps[bank][:, :n]
        self._full_bank(bank)
        for c in range(NCH):
            s = c % 2
            kb.emit("act", lambda e, c=c, s=s: e.activation(out=SQ[s][:, :n], in_=self.X[:, c, t0:t0 + n], func=AF.Square),
                    reads=[self.tX[c, ti]], writes=[tSQ[s]])
            kb.emit("pe", lambda e, c=c, s=s: e.matmul(ps, self.ONES, SQ[s][:, :n], start=(c == 0), stop=(c == NCH - 1)),
                    reads=[tSQ[s], self.tONES], writes=[self.tps[bank]], signal=True)
        kb.emit("act", lambda e: e.activation(out=LN[:, :n], in_=ps, func=AF.Ln, scale=1.0 / D, bias=self.EPSC),
                reads=[self.tps[bank], self.tCONST], writes=[tLN])
        kb.emit("act", lambda e: e.activation(out=RS[:, :n], in_=LN[:, :n], func=AF.Exp, scale=-0.5),
                reads=[tLN], writes=[tRS])
        for c in range(NCH):
            out_fn(c, RS[:, :n], tRS)

    def consts(self):
        kb, ar = self.kb, self.ar
        self.CONST = ar.f32(8)
        self.tCONST = Tok("const")
        self.EPSC = self.CONST[:, 0:1]
        self.ONEC = self.CONST[:, 1:2]
        self.NHALFC = self.CONST[:, 2:3]
        self.GNEPSC = self.CONST[:, 3:4]
        for col, val in ((0, EPS), (1, 1.0), (2, -0.5), (3, 64e-5)):
            kb.emit("dve", lambda e, col=col, val=val: e.memset(self.CONST[:, col:col + 1], val), (), [self.tCONST])
        ONESF = ar.f32(128)
        tO = Tok("onesf")
        self.memset("pool", ONESF, 1.0, [tO])
        self.IDENTF = ar.f32(128)
        self.IDENTB = ar.bf16(128)
        self.BONES = ar.bf16(128)
        self.tMASK = Tok("masks")
        kb.emit("pool", lambda e: e.affine_select(out=self.IDENTF, in_=ONESF, pattern=[[-1, 128]], compare_op=ALU.is_equal,
                                                  fill=0.0, base=0, channel_multiplier=1), [tO], [self.tMASK])
        self.cp("pool", self.IDENTB, self.IDENTF, [self.tMASK], [self.tMASK])
        self.memset("pool", self.BONES, 0.0, [self.tMASK])
        self.memset("pool", self.BONES[0:64, 0:64], 1.0, [self.tMASK])
        self.memset("pool", self.BONES[64:128, 64:128], 1.0, [self.tMASK])
        MSU = ar.f32(64)
        MIU = ar.f32(64)
        self.MASKXT = ar.f32(64)
        kb.emit("pool", lambda e: e.affine_select(out=MSU[0:64, :], in_=ONESF[0:64, 0:64], pattern=[[1, 64]], compare_op=ALU.is_gt,
                                                  fill=0.0, base=0, channel_multiplier=-1), [tO], [self.tMASK])
        kb.emit("pool", lambda e: e.affine_select(out=MIU[0:64, :], in_=ONESF[0:64, 0:64], pattern=[[1, 64]], compare_op=ALU.is_ge,
                                                  fill=0.0, base=0, channel_multiplier=-1), [tO], [self.tMASK])
        kb.emit("pool", lambda e: e.affine_select(out=self.MASKXT[0:64, :], in_=ONESF[0:64, 0:64], pattern=[[-1, 64]], compare_op=ALU.is_gt,
                                                  fill=0.0, base=0, channel_multiplier=1), [tO], [self.tMASK])
        self.ts("pool", self.MASKXT[0:64, :], self.MASKXT[0:64, :], -1.0, None, ALU.mult, None, [self.tMASK], [self.tMASK])
        self.MIU = MIU
        self.MASKLL = ar.f32(2 * 4 * 64).rearrange("p (h b t) -> p h b t", h=2, b=4)
        for h in range(2):
            self.cp("pool", self.MASKLL[0:64, h, 0, :], MSU[0:64, :], [self.tMASK], [self.tMASK])
            self.cp("pool", self.MASKLL[0:64, h, 1, :], MIU[0:64, :], [self.tMASK], [self.tMASK])
            self.ts("pool", self.MASKLL[0:64, h, 2, :], MSU[0:64, :], -1.0, None, ALU.mult, None, [self.tMASK], [self.tMASK])
            self.cp("pool", self.MASKLL[0:64, h, 3, :], MIU[0:64, :], [self.tMASK], [self.tMASK])
        self.SM64 = ar.f32(256)
        self.SM4 = ar.f32(16)
        self.memset("pool", self.SM64, 1.0, [self.tMASK])
        self.memset("pool", self.SM64.rearrange("p (j l) -> p j l", l=64)[:, :, 0:1], 0.0, [self.tMASK])
        self.memset("pool", self.SM4, 1.0, [self.tMASK])
        self.memset("pool", self.SM4.rearrange("p (j l) -> p j l", l=4)[:, :, 0:1], 0.0, [self.tMASK])
        self.NEGW0 = ar.f32(8)
        j = VID["w0"] * 8
        self.ts("dve", self.NEGW0, self.VEC[:, j:j + 8], -1.0, None, ALU.mult, None, [self.tVEC], [self.tMASK])

    def rwkv(self):
        kb, d, ar = self.kb, self.d, self.ar
        m0 = ar.mark()
        XNS = self.XNS
        tXNS = TokMap()
        gname = "norm_mix0"
        mu = lambda i, K: self.vcol("mu%d" % i, K)

        HWA = ar.bf16(NT)
        HG1 = ar.bf16(NT)
        HG2 = ar.bf16(NT)
        tHWA, tHG = TokMap(), TokMap()
        W2A2 = ar.bf16(D)
        G2A = ar.bf16(D)
        G2B = ar.bf16(D)
        tW2 = Tok("w2a2g2")
        SHO = ar.f32(NCH * 17).rearrange("p (c j) -> p c j", c=NCH)
        tSHO = Tok("sho")
        SHI = ar.f32(NCH * NS).rearrange("p (c j) -> p c j", c=NCH)
        tSHI = Tok("shi")

        kb.dma("pool", W2A2[0:64, :], d["rw_w2"][0], writes=[tW2])
        kb.dma("pool", W2A2[64:128, :], d["rw_a2"][0], writes=[tW2])
        kb.dma("pool", G2A, d["rw_g2"][0, 0:128, :], writes=[tW2])
        self.memset("pool", G2B, 0.0, [tW2])
        self.memset("pool", HG2, 0.0, [tHG[0]])
        kb.dma("pool", G2B[0:32, :], d["rw_g2"][0, 128:160, :], writes=[tW2])
        kb.dma("sp", SHI, d["shiftT"], writes=[tSHI])

        for c in range(NCH):
            self.memset("pool", XNS[:, c, 0:1], 0.0, [tXNS[c, "init"]])
            sv = XNS[:, c, SOFF:SOFF + NS * 5].rearrange("p (j u) -> p j u", u=5)
            self.cp("pool", sv[:, :, 0], SHI[:, c, :], [tSHI], [tXNS[c, "init"]])

        def xn_aps(K, t0, n):
            if t0 < SEQ:
                return XNS[:, K, 1 + t0:1 + t0 + n], XNS[:, K, t0:t0 + n]
            j0 = (t0 - SEQ) // TS
            nj = n // TS
            sv = XNS[:, K, SOFF + 5 * j0:SOFF + 5 * (j0 + nj)].rearrange("p (j u) -> p j u", u=5)
            return sv[:, :, 1:5], sv[:, :, 0:4]

        def xn_toks(K, t0):
            ti = min(t0 // 512, 4)
            return [tXNS[K, ti], tXNS[K, max(ti - 1, 0)], tXNS[K, "init"]]

        def pview(ps_ap, t0, n):
            if t0 < SEQ:
                return ps_ap
            return ps_ap.rearrange("p (j t) -> p j t", t=TS)

        def mixproj(out, wa, wb, cols, t0, n, wtok, ptok):
            o = pview(out, t0, n)
            for K in range(NCH):
                xa, xb = xn_aps(K, t0, n)
                self.mm(o, wa[:, K, cols], xa, K == 0, False, [wtok] + xn_toks(K, t0), [ptok])
                self.mm(o, wb[:, K, cols], xb, False, K == NCH - 1, [wtok] + xn_toks(K, t0), [ptok])

        def scale_w(raw, wb, Mcols, mu_list, tok):
            for K in range(NCH):
                for (cs, mi) in mu_list:
                    self.ts("pool", wb[:, K, cs], raw[:, K, cs], mu(mi, K), None, ALU.mult, None, [tok, self.tVEC], [tok])
            self.tt("pool", raw, raw, wb, ALU.subtract, [tok], [tok])

        m1 = ar.mark()
        W1A = ar.bf16(NCH * 128).rearrange("p (k m) -> p k m", k=NCH)
        W1B = ar.bf16(NCH * 128).rearrange("p (k m) -> p k m", k=NCH)
        G1A = ar.bf16(NCH * 160).rearrange("p (k m) -> p k m", k=NCH)
        G1B = ar.bf16(NCH * 160).rearrange("p (k m) -> p k m", k=NCH)
        tW1, tG1 = Tok("w1a1"), Tok("g1")
        for K in range(NCH):
            kb.dma("pool", W1A[:, K, 0:64], d["rw_w1"][0, K * 128:(K + 1) * 128, :], writes=[tW1])
            kb.dma("pool", W1A[:, K, 64:128], d["rw_a1"][0, K * 128:(K + 1) * 128, :], writes=[tW1])
            kb.dma("pool", G1A[:, K, :], d["rw_g1"][0, K * 128:(K + 1) * 128, :], writes=[tG1])
        scale_w(W1A, W1B, 128, [(slice(0, 64), 1), (slice(64, 128), 4)], tW1)
        scale_w(G1A, G1B, 160, [(slice(0, 160), 5)], tG1)
        XNF = [ar.f32(512) for _ in range(2)]
        SQ = [ar.bf16(512) for _ in range(2)]
        LN = ar.f32(512)
        RS = ar.f32(512)
        tXNF, tSQ = TokMap(), TokMap()
        tLN, tRS = Tok("ln"), Tok("rs")
        rr = [0]
        for ti, (t0, n) in enumerate(TILES):
            def out_fn(c, rs, trs, ti=ti, t0=t0, n=n):
                s = rr[0] % 2
                rr[0] += 1
                xf = XNF[s][:, :n]
                self.stt(xf, self.X[:, c, t0:t0 + n], self.vcol(gname, c), rs, ALU.mult, ALU.mult,
                         [self.tX[c, ti], trs, self.tVEC], [tXNF[s]])
                if t0 < SEQ:
                    self.cp("act", XNS[:, c, 1 + t0:1 + t0 + n], xf, [tXNF[s]], [tXNS[c, ti]])
                    if t0 + n == SEQ:
                        self.cp("pool", SHO[:, c, 0:1], xf[:, n - 1:n], [tXNF[s]], [tSHO])
                else:
                    sv = XNS[:, c, SOFF:SOFF + NS * 5].rearrange("p (j u) -> p j u", u=5)
                    xv = xf.rearrange("p (j t) -> p j t", t=TS)
                    self.cp("act", sv[:, :, 1:5], xv, [tXNF[s]], [tXNS[c, ti]])
                    self.cp("pool", SHO[:, c, 1:17], xv[:, :, 3], [tXNF[s]], [tSHO])
            self.rmsnorm_tile(ti, gname, out_fn, (SQ, tSQ, LN, tLN, RS, tRS, self.bank()))
            b1, b2, b3 = self.bank(), self.bank(), self.bank()
            mixproj(self.ps[b1][:, :n], W1A, W1B, slice(0, 128), t0, n, tW1, self.tps[b1])
            mixproj(self.ps[b2][:, :n], G1A, G1B, slice(0, 128), t0, n, tG1, self.tps[b2])
            mixproj(self.ps[b3][0:32, :n], G1A, G1B, slice(128, 160), t0, n, tG1, self.tps[b3])
            self.act(HWA[0:64, t0:t0 + n], self.ps[b1][0:64, :n], AF.Tanh, [self.tps[b1]], [tHWA[ti]])
            self.cp("act", HWA[64:128, t0:t0 + n], self.ps[b1][64:128, :n], [self.tps[b1]], [tHWA[ti]])
            self.act(HG1[:, t0:t0 + n], self.ps[b2][:, :n], AF.Sigmoid, [self.tps[b2]], [tHG[ti]])
            self.act(HG2[0:32, t0:t0 + n], self.ps[b3][0:32, :n], AF.Sigmoid, [self.tps[b3]], [tHG[ti]])
        ar.release(m1)
        kb.barrier()
        kb.dma("sp", d["o_shift"], SHO, reads=[tSHO])

        WN = 256

        def f32t():
            return ar.f32(WN)

        def bf16t():
            return ar.bf16(WN)
        WA = {nm: ar.bf16(NCH * 128).rearrange("p (k m) -> p k m", k=NCH) for nm in "rkv"}
        WB = {nm: ar.bf16(NCH * 128).rearrange("p (k m) -> p k m", k=NCH) for nm in "rkv"}
        WO2 = [ar.bf16(D) for _ in range(2)]
        tWc = {nm: Tok("w" + nm) for nm in "rkv"}
        tWO = [Tok("wo0"), Tok("wo1")]
        Rf, Kf, Vf, A_, EW, CUM, EM, EQ, KK, KF, Bv, T1, T2, YF = [f32t() for _ in range(14)]
        Vb, SQb, RKR, KTb, BTb, YG = [bf16t() for _ in range(6)]
        S3 = []
        for _ in range(3):
            S3.append(dict(
                KR=ar.bf16(2 * WN).rearrange("p (a n) -> p a n", a=2),
                KTt=ar.bf16(4 * 128).rearrange("p (j m) -> p j m", j=4),
                BTt=ar.bf16(4 * 128).rearrange("p (j m) -> p j m", j=4),
                VTt=ar.bf16(4 * 128).rearrange("p (j m) -> p j m", j=4),
                EP=f32t(), BONUS=f32t(), Gf=f32t(), LLs=ar.bf16(4 * 2 * 4 * 64)))
        S2 = []
        for _ in range(2):
            S2.append(dict(XTs=ar.bf16(4 * 2 * 64), PW=[ar.bf16(8 * 2 * 64) for _ in range(2)]))
        PT = [ar.bf16(8 * 64) for _ in range(2)]
        Gs = ar.bf16(128)
        NU = ar.bf16(128)
        YT = ar.f32(512)
        SQ2 = ar.f32(512)
        STAT = ar.f32(32)
        H = ar.f32(64)
        H0d = ar.f32(64)
        Hb = ar.bf16(64)
        HS = ar.f32(4 * 64).rearrange("p (j v) -> p j v", j=4)
        HSd = ar.f32(4 * 64).rearrange("p (j v) -> p j v", j=4)
        HSb = ar.bf16(4 * 64).rearrange("p (j v) -> p j v", j=4)
        T = TokMap()

        main_tiles = [(t0, 256, 64) for t0 in range(0, SEQ, 256)] + [(SEQ + 16 * q, 16, 4) for q in range(4)]
        import os as _os
        if "RW_TILES" in _os.environ:
            main_tiles = [main_tiles[int(i)] for i in _os.environ["RW_TILES"].split(",")]
        NCc = int(_os.environ.get("RW_NC", NCH))
        units = []
        for c in range(NCc):
            for k_, (t0, n, L) in enumerate(main_tiles):
                u = len(units)
                units.append(dict(u=u, c=c, t0=t0, n=n, L=L, first=(k_ == 0), last=(k_ == len(main_tiles) - 1),
                                  lastprompt=(t0 < SEQ and (k_ + 1 == len(main_tiles) or main_tiles[k_ + 1][0] >= SEQ))))

        def load_weights(c):
            ccols = slice(c * 128, (c + 1) * 128)
            for nm, key, mi in (("r", "rw_wr", 0), ("k", "rw_wk", 2), ("v", "rw_wv", 3)):
                for K in range(NCH):
                    kb.dma("pool", WA[nm][:, K, :], d[key][0, K * 128:(K + 1) * 128, ccols], writes=[tWc[nm]])
                scale_w(WA[nm], WB[nm], 128, [(slice(0, 128), mi)], tWc[nm])
            kb.dma("pool", WO2[c % 2], d["rw_wo"][0, c * 128:(c + 1) * 128, :], writes=[tWO[c % 2]])

        def stageA(U):
            u, c, t0, n, L = U["u"], U["c"], U["t0"], U["n"], U["L"]
            s3, s2 = S3[u % 3], S2[u % 2]
            k3, k2 = u % 3, u % 2
            sample = t0 >= SEQ
            NCk = 4
            ti5 = min(t0 // 512, 4)
            tsl = slice(t0, t0 + n)
            cs = lambda j: slice(j * L, (j + 1) * L)
            ccols = slice(c * 128, (c + 1) * 128)
            if U["first"]:
                load_weights(c)
                yield
            KR, KTt, BTt, VTt, EP, BONUS, Gf = s3["KR"], s3["KTt"], s3["BTt"], s3["VTt"], s3["EP"], s3["BONUS"], s3["Gf"]
            tKR, tKTt, tBTt, tVTt, tEP, tBONUS, tGf, tLL = (T["KR", k3], T["KTt", k3], T["BTt", k3], T["VTt", k3], T["EP", k3],
                                                          T["BONUS", k3], T["Gf", k3], T["LLs", k3])
            tXT = T["XTs", k2]
            bA, bB, bC, bD = self.bank("A"), self.bank("A"), self.bank("A"), self.bank("A")
            PR, PK = self.ps[bA][:, 0:n], self.ps[bA][:, 256:256 + n]
            PV, PGt = self.ps[bB][:, 0:n], self.ps[bB][:, 256:256 + n]
            PWL, PAL = self.ps[bC][:, 0:n], self.ps[bC][:, 256:256 + n]
            PKK, PSm = self.ps[bD][:, 0:n], self.ps[bD][:, 256:256 + n]
            for K0 in range(0, NCH, 2):
                pass
            mixproj(PR, WA["r"], WB["r"], slice(0, 128), t0, n, tWc["r"], self.tps[bA])
            yield
            mixproj(PK, WA["k"], WB["k"], slice(0, 128), t0, n, tWc["k"], self.tps[bA])
            yield
            mixproj(PV, WA["v"], WB["v"], slice(0, 128), t0, n, tWc["v"], self.tps[bB])
            self.mm(PGt, G2A[:, ccols], HG1[:, tsl], True, False, [tW2, tHG[ti5]], [self.tps[bB]])
            self.mm(PGt, G2B[:, ccols], HG2[:, tsl], False, True, [tW2, tHG[ti5], tHG[0]], [self.tps[bB]])
            self.mm(PWL, W2A2[0:64, ccols], HWA[0:64, tsl], True, True, [tW2, tHWA[ti5]], [self.tps[bC]])
            self.mm(PAL, W2A2[64:128, ccols], HWA[64:128, tsl], True, True, [tW2, tHWA[ti5]], [self.tps[bC]])
            yield
            w = lambda a: a[:, 0:n]
            tV = self.tVEC
            self.cp("act", w(Rf), PR, [self.tps[bA]], [T["Rf"]])
            self.cp("act", w(Kf), PK, [self.tps[bA]], [T["Kf"]])
            yield
            self.cp("act", w(Vf), PV, [self.tps[bB]], [T["Vf"]])
            self.cp("act", w(Gf), PGt, [self.tps[bB]], [tGf])
            self.cp("dve", w(Vb), w(Vf), [T["Vf"]], [T["Vb"]])
            yield
            self.act(w(A_), PAL, AF.Sigmoid, [self.tps[bC], tV], [T["A"]], bias=self.vcol("a0", c))
            self.act(w(T1), PWL, AF.Exp, [self.tps[bC], self.tMASK], [T["T1"]], scale=-1.0, bias=self.NEGW0[:, c:c + 1])
            self.ts("dve", w(KK), w(Kf), self.vcol("k_k", c), None, ALU.mult, None, [T["Kf"], tV], [T["KK"]])
            yield
            self.act(w(T1), w(T1), AF.Ln, [T["T1"], self.tCONST], [T["T1"]], bias=self.ONEC)
            self.act(w(SQb), w(KK), AF.Square, [T["KK"]], [T["SQb"]])
            self.mm(PKK, self.BONES, w(SQb), True, True, [T["SQb"], self.tMASK], [self.tps[bD]], signal=True)
            yield
            self.act(w(EW), w(T1), AF.Exp, [T["T1"], self.tCONST], [T["EW"]], scale=-1.0, bias=self.NHALFC)
            self.ts("dve", w(T1), w(A_), -1.0, self.vcol("k_a", c), ALU.add, ALU.mult, [T["A"], tV], [T["T1"]])
            self.stt(w(KF), w(T1), 1.0, w(Kf), ALU.add, ALU.mult, [T["T1"], T["Kf"]], [T["KF"]])
            yield
            SM = self.SM4[:, 0:n] if sample else self.SM64[:, 0:n]
            self.scan(w(CUM), SM, w(EW), 0.0, ALU.mult, ALU.subtract, [T["EW"], self.tMASK], [T["CUM"]])
            self.act(w(T2), PKK, AF.Sqrt, [self.tps[bD]], [T["T2"]])
            yield
            self.act(w(EP), w(CUM), AF.Exp, [T["CUM"]], [tEP])
            self.act(w(EM), w(CUM), AF.Exp, [T["CUM"]], [T["EM"]], scale=-1.0)
            self.ts("dve", w(T2), w(T2), 1e-12, None, ALU.max, None, [T["T2"]], [T["T2"]])
            self.recip(w(T2), w(T2), [T["T2"]], [T["T2"]])
            yield
            self.tt("dve", w(KK), w(KK), w(T2), ALU.mult, [T["KK"], T["T2"]], [T["KK"]])
            self.tt("pool", w(T2), w(CUM), w(EW), ALU.add, [T["CUM"], T["EW"], T["KK"]], [T["T2"]])
            self.act(w(EQ), w(T2), AF.Exp, [T["T2"]], [T["EQ"]])
            yield
            self.stt(w(RKR), w(Rf), self.vcol("r_k", c), w(KF), ALU.mult, ALU.mult, [T["Rf"], T["KF"], tV], [T["RKR"]])
            self.mm(PSm, self.BONES, w(RKR), True, True, [T["RKR"], self.tMASK], [self.tps[bD]], signal=True)
            self.tt("pool", w(Bv), w(KK), w(A_), ALU.mult, [T["KK"], T["A"]], [T["Bv"]])
            self.tt("dve", KR[:, 1, 0:n], w(Rf), w(EP), ALU.mult, [T["Rf"], tEP], [tKR])
            yield
            self.tt("dve", KR[:, 0, 0:n], w(KK), w(EQ), ALU.mult, [T["KK"], T["EQ"]], [tKR])
            self.tt("pool", w(KTb), w(KF), w(EM), ALU.mult, [T["KF"], T["EM"]], [T["KTb"]])
            self.tt("pool", w(BTb), w(Bv), w(EM), ALU.mult, [T["Bv"], T["EM"]], [T["BTb"]])
            self.tt("dve", w(BONUS), PSm, w(Vf), ALU.mult, [self.tps[bD], T["Vf"]], [tBONUS])
            yield
            bT = self.bank("A")
            PTr = self.ps[bT].bitcast(BF16)
            for (src, ts_, off) in ((KTb, "KTb", 0), (BTb, "BTb", 1)):
                for j in range(NCk):
                    self.tr(PTr[0:L, off * 512 + j * 128:off * 512 + (j + 1) * 128], src[:, cs(j)], self.IDENTB,
                            [T[ts_], self.tMASK], [self.tps[bT]])
            self.cp("act", KTt[0:L, :, :], PTr[0:L, 0:512].rearrange("p (j m) -> p j m", j=4), [self.tps[bT]], [tKTt])
            self.cp("act", BTt[0:L, :, :], PTr[0:L, 512:1024].rearrange("p (j m) -> p j m", j=4), [self.tps[bT]], [tBTt])
            yield
            bT2 = self.bank("A")
            PTr2 = self.ps[bT2].bitcast(BF16)
            for j in range(NCk):
                self.tr(PTr2[0:L, j * 128:(j + 1) * 128], Vb[:, cs(j)], self.IDENTB, [T["Vb"], self.tMASK], [self.tps[bT2]])
            self.cp("act", VTt[0:L, :, :], PTr2[0:L, 0:512].rearrange("p (j m) -> p j m", j=4), [self.tps[bT2]], [tVTt])
            yield
            LLv = s3["LLs"][:, 0:4 * 2 * 4 * L].rearrange("p (j h b t) -> p j h b t", j=4, h=2, b=4)
            XTv = s2["XTs"][:, 0:4 * 2 * L].rearrange("p (j h t) -> p j h t", j=4, h=2)
            mk = self.MASKLL[0:L, :, :, 0:L]
            for h in range(2):
                hs = slice(64 * h, 64 * h + 64)
                bX = self.bank("A")
                PXT = self.ps[bX][:, 0:4 * L].rearrange("p (j t) -> p j t", j=4)
                for g0 in (0, 2):
                    bL = self.bank("A")
                    PLL = self.ps[bL][:, 0:2 * 4 * L].rearrange("p (j b t) -> p j b t", j=2, b=4)
                    for jj in range(2):
                        j = g0 + jj
                        self.mm(PLL[0:L, jj, 0:2, :], KTb[hs, cs(j)], KR[hs, :, cs(j)], True, True, [T["KTb"], tKR], [self.tps[bL]])
                        self.mm(PLL[0:L, jj, 2:4, :], BTb[hs, cs(j)], KR[hs, :, cs(j)], True, True, [T["BTb"], tKR], [self.tps[bL]])
                        self.mm(PXT[0:L, j, :], KR[hs, 0, cs(j)], BTb[hs, cs(j)], True, True, [T["BTb"], tKR], [self.tps[bX]])
                    self.tt("dve", LLv[0:L, g0:g0 + 2, h], PLL[0:L], mk, ALU.mult, [self.tps[bL], self.tMASK], [tLL])
                    yield
                self.tt("dve", XTv[0:L, :, h, :], PXT[0:L], self.MASKXT[0:L, 0:L].unsqueeze(1).to_broadcast([L, 4, L]), ALU.mult,
                        [self.tps[bX], self.tMASK], [tXT])
                yield

        def stageB(U):
            u, L = U["u"], U["L"]
            s3, s2 = S3[u % 3], S2[u % 2]
            k3, k2 = u % 3, u % 2
            NCk, NM = 4, 8
            LLv = s3["LLs"][:, 0:4 * 2 * 4 * L].rearrange("p (j h b t) -> p j h b t", j=4, h=2, b=4)
            XTv = s2["XTs"][:, 0:4 * 2 * L].rearrange("p (j h t) -> p j h t", j=4, h=2)
            tLL, tXT = T["LLs", k3], T["XTs", k2]
            PWv = [p[:, 0:NM * 2 * L].rearrange("p (i a t) -> p i a t", i=NM, a=2) for p in s2["PW"]]
            PTv = [p[:, 0:NM * L].rearrange("p (i t) -> p i t", i=NM) for p in PT]
            tPW = [T["PW", k2, 0], T["PW", k2, 1]]
            PW4 = PWv[0].rearrange("p (j h) a t -> p j h a t", h=2)
            self.cp("pool", PW4[0:L, :, :, 0, :], LLv[0:L, :, :, 2, :], [tLL], [tPW[0]])
            self.cp("pool", PWv[0][0:L, :, 1, :], self.IDENTB[0:L, 0:L].unsqueeze(1).to_broadcast([L, NM, L]), [self.tMASK], [tPW[0]])
            self.cp("pool", PTv[0][0:L].rearrange("p (j h) t -> p j h t", h=2), XTv[0:L], [tXT], [T["PT", 0]])
            yield
            nlev = 6 if L == 64 else 2
            cur = 0
            mpb = 4 if L == 64 else 8
            for lev in range(nlev):
                nxt = 1 - cur
                last = lev == nlev - 1
                for i0 in range(0, NM, mpb):
                    bI = self.bank("B")
                    PA = self.ps[bI][:, 0:mpb * 2 * L].rearrange("p (i a t) -> p i a t", i=mpb, a=2)
                    for ii in range(mpb):
                        i = i0 + ii
                        self.mm(PA[0:L, ii], PTv[cur][0:L, i, :], PWv[cur][0:L, i], True, True,
                                [T["PT", cur], tPW[cur]], [self.tps[bI]], signal=(ii == mpb - 1))
                    if not last:
                        self.cp("act", PWv[nxt][0:L, i0:i0 + mpb, 0, :], PA[0:L, :, 0, :], [self.tps[bI]], [tPW[nxt]])
                    self.tt("dve", PWv[nxt][0:L, i0:i0 + mpb, 1, :], PA[0:L, :, 1, :], PWv[cur][0:L, i0:i0 + mpb, 1, :], ALU.add,
                            [self.tps[bI], tPW[cur]], [tPW[nxt]])
                    yield
                if not last:
                    bJ = self.bank("B")
                    PB = self.ps[bJ][:, 0:NM * L].rearrange("p (i t) -> p i t", i=NM)
                    for i in range(NM):
                        self.mm(PB[0:L, i, :], PWv[cur][0:L, i, 0, :], PTv[cur][0:L, i, :], True, True,
                                [T["PT", cur], tPW[cur]], [self.tps[bJ]], signal=(i == NM - 1))
                    self.cp("act", PTv[nxt][0:L], PB[0:L], [self.tps[bJ]], [T["PT", nxt]])
                    yield
                cur = nxt
            U["Wv"] = PWv[cur]
            U["tWv"] = tPW[cur]

        def stageC(U):
            u, c, t0, n, L = U["u"], U["c"], U["t0"], U["n"], U["L"]
            s3 = S3[u % 3]
            k3 = u % 3
            sample = t0 >= SEQ
            NCk = 4
            ti5 = min(t0 // 512, 4)
            tsl = slice(t0, t0 + n)
            cs = lambda j: slice(j * L, (j + 1) * L)
            w = lambda a: a[:, 0:n]
            tV = self.tVEC
            KR, KTt, BTt, VTt, EP, BONUS, Gf = s3["KR"], s3["KTt"], s3["BTt"], s3["VTt"], s3["EP"], s3["BONUS"], s3["Gf"]
            tKR, tKTt, tBTt, tVTt, tEP, tBONUS, tGf, tLL = (T["KR", k3], T["KTt", k3], T["BTt", k3], T["VTt", k3], T["EP", k3],
                                                          T["BONUS", k3], T["Gf", k3], T["LLs", k3])
            LLv = s3["LLs"][:, 0:4 * 2 * 4 * L].rearrange("p (j h b t) -> p j h b t", j=4, h=2, b=4)
            Wv, tWv = U["Wv"], U["tWv"]
            WO = WO2[c % 2]
            if U["first"]:
                self.memset("pool", H, 0.0, [T["H"]])
                self.memset("pool", Hb, 0.0, [T["Hb"]])
            if sample:
                q = (t0 - SEQ) // 16
                for jj in range(4):
                    kb.dma("sp", HS[:, jj, :], d["rw_S0T"][4 * q + jj, 2 * c:2 * c + 2].rearrange("h k v -> (h k) v"),
                           writes=[T["HS", jj]])
                    self.cp("act", HSb[:, jj, :], HS[:, jj, :], [T["HS", jj]], [T["HSb", jj]])
                yield
            YTv = YT[:, 0:4 * 128].rearrange("p (j m) -> p j m", j=4)
            for j in range(NCk):
                if sample:
                    Hc, Hbc, Hdc = HS[:, j, :], HSb[:, j, :], HSd[:, j, :]
                    tH, tHb, tHd = T["HS", j], T["HSb", j], T["HSd", j]
                else:
                    Hc, Hbc, Hdc = H, Hb, H0d
                    tH, tHb, tHd = T["H"], T["Hb"], T["H0d"]
                DL = EP[:, (j + 1) * L - 1:(j + 1) * L]
                bS = self.bank("C")
                PG = self.ps[bS][0:L, 0:128]
                PU = self.ps[bS][0:L, 128:256]
                PY = self.ps[bS][0:L, 256:384]
                bH = self.bank("C")
                PH = self.ps[bH][:, 0:64]
                tS = self.tps[bS]
                tSH = self.tps[bH]
                for h in range(2):
                    hs = slice(64 * h, 64 * h + 64)
                    self.mm(PG[:, hs], LLv[0:L, j, h, 0, :], VTt[0:L, j, hs], True, False, [tLL, tVTt], [tS])
                    self.mm(PG[:, hs], KR[hs, 0, cs(j)], Hbc[hs, :], False, True, [tKR, tHb], [tS], signal=True)
                self.ts("pool", Hdc, Hc, DL, None, ALU.mult, None, [tH, tEP], [tHd])
                yield
                self.cp("act", Gs[0:L, :], PG, [tS], [T["Gs"]])
                yield
                for h in range(2):
                    hs = slice(64 * h, 64 * h + 64)
                    self.mm(PU[:, hs], Wv[0:L, 2 * j + h, 1, :], Gs[0:L, hs], True, True, [tWv, T["Gs"]], [tS], signal=True)
                yield
                self.act(NU[0:L, :], PU, AF.Identity, [tS], [T["NU"]], scale=-1.0)
                yield
                for h in range(2):
                    hs = slice(64 * h, 64 * h + 64)
                    self.mm(PH[hs, :], KTt[0:L, j, hs], VTt[0:L, j, hs], True, False, [tKTt, tVTt], [tSH])
                    self.mm(PH[hs, :], BTt[0:L, j, hs], NU[0:L, hs], False, True, [tBTt, T["NU"]], [tSH], signal=True)
                for h in range(2):
                    hs = slice(64 * h, 64 * h + 64)
                    self.mm(PY[:, hs], LLv[0:L, j, h, 1, :], VTt[0:L, j, hs], True, False, [tLL, tVTt], [tS])
                    self.mm(PY[:, hs], LLv[0:L, j, h, 3, :], NU[0:L, hs], False, False, [tLL, T["NU"]], [tS])
                    self.mm(PY[:, hs], KR[hs, 1, cs(j)], Hbc[hs, :], False, True, [tKR, tHb], [tS], signal=True)
                yield
                self.stt(Hc, PH, DL, Hdc, ALU.mult, ALU.add, [tSH, tEP, tHd], [tH])
                self.cp("act", YTv[0:L, j, :], PY, [tS], [T["YT"]])
                yield
                self.cp("act", Hbc, Hc, [tH], [tHb])
                yield
            if sample:
                for jj in range(4):
                    kb.dma("sp", d["o_rw_Ss"][4 * q + jj, 2 * c:2 * c + 2].rearrange("h k v -> (h k) v"), HS[:, jj, :],
                           reads=[T["HS", jj]])
            if U["lastprompt"]:
                kb.dma("sp", d["o_rw_Sp"][2 * c:2 * c + 2].rearrange("h k v -> (h k) v"), H, reads=[T["H"]])
            G8 = 8
            YT3 = YT[:, 0:512].rearrange("p (g v) -> p g v", g=G8)
            SQ3 = SQ2[:, 0:512].rearrange("p (g v) -> p g v", g=G8)
            SUMv, VARv, RSTv = STAT[:, 0:8], STAT[:, 8:16], STAT[:, 16:24]
            self.reduce(SUMv[0:L, :], YT3[0:L], ALU.add, [T["YT"]], [T["SUM"]])
            self.ts("dve", SUMv[0:L, :], SUMv[0:L, :], 1.0 / 64, None, ALU.mult, None, [T["SUM"]], [T["SUM"]])
            yield
            self.tt("dve", YT3[0:L], YT3[0:L], SUMv[0:L, :].unsqueeze(2).to_broadcast([L, G8, 64]), ALU.subtract,
                    [T["YT"], T["SUM"]], [T["YT"]])
            yield
            self.act(SQ2[0:L, 0:512], YT[0:L, 0:512], AF.Square, [T["YT"]], [T["SQ2"]])
            yield
            self.reduce(VARv[0:L, :], SQ3[0:L], ALU.add, [T["SQ2"]], [T["VAR"]])
            yield
            self.act(RSTv[0:L, :], VARv[0:L, :], AF.Ln, [T["VAR"], self.tCONST], [T["RST"]], scale=1.0 / 64, bias=self.GNEPSC[0:L, :])
            yield
            self.act(RSTv[0:L, :], RSTv[0:L, :], AF.Exp, [T["RST"]], [T["RST"]], scale=-0.5)
            yield
            self.tt("dve", YT3[0:L], YT3[0:L], RSTv[0:L, :].unsqueeze(2).to_broadcast([L, G8, 64]), ALU.mult,
                    [T["YT"], T["RST"]], [T["YT"]])
            yield
            bY = self.bank("C")
            PYF = self.ps[bY][:, 0:n]
            for j in range(NCk):
                self.tr(PYF[:, cs(j)], YTv[0:L, j, :], self.IDENTF[0:L, 0:L], [T["YT"], self.tMASK], [self.tps[bY]])
            yield
            self.act(w(YF), PYF, AF.Identity, [self.tps[bY], tV], [T["YF"]], scale=self.vcol("gn_w", c), bias=self.vcol("gn_b", c))
            yield
            self.tt("pool", w(YF), w(YF), w(BONUS), ALU.add, [T["YF"], tBONUS], [T["YF"]])
            yield
            self.tt("dve", w(YG), w(YF), w(Gf), ALU.mult, [T["YF"], tGf], [T["YG"]])
            yield
            for dc0 in range(0, NCH, 2):
                bO = self.bank("C")
                for k2_ in range(2):
                    dc = dc0 + k2_
                    PO = self.ps[bO][:, 256 * k2_:256 * k2_ + n]
                    self.mm(PO, WO[:, dc * 128:(dc + 1) * 128], w(YG), True, True, [tWO[c % 2], T["YG"]], [self.tps[bO]], signal=True)
                for k2_ in range(2):
                    dc = dc0 + k2_
                    PO = self.ps[bO][:, 256 * k2_:256 * k2_ + n]
                    self.tt("dve", self.X[:, dc, tsl], PO, self.X[:, dc, tsl], ALU.add, [self.tps[bO], self.tX[dc, ti5]], [self.tX[dc, ti5]])
                yield

        def drain(gens):
            gens = [g for g in gens if g is not None]
            while gens:
                for g in list(gens):
                    try:
                        next(g)
                    except StopIteration:
                        gens.remove(g)

        PIPE = int(_os.environ.get("RW_PIPE", "1"))
        NU_ = len(units)
        if PIPE:
            for s in range(NU_ + 2):
                gA = stageA(units[s]) if s < NU_ else None
                gB = stageB(units[s - 1]) if 0 <= s - 1 < NU_ else None
                gC = stageC(units[s - 2]) if 0 <= s - 2 < NU_ else None
                drain([gC, gB, gA])
        else:
            for U in units:
                drain([stageA(U)])
                drain([stageB(U)])
                drain([stageC(U)])
        ar.release(m0)
        kb.barrier()

    def mlstm(self):
        kb, d, ar = self.kb, self.d, self.ar
        m0 = ar.mark()
        gname = "norm_mix1"
        XN, tXN = self.XN, self.tXN
        T = TokMap()
        NEG = -1.0e30
        EKA = ar.f32(NT)
        EQA = ar.f32(NT)
        EMTT = ar.f32(48 * 8).rearrange("p (j h) -> p j h", h=8)
        self.EMTP = EMTT[:, 0:32, :]
        EMTS = EMTT[:, 32:48, :]
        self.BBP = ar.f32(2)
        self.ABP = ar.f32(2)
        MLV = ar.f32(8 * 10).rearrange("p (h k) -> p h k", h=8)
        BIF = ar.f32(4)
        M0T = ar.f32(NS)
        MOUT = ar.f32(17)
        SEL = ar.f32(8 * 64).rearrange("p (h m) -> p h m", h=8)
        CONVO = ar.f32(8 * 2 * 17 * 3).rearrange("p (h w s k) -> p h w s k", h=8, w=2, s=17)
        tEK, tEQ = TokMap(), TokMap()
        kb.dma("sp", MLV[0:64], d["mlv"], writes=[T["MLV"]])
        kb.dma("sp", BIF[0:8, 0:2], d["bif"], writes=[T["BIF"]])
        kb.dma("sp", M0T[0:8, :], d["ml_m0T"], writes=[T["M0T"]])
        self.ts("dve", BIF[0:8, 2:3], BIF[0:8, 1:2], -1.0, None, ALU.mult, None, [T["BIF"]], [T["BIF"]])
        self.cp("pool", SEL[0:8], self.IDENTF[0:8, 0:8].unsqueeze(2).to_broadcast([8, 8, 64]), [self.tMASK], [T["SEL"]])

        m1 = ar.mark()
        WIF = ar.bf16(NCH * 16).rearrange("p (k m) -> p k m", k=NCH)
        for K in range(NCH):
            kb.dma("pool", WIF[:, K, :], d["ml_w_in"][0, K * 128:(K + 1) * 128, 3072:3088], writes=[T["WIF"]])
        SQ = [ar.bf16(512) for _ in range(2)]
        LN = ar.f32(512)
        RS = ar.f32(512)
        tSQ = TokMap()
        tLN, tRS = Tok("ln"), Tok("rs")
        LI, LF, BB, AA = [ar.f32(512) for _ in range(4)]
        ABX = ar.f32(513)
        D0, D1, TMPg, MTg, EMg = [ar.f32(512) for _ in range(5)]
        for ti, (t0, n) in enumerate(TILES):
            sample = t0 >= SEQ
            Lc = 4 if sample else 64
            nck = n // Lc

            def out_fn(c, rs, trs, ti=ti, t0=t0, n=n):
                self.stt(XN[:, c, t0:t0 + n], self.X[:, c, t0:t0 + n], self.vcol(gname, c), rs, ALU.mult, ALU.mult,
                         [self.tX[c, ti], trs, self.tVEC], [tXN[c, ti]])
            self.rmsnorm_tile(ti, gname, out_fn, (SQ, tSQ, LN, tLN, RS, tRS, self.bank()))
            bI, bF = self.bank(), self.bank()
            PI, PF = self.ps[bI][0:8, :n], self.ps[bF][0:8, :n]
            for K in range(NCH):
                self.mm(PI, WIF[:, K, 0:8], XN[:, K, t0:t0 + n], K == 0, K == NCH - 1, [T["WIF"], tXN[K, ti]], [self.tps[bI]])
            for K in range(NCH):
                self.mm(PF, WIF[:, K, 8:16], XN[:, K, t0:t0 + n], K == 0, K == NCH - 1, [T["WIF"], tXN[K, ti]], [self.tps[bF]])
            g = lambda a: a[0:8, 0:n]
            self.act(g(LI), PI, AF.Identity, [self.tps[bI], T["BIF"]], [T["LI"]], bias=BIF[0:8, 0:1])
            self.act(g(TMPg), PF, AF.Exp, [self.tps[bF], T["BIF"]], [T["TMP"]], scale=-1.0, bias=BIF[0:8, 2:3])
            self.act(g(LF), g(TMPg), AF.Ln, [T["TMP"], self.tCONST], [T["LF"]], bias=self.ONEC[0:8, :])
            self.memset("dve", g(D0), 1.0, [T["D0"]])
            init = 0.0
            rd = []
            if sample:
                self.memset("dve", g(D0).rearrange("p (s t) -> p s t", t=TS)[:, :, 0:1], 0.0, [T["D0"]])
            elif ti > 0:
                init = self.BBP[0:8, 0:1]
                rd = [T["BBprev"]]
            self.scan(g(BB), g(D0), g(LF), init, ALU.mult, ALU.subtract, [T["D0"], T["LF"]] + rd, [T["BB"]])
            self.tt("dve", g(AA), g(LI), g(BB), ALU.subtract, [T["LI"], T["BB"]], [T["AA"]])
            ab = ABX[0:8, 1:1 + n]
            if sample:
                self.memset("dve", g(D1), 0.0, [T["D1"]])
                self.memset("dve", g(D1).rearrange("p (s t) -> p s t", t=TS)[:, :, 0:1], NEG, [T["D1"]])
                a3 = g(AA).rearrange("p (s t) -> p s t", t=TS)
                self.tt("dve", a3[:, :, 0], a3[:, :, 0], M0T[0:8, :], ALU.max, [T["AA"], T["M0T"]], [T["AA"]])
                self.scan(ab, g(D1), g(AA), 0.0, ALU.add, ALU.max, [T["D1"], T["AA"]], [T["ABX"]])
                rho = M0T[0:8, :].unsqueeze(2).to_broadcast([8, NS, TS])
                rtok = [T["M0T"]]
                a_v = g(AA).rearrange("p (s t) -> p s t", t=TS)
                ab_v = ab.rearrange("p (s t) -> p s t", t=TS)
                ek_v = g(TMPg).rearrange("p (s t) -> p s t", t=TS)
                eq_v = g(D0).rearrange("p (s t) -> p s t", t=TS)
                self.tt("dve", g(AA), g(LI), g(BB), ALU.subtract, [T["LI"], T["BB"], T["ABX"]], [T["AA"]])
            else:
                self.memset("dve", g(D1), 0.0, [T["D1"]])
                if ti == 0:
                    self.memset("dve", ABX[0:8, 0:1], 0.0, [T["ABX"]])
                    ainit = 0.0
                else:
                    self.cp("dve", ABX[0:8, 0:1], self.ABP[0:8, 0:1], [T["ABprev"]], [T["ABX"]])
                    ainit = self.ABP[0:8, 0:1]
                self.scan(ab, g(D1), g(AA), ainit, ALU.add, ALU.max, [T["D1"], T["AA"], T["ABX"]] + ([T["ABprev"]] if ti else []), [T["ABX"]])
                rho = ABX[0:8, 0:n].rearrange("p (j l) -> p j l", l=64)[:, :, 0:1].to_broadcast([8, nck, 64])
                rtok = [T["ABX"]]
                a_v = g(AA).rearrange("p (j l) -> p j l", l=64)
                ab_v = ab.rearrange("p (j l) -> p j l", l=64)
                ek_v = g(TMPg).rearrange("p (j l) -> p j l", l=64)
                eq_v = g(D0).rearrange("p (j l) -> p j l", l=64)
            self.tt("dve", ek_v, a_v, rho, ALU.subtract, [T["AA"]] + rtok, [T["TMP"]])
            self.act(EKA[0:8, t0:t0 + n], g(TMPg), AF.Exp, [T["TMP"]], [tEK[ti]])
            self.tt("dve", eq_v, rho, ab_v, ALU.subtract, [T["ABX"], T["D0"]] + rtok, [T["D0"]])
            self.act(EQA[0:8, t0:t0 + n], g(D0), AF.Exp, [T["D0"]], [tEQ[ti]])
            self.tt("dve", g(MTg), g(BB), ab, ALU.add, [T["BB"], T["ABX"]], [T["MT"]])
            self.act(g(EMg), g(MTg), AF.Exp, [T["MT"]], [T["EM"]], scale=-1.0)
            bT = self.bank()
            for j in range(nck):
                self.tr(self.ps[bT][0:Lc, j * 8:(j + 1) * 8], EMg[0:8, j * Lc:(j + 1) * Lc], self.IDENTF[0:8, 0:8],
                        [T["EM"], self.tMASK], [self.tps[bT]])
            cb0 = t0 // 64 if not sample else 32
            if sample:
                self.cp("act", EMTS[0:Lc, 0:nck, :], self.ps[bT][0:Lc, 0:nck * 8].rearrange("p (j h) -> p j h", h=8),
                        [self.tps[bT]], [T["EMTS"]])
                self.cp("pool", MOUT[0:8, 1:17], g(MTg).rearrange("p (s t) -> p s t", t=TS)[:, :, 3], [T["MT"]], [T["MOUT"]])
            else:
                self.cp("act", self.EMTP[0:Lc, cb0:cb0 + nck, :], self.ps[bT][0:Lc, 0:nck * 8].rearrange("p (j h) -> p j h", h=8),
                        [self.tps[bT]], [T["EMTP"]])
                if ti == 3:
                    self.cp("pool", MOUT[0:8, 0:1], MTg[0:8, n - 1:n], [T["MT"]], [T["MOUT"]])
                self.cp("pool", self.BBP[0:8, 0:1], BB[0:8, n - 1:n], [T["BB"]], [T["BBprev"]])
                self.cp("pool", self.ABP[0:8, 0:1], ABX[0:8, n:n + 1], [T["ABX"]], [T["ABprev"]])
        ar.release(m1)
        kb.barrier()
        kb.dma("sp", d["o_ml_m"], MOUT[0:8, :], reads=[T["MOUT"]])

        WIN = ar.bf16(NCH * 384).rearrange("p (k m) -> p k m", k=NCH)
        WOh = ar.bf16(D)
        RAW = [ar.f32(520) for _ in range(2)]
        ACC = [ar.f32(512) for _ in range(2)]
        SIL = [ar.f32(512) for _ in range(2)]
        QP = ar.bf16(512)
        KP = ar.bf16(512)
        Osig = ar.f32(512)
        LAMB = ar.f32(512)
        VA = ar.bf16(8 * 130).rearrange("p (j m) -> p j m", j=8)
        STs = [ar.bf16(64) for _ in range(2)]
        KTt = ar.bf16(8 * 64).rearrange("p (j m) -> p j m", j=8)
        HT = ar.f32(8 * 128).rearrange("p (j m) -> p j m", j=8)
        SQH = ar.f32(8 * 128)
        STATH = ar.f32(32)
        DEN = ar.f32(16)
        HG = ar.bf16(512)
        C = ar.f32(130)
        Cd = ar.f32(130)
        Cb = ar.bf16(130)
        CS = ar.f32(4 * 130).rearrange("p (s m) -> p s m", s=4)
        CSd = ar.f32(4 * 130).rearrange("p (s m) -> p s m", s=4)
        CSb = ar.bf16(4 * 130).rearrange("p (s m) -> p s m", s=4)
        main_tiles = [(t0, 512, 64, 8) for t0 in range(0, SEQ, 512)] + [(SEQ + 16 * q, 16, 4, 4) for q in range(4)]
        import os as _os
        if "ML_TILES" in _os.environ:
            main_tiles = [main_tiles[int(i)] for i in _os.environ["ML_TILES"].split(",")]
        for h in range(int(_os.environ.get("ML_NH", 8))):
            for K in range(NCH):
                rows = slice(K * 128, (K + 1) * 128)
                kb.dma("pool", WIN[:, K, 0:64], d["ml_w_in"][0, rows, h * 64:(h + 1) * 64], writes=[T["WIN"]])
                kb.dma("pool", WIN[:, K, 64:128], d["ml_w_in"][0, rows, 512 + h * 64:512 + (h + 1) * 64], writes=[T["WIN"]])
                kb.dma("pool", WIN[:, K, 128:256], d["ml_w_in"][0, rows, 1024 + h * 128:1024 + (h + 1) * 128], writes=[T["WIN"]])
                kb.dma("pool", WIN[:, K, 256:384], d["ml_w_in"][0, rows, 2048 + h * 128:2048 + (h + 1) * 128], writes=[T["WIN"]])
            kb.dma("pool", WOh, d["ml_w_out"][0, h * 128:(h + 1) * 128, :], writes=[T["WO"]])
            self.memset("pool", C[0:64], 0.0, [T["C"]])
            self.memset("pool", Cb[0:64], 0.0, [T["Cb"]])
            for w_ in range(2):
                self.memset("pool", RAW[w_][0:64, 0:3], 0.0, [T["RAW", w_]])
            self.memset("pool", VA[0:64, :, 128:129], 1.0, [T["VA"]])
            for (t0, n, L, NCk) in main_tiles:
                sample = t0 >= SEQ
                ti5 = min(t0 // 512, 4)
                tsl = slice(t0, t0 + n)
                cs = lambda j: slice(j * L, (j + 1) * L)
                xt = [tXN[K, ti5] for K in range(NCH)]
                if sample:
                    q4 = (t0 - SEQ) // 16
                    for s in range(4):
                        kb.dma("sp", CS[0:64, s, 0:129], d["ml_C0T"][4 * q4 + s, h], writes=[T["CS", s]])
                        self.cp("act", CSb[0:64, s, 0:129], CS[0:64, s, 0:129], [T["CS", s]], [T["CSb", s]])
                    for w_ in range(2):
                        rv = RAW[w_][0:64, 0:28].rearrange("p (s u) -> p s u", u=7)
                        kb.dma("sp", rv[:, :, 0:3], d["ml_convT"][h, w_, :, 4 * q4:4 * q4 + 4, :], writes=[T["RAW", w_]])
                bQ, bK, bO = self.bank(), self.bank(), self.bank()
                PQ, PK, PO_ = self.ps[bQ][0:64, :n], self.ps[bK][0:64, :n], self.ps[bO][:, :n]
                for (P_, cols, bb) in ((PQ, slice(0, 64), bQ), (PK, slice(64, 128), bK), (PO_, slice(256, 384), bO)):
                    for K in range(NCH):
                        self.mm(P_, WIN[:, K, cols], XN[:, K, tsl], K == 0, K == NCH - 1, [T["WIN"], xt[K]], [self.tps[bb]])
                self.act(Osig[:, :n], PO_, AF.Sigmoid, [self.tps[bO]], [T["O"]])
                for j0 in range(0, NCk, 2):
                    bV = self.bank()
                    for jj in range(min(2, NCk - j0)):
                        j = j0 + jj
                        PVt = self.ps[bV][0:L, jj * 128:(jj + 1) * 128]
                        for K in range(NCH):
                            self.mm(PVt, XN[:, K, t0 + j * L:t0 + (j + 1) * L], WIN[:, K, 128:256], K == 0, K == NCH - 1,
                                    [T["WIN"], xt[K]], [self.tps[bV]])
                    nj = min(2, NCk - j0)
                    self.cp("act", VA[0:L, j0:j0 + nj, 0:128], self.ps[bV][0:L, 0:nj * 128].rearrange("p (j m) -> p j m", m=128),
                            [self.tps[bV]], [T["VA"]])
                for w_, (P_, bb) in enumerate(((PQ, bQ), (PK, bK))):
                    mv = lambda k_: MLV[0:64, h, 5 * w_ + k_:5 * w_ + k_ + 1]
                    R_ = RAW[w_]
                    if sample:
                        rv = R_[0:64, 0:28].rearrange("p (s u) -> p s u", u=7)
                        self.cp("act", rv[:, :, 3:7], P_.rearrange("p (s t) -> p s t", t=TS), [self.tps[bb]], [T["RAW", w_]])
                        taps = [rv[:, :, k_:k_ + 4] for k_ in range(4)]
                        acc = ACC[w_][0:64, 0:n].rearrange("p (s t) -> p s t", t=TS)
                        self.cp("pool", CONVO[0:64, h, w_, 1 + 4 * q4:5 + 4 * q4, :], rv[:, :, 4:7], [T["RAW", w_]], [T["CONVO"]])
                    else:
                        self.cp("act", R_[0:64, 3:3 + n], P_, [self.tps[bb]], [T["RAW", w_]])
                        taps = [R_[0:64, k_:k_ + n] for k_ in range(4)]
                        acc = ACC[w_][0:64, 0:n]
                    self.ts("dve", acc, taps[0], mv(0), mv(4), ALU.mult, ALU.add, [T["RAW", w_], T["MLV"]], [T["ACC", w_]])
                    for k_ in range(1, 4):
                        self.stt(acc, taps[k_], mv(k_), acc, ALU.mult, ALU.add, [T["RAW", w_], T["MLV"], T["ACC", w_]], [T["ACC", w_]])
                    self.act(SIL[w_][0:64, 0:n], ACC[w_][0:64, 0:n], AF.Silu, [T["ACC", w_]], [T["SIL", w_]])
                    if not sample:
                        if t0 + n == SEQ:
                            self.cp("pool", CONVO[0:64, h, w_, 0, :], R_[0:64, n:n + 3], [T["RAW", w_]], [T["CONVO"]])
                        self.cp("pool", R_[0:64, 0:3], R_[0:64, n:n + 3], [T["RAW", w_], T["ACC", w_]], [T["RAW", w_]])
                bM, bM2 = self.bank(), self.bank()
                PBK, PBQ = self.ps[bM][0:64, 0:n], self.ps[bM2][0:64, 0:n]
                tBQ = self.tps[bM2]
                self.mm(PBK, SEL[0:8, h, :], EKA[0:8, tsl], True, True, [T["SEL"], tEK[ti5]], [self.tps[bM]])
                self.mm(PBQ, SEL[0:8, h, :], EQA[0:8, tsl], True, True, [T["SEL"], tEQ[ti5]], [tBQ])
                self.tt("dve", KP[0:64, 0:n], SIL[1][0:64, 0:n], PBK, ALU.mult, [T["SIL", 1], self.tps[bM]], [T["KP"]])
                self.stt(QP[0:64, 0:n], SIL[0][0:64, 0:n], 0.125, PBQ, ALU.mult, ALU.mult, [T["SIL", 0], tBQ], [T["QP"]])
                self.cp("act", LAMB[0:64, 0:n], PBQ, [tBQ], [T["LAM"]])
                LAM = LAMB
                bT = self.bank()
                PTr = self.ps[bT].bitcast(BF16)
                for j in range(NCk):
                    self.tr(PTr[0:L, j * 64:(j + 1) * 64], KP[0:64, cs(j)], self.IDENTB[0:64, 0:64], [T["KP"], self.tMASK], [self.tps[bT]])
                self.cp("act", KTt[0:L, 0:NCk, :], PTr[0:L, 0:NCk * 64].rearrange("p (j m) -> p j m", m=64), [self.tps[bT]], [T["KTt"]])
                for j in range(NCk):
                    if sample:
                        Cc, Cbc, Cdc = CS[0:64, j, 0:129], CSb[0:64, j, 0:129], CSd[0:64, j, 0:129]
                        tC, tCb, tCd = T["CS", j], T["CSb", j], T["CSd", j]
                        em = EMTS[0:L, j + 4 * q4, h:h + 1]
                        tem = T["EMTS"]
                    else:
                        Cc, Cbc, Cdc = C[0:64, 0:129], Cb[0:64, 0:129], Cd[0:64, 0:129]
                        tC, tCb, tCd = T["C"], T["Cb"], T["Cd"]
                        em = self.EMTP[0:L, t0 // 64 + j, h:h + 1]
                        tem = T["EMTP"]
                    bS = self.bank()
                    PST = self.ps[bS][0:L, 0:L]
                    PND = self.ps[bS][0:L, 64:64 + 129]
                    PCS = self.ps[bS][0:64, 256:256 + 129]
                    tS = self.tps[bS]
                    st = STs[j % 2]
                    self.mm(PST, KP[0:64, cs(j)], QP[0:64, cs(j)], True, True, [T["KP"], T["QP"]], [tS])
                    self.tt("dve", st[0:L, 0:L], PST, self.MIU[0:L, 0:L], ALU.mult, [tS, self.tMASK], [T["ST", j % 2]])
                    self.mm(PND, QP[0:64, cs(j)], Cbc, True, False, [T["QP"], tCb], [tS])
                    self.mm(PND, st[0:L, 0:L], VA[0:L, j, 0:129], False, True, [T["ST", j % 2], T["VA"]], [tS])
                    self.mm(PCS, KTt[0:L, j, :], VA[0:L, j, 0:129], True, True, [T["KTt"], T["VA"]], [tS])
                    lam = LAM[0:64, (j + 1) * L - 1:(j + 1) * L]
                    self.ts("pool", Cdc, Cc, lam, None, ALU.mult, None, [tC, T["LAM"]], [tCd])
                    self.stt(Cc, PCS, lam, Cdc, ALU.mult, ALU.add, [tS, T["LAM"], tCd], [tC])
                    self.cp("act", Cbc, Cc, [tC], [tCb])
                    self.act(DEN[0:L, j:j + 1], PND[:, 128:129], AF.Abs, [tS], [T["DEN"]])
                    self.ts("dve", DEN[0:L, j:j + 1], DEN[0:L, j:j + 1], em, None, ALU.max, None, [T["DEN"], tem], [T["DEN"]])
                    self.recip(DEN[0:L, j:j + 1], DEN[0:L, j:j + 1], [T["DEN"]], [T["DEN"]])
                    self.act(HT[0:L, j, :], PND[:, 0:128], AF.Identity, [tS, T["DEN"]], [T["HT"]], scale=DEN[0:L, j:j + 1])
                if sample:
                    for s in range(4):
                        kb.dma("sp", d["o_ml_Cs"][4 * q4 + s, h], CS[0:64, s, 0:129], reads=[T["CS", s]])
                G = NCk
                HTf = HT[0:L, 0:G, :]
                SQv = SQH[0:L, 0:G * 128].rearrange("p (j m) -> p j m", m=128)
                self.act(SQv, HTf, AF.Square, [T["HT"]], [T["SQH"]])
                self.reduce(STATH[0:L, 0:G], SQv, ALU.add, [T["SQH"]], [T["STATH"]])
                self.act(STATH[0:L, 0:G], STATH[0:L, 0:G], AF.Ln, [T["STATH"], self.tCONST], [T["STATH"]], scale=1.0 / 128, bias=self.EPSC[0:L, :])
                self.act(STATH[0:L, 0:G], STATH[0:L, 0:G], AF.Exp, [T["STATH"]], [T["STATH"]], scale=-0.5)
                self.tt("dve", HTf, HTf, STATH[0:L, 0:G].unsqueeze(2).to_broadcast([L, G, 128]), ALU.mult, [T["HT"], T["STATH"]], [T["HT"]])
                bY = self.bank()
                PYF = self.ps[bY][:, 0:n]
                for j in range(NCk):
                    self.tr(PYF[:, cs(j)], HT[0:L, j, :], self.IDENTF[0:L, 0:L], [T["HT"], self.tMASK], [self.tps[bY]])
                self.stt(HG[:, 0:n], PYF, self.vcol("ml_norm_w", h), Osig[:, 0:n], ALU.mult, ALU.mult, [self.tps[bY], T["O"], self.tVEC], [T["HG"]])
                for dc in range(NCH):
                    bO2 = self.bank()
                    PO2 = self.ps[bO2][:, 0:n]
                    self.mm(PO2, WOh[:, dc * 128:(dc + 1) * 128], HG[:, 0:n], True, True, [T["WO"], T["HG"]], [self.tps[bO2]])
                    self.tt("dve", self.X[:, dc, tsl], PO2, self.X[:, dc, tsl], ALU.add, [self.tps[bO2], self.tX[dc, ti5]], [self.tX[dc, ti5]])
            kb.dma("sp", d["o_ml_Cp"][h], C[0:64, 0:129], reads=[T["C"]])
        kb.dma("sp", d["o_ml_conv"], CONVO[0:64], reads=[T["CONVO"]])
        ar.release(m0)
        kb.barrier()

    def ffn(self, L, which):
        kb, d, ar = self.kb, self.d, self.ar
        m = ar.mark()
        wg = d["ff%s_wg" % which][L]
        wu = d["ff%s_wu" % which][L]
        wd = d["ff%s_wd" % which][L]
        gname = "norm_ff%s%d" % (which, L)
        G = 4
        groups = [(f0, min(G, NFF - f0)) for f0 in range(0, NFF, G)]
        WG = [ar.bf16(NCH * 512).rearrange("p (c f) -> p c f", c=NCH) for _ in range(2)]
        WU = [ar.bf16(NCH * 512).rearrange("p (c f) -> p c f", c=NCH) for _ in range(2)]
        WD = [ar.bf16(G * D).rearrange("p (f o) -> p f o", f=G) for _ in range(2)]
        H = [ar.bf16(G * 512).rearrange("p (f n) -> p f n", f=G) for _ in range(2)]
        SG = [ar.f32(512) for _ in range(2)]
        SQ = [ar.bf16(512) for _ in range(2)]
        LN = ar.f32(512)
        RS = ar.f32(512)
        tW = TokMap()
        tH = TokMap()
        tSG = TokMap()
        tSQ = TokMap()
        tLN, tRS = Tok("ln"), Tok("rs")

        def load_group(gi):
            f0, nf = groups[gi]
            s = gi % 2
            for c in range(NCH):
                kb.dma("pool", WG[s][:, c, 0:nf * 128], wg[c * 128:(c + 1) * 128, f0 * 128:(f0 + nf) * 128],
                       writes=[tW["g", s, c]])
                kb.dma("pool", WU[s][:, c, 0:nf * 128], wu[c * 128:(c + 1) * 128, f0 * 128:(f0 + nf) * 128],
                       writes=[tW["u", s, c]])
            for fi in range(nf):
                kb.dma("pool", WD[s][:, fi, :], wd[(f0 + fi) * 128:(f0 + fi + 1) * 128, :], writes=[tW["d", s, fi]])

        load_group(0)
        load_group(1)

        for ti, (t0, n) in enumerate(TILES):
            def out_fn(c, rs, trs, ti=ti, t0=t0, n=n):
                kb.emit("dve", lambda e: e.scalar_tensor_tensor(out=self.XN[:, c, t0:t0 + n], in0=self.X[:, c, t0:t0 + n],
                                                                scalar=self.vcol(gname, c), in1=rs,
                                                                op0=ALU.mult, op1=ALU.mult),
                        reads=[self.tX[c, ti], trs, self.tVEC], writes=[self.tXN[c, ti]])
            self.rmsnorm_tile(ti, gname, out_fn, (SQ, tSQ, LN, tLN, RS, tRS, 4 + ti % 4))

        items = [(gi, ti) for gi in range(len(groups)) for ti in range(len(TILES))]
        po_rr = [0]

        def up(idx):
            gi, ti = items[idx]
            f0, nf = groups[gi]
            s = gi % 2
            hs = idx % 2
            t0, n = TILES[ti]
            for fi in range(nf):
                b = fi % 2
                pg = self.ps[b][:, :n]
                pu = self.ps[2 + b][:, :n]
                for c in range(NCH):
                    kb.emit("pe", lambda e, c=c, fi=fi, pg=pg: e.matmul(pg, WG[s][:, c, fi * 128:(fi + 1) * 128],
                                                                     self.XN[:, c, t0:t0 + n], start=(c == 0), stop=(c == NCH - 1)),
                            reads=[tW["g", s, c], self.tXN[c, ti]], writes=[self.tps[b]], signal=(c == NCH - 1))
                for c in range(NCH):
                    kb.emit("pe", lambda e, c=c, fi=fi, pu=pu: e.matmul(pu, WU[s][:, c, fi * 128:(fi + 1) * 128],
                                                                     self.XN[:, c, t0:t0 + n], start=(c == 0), stop=(c == NCH - 1)),
                            reads=[tW["u", s, c], self.tXN[c, ti]], writes=[self.tps[2 + b]], signal=(c == NCH - 1))
                kb.emit("act", lambda e, b=b, pg=pg: e.activation(out=SG[b][:, :n], in_=pg, func=AF.Silu),
                        reads=[self.tps[b]], writes=[tSG[b]])
                kb.emit("dve", lambda e, b=b, fi=fi, pu=pu: e.tensor_tensor(out=H[hs][:, fi, :n], in0=SG[b][:, :n], in1=pu, op=ALU.mult),
                        reads=[tSG[b], self.tps[2 + b]], writes=[tH[hs, fi]])

        def down(idx):
            gi, ti = items[idx]
            f0, nf = groups[gi]
            s = gi % 2
            hs = idx % 2
            t0, n = TILES[ti]
            for dc in range(NCH):
                bank = 4 + po_rr[0] % 4
                po_rr[0] += 1
                po = self.ps[bank][:, :n]
                for fi in range(nf):
                    kb.emit("pe", lambda e, fi=fi, dc=dc, po=po: e.matmul(po, WD[s][:, fi, dc * 128:(dc + 1) * 128], H[hs][:, fi, :n],
                                                                       start=(fi == 0), stop=(fi == nf - 1)),
                            reads=[tW["d", s, fi], tH[hs, fi]], writes=[self.tps[bank]], signal=(fi == nf - 1))
                kb.emit("dve", lambda e, dc=dc, po=po: e.scalar_tensor_tensor(out=self.X[:, dc, t0:t0 + n], in0=po, scalar=0.5,
                                                                           in1=self.X[:, dc, t0:t0 + n], op0=ALU.mult, op1=ALU.add),
                        reads=[self.tps[bank], self.tX[dc, ti]], writes=[self.tX[dc, ti]])

        ntile = len(TILES)
        for idx in range(len(items)):
            up(idx)
            if idx > 0:
                down(idx - 1)
                gi_prev, ti_prev = items[idx - 1]
                if ti_prev == ntile - 1 and gi_prev + 2 < len(groups):
                    load_group(gi_prev + 2)
        down(len(items) - 1)
        ar.release(m)
        kb.barrier()

    def final_norm(self):
        kb, d, ar = self.kb, self.d, self.ar
        m = ar.mark()
        SQ = [ar.bf16(512) for _ in range(2)]
        LN = ar.f32(512)
        RS = ar.f32(512)
        Y = [ar.f32(512) for _ in range(4)]
        tSQ, tY = TokMap(), TokMap()
        tLN, tRS = Tok("ln"), Tok("rs")
        rr = [0]
        for ti, (t0, n) in enumerate(TILES):
            def out_fn(c, rs, trs, ti=ti, t0=t0, n=n):
                s = rr[0] % 4
                rr[0] += 1
                kb.emit("dve", lambda e: e.scalar_tensor_tensor(out=Y[s][:, :n], in0=self.X[:, c, t0:t0 + n],
                                                                scalar=self.vcol("norm_final", c), in1=rs,
                                                                op0=ALU.mult, op1=ALU.mult),
                        reads=[self.tX[c, ti], trs, self.tVEC], writes=[tY[s]])
                kb.dma("sp", d["yT"][c * 128:(c + 1) * 128, t0:t0 + n], Y[s][:, :n], reads=[tY[s]])
            self.rmsnorm_tile(ti, "norm_final", out_fn, (SQ, tSQ, LN, tLN, RS, tRS, 4 + ti % 4))
        ar.release(m)
        kb.barrier()

    def dump_x(self):
        kb, d = self.kb, self.d
        for c in range(NCH):
            for ti, (t0, n) in enumerate(TILES):
                kb.dma("sp", d["yT"][c * 128:(c + 1) * 128, t0:t0 + n], self.X[:, c, t0:t0 + n], reads=[self.tX[c, ti]])

    def build(self):
        st = self.stages
        self.load_inputs()
        self.consts()
        for L in range(2):
            if "ffa%d" % L in st:
                self.ffn(L, "a")
            if "mix%d" % L in st and L == 0:
                self.rwkv()
            if "mix%d" % L in st and L == 1:
                self.mlstm()
            if "ffb%d" % L in st:
                self.ffn(L, "b")
        if "final" in st:
            self.final_norm()
        else:
            self.dump_x()
        self.kb.flush()
        return self.nc


ALL_STAGES = ("ffa0", "mix0", "ffb0", "ffa1", "mix1", "ffb1", "final")


def pack_vecs(inp):
    vecs = np.zeros((NVEC, D), np.float32)

    def put(name, v):
        vecs[VID[name]] = np.asarray(v, np.float32).reshape(D)

    for L in range(2):
        put("norm_ffa%d" % L, inp["norm_ffa"][L])
        put("norm_mix%d" % L, inp["norm_mix"][L])
        put("norm_ffb%d" % L, inp["norm_ffb"][L])
    put("norm_final", inp["norm_final"])
    for i in range(6):
        put("mu%d" % i, inp["rw_mu"][0, i])
    put("w0", inp["rw_w0"][0])
    put("a0", inp["rw_a0"][0])
    put("k_k", inp["rw_k_k"][0])
    put("k_a", inp["rw_k_a"][0])
    put("r_k", inp["rw_r_k"][0])
    put("gn_w", inp["rw_gn_w"][0])
    put("gn_b", inp["rw_gn_b"][0])
    for j in range(4):
        put("cw%d" % j, inp["ml_conv_w"][0, j])
    put("cb", inp["ml_conv_b"][0])
    put("ml_norm_w", inp["ml_norm_w"][0])
    return np.ascontiguousarray(vecs.reshape(NVEC, NCH, 128).transpose(2, 0, 1).reshape(128, NVEC * NCH))


def make_in_maps(inp):
    vecs = pack_vecs(inp)
    shared = {"vecs": vecs}
    for nm in ("ffa_wg", "ffa_wu", "ffb_wg", "ffb_wu", "ffa_wd", "ffb_wd"):
        shared[nm] = np.ascontiguousarray(inp[nm], dtype=np.float32)
    cw = np.asarray(inp["ml_conv_w"][0], np.float32)
    cbv = np.asarray(inp["ml_conv_b"][0], np.float32)
    mlv = np.zeros((64, 8, 10), np.float32)
    for w in range(2):
        for k in range(4):
            mlv[:, :, 5 * w + k] = cw[k, w * 512:(w + 1) * 512].reshape(8, 64).T
        mlv[:, :, 5 * w + 4] = cbv[w * 512:(w + 1) * 512].reshape(8, 64).T
    shared["mlv"] = mlv
    shared["bif"] = np.ascontiguousarray(np.asarray(inp["ml_b_if"][0], np.float32).reshape(2, 8).T)
    for nm in ("rw_wr", "rw_wk", "rw_wv", "rw_wo", "rw_w1", "rw_w2", "rw_a1", "rw_a2", "rw_g1", "rw_g2", "ml_w_in", "ml_w_out"):
        shared[nm] = np.ascontiguousarray(inp[nm], dtype=np.float32)
    maps = []
    for core in range(NCORES):
        xs = np.concatenate([inp["x_prompt"][core], inp["x_sample"][core * NS:(core + 1) * NS].reshape(NS * TS, D)], axis=0)
        m = dict(shared)
        m["xT"] = np.ascontiguousarray(xs.T.astype(np.float32))
        sq = slice(core * NS, (core + 1) * NS)
        sh = inp["state_rwkv_shift"][0, sq]
        m["shiftT"] = np.ascontiguousarray(sh.reshape(NS, NCH, 128).transpose(2, 1, 0).astype(np.float32))
        m["rw_S0T"] = np.ascontiguousarray(inp["state_rwkv_S"][0, sq].transpose(0, 1, 3, 2).astype(np.float32))
        m["ml_m0T"] = np.ascontiguousarray(inp["state_mlstm_m"][0, sq].T.astype(np.float32))
        c0 = np.concatenate([inp["state_mlstm_C"][0, sq].transpose(0, 1, 3, 2), inp["state_mlstm_n"][0, sq][..., None]], axis=-1)
        m["ml_C0T"] = np.ascontiguousarray(c0.astype(np.float32))
        cv = inp["state_mlstm_conv"][0, sq]
        m["ml_convT"] = np.ascontiguousarray(cv.reshape(NS, 3, 2, 8, 64).transpose(3, 2, 4, 0, 1).astype(np.float32))
        maps.append(m)
    return maps


def run(inp, stages=ALL_STAGES, trace=False):
    b = Builder(stages)
    nc = b.build()
    maps = make_in_maps(inp)
    res = run_bass_kernel_spmd(nc, maps, core_ids=list(range(NCORES)), trace=trace)
    return b, res


def assemble(results):
    f = np.float32
    yp = np.zeros((8, SEQ, D), f); ys = np.zeros((128, TS, D), f)
    p_S = np.zeros((1, 8, 16, 64, 64), f); p_sh = np.zeros((1, 8, D), f)
    p_C = np.zeros((1, 8, 8, 128, 64), f); p_n = np.zeros((1, 8, 8, 64), f); p_m = np.zeros((1, 8, 8), f)
    p_cv = np.zeros((1, 8, 3, D), f)
    s_S = np.zeros((1, 128, 16, 64, 64), f); s_sh = np.zeros((1, 128, D), f)
    s_C = np.zeros((1, 128, 8, 128, 64), f); s_n = np.zeros((1, 128, 8, 64), f); s_m = np.zeros((1, 128, 8), f)
    s_cv = np.zeros((1, 128, 3, D), f)
    for core in range(NCORES):
        r = results[core]
        sq = slice(core * NS, (core + 1) * NS)
        y = r["yT"].T
        yp[core] = y[:SEQ]
        ys[sq] = y[SEQ:].reshape(NS, TS, D)
        sho = r["o_shift"]
        p_sh[0, core] = sho[:, :, 0].T.reshape(D)
        s_sh[0, sq] = sho[:, :, 1:].transpose(2, 1, 0).reshape(NS, D)
        p_S[0, core] = r["o_rw_Sp"].transpose(0, 2, 1)
        s_S[0, sq] = r["o_rw_Ss"].transpose(0, 1, 3, 2)
        cp = r["o_ml_Cp"]
        p_C[0, core] = cp[:, :, 0:128].transpose(0, 2, 1)
        p_n[0, core] = cp[:, :, 128]
        cs_ = r["o_ml_Cs"]
        s_C[0, sq] = cs_[:, :, :, 0:128].transpose(0, 1, 3, 2)
        s_n[0, sq] = cs_[:, :, :, 128]
        mo = r["o_ml_m"]
        p_m[0, core] = mo[:, 0]
        s_m[0, sq] = mo[:, 1:].T
        cv = r["o_ml_conv"]
        cvt = cv.transpose(3, 4, 2, 1, 0).reshape(17, 3, D)
        p_cv[0, core] = cvt[0]
        s_cv[0, sq] = cvt[1:]
    return (yp, ys, p_S, p_sh, p_C, p_n, p_m, p_cv, s_S, s_sh, s_C, s_n, s_m, s_cv)


def kernel(**inp):
    b, res = run(inp)
    return assemble(res.results)
```

```python
import numpy as np
from contextlib import ExitStack
import concourse.bass as bass
import concourse.mybir as mybir
from concourse.bass_utils import run_bass_kernel_spmd

F32 = mybir.dt.float32
BF16 = mybir.dt.bfloat16
AF = mybir.ActivationFunctionType
ALU = mybir.AluOpType
AX = mybir.AxisListType

NCORES = 8
D = 1024
NCH = 8
SEQ = 2048
NS = 16
TS = 4
NT = SEQ + NS * TS
DFF = 2816
NFF = DFF // 128
EPS = 1e-6

ENGS = ("pe", "act", "dve", "pool", "sp")
SAME_ENGINE_SYNC = True

VID = {}
_v = 0
for _nm in ("norm_ffa0", "norm_ffa1", "norm_mix0", "norm_mix1", "norm_ffb0", "norm_ffb1", "norm_final",
            "mu0", "mu1", "mu2", "mu3", "mu4", "mu5", "w0", "a0", "k_k", "k_a", "r_k", "gn_w", "gn_b",
            "cw0", "cw1", "cw2", "cw3", "cb", "ml_norm_w"):
    VID[_nm] = _v
    _v += 1
NVEC = _v


class Tok:
    __slots__ = ("name", "w", "r", "excl")

    def __init__(self, name="", excl=False):
        self.name = name
        self.w = []
        self.r = []
        self.excl = excl


class TokMap(dict):
    def __missing__(self, key):
        t = Tok(str(key))
        self[key] = t
        return t


class KB:
    def __init__(self, n_dma_sems=12):
        self.nc = bass.Bass("TRN2", target_bir_lowering=False, dynamic_dma_scratch_size=8192)
        self.es = ExitStack()
        nc = self.nc
        self.sem = {}
        self.count = {}
        self.prog = {e: [] for e in ENGS}
        self.waited = {e: {} for e in ENGS}
        for e in ENGS:
            self.sem[e] = self.es.enter_context(nc.semaphore("s_" + e))
            self.count[e] = 0
        self.dsem = {}
        self.dval = {}
        self.dnext = {}
        for q in ("sp", "pool", "act"):
            self.dsem[q] = []
            for j in range(n_dma_sems):
                key = "d_%s_%d" % (q, j)
                self.sem[key] = self.es.enter_context(nc.semaphore(key))
                self.dsem[q].append(key)
                self.dval[key] = 0
            self.dnext[q] = 0
        self.ninstr = 0

    def _wait(self, eng, semkey, value):
        if value <= 0:
            return
        if self.waited[eng].get(semkey, 0) >= value:
            return
        self.waited[eng][semkey] = value
        sem = self.sem[semkey]
        self.prog[eng].append(lambda e, sem=sem, value=value: e.wait_ge(sem, value))

    def _deps(self, eng, reads, writes):
        deps = set()
        for t in reads:
            deps.update(t.w)
            if t.excl:
                deps.update(x for x in t.r if x[0] != eng)
        for t in writes:
            deps.update(t.w)
            deps.update(t.r)
        for (sk, v) in deps:
            if sk == eng and (eng == "pe" or not SAME_ENGINE_SYNC):
                continue
            self._wait(eng, sk, v)

    def emit(self, eng, fn, reads=(), writes=(), signal=True):
        self._deps(eng, reads, writes)
        self.ninstr += 1
        if signal:
            self.count[eng] += 1
            cid = (eng, self.count[eng])
            sem = self.sem[eng]
            self.prog[eng].append(lambda e, fn=fn, sem=sem: fn(e).then_inc(sem, 1))
        else:
            cid = (eng, self.count[eng] + 1)
            self.prog[eng].append(lambda e, fn=fn: fn(e))
        for t in reads:
            t.r.append(cid)
        for t in writes:
            t.w = [cid]
            t.r = []
        return cid

    def dma(self, q, out, in_, reads=(), writes=()):
        self._deps(q, reads, writes)
        j = self.dnext[q]
        self.dnext[q] = (j + 1) % len(self.dsem[q])
        key = self.dsem[q][j]
        self._wait(q, key, self.dval[key])
        self.dval[key] += 16
        cid = (key, self.dval[key])
        sem = self.sem[key]
        self.ninstr += 1
        self.prog[q].append(lambda e, out=out, in_=in_, sem=sem: e.dma_start(out=out, in_=in_).then_inc(sem, 16))
        for t in reads:
            t.r.append(cid)
        for t in writes:
            t.w = [cid]
            t.r = []
        return cid

    def barrier(self):
        for e in ENGS:
            for e2 in ENGS:
                if e2 != e:
                    self._wait(e, e2, self.count[e2])
            for q in self.dsem:
                for key in self.dsem[q]:
                    self._wait(e, key, self.dval[key])

    def flush(self):
        self.barrier()
        nc = self.nc
        prog = self.prog
        with nc.Block() as block:
            @block.tensor
            def _(e):
                for f in prog["pe"]:
                    f(e)

            @block.scalar
            def _(e):
                for f in prog["act"]:
                    f(e)

            @block.vector
            def _(e):
                for f in prog["dve"]:
                    f(e)

            @block.gpsimd
            def _(e):
                for f in prog["pool"]:
                    f(e)

            @block.sync
            def _(e):
                for f in prog["sp"]:
                    f(e)
        self.prog = {e: [] for e in ENGS}


class Arena:
    def __init__(self, ap, nwords):
        self.ap = ap
        self.n = nwords
        self.top = 0

    def mark(self):
        return self.top

    def release(self, m):
        self.top = m

    def f32(self, nwords):
        assert self.top + nwords <= self.n, ("arena overflow", self.top, nwords, self.n)
        a = self.ap[:, self.top:self.top + nwords]
        self.top += nwords
        return a

    def bf16(self, nelem):
        nwords = (nelem + 1) // 2
        a = self.f32(nwords).bitcast(BF16)
        return a[:, 0:nelem]


ARENA_WORDS = 55000
XNW = 1 + SEQ + NS * (TS + 1)
SOFF = 1 + SEQ
TILES = [(0, 512), (512, 512), (1024, 512), (1536, 512), (2048, 64)]


class Builder:
    def __init__(self, stages):
        self.stages = stages
        self.kb = KB()
        kb = self.kb
        nc = kb.nc
        self.nc = nc
        es = kb.es
        d = {}

        def din(name, shape):
            d[name] = nc.dram_tensor(name, list(shape), F32, kind="ExternalInput").ap()

        def dout(name, shape):
            d[name] = nc.dram_tensor(name, list(shape), F32, kind="ExternalOutput").ap()

        din("xT", (D, NT))
        din("vecs", (128, NVEC * 8))
        for nm in ("ffa_wg", "ffa_wu", "ffb_wg", "ffb_wu"):
            din(nm, (2, D, DFF))
        for nm in ("ffa_wd", "ffb_wd"):
            din(nm, (2, DFF, D))
        for nm in ("rw_wr", "rw_wk", "rw_wv", "rw_wo"):
            din(nm, (1, D, D))
        din("rw_w1", (1, D, 64)); din("rw_w2", (1, 64, D))
        din("rw_a1", (1, D, 64)); din("rw_a2", (1, 64, D))
        din("rw_g1", (1, D, 160)); din("rw_g2", (1, 160, D))
        din("ml_w_in", (1, D, 3088)); din("ml_w_out", (1, D, D))
        din("mlv", (64, 8, 10)); din("bif", (8, 2)); din("ml_m0T", (8, NS))
        din("ml_C0T", (NS, 8, 64, 129)); din("ml_convT", (8, 2, 64, NS, 3))
        din("shiftT", (128, NCH, NS))
        din("rw_S0T", (NS, 16, 64, 64))
        dout("yT", (D, NT))
        dout("o_ml_m", (8, 17)); dout("o_ml_Cp", (8, 64, 129)); dout("o_ml_Cs", (NS, 8, 64, 129))
        dout("o_ml_conv", (64, 8, 2, 17, 3))
        dout("o_shift", (128, NCH, 17))
        dout("o_rw_Sp", (16, 64, 64))
        dout("o_rw_Ss", (NS, 16, 64, 64))
        self.d = d
        self.out_names = [k for k in d if k == "yT" or k.startswith("o_")]

        arena_t = es.enter_context(nc.sbuf_tensor("arena", [128, ARENA_WORDS], F32))
        self.ar = Arena(arena_t, ARENA_WORDS)
        self.psall = es.enter_context(nc.psum_tensor("psall", [128, 8, 512], F32))
        self.ps = [self.psall[:, i, :] for i in range(8)]
        self.tps = [Tok("ps%d" % i, excl=True) for i in range(8)]
        self.bank_rr = 0
        self.bank_pe = {}

        ar = self.ar
        self.X = ar.f32(NCH * NT).rearrange("p (c n) -> p c n", c=NCH)
        self.tX = TokMap()
        self.VEC = ar.f32(NVEC * 8)
        self.tVEC = Tok("vec")
        self.ONES = ar.bf16(128)
        self.tONES = Tok("ones")
        self.XNraw = ar.bf16(NCH * XNW)
        self.XN = self.XNraw[:, 0:NCH * NT].rearrange("p (c n) -> p c n", c=NCH)
        self.XNS = self.XNraw.rearrange("p (c n) -> p c n", c=NCH)
        self.tXN = TokMap()


    BANK_GROUPS = {"A": (0, 1, 2, 3), "B": (4, 5), "C": (6, 7), "C2": (4, 5)}

    def bank(self, group=None):
        if group is None:
            b = self.bank_rr % 8
            self.bank_rr += 1
            return b
        if not hasattr(self, "_grr"):
            self._grr = {}
        k = self._grr.get(group, 0)
        self._grr[group] = k + 1
        g = self.BANK_GROUPS[group]
        return g[k % len(g)]

    def act(self, out, in_, func, reads, writes, **kw):
        return self.kb.emit("act", lambda e: e.activation(out=out, in_=in_, func=func, **kw), reads, writes)

    def cp(self, eng, out, in_, reads, writes):
        if eng == "act":
            return self.kb.emit("act", lambda e: e.activation(out=out, in_=in_, func=AF.Copy), reads, writes)
        return self.kb.emit(eng, lambda e: e.tensor_copy(out=out, in_=in_), reads, writes)

    def tt(self, eng, out, in0, in1, op, reads, writes):
        return self.kb.emit(eng, lambda e: e.tensor_tensor(out=out, in0=in0, in1=in1, op=op), reads, writes)

    def ts(self, eng, out, in0, s1, s2, op0, op1, reads, writes):
        if s2 is None:
            return self.kb.emit(eng, lambda e: e.tensor_scalar(out=out, in0=in0, scalar1=s1, scalar2=None, op0=op0), reads, writes)
        return self.kb.emit(eng, lambda e: e.tensor_scalar(out=out, in0=in0, scalar1=s1, scalar2=s2, op0=op0, op1=op1), reads, writes)

    def stt(self, out, in0, scalar, in1, op0, op1, reads, writes):
        return self.kb.emit("dve", lambda e: e.scalar_tensor_tensor(out=out, in0=in0, scalar=scalar, in1=in1, op0=op0, op1=op1),
                            reads, writes)

    def _pe_rows(self, lhsT, writes):
        K = lhsT.partition_size()
        base = lhsT.base_partition()
        tile = 32 if K <= 32 else (64 if K <= 64 else 128)
        lo, hi = (base // tile) * tile, (base // tile) * tile + tile
        if tile == 128:
            lo, hi = 0, 128
        for t in writes:
            for b in range(8):
                if t is self.tps[b]:
                    prev = self.bank_pe.get(b)
                    if prev is not None and (prev[1] <= lo or hi <= prev[0]):
                        self.kb._wait("pe", "pe", prev[2][1])
                    self.bank_pe[b] = [lo, hi, None]
        return tile < 128

    def _pe_done(self, writes, cid):
        for t in writes:
            for b in range(8):
                if t is self.tps[b] and self.bank_pe.get(b) is not None:
                    self.bank_pe[b][2] = cid

    def mm(self, out, lhsT, rhs, start, stop, reads, writes, signal=None):
        if signal is None:
            signal = stop
        if self._pe_rows(lhsT, writes):
            signal = True
        cid = self.kb.emit("pe", lambda e: e.matmul(out, lhsT, rhs, start=start, stop=stop), reads, writes, signal=signal)
        self._pe_done(writes, cid)
        return cid

    def tr(self, out, in_, ident, reads, writes):
        self._pe_rows(in_, writes)
        cid = self.kb.emit("pe", lambda e: e.transpose(out, in_, ident), reads, writes)
        self._pe_done(writes, cid)
        return cid

    def memset(self, eng, ap, val, writes):
        return self.kb.emit(eng, lambda e: e.memset(ap, val), (), writes)

    def scan(self, out, d0, d1, init, op0, op1, reads, writes):
        return self.kb.emit("dve", lambda e: e.tensor_tensor_scan(out=out, data0=d0, data1=d1, initial=init, op0=op0, op1=op1), reads, writes)

    def recip(self, out, in_, reads, writes):
        return self.kb.emit("dve", lambda e: e.reciprocal(out=out, in_=in_), reads, writes)

    def reduce(self, out, in_, op, reads, writes, axis=None):
        axis = AX.X if axis is None else axis
        return self.kb.emit("dve", lambda e: e.tensor_reduce(out=out, in_=in_, axis=axis, op=op), reads, writes)

    def vcol(self, name, c):
        j = VID[name] * 8 + c
        return self.VEC[:, j:j + 1]

    def load_inputs(self):
        kb, d = self.kb, self.d
        kb.dma("sp", self.VEC, d["vecs"][:, :], writes=[self.tVEC])
        for c in range(NCH):
            for ti, (t0, n) in enumerate(TILES):
                kb.dma("sp", self.X[:, c, t0:t0 + n], d["xT"][c * 128:(c + 1) * 128, t0:t0 + n],
                       writes=[self.tX[c, ti]])
        kb.emit("dve", lambda e: e.memset(self.ONES, 1.0), writes=[self.tONES])

    def _full_bank(self, b):
        self.bank_pe[b] = [0, 128, ("pe", 0)]

    def rmsnorm_tile(self, ti, gname, out_fn, scratch):
        kb = self.kb
        t0, n = TILES[ti]
        SQ, tSQ, LN, tLN, RS, tRS, bank = scratch
        ps = self.ps[bank][:, :n]
        self._full_bank(bank)
        for c in range(NCH):
            s = c % 2
            kb.emit("act", lambda e, c=c, s=s: e.activation(out=SQ[s][:, :n], in_=self.X[:, c, t0:t0 + n], func=AF.Square),
                    reads=[self.tX[c, ti]], writes=[tSQ[s]])
            kb.emit("pe", lambda e, c=c, s=s: e.matmul(ps, self.ONES, SQ[s][:, :n], start=(c == 0), stop=(c == NCH - 1)),
                    reads=[tSQ[s], self.tONES], writes=[self.tps[bank]], signal=True)
        kb.emit("act", lambda e: e.activation(out=LN[:, :n], in_=ps, func=AF.Ln, scale=1.0 / D, bias=self.EPSC),
                reads=[self.tps[bank], self.tCONST], writes=[tLN])
        kb.emit("act", lambda e: e.activation(out=RS[:, :n], in_=LN[:, :n], func=AF.Exp, scale=-0.5),
                reads=[tLN], writes=[tRS])
        for c in range(NCH):
            out_fn(c, RS[:, :n], tRS)

    def consts(self):
        kb, ar = self.kb, self.ar
        self.CONST = ar.f32(8)
        self.tCONST = Tok("const")
        self.EPSC = self.CONST[:, 0:1]
        self.ONEC = self.CONST[:, 1:2]
        self.NHALFC = self.CONST[:, 2:3]
        self.GNEPSC = self.CONST[:, 3:4]
        for col, val in ((0, EPS), (1, 1.0), (2, -0.5), (3, 64e-5)):
            kb.emit("dve", lambda e, col=col, val=val: e.memset(self.CONST[:, col:col + 1], val), (), [self.tCONST])
        ONESF = ar.f32(128)
        tO = Tok("onesf")
        self.memset("pool", ONESF, 1.0, [tO])
        self.IDENTF = ar.f32(128)
        self.IDENTB = ar.bf16(128)
        self.BONES = ar.bf16(128)
        self.tMASK = Tok("masks")
        kb.emit("pool", lambda e: e.affine_select(out=self.IDENTF, in_=ONESF, pattern=[[-1, 128]], compare_op=ALU.is_equal,
                                                  fill=0.0, base=0, channel_multiplier=1), [tO], [self.tMASK])
        self.cp("pool", self.IDENTB, self.IDENTF, [self.tMASK], [self.tMASK])
        self.memset("pool", self.BONES, 0.0, [self.tMASK])
        self.memset("pool", self.BONES[0:64, 0:64], 1.0, [self.tMASK])
        self.memset("pool", self.BONES[64:128, 64:128], 1.0, [self.tMASK])
        MSU = ar.f32(64)
        MIU = ar.f32(64)
        self.MASKXT = ar.f32(64)
        kb.emit("pool", lambda e: e.affine_select(out=MSU[0:64, :], in_=ONESF[0:64, 0:64], pattern=[[1, 64]], compare_op=ALU.is_gt,
                                                  fill=0.0, base=0, channel_multiplier=-1), [tO], [self.tMASK])
        kb.emit("pool", lambda e: e.affine_select(out=MIU[0:64, :], in_=ONESF[0:64, 0:64], pattern=[[1, 64]], compare_op=ALU.is_ge,
                                                  fill=0.0, base=0, channel_multiplier=-1), [tO], [self.tMASK])
        kb.emit("pool", lambda e: e.affine_select(out=self.MASKXT[0:64, :], in_=ONESF[0:64, 0:64], pattern=[[-1, 64]], compare_op=ALU.is_gt,
                                                  fill=0.0, base=0, channel_multiplier=1), [tO], [self.tMASK])
        self.ts("pool", self.MASKXT[0:64, :], self.MASKXT[0:64, :], -1.0, None, ALU.mult, None, [self.tMASK], [self.tMASK])
        self.MIU = MIU
        self.MASKLL = ar.f32(2 * 4 * 64).rearrange("p (h b t) -> p h b t", h=2, b=4)
        for h in range(2):
            self.cp("pool", self.MASKLL[0:64, h, 0, :], MSU[0:64, :], [self.tMASK], [self.tMASK])
            self.cp("pool", self.MASKLL[0:64, h, 1, :], MIU[0:64, :], [self.tMASK], [self.tMASK])
            self.ts("pool", self.MASKLL[0:64, h, 2, :], MSU[0:64, :], -1.0, None, ALU.mult, None, [self.tMASK], [self.tMASK])
            self.cp("pool", self.MASKLL[0:64, h, 3, :], MIU[0:64, :], [self.tMASK], [self.tMASK])
        self.SM64 = ar.f32(256)
        self.SM4 = ar.f32(16)
        self.memset("pool", self.SM64, 1.0, [self.tMASK])
        self.memset("pool", self.SM64.rearrange("p (j l) -> p j l", l=64)[:, :, 0:1], 0.0, [self.tMASK])
        self.memset("pool", self.SM4, 1.0, [self.tMASK])
        self.memset("pool", self.SM4.rearrange("p (j l) -> p j l", l=4)[:, :, 0:1], 0.0, [self.tMASK])
        self.NEGW0 = ar.f32(8)
        j = VID["w0"] * 8
        self.ts("dve", self.NEGW0, self.VEC[:, j:j + 8], -1.0, None, ALU.mult, None, [self.tVEC], [self.tMASK])

    def rwkv(self):
        kb, d, ar = self.kb, self.d, self.ar
        m0 = ar.mark()
        XNS = self.XNS
        tXNS = TokMap()
        gname = "norm_mix0"
        mu = lambda i, K: self.vcol("mu%d" % i, K)

        HWA = ar.bf16(NT)
        HG1 = ar.bf16(NT)
        HG2 = ar.bf16(NT)
        tHWA, tHG = TokMap(), TokMap()
        W2A2 = ar.bf16(D)
        G2A = ar.bf16(D)
        G2B = ar.bf16(D)
        tW2 = Tok("w2a2g2")
        SHO = ar.f32(NCH * 17).rearrange("p (c j) -> p c j", c=NCH)
        tSHO = Tok("sho")
        SHI = ar.f32(NCH * NS).rearrange("p (c j) -> p c j", c=NCH)
        tSHI = Tok("shi")

        kb.dma("pool", W2A2[0:64, :], d["rw_w2"][0], writes=[tW2])
        kb.dma("pool", W2A2[64:128, :], d["rw_a2"][0], writes=[tW2])
        kb.dma("pool", G2A, d["rw_g2"][0, 0:128, :], writes=[tW2])
        self.memset("pool", G2B, 0.0, [tW2])
        self.memset("pool", HG2, 0.0, [tHG[0]])
        kb.dma("pool", G2B[0:32, :], d["rw_g2"][0, 128:160, :], writes=[tW2])
        kb.dma("sp", SHI, d["shiftT"], writes=[tSHI])

        for c in range(NCH):
            self.memset("pool", XNS[:, c, 0:1], 0.0, [tXNS[c, "init"]])
            sv = XNS[:, c, SOFF:SOFF + NS * 5].rearrange("p (j u) -> p j u", u=5)
            self.cp("pool", sv[:, :, 0], SHI[:, c, :], [tSHI], [tXNS[c, "init"]])

        def xn_aps(K, t0, n):
            if t0 < SEQ:
                return XNS[:, K, 1 + t0:1 + t0 + n], XNS[:, K, t0:t0 + n]
            j0 = (t0 - SEQ) // TS
            nj = n // TS
            sv = XNS[:, K, SOFF + 5 * j0:SOFF + 5 * (j0 + nj)].rearrange("p (j u) -> p j u", u=5)
            return sv[:, :, 1:5], sv[:, :, 0:4]

        def xn_toks(K, t0):
            ti = min(t0 // 512, 4)
            return [tXNS[K, ti], tXNS[K, max(ti - 1, 0)], tXNS[K, "init"]]

        def pview(ps_ap, t0, n):
            if t0 < SEQ:
                return ps_ap
            return ps_ap.rearrange("p (j t) -> p j t", t=TS)

        def mixproj(out, wa, wb, cols, t0, n, wtok, ptok):
            o = pview(out, t0, n)
            for K in range(NCH):
                xa, xb = xn_aps(K, t0, n)
                self.mm(o, wa[:, K, cols], xa, K == 0, False, [wtok] + xn_toks(K, t0), [ptok])
                self.mm(o, wb[:, K, cols], xb, False, K == NCH - 1, [wtok] + xn_toks(K, t0), [ptok])

        def scale_w(raw, wb, Mcols, mu_list, tok):
            for K in range(NCH):
                for (cs, mi) in mu_list:
                    self.ts("pool", wb[:, K, cs], raw[:, K, cs], mu(mi, K), None, ALU.mult, None, [tok, self.tVEC], [tok])
            self.tt("pool", raw, raw, wb, ALU.subtract, [tok], [tok])

        m1 = ar.mark()
        W1A = ar.bf16(NCH * 128).rearrange("p (k m) -> p k m", k=NCH)
        W1B = ar.bf16(NCH * 128).rearrange("p (k m) -> p k m", k=NCH)
        G1A = ar.bf16(NCH * 160).rearrange("p (k m) -> p k m", k=NCH)
        G1B = ar.bf16(NCH * 160).rearrange("p (k m) -> p k m", k=NCH)
        tW1, tG1 = Tok("w1a1"), Tok("g1")
        for K in range(NCH):
            kb.dma("pool", W1A[:, K, 0:64], d["rw_w1"][0, K * 128:(K + 1) * 128, :], writes=[tW1])
            kb.dma("pool", W1A[:, K, 64:128], d["rw_a1"][0, K * 128:(K + 1) * 128, :], writes=[tW1])
            kb.dma("pool", G1A[:, K, :], d["rw_g1"][0, K * 128:(K + 1) * 128, :], writes=[tG1])
        scale_w(W1A, W1B, 128, [(slice(0, 64), 1), (slice(64, 128), 4)], tW1)
        scale_w(G1A, G1B, 160, [(slice(0, 160), 5)], tG1)
        XNF = [ar.f32(512) for _ in range(2)]
        SQ = [ar.bf16(512) for _ in range(2)]
        LN = ar.f32(512)
        RS = ar.f32(512)
        tXNF, tSQ = TokMap(), TokMap()
        tLN, tRS = Tok("ln"), Tok("rs")
        rr = [0]
        for ti, (t0, n) in enumerate(TILES):
            def out_fn(c, rs, trs, ti=ti, t0=t0, n=n):
                s = rr[0] % 2
                rr[0] += 1
                xf = XNF[s][:, :n]
                self.stt(xf, self.X[:, c, t0:t0 + n], self.vcol(gname, c), rs, ALU.mult, ALU.mult,
                         [self.tX[c, ti], trs, self.tVEC], [tXNF[s]])
                if t0 < SEQ:
                    self.cp("act", XNS[:, c, 1 + t0:1 + t0 + n], xf, [tXNF[s]], [tXNS[c, ti]])
                    if t0 + n == SEQ:
                        self.cp("pool", SHO[:, c, 0:1], xf[:, n - 1:n], [tXNF[s]], [tSHO])
                else:
                    sv = XNS[:, c, SOFF:SOFF + NS * 5].rearrange("p (j u) -> p j u", u=5)
                    xv = xf.rearrange("p (j t) -> p j t", t=TS)
                    self.cp("act", sv[:, :, 1:5], xv, [tXNF[s]], [tXNS[c, ti]])
                    self.cp("pool", SHO[:, c, 1:17], xv[:, :, 3], [tXNF[s]], [tSHO])
            self.rmsnorm_tile(ti, gname, out_fn, (SQ, tSQ, LN, tLN, RS, tRS, self.bank()))
            b1, b2, b3 = self.bank(), self.bank(), self.bank()
            mixproj(self.ps[b1][:, :n], W1A, W1B, slice(0, 128), t0, n, tW1, self.tps[b1])
            mixproj(self.ps[b2][:, :n], G1A, G1B, slice(0, 128), t0, n, tG1, self.tps[b2])
            mixproj(self.ps[b3][0:32, :n], G1A, G1B, slice(128, 160), t0, n, tG1, self.tps[b3])
            self.act(HWA[0:64, t0:t0 + n], self.ps[b1][0:64, :n], AF.Tanh, [self.tps[b1]], [tHWA[ti]])
            self.cp("act", HWA[64:128, t0:t0 + n], self.ps[b1][64:128, :n], [self.tps[b1]], [tHWA[ti]])
            self.act(HG1[:, t0:t0 + n], self.ps[b2][:, :n], AF.Sigmoid, [self.tps[b2]], [tHG[ti]])
            self.act(HG2[0:32, t0:t0 + n], self.ps[b3][0:32, :n], AF.Sigmoid, [self.tps[b3]], [tHG[ti]])
        ar.release(m1)
        kb.barrier()
        kb.dma("sp", d["o_shift"], SHO, reads=[tSHO])

        WN = 256

        def f32t():
            return ar.f32(WN)

        def bf16t():
            return ar.bf16(WN)
        WA = {nm: ar.bf16(NCH * 128).rearrange("p (k m) -> p k m", k=NCH) for nm in "rkv"}
        WB = {nm: ar.bf16(NCH * 128).rearrange("p (k m) -> p k m", k=NCH) for nm in "rkv"}
        WO2 = [ar.bf16(D) for _ in range(2)]
        tWc = {nm: Tok("w" + nm) for nm in "rkv"}
        tWO = [Tok("wo0"), Tok("wo1")]
        Rf, Kf, Vf, A_, EW, CUM, EM, EQ, KK, KF, Bv, T1, T2, YF = [f32t() for _ in range(14)]
        Vb, SQb, RKR, KTb, BTb, YG = [bf16t() for _ in range(6)]
        S3 = []
        for _ in range(3):
            S3.append(dict(
                KR=ar.bf16(2 * WN).rearrange("p (a n) -> p a n", a=2),
                KTt=ar.bf16(4 * 128).rearrange("p (j m) -> p j m", j=4),
                BTt=ar.bf16(4 * 128).rearrange("p (j m) -> p j m", j=4),
                VTt=ar.bf16(4 * 128).rearrange("p (j m) -> p j m", j=4),
                EP=f32t(), BONUS=f32t(), Gf=f32t(), LLs=ar.bf16(4 * 2 * 4 * 64)))
        S2 = []
        for _ in range(2):
            S2.append(dict(XTs=ar.bf16(4 * 2 * 64), PW=[ar.bf16(8 * 2 * 64) for _ in range(2)]))
        PT = [ar.bf16(8 * 64) for _ in range(2)]
        Gs = ar.bf16(128)
        NU = ar.bf16(128)
        YT = ar.f32(512)
        SQ2 = ar.f32(512)
        STAT = ar.f32(32)
        H = ar.f32(64)
        H0d = ar.f32(64)
        Hb = ar.bf16(64)
        HS = ar.f32(4 * 64).rearrange("p (j v) -> p j v", j=4)
        HSd = ar.f32(4 * 64).rearrange("p (j v) -> p j v", j=4)
        HSb = ar.bf16(4 * 64).rearrange("p (j v) -> p j v", j=4)
        T = TokMap()

        main_tiles = [(t0, 256, 64) for t0 in range(0, SEQ, 256)] + [(SEQ + 16 * q, 16, 4) for q in range(4)]
        import os as _os
        if "RW_TILES" in _os.environ:
            main_tiles = [main_tiles[int(i)] for i in _os.environ["RW_TILES"].split(",")]
        NCc = int(_os.environ.get("RW_NC", NCH))
        units = []
        for c in range(NCc):
            for k_, (t0, n, L) in enumerate(main_tiles):
                u = len(units)
                units.append(dict(u=u, c=c, t0=t0, n=n, L=L, first=(k_ == 0), last=(k_ == len(main_tiles) - 1),
                                  lastprompt=(t0 < SEQ and (k_ + 1 == len(main_tiles) or main_tiles[k_ + 1][0] >= SEQ))))

        def load_weights(c):
            ccols = slice(c * 128, (c + 1) * 128)
            for nm, key, mi in (("r", "rw_wr", 0), ("k", "rw_wk", 2), ("v", "rw_wv", 3)):
                for K in range(NCH):
                    kb.dma("pool", WA[nm][:, K, :], d[key][0, K * 128:(K + 1) * 128, ccols], writes=[tWc[nm]])
                scale_w(WA[nm], WB[nm], 128, [(slice(0, 128), mi)], tWc[nm])
            kb.dma("pool", WO2[c % 2], d["rw_wo"][0, c * 128:(c + 1) * 128, :], writes=[tWO[c % 2]])

        def stageA(U):
            u, c, t0, n, L = U["u"], U["c"], U["t0"], U["n"], U["L"]
            s3, s2 = S3[u % 3], S2[u % 2]
            k3, k2 = u % 3, u % 2
            sample = t0 >= SEQ
            NCk = 4
            ti5 = min(t0 // 512, 4)
            tsl = slice(t0, t0 + n)
            cs = lambda j: slice(j * L, (j + 1) * L)
            ccols = slice(c * 128, (c + 1) * 128)
            if U["first"]:
                load_weights(c)
                yield
            KR, KTt, BTt, VTt, EP, BONUS, Gf = s3["KR"], s3["KTt"], s3["BTt"], s3["VTt"], s3["EP"], s3["BONUS"], s3["Gf"]
            tKR, tKTt, tBTt, tVTt, tEP, tBONUS, tGf, tLL = (T["KR", k3], T["KTt", k3], T["BTt", k3], T["VTt", k3], T["EP", k3],
                                                          T["BONUS", k3], T["Gf", k3], T["LLs", k3])
            tXT = T["XTs", k2]
            bA, bB, bC, bD = self.bank("A"), self.bank("A"), self.bank("A"), self.bank("A")
            PR, PK = self.ps[bA][:, 0:n], self.ps[bA][:, 256:256 + n]
            PV, PGt = self.ps[bB][:, 0:n], self.ps[bB][:, 256:256 + n]
            PWL, PAL = self.ps[bC][:, 0:n], self.ps[bC][:, 256:256 + n]
            PKK, PSm = self.ps[bD][:, 0:n], self.ps[bD][:, 256:256 + n]
            for K0 in range(0, NCH, 2):
                pass
            mixproj(PR, WA["r"], WB["r"], slice(0, 128), t0, n, tWc["r"], self.tps[bA])
            yield
            mixproj(PK, WA["k"], WB["k"], slice(0, 128), t0, n, tWc["k"], self.tps[bA])
            yield
            mixproj(PV, WA["v"], WB["v"], slice(0, 128), t0, n, tWc["v"], self.tps[bB])
            self.mm(PGt, G2A[:, ccols], HG1[:, tsl], True, False, [tW2, tHG[ti5]], [self.tps[bB]])
            self.mm(PGt, G2B[:, ccols], HG2[:, tsl], False, True, [tW2, tHG[ti5], tHG[0]], [self.tps[bB]])
            self.mm(PWL, W2A2[0:64, ccols], HWA[0:64, tsl], True, True, [tW2, tHWA[ti5]], [self.tps[bC]])
            self.mm(PAL, W2A2[64:128, ccols], HWA[64:128, tsl], True, True, [tW2, tHWA[ti5]], [self.tps[bC]])
            yield
            w = lambda a: a[:, 0:n]
            tV = self.tVEC
            self.cp("act", w(Rf), PR, [self.tps[bA]], [T["Rf"]])
            self.cp("act", w(Kf), PK, [self.tps[bA]], [T["Kf"]])
            yield
            self.cp("act", w(Vf), PV, [self.tps[bB]], [T["Vf"]])
            self.cp("act", w(Gf), PGt, [self.tps[bB]], [tGf])
            self.cp("dve", w(Vb), w(Vf), [T["Vf"]], [T["Vb"]])
            yield
            self.act(w(A_), PAL, AF.Sigmoid, [self.tps[bC], tV], [T["A"]], bias=self.vcol("a0", c))
            self.act(w(T1), PWL, AF.Exp, [self.tps[bC], self.tMASK], [T["T1"]], scale=-1.0, bias=self.NEGW0[:, c:c + 1])
            self.ts("dve", w(KK), w(Kf), self.vcol("k_k", c), None, ALU.mult, None, [T["Kf"], tV], [T["KK"]])
            yield
            self.act(w(T1), w(T1), AF.Ln, [T["T1"], self.tCONST], [T["T1"]], bias=self.ONEC)
            self.act(w(SQb), w(KK), AF.Square, [T["KK"]], [T["SQb"]])
            self.mm(PKK, self.BONES, w(SQb), True, True, [T["SQb"], self.tMASK], [self.tps[bD]], signal=True)
            yield
            self.act(w(EW), w(T1), AF.Exp, [T["T1"], self.tCONST], [T["EW"]], scale=-1.0, bias=self.NHALFC)
            self.ts("dve", w(T1), w(A_), -1.0, self.vcol("k_a", c), ALU.add, ALU.mult, [T["A"], tV], [T["T1"]])
            self.stt(w(KF), w(T1), 1.0, w(Kf), ALU.add, ALU.mult, [T["T1"], T["Kf"]], [T["KF"]])
            yield
            SM = self.SM4[:, 0:n] if sample else self.SM64[:, 0:n]
            self.scan(w(CUM), SM, w(EW), 0.0, ALU.mult, ALU.subtract, [T["EW"], self.tMASK], [T["CUM"]])
            self.act(w(T2), PKK, AF.Sqrt, [self.tps[bD]], [T["T2"]])
            yield
            self.act(w(EP), w(CUM), AF.Exp, [T["CUM"]], [tEP])
            self.act(w(EM), w(CUM), AF.Exp, [T["CUM"]], [T["EM"]], scale=-1.0)
            self.ts("dve", w(T2), w(T2), 1e-12, None, ALU.max, None, [T["T2"]], [T["T2"]])
            self.recip(w(T2), w(T2), [T["T2"]], [T["T2"]])
            yield
            self.tt("dve", w(KK), w(KK), w(T2), ALU.mult, [T["KK"], T["T2"]], [T["KK"]])
            self.tt("dve", w(T2), w(CUM), w(EW), ALU.add, [T["CUM"], T["EW"], T["KK"]], [T["T2"]])
            self.act(w(EQ), w(T2), AF.Exp, [T["T2"]], [T["EQ"]])
            yield
            self.stt(w(RKR), w(Rf), self.vcol("r_k", c), w(KF), ALU.mult, ALU.mult, [T["Rf"], T["KF"], tV], [T["RKR"]])
            self.mm(PSm, self.BONES, w(RKR), True, True, [T["RKR"], self.tMASK], [self.tps[bD]], signal=True)
            self.tt("dve", w(Bv), w(KK), w(A_), ALU.mult, [T["KK"], T["A"]], [T["Bv"]])
            self.tt("dve", KR[:, 1, 0:n], w(Rf), w(EP), ALU.mult, [T["Rf"], tEP], [tKR])
            yield
            self.tt("dve", KR[:, 0, 0:n], w(KK), w(EQ), ALU.mult, [T["KK"], T["EQ"]], [tKR])
            self.tt("dve", w(KTb), w(KF), w(EM), ALU.mult, [T["KF"], T["EM"]], [T["KTb"]])
            self.tt("dve", w(BTb), w(Bv), w(EM), ALU.mult, [T["Bv"], T["EM"]], [T["BTb"]])
            self.tt("dve", w(BONUS), PSm, w(Vf), ALU.mult, [self.tps[bD], T["Vf"]], [tBONUS])
            yield
            bT = self.bank("A")
            PTr = self.ps[bT].bitcast(BF16)
            for (src, ts_, off) in ((KTb, "KTb", 0), (BTb, "BTb", 1)):
                for j in range(NCk):
                    self.tr(PTr[0:L, off * 512 + j * 128:off * 512 + (j + 1) * 128], src[:, cs(j)], self.IDENTB,
                            [T[ts_], self.tMASK], [self.tps[bT]])
            self.cp("act", KTt[0:L, :, :], PTr[0:L, 0:512].rearrange("p (j m) -> p j m", j=4), [self.tps[bT]], [tKTt])
            self.cp("act", BTt[0:L, :, :], PTr[0:L, 512:1024].rearrange("p (j m) -> p j m", j=4), [self.tps[bT]], [tBTt])
            yield
            bT2 = self.bank("A")
            PTr2 = self.ps[bT2].bitcast(BF16)
            for j in range(NCk):
                self.tr(PTr2[0:L, j * 128:(j + 1) * 128], Vb[:, cs(j)], self.IDENTB, [T["Vb"], self.tMASK], [self.tps[bT2]])
            self.cp("act", VTt[0:L, :, :], PTr2[0:L, 0:512].rearrange("p (j m) -> p j m", j=4), [self.tps[bT2]], [tVTt])
            yield
            LLv = s3["LLs"][:, 0:4 * 2 * 4 * L].rearrange("p (j h b t) -> p j h b t", j=4, h=2, b=4)
            XTv = s2["XTs"][:, 0:4 * 2 * L].rearrange("p (j h t) -> p j h t", j=4, h=2)
            mk = self.MASKLL[0:L, :, :, 0:L]
            for h in range(2):
                hs = slice(64 * h, 64 * h + 64)
                bX = self.bank("A")
                PXT = self.ps[bX][:, 0:4 * L].rearrange("p (j t) -> p j t", j=4)
                for g0 in (0, 2):
                    bL = self.bank("A")
                    PLL = self.ps[bL][:, 0:2 * 4 * L].rearrange("p (j b t) -> p j b t", j=2, b=4)
                    for jj in range(2):
                        j = g0 + jj
                        self.mm(PLL[0:L, jj, 0:2, :], KTb[hs, cs(j)], KR[hs, :, cs(j)], True, True, [T["KTb"], tKR], [self.tps[bL]])
                        self.mm(PLL[0:L, jj, 2:4, :], BTb[hs, cs(j)], KR[hs, :, cs(j)], True, True, [T["BTb"], tKR], [self.tps[bL]])
                        self.mm(PXT[0:L, j, :], KR[hs, 0, cs(j)], BTb[hs, cs(j)], True, True, [T["BTb"], tKR], [self.tps[bX]])
                    self.tt("dve", LLv[0:L, g0:g0 + 2, h], PLL[0:L], mk, ALU.mult, [self.tps[bL], self.tMASK], [tLL])
                    yield
                self.tt("dve", XTv[0:L, :, h, :], PXT[0:L], self.MASKXT[0:L, 0:L].unsqueeze(1).to_broadcast([L, 4, L]), ALU.mult,
                        [self.tps[bX], self.tMASK], [tXT])
                yield

        def stageB(U):
            u, L = U["u"], U["L"]
            s3, s2 = S3[u % 3], S2[u % 2]
            k3, k2 = u % 3, u % 2
            NCk, NM = 4, 8
            LLv = s3["LLs"][:, 0:4 * 2 * 4 * L].rearrange("p (j h b t) -> p j h b t", j=4, h=2, b=4)
            XTv = s2["XTs"][:, 0:4 * 2 * L].rearrange("p (j h t) -> p j h t", j=4, h=2)
            tLL, tXT = T["LLs", k3], T["XTs", k2]
            PWv = [p[:, 0:NM * 2 * L].rearrange("p (i a t) -> p i a t", i=NM, a=2) for p in s2["PW"]]
            PTv = [p[:, 0:NM * L].rearrange("p (i t) -> p i t", i=NM) for p in PT]
            tPW = [T["PW", k2, 0], T["PW", k2, 1]]
            PW4 = PWv[0].rearrange("p (j h) a t -> p j h a t", h=2)
            self.cp("dve", PW4[0:L, :, :, 0, :], LLv[0:L, :, :, 2, :], [tLL], [tPW[0]])
            self.cp("pool", PWv[0][0:L, :, 1, :], self.IDENTB[0:L, 0:L].unsqueeze(1).to_broadcast([L, NM, L]), [self.tMASK], [tPW[0]])
            self.cp("act", PTv[0][0:L].rearrange("p (j h) t -> p j h t", h=2), XTv[0:L], [tXT], [T["PT", 0]])
            yield
            nlev = 6 if L == 64 else 2
            cur = 0
            mpb = 4 if L == 64 else 8
            for lev in range(nlev):
                nxt = 1 - cur
                last = lev == nlev - 1
                for i0 in range(0, NM, mpb):
                    bI = self.bank("B")
                    PA = self.ps[bI][:, 0:mpb * 2 * L].rearrange("p (i a t) -> p i a t", i=mpb, a=2)
                    for ii in range(mpb):
                        i = i0 + ii
                        self.mm(PA[0:L, ii], PTv[cur][0:L, i, :], PWv[cur][0:L, i], True, True,
                                [T["PT", cur], tPW[cur]], [self.tps[bI]], signal=(ii == mpb - 1))
                    if not last:
                        self.cp("act", PWv[nxt][0:L, i0:i0 + mpb, 0, :], PA[0:L, :, 0, :], [self.tps[bI]], [tPW[nxt]])
                    self.tt("dve", PWv[nxt][0:L, i0:i0 + mpb, 1, :], PA[0:L, :, 1, :], PWv[cur][0:L, i0:i0 + mpb, 1, :], ALU.add,
                            [self.tps[bI], tPW[cur]], [tPW[nxt]])
                    yield
                if not last:
                    bJ = self.bank("B")
                    PB = self.ps[bJ][:, 0:NM * L].rearrange("p (i t) -> p i t", i=NM)
                    for i in range(NM):
                        self.mm(PB[0:L, i, :], PWv[cur][0:L, i, 0, :], PTv[cur][0:L, i, :], True, True,
                                [T["PT", cur], tPW[cur]], [self.tps[bJ]], signal=(i == NM - 1))
                    self.cp("act", PTv[nxt][0:L], PB[0:L], [self.tps[bJ]], [T["PT", nxt]])
                    yield
                cur = nxt
            U["Wv"] = PWv[cur]
            U["tWv"] = tPW[cur]

        def stageC(U):
            u, c, t0, n, L = U["u"], U["c"], U["t0"], U["n"], U["L"]
            s3 = S3[u % 3]
            k3 = u % 3
            sample = t0 >= SEQ
            NCk = 4
            ti5 = min(t0 // 512, 4)
            tsl = slice(t0, t0 + n)
            cs = lambda j: slice(j * L, (j + 1) * L)
            w = lambda a: a[:, 0:n]
            tV = self.tVEC
            KR, KTt, BTt, VTt, EP, BONUS, Gf = s3["KR"], s3["KTt"], s3["BTt"], s3["VTt"], s3["EP"], s3["BONUS"], s3["Gf"]
            tKR, tKTt, tBTt, tVTt, tEP, tBONUS, tGf, tLL = (T["KR", k3], T["KTt", k3], T["BTt", k3], T["VTt", k3], T["EP", k3],
                                                          T["BONUS", k3], T["Gf", k3], T["LLs", k3])
            LLv = s3["LLs"][:, 0:4 * 2 * 4 * L].rearrange("p (j h b t) -> p j h b t", j=4, h=2, b=4)
            Wv, tWv = U["Wv"], U["tWv"]
            WO = WO2[c % 2]
            if U["first"]:
                self.memset("pool", H, 0.0, [T["H"]])
                self.memset("pool", Hb, 0.0, [T["Hb"]])
            if sample:
                q = (t0 - SEQ) // 16
                for jj in range(4):
                    kb.dma("sp", HS[:, jj, :], d["rw_S0T"][4 * q + jj, 2 * c:2 * c + 2].rearrange("h k v -> (h k) v"),
                           writes=[T["HS", jj]])
                    self.cp("act", HSb[:, jj, :], HS[:, jj, :], [T["HS", jj]], [T["HSb", jj]])
                yield
            YTv = YT[:, 0:4 * 128].rearrange("p (j m) -> p j m", j=4)
            for j in range(NCk):
                if sample:
                    Hc, Hbc, Hdc = HS[:, j, :], HSb[:, j, :], HSd[:, j, :]
                    tH, tHb, tHd = T["HS", j], T["HSb", j], T["HSd", j]
                else:
                    Hc, Hbc, Hdc = H, Hb, H0d
                    tH, tHb, tHd = T["H"], T["Hb"], T["H0d"]
                DL = EP[:, (j + 1) * L - 1:(j + 1) * L]
                bS = self.bank("C")
                PG = self.ps[bS][0:L, 0:128]
                PU = self.ps[bS][0:L, 128:256]
                PY = self.ps[bS][0:L, 256:384]
                bH = self.bank("C")
                PH = self.ps[bH][:, 0:64]
                tS = self.tps[bS]
                tSH = self.tps[bH]
                for h in range(2):
                    hs = slice(64 * h, 64 * h + 64)
                    self.mm(PG[:, hs], LLv[0:L, j, h, 0, :], VTt[0:L, j, hs], True, False, [tLL, tVTt], [tS])
                    self.mm(PG[:, hs], KR[hs, 0, cs(j)], Hbc[hs, :], False, True, [tKR, tHb], [tS], signal=True)
                self.act(Hdc, Hc, AF.Identity, [tH, tEP], [tHd], scale=DL)
                yield
                self.cp("act", Gs[0:L, :], PG, [tS], [T["Gs"]])
                yield
                for h in range(2):
                    hs = slice(64 * h, 64 * h + 64)
                    self.mm(PU[:, hs], Wv[0:L, 2 * j + h, 1, :], Gs[0:L, hs], True, True, [tWv, T["Gs"]], [tS], signal=True)
                yield
                self.act(NU[0:L, :], PU, AF.Identity, [tS], [T["NU"]], scale=-1.0)
                yield
                for h in range(2):
                    hs = slice(64 * h, 64 * h + 64)
                    self.mm(PH[hs, :], KTt[0:L, j, hs], VTt[0:L, j, hs], True, False, [tKTt, tVTt], [tSH])
                    self.mm(PH[hs, :], BTt[0:L, j, hs], NU[0:L, hs], False, True, [tBTt, T["NU"]], [tSH], signal=True)
                for h in range(2):
                    hs = slice(64 * h, 64 * h + 64)
                    self.mm(PY[:, hs], LLv[0:L, j, h, 1, :], VTt[0:L, j, hs], True, False, [tLL, tVTt], [tS])
                    self.mm(PY[:, hs], LLv[0:L, j, h, 3, :], NU[0:L, hs], False, False, [tLL, T["NU"]], [tS])
                    self.mm(PY[:, hs], KR[hs, 1, cs(j)], Hbc[hs, :], False, True, [tKR, tHb], [tS], signal=True)
                yield
                self.stt(Hbc, PH, DL, Hdc, ALU.mult, ALU.add, [tSH, tEP, tHd], [tHb])
                self.stt(Hc, PH, DL, Hdc, ALU.mult, ALU.add, [tSH, tEP, tHd], [tH])
                self.cp("act", YTv[0:L, j, :], PY, [tS], [T["YT"]])
                yield
            if sample:
                for jj in range(4):
                    kb.dma("sp", d["o_rw_Ss"][4 * q + jj, 2 * c:2 * c + 2].rearrange("h k v -> (h k) v"), HS[:, jj, :],
                           reads=[T["HS", jj]])
            if U["lastprompt"]:
                kb.dma("sp", d["o_rw_Sp"][2 * c:2 * c + 2].rearrange("h k v -> (h k) v"), H, reads=[T["H"]])
            G8 = 8
            YT3 = YT[:, 0:512].rearrange("p (g v) -> p g v", g=G8)
            SQ3 = SQ2[:, 0:512].rearrange("p (g v) -> p g v", g=G8)
            SUMv, VARv, RSTv = STAT[:, 0:8], STAT[:, 8:16], STAT[:, 16:24]
            self.reduce(SUMv[0:L, :], YT3[0:L], ALU.add, [T["YT"]], [T["SUM"]])
            self.ts("dve", SUMv[0:L, :], SUMv[0:L, :], 1.0 / 64, None, ALU.mult, None, [T["SUM"]], [T["SUM"]])
            yield
            self.tt("dve", YT3[0:L], YT3[0:L], SUMv[0:L, :].unsqueeze(2).to_broadcast([L, G8, 64]), ALU.subtract,
                    [T["YT"], T["SUM"]], [T["YT"]])
            yield
            self.act(SQ2[0:L, 0:512], YT[0:L, 0:512], AF.Square, [T["YT"]], [T["SQ2"]])
            yield
            self.reduce(VARv[0:L, :], SQ3[0:L], ALU.add, [T["SQ2"]], [T["VAR"]])
            yield
            self.act(RSTv[0:L, :], VARv[0:L, :], AF.Ln, [T["VAR"], self.tCONST], [T["RST"]], scale=1.0 / 64, bias=self.GNEPSC[0:L, :])
            yield
            self.act(RSTv[0:L, :], RSTv[0:L, :], AF.Exp, [T["RST"]], [T["RST"]], scale=-0.5)
            yield
            self.tt("dve", YT3[0:L], YT3[0:L], RSTv[0:L, :].unsqueeze(2).to_broadcast([L, G8, 64]), ALU.mult,
                    [T["YT"], T["RST"]], [T["YT"]])
            yield
            bY = self.bank("C")
            PYF = self.ps[bY][:, 0:n]
            for j in range(NCk):
                self.tr(PYF[:, cs(j)], YTv[0:L, j, :], self.IDENTF[0:L, 0:L], [T["YT"], self.tMASK], [self.tps[bY]])
            yield
            self.act(w(YF), PYF, AF.Identity, [self.tps[bY], tV], [T["YF"]], scale=self.vcol("gn_w", c), bias=self.vcol("gn_b", c))
            yield
            self.tt("pool", w(YF), w(YF), w(BONUS), ALU.add, [T["YF"], tBONUS], [T["YF"]])
            yield
            self.tt("dve", w(YG), w(YF), w(Gf), ALU.mult, [T["YF"], tGf], [T["YG"]])
            yield
            for dc0 in range(0, NCH, 2):
                bO = self.bank("C")
                for k2_ in range(2):
                    dc = dc0 + k2_
                    PO = self.ps[bO][:, 256 * k2_:256 * k2_ + n]
                    self.mm(PO, WO[:, dc * 128:(dc + 1) * 128], w(YG), True, True, [tWO[c % 2], T["YG"]], [self.tps[bO]], signal=True)
                for k2_ in range(2):
                    dc = dc0 + k2_
                    PO = self.ps[bO][:, 256 * k2_:256 * k2_ + n]
                    self.tt("dve", self.X[:, dc, tsl], PO, self.X[:, dc, tsl], ALU.add, [self.tps[bO], self.tX[dc, ti5]], [self.tX[dc, ti5]])
                yield

        def drain(gens):
            gens = [g for g in gens if g is not None]
            while gens:
                for g in list(gens):
                    try:
                        next(g)
                    except StopIteration:
                        gens.remove(g)

        PIPE = int(_os.environ.get("RW_PIPE", "1"))
        NU_ = len(units)
        if PIPE:
            for s in range(NU_ + 2):
                gA = stageA(units[s]) if s < NU_ else None
                gB = stageB(units[s - 1]) if 0 <= s - 1 < NU_ else None
                gC = stageC(units[s - 2]) if 0 <= s - 2 < NU_ else None
                drain([gC, gB, gA])
        else:
            for U in units:
                drain([stageA(U)])
                drain([stageB(U)])
                drain([stageC(U)])
        ar.release(m0)
        kb.barrier()

    def mlstm(self):
        kb, d, ar = self.kb, self.d, self.ar
        m0 = ar.mark()
        gname = "norm_mix1"
        XN, tXN = self.XN, self.tXN
        T = TokMap()
        NEG = -1.0e30
        EKA = ar.f32(NT)
        EQA = ar.f32(NT)
        EMTT = ar.f32(48 * 8).rearrange("p (j h) -> p j h", h=8)
        self.EMTP = EMTT[:, 0:32, :]
        EMTS = EMTT[:, 32:48, :]
        self.BBP = ar.f32(2)
        self.ABP = ar.f32(2)
        MLV = ar.f32(8 * 10).rearrange("p (h k) -> p h k", h=8)
        BIF = ar.f32(4)
        M0T = ar.f32(NS)
        MOUT = ar.f32(17)
        SEL = ar.f32(8 * 64).rearrange("p (h m) -> p h m", h=8)
        CONVO = ar.f32(8 * 2 * 17 * 3).rearrange("p (h w s k) -> p h w s k", h=8, w=2, s=17)
        tEK, tEQ = TokMap(), TokMap()
        kb.dma("sp", MLV[0:64], d["mlv"], writes=[T["MLV"]])
        kb.dma("sp", BIF[0:8, 0:2], d["bif"], writes=[T["BIF"]])
        kb.dma("sp", M0T[0:8, :], d["ml_m0T"], writes=[T["M0T"]])
        self.ts("dve", BIF[0:8, 2:3], BIF[0:8, 1:2], -1.0, None, ALU.mult, None, [T["BIF"]], [T["BIF"]])
        self.cp("pool", SEL[0:8], self.IDENTF[0:8, 0:8].unsqueeze(2).to_broadcast([8, 8, 64]), [self.tMASK], [T["SEL"]])

        m1 = ar.mark()
        WIF = ar.bf16(NCH * 16).rearrange("p (k m) -> p k m", k=NCH)
        for K in range(NCH):
            kb.dma("pool", WIF[:, K, :], d["ml_w_in"][0, K * 128:(K + 1) * 128, 3072:3088], writes=[T["WIF"]])
        SQ = [ar.bf16(512) for _ in range(2)]
        LN = ar.f32(512)
        RS = ar.f32(512)
        tSQ = TokMap()
        tLN, tRS = Tok("ln"), Tok("rs")
        LI, LF, BB, AA = [ar.f32(512) for _ in range(4)]
        ABX = ar.f32(513)
        D0, D1, TMPg, MTg, EMg = [ar.f32(512) for _ in range(5)]
        for ti, (t0, n) in enumerate(TILES):
            sample = t0 >= SEQ
            Lc = 4 if sample else 64
            nck = n // Lc

            def out_fn(c, rs, trs, ti=ti, t0=t0, n=n):
                self.stt(XN[:, c, t0:t0 + n], self.X[:, c, t0:t0 + n], self.vcol(gname, c), rs, ALU.mult, ALU.mult,
                         [self.tX[c, ti], trs, self.tVEC], [tXN[c, ti]])
            self.rmsnorm_tile(ti, gname, out_fn, (SQ, tSQ, LN, tLN, RS, tRS, self.bank()))
            bI, bF = self.bank(), self.bank()
            PI, PF = self.ps[bI][0:8, :n], self.ps[bF][0:8, :n]
            for K in range(NCH):
                self.mm(PI, WIF[:, K, 0:8], XN[:, K, t0:t0 + n], K == 0, K == NCH - 1, [T["WIF"], tXN[K, ti]], [self.tps[bI]])
            for K in range(NCH):
                self.mm(PF, WIF[:, K, 8:16], XN[:, K, t0:t0 + n], K == 0, K == NCH - 1, [T["WIF"], tXN[K, ti]], [self.tps[bF]])
            g = lambda a: a[0:8, 0:n]
            self.act(g(LI), PI, AF.Identity, [self.tps[bI], T["BIF"]], [T["LI"]], bias=BIF[0:8, 0:1])
            self.act(g(TMPg), PF, AF.Exp, [self.tps[bF], T["BIF"]], [T["TMP"]], scale=-1.0, bias=BIF[0:8, 2:3])
            self.act(g(LF), g(TMPg), AF.Ln, [T["TMP"], self.tCONST], [T["LF"]], bias=self.ONEC[0:8, :])
            self.memset("dve", g(D0), 1.0, [T["D0"]])
            init = 0.0
            rd = []
            if sample:
                self.memset("dve", g(D0).rearrange("p (s t) -> p s t", t=TS)[:, :, 0:1], 0.0, [T["D0"]])
            elif ti > 0:
                init = self.BBP[0:8, 0:1]
                rd = [T["BBprev"]]
            self.scan(g(BB), g(D0), g(LF), init, ALU.mult, ALU.subtract, [T["D0"], T["LF"]] + rd, [T["BB"]])
            self.tt("dve", g(AA), g(LI), g(BB), ALU.subtract, [T["LI"], T["BB"]], [T["AA"]])
            ab = ABX[0:8, 1:1 + n]
            if sample:
                self.memset("dve", g(D1), 0.0, [T["D1"]])
                self.memset("dve", g(D1).rearrange("p (s t) -> p s t", t=TS)[:, :, 0:1], NEG, [T["D1"]])
                a3 = g(AA).rearrange("p (s t) -> p s t", t=TS)
                self.tt("dve", a3[:, :, 0], a3[:, :, 0], M0T[0:8, :], ALU.max, [T["AA"], T["M0T"]], [T["AA"]])
                self.scan(ab, g(D1), g(AA), 0.0, ALU.add, ALU.max, [T["D1"], T["AA"]], [T["ABX"]])
                rho = M0T[0:8, :].unsqueeze(2).to_broadcast([8, NS, TS])
                rtok = [T["M0T"]]
                a_v = g(AA).rearrange("p (s t) -> p s t", t=TS)
                ab_v = ab.rearrange("p (s t) -> p s t", t=TS)
                ek_v = g(TMPg).rearrange("p (s t) -> p s t", t=TS)
                eq_v = g(D0).rearrange("p (s t) -> p s t", t=TS)
                self.tt("dve", g(AA), g(LI), g(BB), ALU.subtract, [T["LI"], T["BB"], T["ABX"]], [T["AA"]])
            else:
                self.memset("dve", g(D1), 0.0, [T["D1"]])
                if ti == 0:
                    self.memset("dve", ABX[0:8, 0:1], 0.0, [T["ABX"]])
                    ainit = 0.0
                else:
                    self.cp("dve", ABX[0:8, 0:1], self.ABP[0:8, 0:1], [T["ABprev"]], [T["ABX"]])
                    ainit = self.ABP[0:8, 0:1]
                self.scan(ab, g(D1), g(AA), ainit, ALU.add, ALU.max, [T["D1"], T["AA"], T["ABX"]] + ([T["ABprev"]] if ti else []), [T["ABX"]])
                rho = ABX[0:8, 0:n].rearrange("p (j l) -> p j l", l=64)[:, :, 0:1].to_broadcast([8, nck, 64])
                rtok = [T["ABX"]]
                a_v = g(AA).rearrange("p (j l) -> p j l", l=64)
                ab_v = ab.rearrange("p (j l) -> p j l", l=64)
                ek_v = g(TMPg).rearrange("p (j l) -> p j l", l=64)
                eq_v = g(D0).rearrange("p (j l) -> p j l", l=64)
            self.tt("dve", ek_v, a_v, rho, ALU.subtract, [T["AA"]] + rtok, [T["TMP"]])
            self.act(EKA[0:8, t0:t0 + n], g(TMPg), AF.Exp, [T["TMP"]], [tEK[ti]])
            self.tt("dve", eq_v, rho, ab_v, ALU.subtract, [T["ABX"], T["D0"]] + rtok, [T["D0"]])
            self.act(EQA[0:8, t0:t0 + n], g(D0), AF.Exp, [T["D0"]], [tEQ[ti]])
            self.tt("dve", g(MTg), g(BB), ab, ALU.add, [T["BB"], T["ABX"]], [T["MT"]])
            self.act(g(EMg), g(MTg), AF.Exp, [T["MT"]], [T["EM"]], scale=-1.0)
            bT = self.bank()
            for j in range(nck):
                self.tr(self.ps[bT][0:Lc, j * 8:(j + 1) * 8], EMg[0:8, j * Lc:(j + 1) * Lc], self.IDENTF[0:8, 0:8],
                        [T["EM"], self.tMASK], [self.tps[bT]])
            cb0 = t0 // 64 if not sample else 32
            if sample:
                self.cp("act", EMTS[0:Lc, 0:nck, :], self.ps[bT][0:Lc, 0:nck * 8].rearrange("p (j h) -> p j h", h=8),
                        [self.tps[bT]], [T["EMTS"]])
                self.cp("pool", MOUT[0:8, 1:17], g(MTg).rearrange("p (s t) -> p s t", t=TS)[:, :, 3], [T["MT"]], [T["MOUT"]])
            else:
                self.cp("act", self.EMTP[0:Lc, cb0:cb0 + nck, :], self.ps[bT][0:Lc, 0:nck * 8].rearrange("p (j h) -> p j h", h=8),
                        [self.tps[bT]], [T["EMTP"]])
                if ti == 3:
                    self.cp("pool", MOUT[0:8, 0:1], MTg[0:8, n - 1:n], [T["MT"]], [T["MOUT"]])
                self.cp("pool", self.BBP[0:8, 0:1], BB[0:8, n - 1:n], [T["BB"]], [T["BBprev"]])
                self.cp("pool", self.ABP[0:8, 0:1], ABX[0:8, n:n + 1], [T["ABX"]], [T["ABprev"]])
        ar.release(m1)
        kb.barrier()
        kb.dma("sp", d["o_ml_m"], MOUT[0:8, :], reads=[T["MOUT"]])

        WIN = ar.bf16(NCH * 384).rearrange("p (k m) -> p k m", k=NCH)
        WO2 = [ar.bf16(D) for _ in range(2)]
        tWO = [Tok("mwo0"), Tok("mwo1")]
        RAW = [ar.f32(520) for _ in range(2)]
        ACC = [ar.f32(512) for _ in range(2)]
        SIL = [ar.f32(512) for _ in range(2)]
        SA = [dict(QP=ar.bf16(512), KP=ar.bf16(512), VA=ar.bf16(8 * 130).rearrange("p (j m) -> p j m", j=8),
                   KTt=ar.bf16(8 * 64).rearrange("p (j m) -> p j m", j=8), LAMB=ar.f32(512)) for _ in range(2)]
        SO = [ar.f32(512) for _ in range(3)]
        SH = [ar.f32(8 * 128).rearrange("p (j m) -> p j m", j=8) for _ in range(2)]
        STs = [ar.bf16(64) for _ in range(2)]
        SQH = ar.f32(8 * 128)
        STATH = ar.f32(32)
        DEN = ar.f32(16)
        HG = ar.bf16(512)
        C = ar.f32(130)
        Cd = ar.f32(130)
        Cb = ar.bf16(130)
        CS = ar.f32(4 * 130).rearrange("p (s m) -> p s m", s=4)
        CSd = ar.f32(4 * 130).rearrange("p (s m) -> p s m", s=4)
        CSb = ar.bf16(4 * 130).rearrange("p (s m) -> p s m", s=4)
        PCLb = ar.f32(8 * 130).rearrange("p (j m) -> p j m", j=8)
        CbAll = ar.bf16(9 * 130).rearrange("p (j m) -> p j m", j=9)
        main_tiles = [(t0, 512, 64, 8) for t0 in range(0, SEQ, 512)] + [(SEQ + 16 * q, 16, 4, 4) for q in range(4)]
        import os as _os
        if "ML_TILES" in _os.environ:
            main_tiles = [main_tiles[int(i)] for i in _os.environ["ML_TILES"].split(",")]
        units = []
        for h in range(int(_os.environ.get("ML_NH", 8))):
            for k_, (t0, n, L, NCk) in enumerate(main_tiles):
                units.append(dict(u=len(units), h=h, t0=t0, n=n, L=L, NCk=NCk, first=(k_ == 0),
                                  lastprompt=(t0 < SEQ and (k_ + 1 == len(main_tiles) or main_tiles[k_ + 1][0] >= SEQ))))
        for sa in SA:
            self.memset("pool", sa["VA"][0:64, :, 128:129], 1.0, [T["VAinit"]])

        def stageA(U):
            u, h, t0, n, L, NCk = U["u"], U["h"], U["t0"], U["n"], U["L"], U["NCk"]
            sa, k2, k3 = SA[u % 2], u % 2, u % 3
            QP, KP, VA, KTt, LAMB, Osig = sa["QP"], sa["KP"], sa["VA"], sa["KTt"], sa["LAMB"], SO[k3]
            tQP, tKP, tVA, tKTt, tLAM, tO = T["QP", k2], T["KP", k2], T["VA", k2], T["KTt", k2], T["LAM", k2], T["O", k3]
            sample = t0 >= SEQ
            ti5 = min(t0 // 512, 4)
            tsl = slice(t0, t0 + n)
            cs = lambda j: slice(j * L, (j + 1) * L)
            xt = [tXN[K, ti5] for K in range(NCH)]
            if U["first"]:
                for K in range(NCH):
                    rows = slice(K * 128, (K + 1) * 128)
                    kb.dma("pool", WIN[:, K, 0:64], d["ml_w_in"][0, rows, h * 64:(h + 1) * 64], writes=[T["WIN"]])
                    kb.dma("pool", WIN[:, K, 64:128], d["ml_w_in"][0, rows, 512 + h * 64:512 + (h + 1) * 64], writes=[T["WIN"]])
                    kb.dma("pool", WIN[:, K, 128:256], d["ml_w_in"][0, rows, 1024 + h * 128:1024 + (h + 1) * 128], writes=[T["WIN"]])
                    kb.dma("pool", WIN[:, K, 256:384], d["ml_w_in"][0, rows, 2048 + h * 128:2048 + (h + 1) * 128], writes=[T["WIN"]])
                kb.dma("pool", WO2[h % 2], d["ml_w_out"][0, h * 128:(h + 1) * 128, :], writes=[tWO[h % 2]])
                for w_ in range(2):
                    self.memset("pool", RAW[w_][0:64, 0:3], 0.0, [T["RAW", w_]])
                yield
            if sample:
                q4 = (t0 - SEQ) // 16
                for w_ in range(2):
                    rv = RAW[w_][0:64, 0:28].rearrange("p (s u) -> p s u", u=7)
                    kb.dma("sp", rv[:, :, 0:3], d["ml_convT"][h, w_, :, 4 * q4:4 * q4 + 4, :], writes=[T["RAW", w_]])
            bQ, bK, bO = self.bank("A"), self.bank("A"), self.bank("A")
            PQ, PK, PO_ = self.ps[bQ][0:64, :n], self.ps[bK][0:64, :n], self.ps[bO][:, :n]
            for (P_, cols, bb) in ((PQ, slice(0, 64), bQ), (PK, slice(64, 128), bK), (PO_, slice(256, 384), bO)):
                for K in range(NCH):
                    self.mm(P_, WIN[:, K, cols], XN[:, K, tsl], K == 0, K == NCH - 1, [T["WIN"], xt[K]], [self.tps[bb]])
                yield
            self.act(Osig[:, :n], PO_, AF.Sigmoid, [self.tps[bO]], [tO])
            for w_, (P_, bb) in enumerate(((PQ, bQ), (PK, bK))):
                mv = lambda k_: MLV[0:64, h, 5 * w_ + k_:5 * w_ + k_ + 1]
                R_ = RAW[w_]
                if sample:
                    rv = R_[0:64, 0:28].rearrange("p (s u) -> p s u", u=7)
                    self.cp("act", rv[:, :, 3:7], P_.rearrange("p (s t) -> p s t", t=TS), [self.tps[bb]], [T["RAW", w_]])
                    taps = [rv[:, :, k_:k_ + 4] for k_ in range(4)]
                    acc = ACC[w_][0:64, 0:n].rearrange("p (s t) -> p s t", t=TS)
                    self.cp("pool", CONVO[0:64, h, w_, 1 + 4 * q4:5 + 4 * q4, :], rv[:, :, 4:7], [T["RAW", w_]], [T["CONVO"]])
                else:
                    self.cp("act", R_[0:64, 3:3 + n], P_, [self.tps[bb]], [T["RAW", w_]])
                    taps = [R_[0:64, k_:k_ + n] for k_ in range(4)]
                    acc = ACC[w_][0:64, 0:n]
                yield
                self.ts("dve", acc, taps[0], mv(0), mv(4), ALU.mult, ALU.add, [T["RAW", w_], T["MLV"]], [T["ACC", w_]])
                for k_ in range(1, 4):
                    self.stt(acc, taps[k_], mv(k_), acc, ALU.mult, ALU.add, [T["RAW", w_], T["MLV"], T["ACC", w_]], [T["ACC", w_]])
                yield
                self.act(SIL[w_][0:64, 0:n], ACC[w_][0:64, 0:n], AF.Silu, [T["ACC", w_]], [T["SIL", w_]])
                if not sample:
                    if t0 + n == SEQ:
                        self.cp("pool", CONVO[0:64, h, w_, 0, :], R_[0:64, n:n + 3], [T["RAW", w_]], [T["CONVO"]])
                    self.cp("pool", R_[0:64, 0:3], R_[0:64, n:n + 3], [T["RAW", w_], T["ACC", w_]], [T["RAW", w_]])
                yield
            for j0 in range(0, NCk, 2):
                bV = self.bank("A")
                nj = min(2, NCk - j0)
                for jj in range(nj):
                    j = j0 + jj
                    PVt = self.ps[bV][0:L, jj * 128:(jj + 1) * 128]
                    for K in range(NCH):
                        self.mm(PVt, XN[:, K, t0 + j * L:t0 + (j + 1) * L], WIN[:, K, 128:256], K == 0, K == NCH - 1,
                                [T["WIN"], xt[K]], [self.tps[bV]])
                self.cp("act", VA[0:L, j0:j0 + nj, 0:128], self.ps[bV][0:L, 0:nj * 128].rearrange("p (j m) -> p j m", m=128),
                        [self.tps[bV], T["VAinit"]], [tVA])
                yield
            bM, bM2 = self.bank("A"), self.bank("A")
            PBK, PBQ = self.ps[bM][0:64, 0:n], self.ps[bM2][0:64, 0:n]
            tBQ = self.tps[bM2]
            self.mm(PBK, SEL[0:8, h, :], EKA[0:8, tsl], True, True, [T["SEL"], tEK[ti5]], [self.tps[bM]])
            self.mm(PBQ, SEL[0:8, h, :], EQA[0:8, tsl], True, True, [T["SEL"], tEQ[ti5]], [tBQ])
            yield
            self.tt("dve", KP[0:64, 0:n], SIL[1][0:64, 0:n], PBK, ALU.mult, [T["SIL", 1], self.tps[bM]], [tKP])
            self.stt(QP[0:64, 0:n], SIL[0][0:64, 0:n], 0.125, PBQ, ALU.mult, ALU.mult, [T["SIL", 0], tBQ], [tQP])
            self.cp("act", LAMB[0:64, 0:n], PBQ, [tBQ], [tLAM])
            yield
            bT = self.bank("A")
            PTr = self.ps[bT].bitcast(BF16)
            for j in range(NCk):
                self.tr(PTr[0:L, j * 64:(j + 1) * 64], KP[0:64, cs(j)], self.IDENTB[0:64, 0:64], [tKP, self.tMASK], [self.tps[bT]])
            self.cp("act", KTt[0:L, 0:NCk, :], PTr[0:L, 0:NCk * 64].rearrange("p (j m) -> p j m", m=64), [self.tps[bT]], [tKTt])
            yield

        def stageC(U):
            u, h, t0, n, L, NCk = U["u"], U["h"], U["t0"], U["n"], U["L"], U["NCk"]
            sa, k2 = SA[u % 2], u % 2
            QP, KP, VA, KTt, LAM = sa["QP"], sa["KP"], sa["VA"], sa["KTt"], sa["LAMB"]
            tQP, tKP, tVA, tKTt, tLAM = T["QP", k2], T["KP", k2], T["VA", k2], T["KTt", k2], T["LAM", k2]
            HT, tHT = SH[k2], T["HT", k2]
            sample = t0 >= SEQ
            cs = lambda j: slice(j * L, (j + 1) * L)
            if U["first"]:
                self.memset("pool", C[0:64], 0.0, [T["C"]])
                self.memset("pool", Cb[0:64], 0.0, [T["Cb"]])
            if sample:
                q4 = (t0 - SEQ) // 16
                for s in range(4):
                    kb.dma("sp", CS[0:64, s, 0:129], d["ml_C0T"][4 * q4 + s, h], writes=[T["CS", s]])
                yield
            PCL = PCLb[0:64, 0:NCk, 0:129]
            for j0 in range(0, NCk, 2):
                bP = self.bank("C")
                for jj in range(2):
                    j = j0 + jj
                    PCS = self.ps[bP][0:64, 256 * jj:256 * jj + 129]
                    self.mm(PCS, KTt[0:L, j, :], VA[0:L, j, 0:129], True, True, [tKTt, tVA], [self.tps[bP]])
                for jj in range(2):
                    j = j0 + jj
                    PCS = self.ps[bP][0:64, 256 * jj:256 * jj + 129]
                    lam = LAM[0:64, (j + 1) * L - 1:(j + 1) * L]
                    self.act(PCL[:, j, :], PCS, AF.Identity, [self.tps[bP], tLAM], [T["PCL"]], scale=lam)
                yield
            CbA = CbAll[0:64, 0:NCk + 1, 0:129]
            for j in range(NCk):
                lam = LAM[0:64, (j + 1) * L - 1:(j + 1) * L]
                if sample:
                    Cc, tC = CS[0:64, j, 0:129], T["CS", j]
                    self.cp("act", CbA[:, j, :], Cc, [tC], [T["CbA"]])
                    self.stt(Cc, Cc, lam, PCL[:, j, :], ALU.mult, ALU.add, [tC, tLAM, T["PCL"]], [tC])
                else:
                    Cc, tC = C[0:64, 0:129], T["C"]
                    if j == 0:
                        self.cp("act", CbA[:, 0, :], Cc, [tC], [T["CbA"]])
                    self.stt(CbA[:, j + 1, :], Cc, lam, PCL[:, j, :], ALU.mult, ALU.add, [tC, tLAM, T["PCL"]], [T["CbA"]])
                    self.stt(Cc, Cc, lam, PCL[:, j, :], ALU.mult, ALU.add, [tC, tLAM, T["PCL"]], [tC])
                yield
            for j in range(NCk):
                if sample:
                    em = EMTS[0:L, j + 4 * q4, h:h + 1]
                    tem = T["EMTS"]
                else:
                    em = self.EMTP[0:L, t0 // 64 + j, h:h + 1]
                    tem = T["EMTP"]
                bS = self.bank("C")
                PST = self.ps[bS][0:L, 0:L]
                PND = self.ps[bS][0:L, 64:64 + 129]
                tS = self.tps[bS]
                st = STs[j % 2]
                self.mm(PST, KP[0:64, cs(j)], QP[0:64, cs(j)], True, True, [tKP, tQP], [tS])
                yield
                self.tt("dve", st[0:L, 0:L], PST, self.MIU[0:L, 0:L], ALU.mult, [tS, self.tMASK], [T["ST", j % 2]])
                yield
                self.mm(PND, QP[0:64, cs(j)], CbA[:, j, :], True, False, [tQP, T["CbA"]], [tS])
                self.mm(PND, st[0:L, 0:L], VA[0:L, j, 0:129], False, True, [T["ST", j % 2], tVA], [tS])
                yield
                self.act(DEN[0:L, j:j + 1], PND[:, 128:129], AF.Abs, [tS], [T["DEN"]])
                yield
                self.ts("dve", DEN[0:L, j:j + 1], DEN[0:L, j:j + 1], em, None, ALU.max, None, [T["DEN"], tem], [T["DEN"]])
                self.recip(DEN[0:L, j:j + 1], DEN[0:L, j:j + 1], [T["DEN"]], [T["DEN"]])
                yield
                self.act(HT[0:L, j, :], PND[:, 0:128], AF.Identity, [tS, T["DEN"]], [tHT], scale=DEN[0:L, j:j + 1])
                yield
            if sample:
                for s in range(4):
                    kb.dma("sp", d["o_ml_Cs"][4 * q4 + s, h], CS[0:64, s, 0:129], reads=[T["CS", s]])
            if U["lastprompt"]:
                kb.dma("sp", d["o_ml_Cp"][h], C[0:64, 0:129], reads=[T["C"]])

        def stageD(U):
            u, h, t0, n, L, NCk = U["u"], U["h"], U["t0"], U["n"], U["L"], U["NCk"]
            k2, k3 = u % 2, u % 3
            HT, tHT = SH[k2], T["HT", k2]
            Osig, tO = SO[k3], T["O", k3]
            WOh = WO2[h % 2]
            ti5 = min(t0 // 512, 4)
            tsl = slice(t0, t0 + n)
            cs = lambda j: slice(j * L, (j + 1) * L)
            G = NCk
            HTf = HT[0:L, 0:G, :]
            SQv = SQH[0:L, 0:G * 128].rearrange("p (j m) -> p j m", m=128)
            self.act(SQv, HTf, AF.Square, [tHT], [T["SQH"]])
            yield
            self.reduce(STATH[0:L, 0:G], SQv, ALU.add, [T["SQH"]], [T["STATH"]])
            yield
            self.act(STATH[0:L, 0:G], STATH[0:L, 0:G], AF.Ln, [T["STATH"], self.tCONST], [T["STATH"]], scale=1.0 / 128, bias=self.EPSC[0:L, :])
            yield
            self.act(STATH[0:L, 0:G], STATH[0:L, 0:G], AF.Exp, [T["STATH"]], [T["STATH"]], scale=-0.5)
            yield
            self.tt("dve", HTf, HTf, STATH[0:L, 0:G].unsqueeze(2).to_broadcast([L, G, 128]), ALU.mult, [tHT, T["STATH"]], [tHT])
            yield
            bY = self.bank("C2")
            PYF = self.ps[bY][:, 0:n]
            for j in range(NCk):
                self.tr(PYF[:, cs(j)], HT[0:L, j, :], self.IDENTF[0:L, 0:L], [tHT, self.tMASK], [self.tps[bY]])
            yield
            self.stt(HG[:, 0:n], PYF, self.vcol("ml_norm_w", h), Osig[:, 0:n], ALU.mult, ALU.mult, [self.tps[bY], tO, self.tVEC], [T["HG"]])
            yield
            for dc in range(NCH):
                bO2 = self.bank("C2")
                PO2 = self.ps[bO2][:, 0:n]
                self.mm(PO2, WOh[:, dc * 128:(dc + 1) * 128], HG[:, 0:n], True, True, [tWO[h % 2], T["HG"]], [self.tps[bO2]])
                self.tt("dve", self.X[:, dc, tsl], PO2, self.X[:, dc, tsl], ALU.add, [self.tps[bO2], self.tX[dc, ti5]], [self.tX[dc, ti5]])
                yield

        def drain(gens):
            gens = [g for g in gens if g is not None]
            while gens:
                for g in list(gens):
                    try:
                        next(g)
                    except StopIteration:
                        gens.remove(g)

        NU_ = len(units)
        for s in range(NU_ + 2):
            gA = stageA(units[s]) if s < NU_ else None
            gC = stageC(units[s - 1]) if 0 <= s - 1 < NU_ else None
            gD = stageD(units[s - 2]) if 0 <= s - 2 < NU_ else None
            drain([gC, gD, gA])
        kb.dma("sp", d["o_ml_conv"], CONVO[0:64], reads=[T["CONVO"]])
        ar.release(m0)
        kb.barrier()

    def ffn(self, L, which):
        kb, d, ar = self.kb, self.d, self.ar
        m = ar.mark()
        wg = d["ff%s_wg" % which][L]
        wu = d["ff%s_wu" % which][L]
        wd = d["ff%s_wd" % which][L]
        gname = "norm_ff%s%d" % (which, L)
        G = 4
        groups = [(f0, min(G, NFF - f0)) for f0 in range(0, NFF, G)]
        WG = [ar.bf16(NCH * 512).rearrange("p (c f) -> p c f", c=NCH) for _ in range(2)]
        WU = [ar.bf16(NCH * 512).rearrange("p (c f) -> p c f", c=NCH) for _ in range(2)]
        WD = [ar.bf16(G * D).rearrange("p (f o) -> p f o", f=G) for _ in range(2)]
        H = [ar.bf16(G * 512).rearrange("p (f n) -> p f n", f=G) for _ in range(2)]
        SG = [ar.f32(512) for _ in range(2)]
        SQ = [ar.bf16(512) for _ in range(2)]
        LN = ar.f32(512)
        RS = ar.f32(512)
        tW = TokMap()
        tH = TokMap()
        tSG = TokMap()
        tSQ = TokMap()
        tLN, tRS = Tok("ln"), Tok("rs")

        def load_group(gi):
            f0, nf = groups[gi]
            s = gi % 2
            for c in range(NCH):
                kb.dma("pool", WG[s][:, c, 0:nf * 128], wg[c * 128:(c + 1) * 128, f0 * 128:(f0 + nf) * 128],
                       writes=[tW["g", s, c]])
                kb.dma("pool", WU[s][:, c, 0:nf * 128], wu[c * 128:(c + 1) * 128, f0 * 128:(f0 + nf) * 128],
                       writes=[tW["u", s, c]])
            for fi in range(nf):
                kb.dma("pool", WD[s][:, fi, :], wd[(f0 + fi) * 128:(f0 + fi + 1) * 128, :], writes=[tW["d", s, fi]])

        load_group(0)
        load_group(1)

        for ti, (t0, n) in enumerate(TILES):
            def out_fn(c, rs, trs, ti=ti, t0=t0, n=n):
                kb.emit("dve", lambda e: e.scalar_tensor_tensor(out=self.XN[:, c, t0:t0 + n], in0=self.X[:, c, t0:t0 + n],
                                                                scalar=self.vcol(gname, c), in1=rs,
                                                                op0=ALU.mult, op1=ALU.mult),
                        reads=[self.tX[c, ti], trs, self.tVEC], writes=[self.tXN[c, ti]])
            self.rmsnorm_tile(ti, gname, out_fn, (SQ, tSQ, LN, tLN, RS, tRS, 4 + ti % 4))

        items = [(gi, ti) for gi in range(len(groups)) for ti in range(len(TILES))]
        po_rr = [0]

        def up(idx):
            gi, ti = items[idx]
            f0, nf = groups[gi]
            s = gi % 2
            hs = idx % 2
            t0, n = TILES[ti]
            for fi in range(nf):
                b = fi % 2
                pg = self.ps[b][:, :n]
                pu = self.ps[2 + b][:, :n]
                for c in range(NCH):
                    kb.emit("pe", lambda e, c=c, fi=fi, pg=pg: e.matmul(pg, WG[s][:, c, fi * 128:(fi + 1) * 128],
                                                                     self.XN[:, c, t0:t0 + n], start=(c == 0), stop=(c == NCH - 1)),
                            reads=[tW["g", s, c], self.tXN[c, ti]], writes=[self.tps[b]], signal=(c == NCH - 1))
                for c in range(NCH):
                    kb.emit("pe", lambda e, c=c, fi=fi, pu=pu: e.matmul(pu, WU[s][:, c, fi * 128:(fi + 1) * 128],
                                                                     self.XN[:, c, t0:t0 + n], start=(c == 0), stop=(c == NCH - 1)),
                            reads=[tW["u", s, c], self.tXN[c, ti]], writes=[self.tps[2 + b]], signal=(c == NCH - 1))
                kb.emit("act", lambda e, b=b, pg=pg: e.activation(out=SG[b][:, :n], in_=pg, func=AF.Silu),
                        reads=[self.tps[b]], writes=[tSG[b]])
                kb.emit("dve", lambda e, b=b, fi=fi, pu=pu: e.tensor_tensor(out=H[hs][:, fi, :n], in0=SG[b][:, :n], in1=pu, op=ALU.mult),
                        reads=[tSG[b], self.tps[2 + b]], writes=[tH[hs, fi]])

        def down(idx):
            gi, ti = items[idx]
            f0, nf = groups[gi]
            s = gi % 2
            hs = idx % 2
            t0, n = TILES[ti]
            for dc in range(NCH):
                bank = 4 + po_rr[0] % 4
                po_rr[0] += 1
                po = self.ps[bank][:, :n]
                for fi in range(nf):
                    kb.emit("pe", lambda e, fi=fi, dc=dc, po=po: e.matmul(po, WD[s][:, fi, dc * 128:(dc + 1) * 128], H[hs][:, fi, :n],
                                                                       start=(fi == 0), stop=(fi == nf - 1)),
                            reads=[tW["d", s, fi], tH[hs, fi]], writes=[self.tps[bank]], signal=(fi == nf - 1))
                kb.emit("dve", lambda e, dc=dc, po=po: e.scalar_tensor_tensor(out=self.X[:, dc, t0:t0 + n], in0=po, scalar=0.5,
                                                                           in1=self.X[:, dc, t0:t0 + n], op0=ALU.mult, op1=ALU.add),
                        reads=[self.tps[bank], self.tX[dc, ti]], writes=[self.tX[dc, ti]])

        ntile = len(TILES)
        for idx in range(len(items)):
            up(idx)
            if idx > 0:
                down(idx - 1)
                gi_prev, ti_prev = items[idx - 1]
                if ti_prev == ntile - 1 and gi_prev + 2 < len(groups):
                    load_group(gi_prev + 2)
        down(len(items) - 1)
        ar.release(m)
        kb.barrier()

    def final_norm(self):
        kb, d, ar = self.kb, self.d, self.ar
        m = ar.mark()
        SQ = [ar.bf16(512) for _ in range(2)]
        LN = ar.f32(512)
        RS = ar.f32(512)
        Y = [ar.f32(512) for _ in range(4)]
        tSQ, tY = TokMap(), TokMap()
        tLN, tRS = Tok("ln"), Tok("rs")
        rr = [0]
        for ti, (t0, n) in enumerate(TILES):
            def out_fn(c, rs, trs, ti=ti, t0=t0, n=n):
                s = rr[0] % 4
                rr[0] += 1
                kb.emit("dve", lambda e: e.scalar_tensor_tensor(out=Y[s][:, :n], in0=self.X[:, c, t0:t0 + n],
                                                                scalar=self.vcol("norm_final", c), in1=rs,
                                                                op0=ALU.mult, op1=ALU.mult),
                        reads=[self.tX[c, ti], trs, self.tVEC], writes=[tY[s]])
                kb.dma("sp", d["yT"][c * 128:(c + 1) * 128, t0:t0 + n], Y[s][:, :n], reads=[tY[s]])
            self.rmsnorm_tile(ti, "norm_final", out_fn, (SQ, tSQ, LN, tLN, RS, tRS, 4 + ti % 4))
        ar.release(m)
        kb.barrier()

    def dump_x(self):
        kb, d = self.kb, self.d
        for c in range(NCH):
            for ti, (t0, n) in enumerate(TILES):
                kb.dma("sp", d["yT"][c * 128:(c + 1) * 128, t0:t0 + n], self.X[:, c, t0:t0 + n], reads=[self.tX[c, ti]])

    def build(self):
        st = self.stages
        self.load_inputs()
        self.consts()
        for L in range(2):
            if "ffa%d" % L in st:
                self.ffn(L, "a")
            if "mix%d" % L in st and L == 0:
                self.rwkv()
            if "mix%d" % L in st and L == 1:
                self.mlstm()
            if "ffb%d" % L in st:
                self.ffn(L, "b")
        if "final" in st:
            self.final_norm()
        else:
            self.dump_x()
        self.kb.flush()
        return self.nc


ALL_STAGES = ("ffa0", "mix0", "ffb0", "ffa1", "mix1", "ffb1", "final")


def pack_vecs(inp):
    vecs = np.zeros((NVEC, D), np.float32)

    def put(name, v):
        vecs[VID[name]] = np.asarray(v, np.float32).reshape(D)

    for L in range(2):
        put("norm_ffa%d" % L, inp["norm_ffa"][L])
        put("norm_mix%d" % L, inp["norm_mix"][L])
        put("norm_ffb%d" % L, inp["norm_ffb"][L])
    put("norm_final", inp["norm_final"])
    for i in range(6):
        put("mu%d" % i, inp["rw_mu"][0, i])
    put("w0", inp["rw_w0"][0])
    put("a0", inp["rw_a0"][0])
    put("k_k", inp["rw_k_k"][0])
    put("k_a", inp["rw_k_a"][0])
    put("r_k", inp["rw_r_k"][0])
    put("gn_w", inp["rw_gn_w"][0])
    put("gn_b", inp["rw_gn_b"][0])
    for j in range(4):
        put("cw%d" % j, inp["ml_conv_w"][0, j])
    put("cb", inp["ml_conv_b"][0])
    put("ml_norm_w", inp["ml_norm_w"][0])
    return np.ascontiguousarray(vecs.reshape(NVEC, NCH, 128).transpose(2, 0, 1).reshape(128, NVEC * NCH))


def make_in_maps(inp):
    vecs = pack_vecs(inp)
    shared = {"vecs": vecs}
    for nm in ("ffa_wg", "ffa_wu", "ffb_wg", "ffb_wu", "ffa_wd", "ffb_wd"):
        shared[nm] = np.ascontiguousarray(inp[nm], dtype=np.float32)
    cw = np.asarray(inp["ml_conv_w"][0], np.float32)
    cbv = np.asarray(inp["ml_conv_b"][0], np.float32)
    mlv = np.zeros((64, 8, 10), np.float32)
    for w in range(2):
        for k in range(4):
            mlv[:, :, 5 * w + k] = cw[k, w * 512:(w + 1) * 512].reshape(8, 64).T
        mlv[:, :, 5 * w + 4] = cbv[w * 512:(w + 1) * 512].reshape(8, 64).T
    shared["mlv"] = mlv
    shared["bif"] = np.ascontiguousarray(np.asarray(inp["ml_b_if"][0], np.float32).reshape(2, 8).T)
    for nm in ("rw_wr", "rw_wk", "rw_wv", "rw_wo", "rw_w1", "rw_w2", "rw_a1", "rw_a2", "rw_g1", "rw_g2", "ml_w_in", "ml_w_out"):
        shared[nm] = np.ascontiguousarray(inp[nm], dtype=np.float32)
    maps = []
    for core in range(NCORES):
        xs = np.concatenate([inp["x_prompt"][core], inp["x_sample"][core * NS:(core + 1) * NS].reshape(NS * TS, D)], axis=0)
        m = dict(shared)
        m["xT"] = np.ascontiguousarray(xs.T.astype(np.float32))
        sq = slice(core * NS, (core + 1) * NS)
        sh = inp["state_rwkv_shift"][0, sq]
        m["shiftT"] = np.ascontiguousarray(sh.reshape(NS, NCH, 128).transpose(2, 1, 0).astype(np.float32))
        m["rw_S0T"] = np.ascontiguousarray(inp["state_rwkv_S"][0, sq].transpose(0, 1, 3, 2).astype(np.float32))
        m["ml_m0T"] = np.ascontiguousarray(inp["state_mlstm_m"][0, sq].T.astype(np.float32))
        c0 = np.concatenate([inp["state_mlstm_C"][0, sq].transpose(0, 1, 3, 2), inp["state_mlstm_n"][0, sq][..., None]], axis=-1)
        m["ml_C0T"] = np.ascontiguousarray(c0.astype(np.float32))
        cv = inp["state_mlstm_conv"][0, sq]
        m["ml_convT"] = np.ascontiguousarray(cv.reshape(NS, 3, 2, 8, 64).transpose(3, 2, 4, 0, 1).astype(np.float32))
        maps.append(m)
    return maps


def run(inp, stages=ALL_STAGES, trace=False):
    b = Builder(stages)
    nc = b.build()
    maps = make_in_maps(inp)
    res = run_bass_kernel_spmd(nc, maps, core_ids=list(range(NCORES)), trace=trace)
    return b, res


def assemble(results):
    f = np.float32
    yp = np.zeros((8, SEQ, D), f); ys = np.zeros((128, TS, D), f)
    p_S = np.zeros((1, 8, 16, 64, 64), f); p_sh = np.zeros((1, 8, D), f)
    p_C = np.zeros((1, 8, 8, 128, 64), f); p_n = np.zeros((1, 8, 8, 64), f); p_m = np.zeros((1, 8, 8), f)
    p_cv = np.zeros((1, 8, 3, D), f)
    s_S = np.zeros((1, 128, 16, 64, 64), f); s_sh = np.zeros((1, 128, D), f)
    s_C = np.zeros((1, 128, 8, 128, 64), f); s_n = np.zeros((1, 128, 8, 64), f); s_m = np.zeros((1, 128, 8), f)
    s_cv = np.zeros((1, 128, 3, D), f)
    for core in range(NCORES):
        r = results[core]
        sq = slice(core * NS, (core + 1) * NS)
        y = r["yT"].T
        yp[core] = y[:SEQ]
        ys[sq] = y[SEQ:].reshape(NS, TS, D)
        sho = r["o_shift"]
        p_sh[0, core] = sho[:, :, 0].T.reshape(D)
        s_sh[0, sq] = sho[:, :, 1:].transpose(2, 1, 0).reshape(NS, D)
        p_S[0, core] = r["o_rw_Sp"].transpose(0, 2, 1)
        s_S[0, sq] = r["o_rw_Ss"].transpose(0, 1, 3, 2)
        cp = r["o_ml_Cp"]
        p_C[0, core] = cp[:, :, 0:128].transpose(0, 2, 1)
        p_n[0, core] = cp[:, :, 128]
        cs_ = r["o_ml_Cs"]
        s_C[0, sq] = cs_[:, :, :, 0:128].transpose(0, 1, 3, 2)
        s_n[0, sq] = cs_[:, :, :, 128]
        mo = r["o_ml_m"]
        p_m[0, core] = mo[:, 0]
        s_m[0, sq] = mo[:, 1:].T
        cv = r["o_ml_conv"]
        cvt = cv.transpose(3, 4, 2, 1, 0).reshape(17, 3, D)
        p_cv[0, core] = cvt[0]
        s_cv[0, sq] = cvt[1:]
    return (yp, ys, p_S, p_sh, p_C, p_n, p_m, p_cv, s_S, s_sh, s_C, s_n, s_m, s_cv)


def kernel(**inp):
    b, res = run(inp)
    return assemble(res.results)
```

```python
import numpy as np
from contextlib import ExitStack
import concourse.bass as bass
import concourse.mybir as mybir
from concourse.bass_utils import run_bass_kernel_spmd

F32 = mybir.dt.float32
BF16 = mybir.dt.bfloat16
AF = mybir.ActivationFunctionType
ALU = mybir.AluOpType
AX = mybir.AxisListType

NCORES = 8
D = 1024
NCH = 8
SEQ = 2048
NS = 16
TS = 4
NT = SEQ + NS * TS
DFF = 2816
NFF = DFF // 128
EPS = 1e-6

ENGS = ("pe", "act", "dve", "pool", "sp")
SAME_ENGINE_SYNC = True

VID = {}
_v = 0
for _nm in ("norm_ffa0", "norm_ffa1", "norm_mix0", "norm_mix1", "norm_ffb0", "norm_ffb1", "norm_final",
            "mu0", "mu1", "mu2", "mu3", "mu4", "mu5", "w0", "a0", "k_k", "k_a", "r_k", "gn_w", "gn_b",
            "cw0", "cw1", "cw2", "cw3", "cb", "ml_norm_w"):
    VID[_nm] = _v
    _v += 1
NVEC = _v


class Tok:
    __slots__ = ("name", "w", "r", "excl")

    def __init__(self, name="", excl=False):
        self.name = name
        self.w = []
        self.r = []
        self.excl = excl


class TokMap(dict):
    def __missing__(self, key):
        t = Tok(str(key))
        self[key] = t
        return t


class KB:
    def __init__(self, n_dma_sems=12):
        self.nc = bass.Bass("TRN2", target_bir_lowering=False, dynamic_dma_scratch_size=4096)
        self.es = ExitStack()
        nc = self.nc
        self.sem = {}
        self.count = {}
        self.prog = {e: [] for e in ENGS}
        self.waited = {e: {} for e in ENGS}
        for e in ENGS:
            self.sem[e] = self.es.enter_context(nc.semaphore("s_" + e))
            self.count[e] = 0
        self.dsem = {}
        self.dval = {}
        self.dnext = {}
        for q in ("sp", "pool", "act"):
            self.dsem[q] = []
            for j in range(n_dma_sems):
                key = "d_%s_%d" % (q, j)
                self.sem[key] = self.es.enter_context(nc.semaphore(key))
                self.dsem[q].append(key)
                self.dval[key] = 0
            self.dnext[q] = 0
        self.ninstr = 0

    def _wait(self, eng, semkey, value):
        if value <= 0:
            return
        if self.waited[eng].get(semkey, 0) >= value:
            return
        self.waited[eng][semkey] = value
        sem = self.sem[semkey]
        self.prog[eng].append(lambda e, sem=sem, value=value: e.wait_ge(sem, value))

    def _deps(self, eng, reads, writes):
        deps = set()
        for t in reads:
            deps.update(t.w)
            if t.excl:
                deps.update(x for x in t.r if x[0] != eng)
        for t in writes:
            deps.update(t.w)
            deps.update(t.r)
        for (sk, v) in deps:
            if sk == eng and (eng == "pe" or not SAME_ENGINE_SYNC):
                continue
            self._wait(eng, sk, v)

    def emit(self, eng, fn, reads=(), writes=(), signal=True):
        self._deps(eng, reads, writes)
        self.ninstr += 1
        if signal:
            self.count[eng] += 1
            cid = (eng, self.count[eng])
            sem = self.sem[eng]
            self.prog[eng].append(lambda e, fn=fn, sem=sem: fn(e).then_inc(sem, 1))
        else:
            cid = (eng, self.count[eng] + 1)
            self.prog[eng].append(lambda e, fn=fn: fn(e))
        for t in reads:
            t.r.append(cid)
        for t in writes:
            t.w = [cid]
            t.r = []
        return cid

    def dma(self, q, out, in_, reads=(), writes=()):
        self._deps(q, reads, writes)
        j = self.dnext[q]
        self.dnext[q] = (j + 1) % len(self.dsem[q])
        key = self.dsem[q][j]
        self._wait(q, key, self.dval[key])
        self.dval[key] += 16
        cid = (key, self.dval[key])
        sem = self.sem[key]
        self.ninstr += 1
        self.prog[q].append(lambda e, out=out, in_=in_, sem=sem: e.dma_start(out=out, in_=in_).then_inc(sem, 16))
        for t in reads:
            t.r.append(cid)
        for t in writes:
            t.w = [cid]
            t.r = []
        return cid

    def barrier(self):
        for e in ENGS:
            for e2 in ENGS:
                if e2 != e:
                    self._wait(e, e2, self.count[e2])
            for q in self.dsem:
                for key in self.dsem[q]:
                    self._wait(e, key, self.dval[key])

    def flush(self):
        self.barrier()
        nc = self.nc
        prog = self.prog
        with nc.Block() as block:
            @block.tensor
            def _(e):
                for f in prog["pe"]:
                    f(e)

            @block.scalar
            def _(e):
                for f in prog["act"]:
                    f(e)

            @block.vector
            def _(e):
                for f in prog["dve"]:
                    f(e)

            @block.gpsimd
            def _(e):
                for f in prog["pool"]:
                    f(e)

            @block.sync
            def _(e):
                for f in prog["sp"]:
                    f(e)
        self.prog = {e: [] for e in ENGS}


class Arena:
    def __init__(self, ap, nwords):
        self.ap = ap
        self.n = nwords
        self.top = 0

    def mark(self):
        return self.top

    def release(self, m):
        self.top = m

    def f32(self, nwords):
        assert self.top + nwords <= self.n, ("arena overflow", self.top, nwords, self.n)
        a = self.ap[:, self.top:self.top + nwords]
        self.top += nwords
        return a

    def bf16(self, nelem):
        nwords = (nelem + 1) // 2
        a = self.f32(nwords).bitcast(BF16)
        return a[:, 0:nelem]


ARENA_WORDS = 56280
XNW = 1 + SEQ + NS * (TS + 1)
SOFF = 1 + SEQ
TILES = [(0, 512), (512, 512), (1024, 512), (1536, 512), (2048, 64)]


class Builder:
    def __init__(self, stages):
        self.stages = stages
        self.kb = KB()
        kb = self.kb
        nc = kb.nc
        self.nc = nc
        es = kb.es
        d = {}

        def din(name, shape):
            d[name] = nc.dram_tensor(name, list(shape), F32, kind="ExternalInput").ap()

        def dout(name, shape):
            d[name] = nc.dram_tensor(name, list(shape), F32, kind="ExternalOutput").ap()

        din("xT", (D, NT))
        din("vecs", (128, NVEC * 8))
        for nm in ("ffa_wg", "ffa_wu", "ffb_wg", "ffb_wu"):
            din(nm, (2, D, DFF))
        for nm in ("ffa_wd", "ffb_wd"):
            din(nm, (2, DFF, D))
        for nm in ("rw_wr", "rw_wk", "rw_wv", "rw_wo"):
            din(nm, (1, D, D))
        din("rw_w1", (1, D, 64)); din("rw_w2", (1, 64, D))
        din("rw_a1", (1, D, 64)); din("rw_a2", (1, 64, D))
        din("rw_g1", (1, D, 160)); din("rw_g2", (1, 160, D))
        din("ml_w_in", (1, D, 3088)); din("ml_w_out", (1, D, D))
        din("mlv", (64, 8, 10)); din("bif", (8, 2)); din("ml_m0T", (8, NS))
        din("ml_C0T", (NS, 8, 64, 129)); din("ml_convT", (8, 2, 64, NS, 3))
        din("shiftT", (128, NCH, NS))
        din("rw_S0T", (NS, 16, 64, 64))
        dout("yT", (D, NT))
        dout("o_ml_m", (8, 17)); dout("o_ml_Cp", (8, 64, 129)); dout("o_ml_Cs", (NS, 8, 64, 129))
        dout("o_ml_conv", (64, 8, 2, 17, 3))
        dout("o_shift", (128, NCH, 17))
        dout("o_rw_Sp", (16, 64, 64))
        dout("o_rw_Ss", (NS, 16, 64, 64))
        self.d = d
        self.out_names = [k for k in d if k == "yT" or k.startswith("o_")]

        arena_t = es.enter_context(nc.sbuf_tensor("arena", [128, ARENA_WORDS], F32))
        self.ar = Arena(arena_t, ARENA_WORDS)
        self.psall = es.enter_context(nc.psum_tensor("psall", [128, 8, 512], F32))
        self.ps = [self.psall[:, i, :] for i in range(8)]
        self.tps = [Tok("ps%d" % i, excl=True) for i in range(8)]
        self.bank_rr = 0
        self.bank_pe = {}

        ar = self.ar
        self.X = ar.f32(NCH * NT).rearrange("p (c n) -> p c n", c=NCH)
        self.tX = TokMap()
        self.VEC = ar.f32(NVEC * 8)
        self.tVEC = Tok("vec")
        self.ONES = ar.bf16(128)
        self.tONES = Tok("ones")
        self.XNraw = ar.bf16(NCH * XNW)
        self.XN = self.XNraw[:, 0:NCH * NT].rearrange("p (c n) -> p c n", c=NCH)
        self.XNS = self.XNraw.rearrange("p (c n) -> p c n", c=NCH)
        self.tXN = TokMap()


    BANK_GROUPS = {"A": (0, 1, 2, 3), "B": (4, 5), "C": (6, 7), "C2": (4, 5)}

    def bank(self, group=None):
        if group is None:
            b = self.bank_rr % 8
            self.bank_rr += 1
            return b
        if not hasattr(self, "_grr"):
            self._grr = {}
        k = self._grr.get(group, 0)
        self._grr[group] = k + 1
        g = self.BANK_GROUPS[group]
        return g[k % len(g)]

    def act(self, out, in_, func, reads, writes, **kw):
        return self.kb.emit("act", lambda e: e.activation(out=out, in_=in_, func=func, **kw), reads, writes)

    def cp(self, eng, out, in_, reads, writes):
        if eng == "act":
            return self.kb.emit("act", lambda e: e.activation(out=out, in_=in_, func=AF.Copy), reads, writes)
        return self.kb.emit(eng, lambda e: e.tensor_copy(out=out, in_=in_), reads, writes)

    def tt(self, eng, out, in0, in1, op, reads, writes):
        return self.kb.emit(eng, lambda e: e.tensor_tensor(out=out, in0=in0, in1=in1, op=op), reads, writes)

    def ts(self, eng, out, in0, s1, s2, op0, op1, reads, writes):
        if s2 is None:
            return self.kb.emit(eng, lambda e: e.tensor_scalar(out=out, in0=in0, scalar1=s1, scalar2=None, op0=op0), reads, writes)
        return self.kb.emit(eng, lambda e: e.tensor_scalar(out=out, in0=in0, scalar1=s1, scalar2=s2, op0=op0, op1=op1), reads, writes)

    def stt(self, out, in0, scalar, in1, op0, op1, reads, writes):
        return self.kb.emit("dve", lambda e: e.scalar_tensor_tensor(out=out, in0=in0, scalar=scalar, in1=in1, op0=op0, op1=op1),
                            reads, writes)

    def _pe_rows(self, lhsT, writes):
        K = lhsT.partition_size()
        base = lhsT.base_partition()
        tile = 32 if K <= 32 else (64 if K <= 64 else 128)
        lo, hi = (base // tile) * tile, (base // tile) * tile + tile
        if tile == 128:
            lo, hi = 0, 128
        for t in writes:
            for b in range(8):
                if t is self.tps[b]:
                    prev = self.bank_pe.get(b)
                    if prev is not None and (prev[1] <= lo or hi <= prev[0]):
                        self.kb._wait("pe", "pe", prev[2][1])
                    self.bank_pe[b] = [lo, hi, None]
        return tile < 128

    def _pe_done(self, writes, cid):
        for t in writes:
            for b in range(8):
                if t is self.tps[b] and self.bank_pe.get(b) is not None:
                    self.bank_pe[b][2] = cid

    def mm(self, out, lhsT, rhs, start, stop, reads, writes, signal=None):
        if signal is None:
            signal = stop
        if self._pe_rows(lhsT, writes):
            signal = True
        cid = self.kb.emit("pe", lambda e: e.matmul(out, lhsT, rhs, start=start, stop=stop), reads, writes, signal=signal)
        self._pe_done(writes, cid)
        return cid

    def tr(self, out, in_, ident, reads, writes):
        self._pe_rows(in_, writes)
        cid = self.kb.emit("pe", lambda e: e.transpose(out, in_, ident), reads, writes)
        self._pe_done(writes, cid)
        return cid

    def memset(self, eng, ap, val, writes):
        return self.kb.emit(eng, lambda e: e.memset(ap, val), (), writes)

    def scan(self, out, d0, d1, init, op0, op1, reads, writes):
        return self.kb.emit("dve", lambda e: e.tensor_tensor_scan(out=out, data0=d0, data1=d1, initial=init, op0=op0, op1=op1), reads, writes)

    def recip(self, out, in_, reads, writes):
        return self.kb.emit("dve", lambda e: e.reciprocal(out=out, in_=in_), reads, writes)

    def reduce(self, out, in_, op, reads, writes, axis=None):
        axis = AX.X if axis is None else axis
        return self.kb.emit("dve", lambda e: e.tensor_reduce(out=out, in_=in_, axis=axis, op=op), reads, writes)

    def vcol(self, name, c):
        j = VID[name] * 8 + c
        return self.VEC[:, j:j + 1]

    def load_inputs(self):
        kb, d = self.kb, self.d
        kb.dma("sp", self.VEC, d["vecs"][:, :], writes=[self.tVEC])
        for c in range(NCH):
            for ti, (t0, n) in enumerate(TILES):
                kb.dma("sp", self.X[:, c, t0:t0 + n], d["xT"][c * 128:(c + 1) * 128, t0:t0 + n],
                       writes=[self.tX[c, ti]])
        kb.emit("dve", lambda e: e.memset(self.ONES, 1.0), writes=[self.tONES])

    def _full_bank(self, b):
        self.bank_pe[b] = [0, 128, ("pe", 0)]

    def rmsnorm_tile(self, ti, gname, out_fn, scratch):
        kb = self.kb
        t0, n = TILES[ti]
        SQ, tSQ, LN, tLN, RS, tRS, bank = scratch
        ps = self.ps[bank][:, :n]
        self._full_bank(bank)
        for c in range(NCH):
            s = c % 2
            kb.emit("act", lambda e, c=c, s=s: e.activation(out=SQ[s][:, :n], in_=self.X[:, c, t0:t0 + n], func=AF.Square),
                    reads=[self.tX[c, ti]], writes=[tSQ[s]])
            kb.emit("pe", lambda e, c=c, s=s: e.matmul(ps, self.ONES, SQ[s][:, :n], start=(c == 0), stop=(c == NCH - 1)),
                    reads=[tSQ[s], self.tONES], writes=[self.tps[bank]], signal=True)
        kb.emit("act", lambda e: e.activation(out=LN[:, :n], in_=ps, func=AF.Ln, scale=1.0 / D, bias=self.EPSC),
                reads=[self.tps[bank], self.tCONST], writes=[tLN])
        kb.emit("act", lambda e: e.activation(out=RS[:, :n], in_=LN[:, :n], func=AF.Exp, scale=-0.5),
                reads=[tLN], writes=[tRS])
        for c in range(NCH):
            out_fn(c, RS[:, :n], tRS)

    def consts(self):
        kb, ar = self.kb, self.ar
        self.CONST = ar.f32(8)
        self.tCONST = Tok("const")
        self.EPSC = self.CONST[:, 0:1]
        self.ONEC = self.CONST[:, 1:2]
        self.NHALFC = self.CONST[:, 2:3]
        self.GNEPSC = self.CONST[:, 3:4]
        for col, val in ((0, EPS), (1, 1.0), (2, -0.5), (3, 64e-5)):
            kb.emit("dve", lambda e, col=col, val=val: e.memset(self.CONST[:, col:col + 1], val), (), [self.tCONST])
        ONESF = ar.f32(128)
        tO = Tok("onesf")
        self.memset("pool", ONESF, 1.0, [tO])
        self.IDENTF = ar.f32(128)
        self.IDENTB = ar.bf16(128)
        self.BONES = ar.bf16(128)
        self.tMASK = Tok("masks")
        kb.emit("pool", lambda e: e.affine_select(out=self.IDENTF, in_=ONESF, pattern=[[-1, 128]], compare_op=ALU.is_equal,
                                                  fill=0.0, base=0, channel_multiplier=1), [tO], [self.tMASK])
        self.cp("pool", self.IDENTB, self.IDENTF, [self.tMASK], [self.tMASK])
        self.memset("pool", self.BONES, 0.0, [self.tMASK])
        self.memset("pool", self.BONES[0:64, 0:64], 1.0, [self.tMASK])
        self.memset("pool", self.BONES[64:128, 64:128], 1.0, [self.tMASK])
        MSU = ar.f32(64)
        MIU = ar.f32(64)
        self.MASKXT = ar.f32(64)
        kb.emit("pool", lambda e: e.affine_select(out=MSU[0:64, :], in_=ONESF[0:64, 0:64], pattern=[[1, 64]], compare_op=ALU.is_gt,
                                                  fill=0.0, base=0, channel_multiplier=-1), [tO], [self.tMASK])
        kb.emit("pool", lambda e: e.affine_select(out=MIU[0:64, :], in_=ONESF[0:64, 0:64], pattern=[[1, 64]], compare_op=ALU.is_ge,
                                                  fill=0.0, base=0, channel_multiplier=-1), [tO], [self.tMASK])
        kb.emit("pool", lambda e: e.affine_select(out=self.MASKXT[0:64, :], in_=ONESF[0:64, 0:64], pattern=[[-1, 64]], compare_op=ALU.is_gt,
                                                  fill=0.0, base=0, channel_multiplier=1), [tO], [self.tMASK])
        self.ts("pool", self.MASKXT[0:64, :], self.MASKXT[0:64, :], -1.0, None, ALU.mult, None, [self.tMASK], [self.tMASK])
        self.MIU = MIU
        self.MASKLL = ar.f32(2 * 4 * 64).rearrange("p (h b t) -> p h b t", h=2, b=4)
        for h in range(2):
            self.cp("pool", self.MASKLL[0:64, h, 0, :], MSU[0:64, :], [self.tMASK], [self.tMASK])
            self.cp("pool", self.MASKLL[0:64, h, 1, :], MIU[0:64, :], [self.tMASK], [self.tMASK])
            self.ts("pool", self.MASKLL[0:64, h, 2, :], MSU[0:64, :], -1.0, None, ALU.mult, None, [self.tMASK], [self.tMASK])
            self.cp("pool", self.MASKLL[0:64, h, 3, :], MIU[0:64, :], [self.tMASK], [self.tMASK])
        self.SM64 = ar.f32(256)
        self.SM4 = ar.f32(16)
        self.memset("pool", self.SM64, 1.0, [self.tMASK])
        self.memset("pool", self.SM64.rearrange("p (j l) -> p j l", l=64)[:, :, 0:1], 0.0, [self.tMASK])
        self.memset("pool", self.SM4, 1.0, [self.tMASK])
        self.memset("pool", self.SM4.rearrange("p (j l) -> p j l", l=4)[:, :, 0:1], 0.0, [self.tMASK])
        self.NEGW0 = ar.f32(8)
        j = VID["w0"] * 8
        self.ts("dve", self.NEGW0, self.VEC[:, j:j + 8], -1.0, None, ALU.mult, None, [self.tVEC], [self.tMASK])

    def rwkv(self):
        kb, d, ar = self.kb, self.d, self.ar
        m0 = ar.mark()
        XNS = self.XNS
        tXNS = TokMap()
        gname = "norm_mix0"
        mu = lambda i, K: self.vcol("mu%d" % i, K)

        HWA = ar.bf16(NT)
        HG1 = ar.bf16(NT)
        HG2 = ar.bf16(NT)
        tHWA, tHG = TokMap(), TokMap()
        W2A2 = ar.bf16(D)
        G2A = ar.bf16(D)
        G2B = ar.bf16(D)
        tW2 = Tok("w2a2g2")
        SHO = ar.f32(NCH * 17).rearrange("p (c j) -> p c j", c=NCH)
        tSHO = Tok("sho")
        SHI = ar.f32(NCH * NS).rearrange("p (c j) -> p c j", c=NCH)
        tSHI = Tok("shi")

        kb.dma("pool", W2A2[0:64, :], d["rw_w2"][0], writes=[tW2])
        kb.dma("pool", W2A2[64:128, :], d["rw_a2"][0], writes=[tW2])
        kb.dma("pool", G2A, d["rw_g2"][0, 0:128, :], writes=[tW2])
        self.memset("pool", G2B, 0.0, [tW2])
        self.memset("pool", HG2, 0.0, [tHG[0]])
        kb.dma("pool", G2B[0:32, :], d["rw_g2"][0, 128:160, :], writes=[tW2])
        kb.dma("sp", SHI, d["shiftT"], writes=[tSHI])

        for c in range(NCH):
            self.memset("pool", XNS[:, c, 0:1], 0.0, [tXNS[c, "init"]])
            sv = XNS[:, c, SOFF:SOFF + NS * 5].rearrange("p (j u) -> p j u", u=5)
            self.cp("pool", sv[:, :, 0], SHI[:, c, :], [tSHI], [tXNS[c, "init"]])

        def xn_aps(K, t0, n):
            if t0 < SEQ:
                return XNS[:, K, 1 + t0:1 + t0 + n], XNS[:, K, t0:t0 + n]
            j0 = (t0 - SEQ) // TS
            nj = n // TS
            sv = XNS[:, K, SOFF + 5 * j0:SOFF + 5 * (j0 + nj)].rearrange("p (j u) -> p j u", u=5)
            return sv[:, :, 1:5], sv[:, :, 0:4]

        def xn_toks(K, t0):
            ti = min(t0 // 512, 4)
            return [tXNS[K, ti], tXNS[K, max(ti - 1, 0)], tXNS[K, "init"]]

        def pview(ps_ap, t0, n):
            if t0 < SEQ:
                return ps_ap
            return ps_ap.rearrange("p (j t) -> p j t", t=TS)

        def mixproj(out, wa, wb, cols, t0, n, wtok, ptok):
            o = pview(out, t0, n)
            for K in range(NCH):
                xa, xb = xn_aps(K, t0, n)
                self.mm(o, wa[:, K, cols], xa, K == 0, False, [wtok] + xn_toks(K, t0), [ptok])
                self.mm(o, wb[:, K, cols], xb, False, K == NCH - 1, [wtok] + xn_toks(K, t0), [ptok])

        def scale_w(raw, wb, Mcols, mu_list, tok):
            for K in range(NCH):
                for (cs, mi) in mu_list:
                    self.ts("pool", wb[:, K, cs], raw[:, K, cs], mu(mi, K), None, ALU.mult, None, [tok, self.tVEC], [tok])
            self.tt("pool", raw, raw, wb, ALU.subtract, [tok], [tok])

        m1 = ar.mark()
        W1A = ar.bf16(NCH * 128).rearrange("p (k m) -> p k m", k=NCH)
        W1B = ar.bf16(NCH * 128).rearrange("p (k m) -> p k m", k=NCH)
        G1A = ar.bf16(NCH * 160).rearrange("p (k m) -> p k m", k=NCH)
        G1B = ar.bf16(NCH * 160).rearrange("p (k m) -> p k m", k=NCH)
        tW1, tG1 = Tok("w1a1"), Tok("g1")
        for K in range(NCH):
            kb.dma("pool", W1A[:, K, 0:64], d["rw_w1"][0, K * 128:(K + 1) * 128, :], writes=[tW1])
            kb.dma("pool", W1A[:, K, 64:128], d["rw_a1"][0, K * 128:(K + 1) * 128, :], writes=[tW1])
            kb.dma("pool", G1A[:, K, :], d["rw_g1"][0, K * 128:(K + 1) * 128, :], writes=[tG1])
        scale_w(W1A, W1B, 128, [(slice(0, 64), 1), (slice(64, 128), 4)], tW1)
        scale_w(G1A, G1B, 160, [(slice(0, 160), 5)], tG1)
        XNF = [ar.f32(512) for _ in range(2)]
        SQ = [ar.bf16(512) for _ in range(2)]
        LN = ar.f32(512)
        RS = ar.f32(512)
        tXNF, tSQ = TokMap(), TokMap()
        tLN, tRS = Tok("ln"), Tok("rs")
        rr = [0]
        for ti, (t0, n) in enumerate(TILES):
            def out_fn(c, rs, trs, ti=ti, t0=t0, n=n):
                s = rr[0] % 2
                rr[0] += 1
                xf = XNF[s][:, :n]
                self.stt(xf, self.X[:, c, t0:t0 + n], self.vcol(gname, c), rs, ALU.mult, ALU.mult,
                         [self.tX[c, ti], trs, self.tVEC], [tXNF[s]])
                if t0 < SEQ:
                    self.cp("act", XNS[:, c, 1 + t0:1 + t0 + n], xf, [tXNF[s]], [tXNS[c, ti]])
                    if t0 + n == SEQ:
                        self.cp("pool", SHO[:, c, 0:1], xf[:, n - 1:n], [tXNF[s]], [tSHO])
                else:
                    sv = XNS[:, c, SOFF:SOFF + NS * 5].rearrange("p (j u) -> p j u", u=5)
                    xv = xf.rearrange("p (j t) -> p j t", t=TS)
                    self.cp("act", sv[:, :, 1:5], xv, [tXNF[s]], [tXNS[c, ti]])
                    self.cp("pool", SHO[:, c, 1:17], xv[:, :, 3], [tXNF[s]], [tSHO])
            self.rmsnorm_tile(ti, gname, out_fn, (SQ, tSQ, LN, tLN, RS, tRS, self.bank()))
            b1, b2, b3 = self.bank(), self.bank(), self.bank()
            mixproj(self.ps[b1][:, :n], W1A, W1B, slice(0, 128), t0, n, tW1, self.tps[b1])
            mixproj(self.ps[b2][:, :n], G1A, G1B, slice(0, 128), t0, n, tG1, self.tps[b2])
            mixproj(self.ps[b3][0:32, :n], G1A, G1B, slice(128, 160), t0, n, tG1, self.tps[b3])
            self.act(HWA[0:64, t0:t0 + n], self.ps[b1][0:64, :n], AF.Tanh, [self.tps[b1]], [tHWA[ti]])
            self.cp("act", HWA[64:128, t0:t0 + n], self.ps[b1][64:128, :n], [self.tps[b1]], [tHWA[ti]])
            self.act(HG1[:, t0:t0 + n], self.ps[b2][:, :n], AF.Sigmoid, [self.tps[b2]], [tHG[ti]])
            self.act(HG2[0:32, t0:t0 + n], self.ps[b3][0:32, :n], AF.Sigmoid, [self.tps[b3]], [tHG[ti]])
        ar.release(m1)
        kb.barrier()
        kb.dma("sp", d["o_shift"], SHO, reads=[tSHO])

        WN = 256

        def f32t():
            return ar.f32(WN)

        def bf16t():
            return ar.bf16(WN)
        WA2 = [{nm: ar.bf16(NCH * 128).rearrange("p (k m) -> p k m", k=NCH) for nm in "rkv"} for _ in range(2)]
        WB2 = [{nm: ar.bf16(NCH * 128).rearrange("p (k m) -> p k m", k=NCH) for nm in "rkv"} for _ in range(2)]
        WO2 = [ar.bf16(D) for _ in range(2)]
        tWc2 = [{nm: Tok("w%s%d" % (nm, i)) for nm in "rkv"} for i in range(2)]
        tWO = [Tok("wo0"), Tok("wo1")]
        Rf, Kf, Vf, A_, EW, CUM, EM, KK, KF, Bv, T1, T2 = [f32t() for _ in range(12)]
        EQ = EW
        Vb, SQb, KTb, BTb, YG = [bf16t() for _ in range(5)]
        RKR = SQb
        S3 = []
        for _ in range(3):
            S3.append(dict(
                KR=ar.bf16(2 * WN).rearrange("p (a n) -> p a n", a=2),
                KTt=ar.bf16(4 * 128).rearrange("p (j m) -> p j m", j=4),
                BTt=ar.bf16(4 * 128).rearrange("p (j m) -> p j m", j=4),
                VTt=ar.bf16(4 * 128).rearrange("p (j m) -> p j m", j=4),
                EP=f32t(), BONUS=f32t(), Gf=f32t(), LLs=ar.bf16(4 * 2 * 4 * 64)))
        S2 = []
        for _ in range(2):
            S2.append(dict(XTs=ar.bf16(4 * 2 * 64), PW=[ar.bf16(8 * 2 * 64) for _ in range(2)]))
        PT = [ar.bf16(8 * 64) for _ in range(2)]
        Gs = ar.bf16(128)
        NU = ar.bf16(128)
        YT = ar.f32(512)
        SQ2 = ar.f32(512)
        YF = SQ2[:, 0:256]
        STAT = ar.f32(32)
        H = ar.f32(64)
        H0d = ar.f32(64)
        Hb = ar.bf16(64)
        HS = ar.f32(4 * 64).rearrange("p (j v) -> p j v", j=4)
        HSb = ar.bf16(4 * 64).rearrange("p (j v) -> p j v", j=4)
        T = TokMap()

        main_tiles = [(t0, 256, 64) for t0 in range(0, SEQ, 256)] + [(SEQ + 16 * q, 16, 4) for q in range(4)]
        import os as _os
        if "KDEBUG" in _os.environ:
            print("rwkv arena top", ar.top, "of", ar.n)
        if "RW_TILES" in _os.environ:
            main_tiles = [main_tiles[int(i)] for i in _os.environ["RW_TILES"].split(",")]
        NCc = int(_os.environ.get("RW_NC", NCH))
        units = []
        for c in range(NCc):
            for k_, (t0, n, L) in enumerate(main_tiles):
                u = len(units)
                units.append(dict(u=u, c=c, t0=t0, n=n, L=L, first=(k_ == 0), last=(k_ == len(main_tiles) - 1),
                                  lastprompt=(t0 < SEQ and (k_ + 1 == len(main_tiles) or main_tiles[k_ + 1][0] >= SEQ))))

        def load_weights(c):
            ccols = slice(c * 128, (c + 1) * 128)
            WA, WB, tWc = WA2[c % 2], WB2[c % 2], tWc2[c % 2]
            for nm, key, mi in (("r", "rw_wr", 0), ("k", "rw_wk", 2), ("v", "rw_wv", 3)):
                for K in range(NCH):
                    kb.dma("pool", WA[nm][:, K, :], d[key][0, K * 128:(K + 1) * 128, ccols], writes=[tWc[nm]])
                scale_w(WA[nm], WB[nm], 128, [(slice(0, 128), mi)], tWc[nm])

        def stageA(U):
            u, c, t0, n, L = U["u"], U["c"], U["t0"], U["n"], U["L"]
            s3, s2 = S3[u % 3], S2[u % 2]
            k3, k2 = u % 3, u % 2
            sample = t0 >= SEQ
            NCk = 4
            ti5 = min(t0 // 512, 4)
            tsl = slice(t0, t0 + n)
            cs = lambda j: slice(j * L, (j + 1) * L)
            ccols = slice(c * 128, (c + 1) * 128)
            WA, WB, tWc = WA2[c % 2], WB2[c % 2], tWc2[c % 2]
            if U["first"]:
                if c == 0:
                    load_weights(0)
                if c + 1 < NCc:
                    load_weights(c + 1)
                yield
            KR, KTt, BTt, VTt, EP, BONUS, Gf = s3["KR"], s3["KTt"], s3["BTt"], s3["VTt"], s3["EP"], s3["BONUS"], s3["Gf"]
            tKR, tKTt, tBTt, tVTt, tEP, tBONUS, tGf, tLL = (T["KR", k3], T["KTt", k3], T["BTt", k3], T["VTt", k3], T["EP", k3],
                                                          T["BONUS", k3], T["Gf", k3], T["LLs", k3])
            tXT = T["XTs", k2]
            bA, bB, bC, bD = self.bank("A"), self.bank("A"), self.bank("A"), self.bank("A")
            PR, PK = self.ps[bA][:, 0:n], self.ps[bA][:, 256:256 + n]
            PV, PGt = self.ps[bB][:, 0:n], self.ps[bB][:, 256:256 + n]
            PWL, PAL = self.ps[bC][:, 0:n], self.ps[bC][:, 256:256 + n]
            PKK, PSm = self.ps[bD][:, 0:n], self.ps[bD][:, 256:256 + n]
            for K0 in range(0, NCH, 2):
                pass
            mixproj(PR, WA["r"], WB["r"], slice(0, 128), t0, n, tWc["r"], self.tps[bA])
            yield
            mixproj(PK, WA["k"], WB["k"], slice(0, 128), t0, n, tWc["k"], self.tps[bA])
            yield
            mixproj(PV, WA["v"], WB["v"], slice(0, 128), t0, n, tWc["v"], self.tps[bB])
            self.mm(PGt, G2A[:, ccols], HG1[:, tsl], True, False, [tW2, tHG[ti5]], [self.tps[bB]])
            self.mm(PGt, G2B[:, ccols], HG2[:, tsl], False, True, [tW2, tHG[ti5], tHG[0]], [self.tps[bB]])
            self.mm(PWL, W2A2[0:64, ccols], HWA[0:64, tsl], True, True, [tW2, tHWA[ti5]], [self.tps[bC]])
            self.mm(PAL, W2A2[64:128, ccols], HWA[64:128, tsl], True, True, [tW2, tHWA[ti5]], [self.tps[bC]])
            yield
            w = lambda a: a[:, 0:n]
            tV = self.tVEC
            self.cp("act", w(Rf), PR, [self.tps[bA]], [T["Rf"]])
            self.cp("act", w(Kf), PK, [self.tps[bA]], [T["Kf"]])
            yield
            self.cp("act", w(Vf), PV, [self.tps[bB]], [T["Vf"]])
            self.cp("act", w(Gf), PGt, [self.tps[bB]], [tGf])
            self.cp("dve", w(Vb), w(Vf), [T["Vf"]], [T["Vb"]])
            yield
            self.act(w(A_), PAL, AF.Sigmoid, [self.tps[bC], tV], [T["A"]], bias=self.vcol("a0", c))
            self.act(w(T1), PWL, AF.Exp, [self.tps[bC], self.tMASK], [T["T1"]], scale=-1.0, bias=self.NEGW0[:, c:c + 1])
            self.ts("dve", w(KK), w(Kf), self.vcol("k_k", c), None, ALU.mult, None, [T["Kf"], tV], [T["KK"]])
            yield
            self.act(w(T1), w(T1), AF.Ln, [T["T1"], self.tCONST], [T["T1"]], bias=self.ONEC)
            self.act(w(SQb), w(KK), AF.Square, [T["KK"]], [T["SQb"]])
            self.mm(PKK, self.BONES, w(SQb), True, True, [T["SQb"], self.tMASK], [self.tps[bD]], signal=True)
            yield
            self.act(w(EW), w(T1), AF.Exp, [T["T1"], self.tCONST], [T["EW"]], scale=-1.0, bias=self.NHALFC)
            self.ts("dve", w(T1), w(A_), -1.0, self.vcol("k_a", c), ALU.add, ALU.mult, [T["A"], tV], [T["T1"]])
            self.stt(w(KF), w(T1), 1.0, w(Kf), ALU.add, ALU.mult, [T["T1"], T["Kf"]], [T["KF"]])
            yield
            SM = self.SM4[:, 0:n] if sample else self.SM64[:, 0:n]
            self.scan(w(CUM), SM, w(EW), 0.0, ALU.mult, ALU.subtract, [T["EW"], self.tMASK], [T["CUM"]])
            self.act(w(T2), PKK, AF.Sqrt, [self.tps[bD]], [T["T2"]])
            yield
            self.act(w(EP), w(CUM), AF.Exp, [T["CUM"]], [tEP])
            self.act(w(EM), w(CUM), AF.Exp, [T["CUM"]], [T["EM"]], scale=-1.0)
            self.ts("dve", w(T2), w(T2), 1e-12, None, ALU.max, None, [T["T2"]], [T["T2"]])
            self.recip(w(T2), w(T2), [T["T2"]], [T["T2"]])
            yield
            self.tt("dve", w(KK), w(KK), w(T2), ALU.mult, [T["KK"], T["T2"]], [T["KK"]])
            self.tt("dve", w(T2), w(CUM), w(EW), ALU.add, [T["CUM"], T["EW"], T["KK"]], [T["T2"]])
            self.act(w(EQ), w(T2), AF.Exp, [T["T2"]], [T["EW"]])
            yield
            self.stt(w(RKR), w(Rf), self.vcol("r_k", c), w(KF), ALU.mult, ALU.mult, [T["Rf"], T["KF"], tV], [T["SQb"]])
            self.mm(PSm, self.BONES, w(RKR), True, True, [T["SQb"], self.tMASK], [self.tps[bD]], signal=True)
            self.tt("dve", w(Bv), w(KK), w(A_), ALU.mult, [T["KK"], T["A"]], [T["Bv"]])
            self.tt("dve", KR[:, 1, 0:n], w(Rf), w(EP), ALU.mult, [T["Rf"], tEP], [tKR])
            yield
            self.tt("dve", KR[:, 0, 0:n], w(KK), w(EQ), ALU.mult, [T["KK"], T["EW"]], [tKR])
            self.tt("dve", w(KTb), w(KF), w(EM), ALU.mult, [T["KF"], T["EM"]], [T["KTb"]])
            self.tt("dve", w(BTb), w(Bv), w(EM), ALU.mult, [T["Bv"], T["EM"]], [T["BTb"]])
            self.tt("dve", w(BONUS), PSm, w(Vf), ALU.mult, [self.tps[bD], T["Vf"]], [tBONUS])
            yield
            bT = self.bank("A")
            PTr = self.ps[bT].bitcast(BF16)
            for (src, ts_, off) in ((KTb, "KTb", 0), (BTb, "BTb", 1)):
                for j in range(NCk):
                    self.tr(PTr[0:L, off * 512 + j * 128:off * 512 + (j + 1) * 128], src[:, cs(j)], self.IDENTB,
                            [T[ts_], self.tMASK], [self.tps[bT]])
            self.cp("act", KTt[0:L, :, :], PTr[0:L, 0:512].rearrange("p (j m) -> p j m", j=4), [self.tps[bT]], [tKTt])
            self.cp("act", BTt[0:L, :, :], PTr[0:L, 512:1024].rearrange("p (j m) -> p j m", j=4), [self.tps[bT]], [tBTt])
            yield
            bT2 = self.bank("A")
            PTr2 = self.ps[bT2].bitcast(BF16)
            for j in range(NCk):
                self.tr(PTr2[0:L, j * 128:(j + 1) * 128], Vb[:, cs(j)], self.IDENTB, [T["Vb"], self.tMASK], [self.tps[bT2]])
            self.cp("act", VTt[0:L, :, :], PTr2[0:L, 0:512].rearrange("p (j m) -> p j m", j=4), [self.tps[bT2]], [tVTt])
            yield
            LLv = s3["LLs"][:, 0:4 * 2 * 4 * L].rearrange("p (j h b t) -> p j h b t", j=4, h=2, b=4)
            XTv = s2["XTs"][:, 0:4 * 2 * L].rearrange("p (j h t) -> p j h t", j=4, h=2)
            mk = self.MASKLL[0:L, :, :, 0:L]
            for h in range(2):
                hs = slice(64 * h, 64 * h + 64)
                bX = self.bank("A")
                PXT = self.ps[bX][:, 0:4 * L].rearrange("p (j t) -> p j t", j=4)
                for g0 in (0, 2):
                    bL = self.bank("A")
                    PLL = self.ps[bL][:, 0:2 * 4 * L].rearrange("p (j b t) -> p j b t", j=2, b=4)
                    for jj in range(2):
                        j = g0 + jj
                        self.mm(PLL[0:L, jj, 0:2, :], KTb[hs, cs(j)], KR[hs, :, cs(j)], True, True, [T["KTb"], tKR], [self.tps[bL]])
                        self.mm(PLL[0:L, jj, 2:4, :], BTb[hs, cs(j)], KR[hs, :, cs(j)], True, True, [T["BTb"], tKR], [self.tps[bL]])
                        self.mm(PXT[0:L, j, :], KR[hs, 0, cs(j)], BTb[hs, cs(j)], True, True, [T["BTb"], tKR], [self.tps[bX]])
                    self.tt("dve", LLv[0:L, g0:g0 + 2, h], PLL[0:L], mk, ALU.mult, [self.tps[bL], self.tMASK], [tLL])
                    yield
                self.tt("dve", XTv[0:L, :, h, :], PXT[0:L], self.MASKXT[0:L, 0:L].unsqueeze(1).to_broadcast([L, 4, L]), ALU.mult,
                        [self.tps[bX], self.tMASK], [tXT])
                yield

        def stageB(U):
            u, L = U["u"], U["L"]
            s3, s2 = S3[u % 3], S2[u % 2]
            k3, k2 = u % 3, u % 2
            NCk, NM = 4, 8
            LLv = s3["LLs"][:, 0:4 * 2 * 4 * L].rearrange("p (j h b t) -> p j h b t", j=4, h=2, b=4)
            XTv = s2["XTs"][:, 0:4 * 2 * L].rearrange("p (j h t) -> p j h t", j=4, h=2)
            tLL, tXT = T["LLs", k3], T["XTs", k2]
            PWv = [p[:, 0:NM * 2 * L].rearrange("p (i a t) -> p i a t", i=NM, a=2) for p in s2["PW"]]
            PTv = [p[:, 0:NM * L].rearrange("p (i t) -> p i t", i=NM) for p in PT]
            tPW = [T["PW", k2, 0], T["PW", k2, 1]]
            PW4 = PWv[0].rearrange("p (j h) a t -> p j h a t", h=2)
            self.cp("dve", PW4[0:L, :, :, 0, :], LLv[0:L, :, :, 2, :], [tLL], [tPW[0]])
            self.cp("pool", PWv[0][0:L, :, 1, :], self.IDENTB[0:L, 0:L].unsqueeze(1).to_broadcast([L, NM, L]), [self.tMASK], [tPW[0]])
            self.cp("act", PTv[0][0:L].rearrange("p (j h) t -> p j h t", h=2), XTv[0:L], [tXT], [T["PT", 0]])
            yield
            nlev = 6 if L == 64 else 2
            cur = 0
            mpb = 4 if L == 64 else 8
            for lev in range(nlev):
                nxt = 1 - cur
                last = lev == nlev - 1
                for i0 in range(0, NM, mpb):
                    bI = self.bank("B")
                    PA = self.ps[bI][:, 0:mpb * 2 * L].rearrange("p (i a t) -> p i a t", i=mpb, a=2)
                    for ii in range(mpb):
                        i = i0 + ii
                        self.mm(PA[0:L, ii], PTv[cur][0:L, i, :], PWv[cur][0:L, i], True, True,
                                [T["PT", cur], tPW[cur]], [self.tps[bI]], signal=(ii == mpb - 1))
                    if not last:
                        self.cp("act", PWv[nxt][0:L, i0:i0 + mpb, 0, :], PA[0:L, :, 0, :], [self.tps[bI]], [tPW[nxt]])
                    self.tt("dve", PWv[nxt][0:L, i0:i0 + mpb, 1, :], PA[0:L, :, 1, :], PWv[cur][0:L, i0:i0 + mpb, 1, :], ALU.add,
                            [self.tps[bI], tPW[cur]], [tPW[nxt]])
                    yield
                if not last:
                    bJ = self.bank("B")
                    PB = self.ps[bJ][:, 0:NM * L].rearrange("p (i t) -> p i t", i=NM)
                    for i in range(NM):
                        self.mm(PB[0:L, i, :], PWv[cur][0:L, i, 0, :], PTv[cur][0:L, i, :], True, True,
                                [T["PT", cur], tPW[cur]], [self.tps[bJ]], signal=(i == NM - 1))
                    self.cp("act", PTv[nxt][0:L], PB[0:L], [self.tps[bJ]], [T["PT", nxt]])
                    yield
                cur = nxt
            U["Wv"] = PWv[cur]
            U["tWv"] = tPW[cur]

        def stageC(U):
            u, c, t0, n, L = U["u"], U["c"], U["t0"], U["n"], U["L"]
            s3 = S3[u % 3]
            k3 = u % 3
            sample = t0 >= SEQ
            NCk = 4
            ti5 = min(t0 // 512, 4)
            tsl = slice(t0, t0 + n)
            cs = lambda j: slice(j * L, (j + 1) * L)
            w = lambda a: a[:, 0:n]
            tV = self.tVEC
            KR, KTt, BTt, VTt, EP, BONUS, Gf = s3["KR"], s3["KTt"], s3["BTt"], s3["VTt"], s3["EP"], s3["BONUS"], s3["Gf"]
            tKR, tKTt, tBTt, tVTt, tEP, tBONUS, tGf, tLL = (T["KR", k3], T["KTt", k3], T["BTt", k3], T["VTt", k3], T["EP", k3],
                                                          T["BONUS", k3], T["Gf", k3], T["LLs", k3])
            LLv = s3["LLs"][:, 0:4 * 2 * 4 * L].rearrange("p (j h b t) -> p j h b t", j=4, h=2, b=4)
            Wv, tWv = U["Wv"], U["tWv"]
            WO = WO2[c % 2]
            if U["first"]:
                kb.dma("pool", WO, d["rw_wo"][0, c * 128:(c + 1) * 128, :], writes=[tWO[c % 2]])
                self.memset("pool", H, 0.0, [T["H"]])
                self.memset("pool", Hb, 0.0, [T["Hb"]])
            if sample:
                q = (t0 - SEQ) // 16
                for jj in range(4):
                    kb.dma("sp", HS[:, jj, :], d["rw_S0T"][4 * q + jj, 2 * c:2 * c + 2].rearrange("h k v -> (h k) v"),
                           writes=[T["HS", jj]])
                    self.cp("act", HSb[:, jj, :], HS[:, jj, :], [T["HS", jj]], [T["HSb", jj]])
                yield
            YTv = YT[:, 0:4 * 128].rearrange("p (j m) -> p j m", j=4)
            for j in range(NCk):
                if sample:
                    Hc, Hbc, Hdc = HS[:, j, :], HSb[:, j, :], H0d
                    tH, tHb, tHd = T["HS", j], T["HSb", j], T["H0d"]
                else:
                    Hc, Hbc, Hdc = H, Hb, H0d
                    tH, tHb, tHd = T["H"], T["Hb"], T["H0d"]
                DL = EP[:, (j + 1) * L - 1:(j + 1) * L]
                bS = self.bank("C")
                PG = self.ps[bS][0:L, 0:128]
                PU = self.ps[bS][0:L, 128:256]
                PY = self.ps[bS][0:L, 256:384]
                bH = self.bank("C")
                PH = self.ps[bH][:, 0:64]
                tS = self.tps[bS]
                tSH = self.tps[bH]
                for h in range(2):
                    hs = slice(64 * h, 64 * h + 64)
                    self.mm(PG[:, hs], LLv[0:L, j, h, 0, :], VTt[0:L, j, hs], True, False, [tLL, tVTt], [tS])
                    self.mm(PG[:, hs], KR[hs, 0, cs(j)], Hbc[hs, :], False, True, [tKR, tHb], [tS], signal=True)
                self.act(Hdc, Hc, AF.Identity, [tH, tEP], [tHd], scale=DL)
                yield
                self.cp("act", Gs[0:L, :], PG, [tS], [T["Gs"]])
                yield
                for h in range(2):
                    hs = slice(64 * h, 64 * h + 64)
                    self.mm(PU[:, hs], Wv[0:L, 2 * j + h, 1, :], Gs[0:L, hs], True, True, [tWv, T["Gs"]], [tS], signal=True)
                yield
                self.act(NU[0:L, :], PU, AF.Identity, [tS], [T["NU"]], scale=-1.0)
                yield
                for h in range(2):
                    hs = slice(64 * h, 64 * h + 64)
                    self.mm(PH[hs, :], KTt[0:L, j, hs], VTt[0:L, j, hs], True, False, [tKTt, tVTt], [tSH])
                    self.mm(PH[hs, :], BTt[0:L, j, hs], NU[0:L, hs], False, True, [tBTt, T["NU"]], [tSH], signal=True)
                for h in range(2):
                    hs = slice(64 * h, 64 * h + 64)
                    self.mm(PY[:, hs], LLv[0:L, j, h, 1, :], VTt[0:L, j, hs], True, False, [tLL, tVTt], [tS])
                    self.mm(PY[:, hs], LLv[0:L, j, h, 3, :], NU[0:L, hs], False, False, [tLL, T["NU"]], [tS])
                    self.mm(PY[:, hs], KR[hs, 1, cs(j)], Hbc[hs, :], False, True, [tKR, tHb], [tS], signal=True)
                yield
                self.stt(Hbc, PH, DL, Hdc, ALU.mult, ALU.add, [tSH, tEP, tHd], [tHb])
                self.stt(Hc, PH, DL, Hdc, ALU.mult, ALU.add, [tSH, tEP, tHd], [tH])
                self.cp("act", YTv[0:L, j, :], PY, [tS], [T["YT"]])
                yield
            if sample:
                for jj in range(4):
                    kb.dma("sp", d["o_rw_Ss"][4 * q + jj, 2 * c:2 * c + 2].rearrange("h k v -> (h k) v"), HS[:, jj, :],
                           reads=[T["HS", jj]])
            if U["lastprompt"]:
                kb.dma("sp", d["o_rw_Sp"][2 * c:2 * c + 2].rearrange("h k v -> (h k) v"), H, reads=[T["H"]])
            G8 = 8
            YT3 = YT[:, 0:512].rearrange("p (g v) -> p g v", g=G8)
            SQ3 = SQ2[:, 0:512].rearrange("p (g v) -> p g v", g=G8)
            SUMv, VARv, RSTv = STAT[:, 0:8], STAT[:, 8:16], STAT[:, 16:24]
            self.reduce(SUMv[0:L, :], YT3[0:L], ALU.add, [T["YT"]], [T["SUM"]])
            self.ts("dve", SUMv[0:L, :], SUMv[0:L, :], 1.0 / 64, None, ALU.mult, None, [T["SUM"]], [T["SUM"]])
            yield
            self.tt("dve", YT3[0:L], YT3[0:L], SUMv[0:L, :].unsqueeze(2).to_broadcast([L, G8, 64]), ALU.subtract,
                    [T["YT"], T["SUM"]], [T["YT"]])
            yield
            self.act(SQ2[0:L, 0:512], YT[0:L, 0:512], AF.Square, [T["YT"]], [T["SQ2"]])
            yield
            self.reduce(VARv[0:L, :], SQ3[0:L], ALU.add, [T["SQ2"]], [T["VAR"]])
            yield
            self.act(RSTv[0:L, :], VARv[0:L, :], AF.Ln, [T["VAR"], self.tCONST], [T["RST"]], scale=1.0 / 64, bias=self.GNEPSC[0:L, :])
            yield
            self.act(RSTv[0:L, :], RSTv[0:L, :], AF.Exp, [T["RST"]], [T["RST"]], scale=-0.5)
            yield
            self.tt("dve", YT3[0:L], YT3[0:L], RSTv[0:L, :].unsqueeze(2).to_broadcast([L, G8, 64]), ALU.mult,
                    [T["YT"], T["RST"]], [T["YT"]])
            yield
            bY = self.bank("C")
            PYF = self.ps[bY][:, 0:n]
            for j in range(NCk):
                self.tr(PYF[:, cs(j)], YTv[0:L, j, :], self.IDENTF[0:L, 0:L], [T["YT"], self.tMASK], [self.tps[bY]])
            yield
            self.act(w(YF), PYF, AF.Identity, [self.tps[bY], tV], [T["SQ2"]], scale=self.vcol("gn_w", c), bias=self.vcol("gn_b", c))
            yield
            self.tt("pool", w(YF), w(YF), w(BONUS), ALU.add, [T["SQ2"], tBONUS], [T["SQ2"]])
            yield
            self.tt("dve", w(YG), w(YF), w(Gf), ALU.mult, [T["SQ2"], tGf], [T["YG"]])
            yield
            for dc0 in range(0, NCH, 2):
                bO = self.bank("C")
                for k2_ in range(2):
                    dc = dc0 + k2_
                    PO = self.ps[bO][:, 256 * k2_:256 * k2_ + n]
                    self.mm(PO, WO[:, dc * 128:(dc + 1) * 128], w(YG), True, True, [tWO[c % 2], T["YG"]], [self.tps[bO]], signal=True)
                for k2_ in range(2):
                    dc = dc0 + k2_
                    PO = self.ps[bO][:, 256 * k2_:256 * k2_ + n]
                    self.tt("dve", self.X[:, dc, tsl], PO, self.X[:, dc, tsl], ALU.add, [self.tps[bO], self.tX[dc, ti5]], [self.tX[dc, ti5]])
                yield

        def drain(gens):
            gens = [g for g in gens if g is not None]
            while gens:
                for g in list(gens):
                    try:
                        next(g)
                    except StopIteration:
                        gens.remove(g)

        PIPE = int(_os.environ.get("RW_PIPE", "1"))
        NU_ = len(units)
        if PIPE:
            for s in range(NU_ + 2):
                gA = stageA(units[s]) if s < NU_ else None
                gB = stageB(units[s - 1]) if 0 <= s - 1 < NU_ else None
                gC = stageC(units[s - 2]) if 0 <= s - 2 < NU_ else None
                drain([gC, gB, gA])
        else:
            for U in units:
                drain([stageA(U)])
                drain([stageB(U)])
                drain([stageC(U)])
        ar.release(m0)
        kb.barrier()

    def mlstm(self):
        kb, d, ar = self.kb, self.d, self.ar
        m0 = ar.mark()
        gname = "norm_mix1"
        XN, tXN = self.XN, self.tXN
        T = TokMap()
        NEG = -1.0e30
        EKA = ar.f32(NT)
        EQA = ar.f32(NT)
        EMTT = ar.f32(48 * 8).rearrange("p (j h) -> p j h", h=8)
        self.EMTP = EMTT[:, 0:32, :]
        EMTS = EMTT[:, 32:48, :]
        self.BBP = ar.f32(2)
        self.ABP = ar.f32(2)
        MLV = ar.f32(8 * 10).rearrange("p (h k) -> p h k", h=8)
        BIF = ar.f32(4)
        M0T = ar.f32(NS)
        MOUT = ar.f32(17)
        SEL = ar.f32(8 * 64).rearrange("p (h m) -> p h m", h=8)
        CONVO = ar.f32(8 * 2 * 17 * 3).rearrange("p (h w s k) -> p h w s k", h=8, w=2, s=17)
        tEK, tEQ = TokMap(), TokMap()
        kb.dma("sp", MLV[0:64], d["mlv"], writes=[T["MLV"]])
        kb.dma("sp", BIF[0:8, 0:2], d["bif"], writes=[T["BIF"]])
        kb.dma("sp", M0T[0:8, :], d["ml_m0T"], writes=[T["M0T"]])
        self.ts("dve", BIF[0:8, 2:3], BIF[0:8, 1:2], -1.0, None, ALU.mult, None, [T["BIF"]], [T["BIF"]])
        self.cp("pool", SEL[0:8], self.IDENTF[0:8, 0:8].unsqueeze(2).to_broadcast([8, 8, 64]), [self.tMASK], [T["SEL"]])

        m1 = ar.mark()
        WIF = ar.bf16(NCH * 16).rearrange("p (k m) -> p k m", k=NCH)
        for K in range(NCH):
            kb.dma("pool", WIF[:, K, :], d["ml_w_in"][0, K * 128:(K + 1) * 128, 3072:3088], writes=[T["WIF"]])
        SQ = [ar.bf16(512) for _ in range(2)]
        LN = ar.f32(512)
        RS = ar.f32(512)
        tSQ = TokMap()
        tLN, tRS = Tok("ln"), Tok("rs")
        LI, LF, BB, AA = [ar.f32(512) for _ in range(4)]
        ABX = ar.f32(513)
        D0, D1, TMPg, MTg, EMg = [ar.f32(512) for _ in range(5)]
        for ti, (t0, n) in enumerate(TILES):
            sample = t0 >= SEQ
            Lc = 4 if sample else 64
            nck = n // Lc

            def out_fn(c, rs, trs, ti=ti, t0=t0, n=n):
                self.stt(XN[:, c, t0:t0 + n], self.X[:, c, t0:t0 + n], self.vcol(gname, c), rs, ALU.mult, ALU.mult,
                         [self.tX[c, ti], trs, self.tVEC], [tXN[c, ti]])
            self.rmsnorm_tile(ti, gname, out_fn, (SQ, tSQ, LN, tLN, RS, tRS, self.bank()))
            bI, bF = self.bank(), self.bank()
            PI, PF = self.ps[bI][0:8, :n], self.ps[bF][0:8, :n]
            for K in range(NCH):
                self.mm(PI, WIF[:, K, 0:8], XN[:, K, t0:t0 + n], K == 0, K == NCH - 1, [T["WIF"], tXN[K, ti]], [self.tps[bI]])
            for K in range(NCH):
                self.mm(PF, WIF[:, K, 8:16], XN[:, K, t0:t0 + n], K == 0, K == NCH - 1, [T["WIF"], tXN[K, ti]], [self.tps[bF]])
            g = lambda a: a[0:8, 0:n]
            self.act(g(LI), PI, AF.Identity, [self.tps[bI], T["BIF"]], [T["LI"]], bias=BIF[0:8, 0:1])
            self.act(g(TMPg), PF, AF.Exp, [self.tps[bF], T["BIF"]], [T["TMP"]], scale=-1.0, bias=BIF[0:8, 2:3])
            self.act(g(LF), g(TMPg), AF.Ln, [T["TMP"], self.tCONST], [T["LF"]], bias=self.ONEC[0:8, :])
            self.memset("dve", g(D0), 1.0, [T["D0"]])
            init = 0.0
            rd = []
            if sample:
                self.memset("dve", g(D0).rearrange("p (s t) -> p s t", t=TS)[:, :, 0:1], 0.0, [T["D0"]])
            elif ti > 0:
                init = self.BBP[0:8, 0:1]
                rd = [T["BBprev"]]
            self.scan(g(BB), g(D0), g(LF), init, ALU.mult, ALU.subtract, [T["D0"], T["LF"]] + rd, [T["BB"]])
            self.tt("dve", g(AA), g(LI), g(BB), ALU.subtract, [T["LI"], T["BB"]], [T["AA"]])
            ab = ABX[0:8, 1:1 + n]
            if sample:
                self.memset("dve", g(D1), 0.0, [T["D1"]])
                self.memset("dve", g(D1).rearrange("p (s t) -> p s t", t=TS)[:, :, 0:1], NEG, [T["D1"]])
                a3 = g(AA).rearrange("p (s t) -> p s t", t=TS)
                self.tt("dve", a3[:, :, 0], a3[:, :, 0], M0T[0:8, :], ALU.max, [T["AA"], T["M0T"]], [T["AA"]])
                self.scan(ab, g(D1), g(AA), 0.0, ALU.add, ALU.max, [T["D1"], T["AA"]], [T["ABX"]])
                rho = M0T[0:8, :].unsqueeze(2).to_broadcast([8, NS, TS])
                rtok = [T["M0T"]]
                a_v = g(AA).rearrange("p (s t) -> p s t", t=TS)
                ab_v = ab.rearrange("p (s t) -> p s t", t=TS)
                ek_v = g(TMPg).rearrange("p (s t) -> p s t", t=TS)
                eq_v = g(D0).rearrange("p (s t) -> p s t", t=TS)
                self.tt("dve", g(AA), g(LI), g(BB), ALU.subtract, [T["LI"], T["BB"], T["ABX"]], [T["AA"]])
            else:
                self.memset("dve", g(D1), 0.0, [T["D1"]])
                if ti == 0:
                    self.memset("dve", ABX[0:8, 0:1], 0.0, [T["ABX"]])
                    ainit = 0.0
                else:
                    self.cp("dve", ABX[0:8, 0:1], self.ABP[0:8, 0:1], [T["ABprev"]], [T["ABX"]])
                    ainit = self.ABP[0:8, 0:1]
                self.scan(ab, g(D1), g(AA), ainit, ALU.add, ALU.max, [T["D1"], T["AA"], T["ABX"]] + ([T["ABprev"]] if ti else []), [T["ABX"]])
                rho = ABX[0:8, 0:n].rearrange("p (j l) -> p j l", l=64)[:, :, 0:1].to_broadcast([8, nck, 64])
                rtok = [T["ABX"]]
                a_v = g(AA).rearrange("p (j l) -> p j l", l=64)
                ab_v = ab.rearrange("p (j l) -> p j l", l=64)
                ek_v = g(TMPg).rearrange("p (j l) -> p j l", l=64)
                eq_v = g(D0).rearrange("p (j l) -> p j l", l=64)
            self.tt("dve", ek_v, a_v, rho, ALU.subtract, [T["AA"]] + rtok, [T["TMP"]])
            self.act(EKA[0:8, t0:t0 + n], g(TMPg), AF.Exp, [T["TMP"]], [tEK[ti]])
            self.tt("dve", eq_v, rho, ab_v, ALU.subtract, [T["ABX"], T["D0"]] + rtok, [T["D0"]])
            self.act(EQA[0:8, t0:t0 + n], g(D0), AF.Exp, [T["D0"]], [tEQ[ti]])
            self.tt("dve", g(MTg), g(BB), ab, ALU.add, [T["BB"], T["ABX"]], [T["MT"]])
            self.act(g(EMg), g(MTg), AF.Exp, [T["MT"]], [T["EM"]], scale=-1.0)
            bT = self.bank()
            for j in range(nck):
                self.tr(self.ps[bT][0:Lc, j * 8:(j + 1) * 8], EMg[0:8, j * Lc:(j + 1) * Lc], self.IDENTF[0:8, 0:8],
                        [T["EM"], self.tMASK], [self.tps[bT]])
            cb0 = t0 // 64 if not sample else 32
            if sample:
                self.cp("act", EMTS[0:Lc, 0:nck, :], self.ps[bT][0:Lc, 0:nck * 8].rearrange("p (j h) -> p j h", h=8),
                        [self.tps[bT]], [T["EMTS"]])
                self.cp("pool", MOUT[0:8, 1:17], g(MTg).rearrange("p (s t) -> p s t", t=TS)[:, :, 3], [T["MT"]], [T["MOUT"]])
            else:
                self.cp("act", self.EMTP[0:Lc, cb0:cb0 + nck, :], self.ps[bT][0:Lc, 0:nck * 8].rearrange("p (j h) -> p j h", h=8),
                        [self.tps[bT]], [T["EMTP"]])
                if ti == 3:
                    self.cp("pool", MOUT[0:8, 0:1], MTg[0:8, n - 1:n], [T["MT"]], [T["MOUT"]])
                self.cp("pool", self.BBP[0:8, 0:1], BB[0:8, n - 1:n], [T["BB"]], [T["BBprev"]])
                self.cp("pool", self.ABP[0:8, 0:1], ABX[0:8, n:n + 1], [T["ABX"]], [T["ABprev"]])
        ar.release(m1)
        kb.barrier()
        kb.dma("sp", d["o_ml_m"], MOUT[0:8, :], reads=[T["MOUT"]])

        WIN = ar.bf16(NCH * 384).rearrange("p (k m) -> p k m", k=NCH)
        WO2 = [ar.bf16(D) for _ in range(2)]
        tWO = [Tok("mwo0"), Tok("mwo1")]
        RAW = [ar.f32(520) for _ in range(2)]
        ACC = [ar.f32(512) for _ in range(2)]
        SIL = [ar.f32(512) for _ in range(2)]
        SA = [dict(QP=ar.bf16(512), KP=ar.bf16(512), VA=ar.bf16(8 * 130).rearrange("p (j m) -> p j m", j=8),
                   KTt=ar.bf16(8 * 64).rearrange("p (j m) -> p j m", j=8), LAMB=ar.f32(512)) for _ in range(2)]
        SO = [ar.f32(512) for _ in range(3)]
        SH = [ar.f32(8 * 128).rearrange("p (j m) -> p j m", j=8) for _ in range(2)]
        STall = ar.bf16(8 * 64)
        SQH = ar.f32(8 * 128)
        STATH = ar.f32(32)
        DEN = ar.f32(16)
        HG = ar.bf16(512)
        C = ar.f32(130)
        Cd = ar.f32(130)
        Cb = ar.bf16(130)
        CS = ar.f32(4 * 130).rearrange("p (s m) -> p s m", s=4)
        CSd = ar.f32(4 * 130).rearrange("p (s m) -> p s m", s=4)
        CSb = ar.bf16(4 * 130).rearrange("p (s m) -> p s m", s=4)
        PCLb = ar.f32(8 * 130).rearrange("p (j m) -> p j m", j=8)
        CbAll = ar.bf16(9 * 130).rearrange("p (j m) -> p j m", j=9)
        main_tiles = [(t0, 512, 64, 8) for t0 in range(0, SEQ, 512)] + [(SEQ + 16 * q, 16, 4, 4) for q in range(4)]
        import os as _os
        if "ML_TILES" in _os.environ:
            main_tiles = [main_tiles[int(i)] for i in _os.environ["ML_TILES"].split(",")]
        units = []
        for h in range(int(_os.environ.get("ML_NH", 8))):
            for k_, (t0, n, L, NCk) in enumerate(main_tiles):
                units.append(dict(u=len(units), h=h, t0=t0, n=n, L=L, NCk=NCk, first=(k_ == 0),
                                  lastprompt=(t0 < SEQ and (k_ + 1 == len(main_tiles) or main_tiles[k_ + 1][0] >= SEQ))))
        for sa in SA:
            self.memset("pool", sa["VA"][0:64, :, 128:129], 1.0, [T["VAinit"]])

        def stageA(U):
            u, h, t0, n, L, NCk = U["u"], U["h"], U["t0"], U["n"], U["L"], U["NCk"]
            sa, k2, k3 = SA[u % 2], u % 2, u % 3
            QP, KP, VA, KTt, LAMB, Osig = sa["QP"], sa["KP"], sa["VA"], sa["KTt"], sa["LAMB"], SO[k3]
            tQP, tKP, tVA, tKTt, tLAM, tO = T["QP", k2], T["KP", k2], T["VA", k2], T["KTt", k2], T["LAM", k2], T["O", k3]
            sample = t0 >= SEQ
            ti5 = min(t0 // 512, 4)
            tsl = slice(t0, t0 + n)
            cs = lambda j: slice(j * L, (j + 1) * L)
            xt = [tXN[K, ti5] for K in range(NCH)]
            if U["first"]:
                for K in range(NCH):
                    rows = slice(K * 128, (K + 1) * 128)
                    kb.dma("pool", WIN[:, K, 0:64], d["ml_w_in"][0, rows, h * 64:(h + 1) * 64], writes=[T["WIN"]])
                    kb.dma("pool", WIN[:, K, 64:128], d["ml_w_in"][0, rows, 512 + h * 64:512 + (h + 1) * 64], writes=[T["WIN"]])
                    kb.dma("pool", WIN[:, K, 128:256], d["ml_w_in"][0, rows, 1024 + h * 128:1024 + (h + 1) * 128], writes=[T["WIN"]])
                    kb.dma("pool", WIN[:, K, 256:384], d["ml_w_in"][0, rows, 2048 + h * 128:2048 + (h + 1) * 128], writes=[T["WIN"]])
                kb.dma("pool", WO2[h % 2], d["ml_w_out"][0, h * 128:(h + 1) * 128, :], writes=[tWO[h % 2]])
                for w_ in range(2):
                    self.memset("pool", RAW[w_][0:64, 0:3], 0.0, [T["RAW", w_]])
                yield
            if sample:
                q4 = (t0 - SEQ) // 16
                for w_ in range(2):
                    rv = RAW[w_][0:64, 0:28].rearrange("p (s u) -> p s u", u=7)
                    kb.dma("sp", rv[:, :, 0:3], d["ml_convT"][h, w_, :, 4 * q4:4 * q4 + 4, :], writes=[T["RAW", w_]])
            bQ, bK, bO = self.bank("A"), self.bank("A"), self.bank("A")
            PQ, PK, PO_ = self.ps[bQ][0:64, :n], self.ps[bK][0:64, :n], self.ps[bO][:, :n]
            for (P_, cols, bb) in ((PQ, slice(0, 64), bQ), (PK, slice(64, 128), bK), (PO_, slice(256, 384), bO)):
                for K in range(NCH):
                    self.mm(P_, WIN[:, K, cols], XN[:, K, tsl], K == 0, K == NCH - 1, [T["WIN"], xt[K]], [self.tps[bb]])
                yield
            self.act(Osig[:, :n], PO_, AF.Sigmoid, [self.tps[bO]], [tO])
            for w_, (P_, bb) in enumerate(((PQ, bQ), (PK, bK))):
                mv = lambda k_: MLV[0:64, h, 5 * w_ + k_:5 * w_ + k_ + 1]
                R_ = RAW[w_]
                if sample:
                    rv = R_[0:64, 0:28].rearrange("p (s u) -> p s u", u=7)
                    self.cp("act", rv[:, :, 3:7], P_.rearrange("p (s t) -> p s t", t=TS), [self.tps[bb]], [T["RAW", w_]])
                    taps = [rv[:, :, k_:k_ + 4] for k_ in range(4)]
                    acc = ACC[w_][0:64, 0:n].rearrange("p (s t) -> p s t", t=TS)
                    self.cp("pool", CONVO[0:64, h, w_, 1 + 4 * q4:5 + 4 * q4, :], rv[:, :, 4:7], [T["RAW", w_]], [T["CONVO"]])
                else:
                    self.cp("act", R_[0:64, 3:3 + n], P_, [self.tps[bb]], [T["RAW", w_]])
                    taps = [R_[0:64, k_:k_ + n] for k_ in range(4)]
                    acc = ACC[w_][0:64, 0:n]
                yield
                self.ts("dve", acc, taps[0], mv(0), mv(4), ALU.mult, ALU.add, [T["RAW", w_], T["MLV"]], [T["ACC", w_]])
                for k_ in range(1, 4):
                    self.stt(acc, taps[k_], mv(k_), acc, ALU.mult, ALU.add, [T["RAW", w_], T["MLV"], T["ACC", w_]], [T["ACC", w_]])
                yield
                self.act(SIL[w_][0:64, 0:n], ACC[w_][0:64, 0:n], AF.Silu, [T["ACC", w_]], [T["SIL", w_]])
                if not sample:
                    if t0 + n == SEQ:
                        self.cp("pool", CONVO[0:64, h, w_, 0, :], R_[0:64, n:n + 3], [T["RAW", w_]], [T["CONVO"]])
                    self.cp("pool", R_[0:64, 0:3], R_[0:64, n:n + 3], [T["RAW", w_], T["ACC", w_]], [T["RAW", w_]])
                yield
            for j0 in range(0, NCk, 2):
                bV = self.bank("A")
                nj = min(2, NCk - j0)
                for jj in range(nj):
                    j = j0 + jj
                    PVt = self.ps[bV][0:L, jj * 128:(jj + 1) * 128]
                    for K in range(NCH):
                        self.mm(PVt, XN[:, K, t0 + j * L:t0 + (j + 1) * L], WIN[:, K, 128:256], K == 0, K == NCH - 1,
                                [T["WIN"], xt[K]], [self.tps[bV]])
                self.cp("act", VA[0:L, j0:j0 + nj, 0:128], self.ps[bV][0:L, 0:nj * 128].rearrange("p (j m) -> p j m", m=128),
                        [self.tps[bV], T["VAinit"]], [tVA])
                yield
            bM, bM2 = self.bank("A"), self.bank("A")
            PBK, PBQ = self.ps[bM][0:64, 0:n], self.ps[bM2][0:64, 0:n]
            tBQ = self.tps[bM2]
            self.mm(PBK, SEL[0:8, h, :], EKA[0:8, tsl], True, True, [T["SEL"], tEK[ti5]], [self.tps[bM]])
            self.mm(PBQ, SEL[0:8, h, :], EQA[0:8, tsl], True, True, [T["SEL"], tEQ[ti5]], [tBQ])
            yield
            self.tt("dve", KP[0:64, 0:n], SIL[1][0:64, 0:n], PBK, ALU.mult, [T["SIL", 1], self.tps[bM]], [tKP])
            self.stt(QP[0:64, 0:n], SIL[0][0:64, 0:n], 0.125, PBQ, ALU.mult, ALU.mult, [T["SIL", 0], tBQ], [tQP])
            self.cp("act", LAMB[0:64, 0:n], PBQ, [tBQ], [tLAM])
            yield
            bT = self.bank("A")
            PTr = self.ps[bT].bitcast(BF16)
            for j in range(NCk):
                self.tr(PTr[0:L, j * 64:(j + 1) * 64], KP[0:64, cs(j)], self.IDENTB[0:64, 0:64], [tKP, self.tMASK], [self.tps[bT]])
            self.cp("act", KTt[0:L, 0:NCk, :], PTr[0:L, 0:NCk * 64].rearrange("p (j m) -> p j m", m=64), [self.tps[bT]], [tKTt])
            yield

        def stageC(U):
            u, h, t0, n, L, NCk = U["u"], U["h"], U["t0"], U["n"], U["L"], U["NCk"]
            sa, k2 = SA[u % 2], u % 2
            QP, KP, VA, KTt, LAM = sa["QP"], sa["KP"], sa["VA"], sa["KTt"], sa["LAMB"]
            tQP, tKP, tVA, tKTt, tLAM = T["QP", k2], T["KP", k2], T["VA", k2], T["KTt", k2], T["LAM", k2]
            HT, tHT = SH[k2], T["HT", k2]
            sample = t0 >= SEQ
            cs = lambda j: slice(j * L, (j + 1) * L)
            if U["first"]:
                self.memset("pool", C[0:64], 0.0, [T["C"]])
                self.memset("pool", Cb[0:64], 0.0, [T["Cb"]])
            if sample:
                q4 = (t0 - SEQ) // 16
                for s in range(4):
                    kb.dma("sp", CS[0:64, s, 0:129], d["ml_C0T"][4 * q4 + s, h], writes=[T["CS", s]])
                yield
            PCL = PCLb[0:64, 0:NCk, 0:129]
            for j0 in range(0, NCk, 2):
                bP = self.bank("C")
                for jj in range(2):
                    j = j0 + jj
                    PCS = self.ps[bP][0:64, 256 * jj:256 * jj + 129]
                    self.mm(PCS, KTt[0:L, j, :], VA[0:L, j, 0:129], True, True, [tKTt, tVA], [self.tps[bP]])
                for jj in range(2):
                    j = j0 + jj
                    PCS = self.ps[bP][0:64, 256 * jj:256 * jj + 129]
                    lam = LAM[0:64, (j + 1) * L - 1:(j + 1) * L]
                    self.act(PCL[:, j, :], PCS, AF.Identity, [self.tps[bP], tLAM], [T["PCL"]], scale=lam)
                yield
            CbA = CbAll[0:64, 0:NCk + 1, 0:129]
            for j in range(NCk):
                lam = LAM[0:64, (j + 1) * L - 1:(j + 1) * L]
                if sample:
                    Cc, tC = CS[0:64, j, 0:129], T["CS", j]
                    self.cp("act", CbA[:, j, :], Cc, [tC], [T["CbA"]])
                    self.stt(Cc, Cc, lam, PCL[:, j, :], ALU.mult, ALU.add, [tC, tLAM, T["PCL"]], [tC])
                else:
                    Cc, tC = C[0:64, 0:129], T["C"]
                    if j == 0:
                        self.cp("act", CbA[:, 0, :], Cc, [tC], [T["CbA"]])
                    self.stt(CbA[:, j + 1, :], Cc, lam, PCL[:, j, :], ALU.mult, ALU.add, [tC, tLAM, T["PCL"]], [T["CbA"]])
                    self.stt(Cc, Cc, lam, PCL[:, j, :], ALU.mult, ALU.add, [tC, tLAM, T["PCL"]], [tC])
                yield
            bS = self.bank("C")
            PSTv = self.ps[bS][0:L, 0:NCk * L].rearrange("p (j t) -> p j t", j=NCk)
            for j in range(NCk):
                self.mm(PSTv[:, j, :], KP[0:64, cs(j)], QP[0:64, cs(j)], True, True, [tKP, tQP], [self.tps[bS]])
            yield
            STv = STall[0:L, 0:NCk * L].rearrange("p (j t) -> p j t", j=NCk)
            self.tt("dve", STv, PSTv, self.MIU[0:L, 0:L].unsqueeze(1).to_broadcast([L, NCk, L]), ALU.mult,
                    [self.tps[bS], self.tMASK], [T["ST"]])
            yield
            if sample:
                emall, tem = EMTS[0:L, 4 * q4:4 * q4 + NCk, h], T["EMTS"]
            else:
                emall, tem = self.EMTP[0:L, t0 // 64:t0 // 64 + NCk, h], T["EMTP"]
            for g0 in range(0, NCk, 3):
                g = min(3, NCk - g0)
                bN = self.bank("C")
                tN = self.tps[bN]
                PNDv = self.ps[bN][0:L, 0:g * 129].rearrange("p (j m) -> p j m", j=g)
                for jj in range(g):
                    j = g0 + jj
                    self.mm(PNDv[:, jj, :], QP[0:64, cs(j)], CbA[:, j, :], True, False, [tQP, T["CbA"]], [tN])
                    self.mm(PNDv[:, jj, :], STv[:, j, :], VA[0:L, j, 0:129], False, True, [T["ST"], tVA], [tN])
                yield
                dn = DEN[0:L, g0:g0 + g]
                self.act(dn, PNDv[:, :, 128], AF.Abs, [tN], [T["DEN"]])
                yield
                self.tt("dve", dn, dn, emall[:, g0:g0 + g], ALU.max, [T["DEN"], tem], [T["DEN"]])
                self.recip(dn, dn, [T["DEN"]], [T["DEN"]])
                yield
                self.tt("dve", HT[0:L, g0:g0 + g, :], PNDv[:, :, 0:128], dn.unsqueeze(2).to_broadcast([L, g, 128]), ALU.mult,
                        [tN, T["DEN"]], [tHT])
                yield
            if sample:
                for s in range(4):
                    kb.dma("sp", d["o_ml_Cs"][4 * q4 + s, h], CS[0:64, s, 0:129], reads=[T["CS", s]])
            if U["lastprompt"]:
                kb.dma("sp", d["o_ml_Cp"][h], C[0:64, 0:129], reads=[T["C"]])

        def stageD(U):
            u, h, t0, n, L, NCk = U["u"], U["h"], U["t0"], U["n"], U["L"], U["NCk"]
            k2, k3 = u % 2, u % 3
            HT, tHT = SH[k2], T["HT", k2]
            Osig, tO = SO[k3], T["O", k3]
            WOh = WO2[h % 2]
            ti5 = min(t0 // 512, 4)
            tsl = slice(t0, t0 + n)
            cs = lambda j: slice(j * L, (j + 1) * L)
            G = NCk
            HTf = HT[0:L, 0:G, :]
            SQv = SQH[0:L, 0:G * 128].rearrange("p (j m) -> p j m", m=128)
            self.act(SQv, HTf, AF.Square, [tHT], [T["SQH"]])
            yield
            self.reduce(STATH[0:L, 0:G], SQv, ALU.add, [T["SQH"]], [T["STATH"]])
            yield
            self.act(STATH[0:L, 0:G], STATH[0:L, 0:G], AF.Ln, [T["STATH"], self.tCONST], [T["STATH"]], scale=1.0 / 128, bias=self.EPSC[0:L, :])
            yield
            self.act(STATH[0:L, 0:G], STATH[0:L, 0:G], AF.Exp, [T["STATH"]], [T["STATH"]], scale=-0.5)
            yield
            self.tt("dve", HTf, HTf, STATH[0:L, 0:G].unsqueeze(2).to_broadcast([L, G, 128]), ALU.mult, [tHT, T["STATH"]], [tHT])
            yield
            bY = self.bank("C2")
            PYF = self.ps[bY][:, 0:n]
            for j in range(NCk):
                self.tr(PYF[:, cs(j)], HT[0:L, j, :], self.IDENTF[0:L, 0:L], [tHT, self.tMASK], [self.tps[bY]])
            yield
            self.stt(HG[:, 0:n], PYF, self.vcol("ml_norm_w", h), Osig[:, 0:n], ALU.mult, ALU.mult, [self.tps[bY], tO, self.tVEC], [T["HG"]])
            yield
            for dc in range(NCH):
                bO2 = self.bank("C2")
                PO2 = self.ps[bO2][:, 0:n]
                self.mm(PO2, WOh[:, dc * 128:(dc + 1) * 128], HG[:, 0:n], True, True, [tWO[h % 2], T["HG"]], [self.tps[bO2]])
                self.tt("dve", self.X[:, dc, tsl], PO2, self.X[:, dc, tsl], ALU.add, [self.tps[bO2], self.tX[dc, ti5]], [self.tX[dc, ti5]])
                yield

        def drain(gens):
            gens = [g for g in gens if g is not None]
            while gens:
                for g in list(gens):
                    try:
                        next(g)
                    except StopIteration:
                        gens.remove(g)

        NU_ = len(units)
        for s in range(NU_ + 2):
            gA = stageA(units[s]) if s < NU_ else None
            gC = stageC(units[s - 1]) if 0 <= s - 1 < NU_ else None
            gD = stageD(units[s - 2]) if 0 <= s - 2 < NU_ else None
            drain([gC, gD, gA])
        kb.dma("sp", d["o_ml_conv"], CONVO[0:64], reads=[T["CONVO"]])
        ar.release(m0)
        kb.barrier()

    def ffn(self, L, which):
        kb, d, ar = self.kb, self.d, self.ar
        m = ar.mark()
        wg = d["ff%s_wg" % which][L]
        wu = d["ff%s_wu" % which][L]
        wd = d["ff%s_wd" % which][L]
        gname = "norm_ff%s%d" % (which, L)
        G = 4
        groups = [(f0, min(G, NFF - f0)) for f0 in range(0, NFF, G)]
        WG = [ar.bf16(NCH * 512).rearrange("p (c f) -> p c f", c=NCH) for _ in range(2)]
        WU = [ar.bf16(NCH * 512).rearrange("p (c f) -> p c f", c=NCH) for _ in range(2)]
        WD = [ar.bf16(G * D).rearrange("p (f o) -> p f o", f=G) for _ in range(2)]
        H = [ar.bf16(G * 512).rearrange("p (f n) -> p f n", f=G) for _ in range(2)]
        SG = [ar.f32(512) for _ in range(2)]
        SQ = [ar.bf16(512) for _ in range(2)]
        LN = ar.f32(512)
        RS = ar.f32(512)
        tW = TokMap()
        tH = TokMap()
        tSG = TokMap()
        tSQ = TokMap()
        tLN, tRS = Tok("ln"), Tok("rs")

        def load_group(gi):
            f0, nf = groups[gi]
            s = gi % 2
            for c in range(NCH):
                kb.dma("pool", WG[s][:, c, 0:nf * 128], wg[c * 128:(c + 1) * 128, f0 * 128:(f0 + nf) * 128],
                       writes=[tW["g", s, c]])
                kb.dma("pool", WU[s][:, c, 0:nf * 128], wu[c * 128:(c + 1) * 128, f0 * 128:(f0 + nf) * 128],
                       writes=[tW["u", s, c]])
            for fi in range(nf):
                kb.dma("pool", WD[s][:, fi, :], wd[(f0 + fi) * 128:(f0 + fi + 1) * 128, :], writes=[tW["d", s, fi]])

        load_group(0)
        load_group(1)

        for ti, (t0, n) in enumerate(TILES):
            def out_fn(c, rs, trs, ti=ti, t0=t0, n=n):
                kb.emit("dve", lambda e: e.scalar_tensor_tensor(out=self.XN[:, c, t0:t0 + n], in0=self.X[:, c, t0:t0 + n],
                                                                scalar=self.vcol(gname, c), in1=rs,
                                                                op0=ALU.mult, op1=ALU.mult),
                        reads=[self.tX[c, ti], trs, self.tVEC], writes=[self.tXN[c, ti]])
            self.rmsnorm_tile(ti, gname, out_fn, (SQ, tSQ, LN, tLN, RS, tRS, 4 + ti % 4))

        items = [(gi, ti) for gi in range(len(groups)) for ti in range(len(TILES))]
        po_rr = [0]

        def up(idx):
            gi, ti = items[idx]
            f0, nf = groups[gi]
            s = gi % 2
            hs = idx % 2
            t0, n = TILES[ti]
            for fi in range(nf):
                b = fi % 2
                pg = self.ps[b][:, :n]
                pu = self.ps[2 + b][:, :n]
                for c in range(NCH):
                    kb.emit("pe", lambda e, c=c, fi=fi, pg=pg: e.matmul(pg, WG[s][:, c, fi * 128:(fi + 1) * 128],
                                                                     self.XN[:, c, t0:t0 + n], start=(c == 0), stop=(c == NCH - 1)),
                            reads=[tW["g", s, c], self.tXN[c, ti]], writes=[self.tps[b]], signal=(c == NCH - 1))
                for c in range(NCH):
                    kb.emit("pe", lambda e, c=c, fi=fi, pu=pu: e.matmul(pu, WU[s][:, c, fi * 128:(fi + 1) * 128],
                                                                     self.XN[:, c, t0:t0 + n], start=(c == 0), stop=(c == NCH - 1)),
                            reads=[tW["u", s, c], self.tXN[c, ti]], writes=[self.tps[2 + b]], signal=(c == NCH - 1))
                kb.emit("act", lambda e, b=b, pg=pg: e.activation(out=SG[b][:, :n], in_=pg, func=AF.Silu),
                        reads=[self.tps[b]], writes=[tSG[b]])
                kb.emit("dve", lambda e, b=b, fi=fi, pu=pu: e.tensor_tensor(out=H[hs][:, fi, :n], in0=SG[b][:, :n], in1=pu, op=ALU.mult),
                        reads=[tSG[b], self.tps[2 + b]], writes=[tH[hs, fi]])

        def down(idx):
            gi, ti = items[idx]
            f0, nf = groups[gi]
            s = gi % 2
            hs = idx % 2
            t0, n = TILES[ti]
            for dc in range(NCH):
                bank = 4 + po_rr[0] % 4
                po_rr[0] += 1
                po = self.ps[bank][:, :n]
                for fi in range(nf):
                    kb.emit("pe", lambda e, fi=fi, dc=dc, po=po: e.matmul(po, WD[s][:, fi, dc * 128:(dc + 1) * 128], H[hs][:, fi, :n],
                                                                       start=(fi == 0), stop=(fi == nf - 1)),
                            reads=[tW["d", s, fi], tH[hs, fi]], writes=[self.tps[bank]], signal=(fi == nf - 1))
                kb.emit("dve", lambda e, dc=dc, po=po: e.scalar_tensor_tensor(out=self.X[:, dc, t0:t0 + n], in0=po, scalar=0.5,
                                                                           in1=self.X[:, dc, t0:t0 + n], op0=ALU.mult, op1=ALU.add),
                        reads=[self.tps[bank], self.tX[dc, ti]], writes=[self.tX[dc, ti]])

        ntile = len(TILES)
        for idx in range(len(items)):
            up(idx)
            if idx > 0:
                down(idx - 1)
                gi_prev, ti_prev = items[idx - 1]
                if ti_prev == ntile - 1 and gi_prev + 2 < len(groups):
                    load_group(gi_prev + 2)
        down(len(items) - 1)
        ar.release(m)
        kb.barrier()

    def final_norm(self):
        kb, d, ar = self.kb, self.d, self.ar
        m = ar.mark()
        SQ = [ar.bf16(512) for _ in range(2)]
        LN = ar.f32(512)
        RS = ar.f32(512)
        Y = [ar.f32(512) for _ in range(4)]
        tSQ, tY = TokMap(), TokMap()
        tLN, tRS = Tok("ln"), Tok("rs")
        rr = [0]
        for ti, (t0, n) in enumerate(TILES):
            def out_fn(c, rs, trs, ti=ti, t0=t0, n=n):
                s = rr[0] % 4
                rr[0] += 1
                kb.emit("dve", lambda e: e.scalar_tensor_tensor(out=Y[s][:, :n], in0=self.X[:, c, t0:t0 + n],
                                                                scalar=self.vcol("norm_final", c), in1=rs,
                                                                op0=ALU.mult, op1=ALU.mult),
                        reads=[self.tX[c, ti], trs, self.tVEC], writes=[tY[s]])
                kb.dma("sp", d["yT"][c * 128:(c + 1) * 128, t0:t0 + n], Y[s][:, :n], reads=[tY[s]])
            self.rmsnorm_tile(ti, "norm_final", out_fn, (SQ, tSQ, LN, tLN, RS, tRS, 4 + ti % 4))
        ar.release(m)
        kb.barrier()

    def dump_x(self):
        kb, d = self.kb, self.d
        for c in range(NCH):
            for ti, (t0, n) in enumerate(TILES):
                kb.dma("sp", d["yT"][c * 128:(c + 1) * 128, t0:t0 + n], self.X[:, c, t0:t0 + n], reads=[self.tX[c, ti]])

    def build(self):
        st = self.stages
        self.load_inputs()
        self.consts()
        for L in range(2):
            if "ffa%d" % L in st:
                self.ffn(L, "a")
            if "mix%d" % L in st and L == 0:
                self.rwkv()
            if "mix%d" % L in st and L == 1:
                self.mlstm()
            if "ffb%d" % L in st:
                self.ffn(L, "b")
        if "final" in st:
            self.final_norm()
        else:
            self.dump_x()
        self.kb.flush()
        return self.nc


ALL_STAGES = ("ffa0", "mix0", "ffb0", "ffa1", "mix1", "ffb1", "final")


def pack_vecs(inp):
    vecs = np.zeros((NVEC, D), np.float32)

    def put(name, v):
        vecs[VID[name]] = np.asarray(v, np.float32).reshape(D)

    for L in range(2):
        put("norm_ffa%d" % L, inp["norm_ffa"][L])
        put("norm_mix%d" % L, inp["norm_mix"][L])
        put("norm_ffb%d" % L, inp["norm_ffb"][L])
    put("norm_final", inp["norm_final"])
    for i in range(6):
        put("mu%d" % i, inp["rw_mu"][0, i])
    put("w0", inp["rw_w0"][0])
    put("a0", inp["rw_a0"][0])
    put("k_k", inp["rw_k_k"][0])
    put("k_a", inp["rw_k_a"][0])
    put("r_k", inp["rw_r_k"][0])
    put("gn_w", inp["rw_gn_w"][0])
    put("gn_b", inp["rw_gn_b"][0])
    for j in range(4):
        put("cw%d" % j, inp["ml_conv_w"][0, j])
    put("cb", inp["ml_conv_b"][0])
    put("ml_norm_w", inp["ml_norm_w"][0])
    return np.ascontiguousarray(vecs.reshape(NVEC, NCH, 128).transpose(2, 0, 1).reshape(128, NVEC * NCH))


def make_in_maps(inp):
    vecs = pack_vecs(inp)
    shared = {"vecs": vecs}
    for nm in ("ffa_wg", "ffa_wu", "ffb_wg", "ffb_wu", "ffa_wd", "ffb_wd"):
        shared[nm] = np.ascontiguousarray(inp[nm], dtype=np.float32)
    cw = np.asarray(inp["ml_conv_w"][0], np.float32)
    cbv = np.asarray(inp["ml_conv_b"][0], np.float32)
    mlv = np.zeros((64, 8, 10), np.float32)
    for w in range(2):
        for k in range(4):
            mlv[:, :, 5 * w + k] = cw[k, w * 512:(w + 1) * 512].reshape(8, 64).T
        mlv[:, :, 5 * w + 4] = cbv[w * 512:(w + 1) * 512].reshape(8, 64).T
    shared["mlv"] = mlv
    shared["bif"] = np.ascontiguousarray(np.asarray(inp["ml_b_if"][0], np.float32).reshape(2, 8).T)
    for nm in ("rw_wr", "rw_wk", "rw_wv", "rw_wo", "rw_w1", "rw_w2", "rw_a1", "rw_a2", "rw_g1", "rw_g2", "ml_w_in", "ml_w_out"):
        shared[nm] = np.ascontiguousarray(inp[nm], dtype=np.float32)
    maps = []
    for core in range(NCORES):
        xs = np.concatenate([inp["x_prompt"][core], inp["x_sample"][core * NS:(core + 1) * NS].reshape(NS * TS, D)], axis=0)
        m = dict(shared)
        m["xT"] = np.ascontiguousarray(xs.T.astype(np.float32))
        sq = slice(core * NS, (core + 1) * NS)
        sh = inp["state_rwkv_shift"][0, sq]
        m["shiftT"] = np.ascontiguousarray(sh.reshape(NS, NCH, 128).transpose(2, 1, 0).astype(np.float32))
        m["rw_S0T"] = np.ascontiguousarray(inp["state_rwkv_S"][0, sq].transpose(0, 1, 3, 2).astype(np.float32))
        m["ml_m0T"] = np.ascontiguousarray(inp["state_mlstm_m"][0, sq].T.astype(np.float32))
        c0 = np.concatenate([inp["state_mlstm_C"][0, sq].transpose(0, 1, 3, 2), inp["state_mlstm_n"][0, sq][..., None]], axis=-1)
        m["ml_C0T"] = np.ascontiguousarray(c0.astype(np.float32))
        cv = inp["state_mlstm_conv"][0, sq]
        m["ml_convT"] = np.ascontiguousarray(cv.reshape(NS, 3, 2, 8, 64).transpose(3, 2, 4, 0, 1).astype(np.float32))
        maps.append(m)
    return maps


def run(inp, stages=ALL_STAGES, trace=False):
    b = Builder(stages)
    nc = b.build()
    maps = make_in_maps(inp)
    res = run_bass_kernel_spmd(nc, maps, core_ids=list(range(NCORES)), trace=trace)
    return b, res


def assemble(results):
    f = np.float32
    yp = np.zeros((8, SEQ, D), f); ys = np.zeros((128, TS, D), f)
    p_S = np.zeros((1, 8, 16, 64, 64), f); p_sh = np.zeros((1, 8, D), f)
    p_C = np.zeros((1, 8, 8, 128, 64), f); p_n = np.zeros((1, 8, 8, 64), f); p_m = np.zeros((1, 8, 8), f)
    p_cv = np.zeros((1, 8, 3, D), f)
    s_S = np.zeros((1, 128, 16, 64, 64), f); s_sh = np.zeros((1, 128, D), f)
    s_C = np.zeros((1, 128, 8, 128, 64), f); s_n = np.zeros((1, 128, 8, 64), f); s_m = np.zeros((1, 128, 8), f)
    s_cv = np.zeros((1, 128, 3, D), f)
    for core in range(NCORES):
        r = results[core]
        sq = slice(core * NS, (core + 1) * NS)
        y = r["yT"].T
        yp[core] = y[:SEQ]
        ys[sq] = y[SEQ:].reshape(NS, TS, D)
        sho = r["o_shift"]
        p_sh[0, core] = sho[:, :, 0].T.reshape(D)
        s_sh[0, sq] = sho[:, :, 1:].transpose(2, 1, 0).reshape(NS, D)
        p_S[0, core] = r["o_rw_Sp"].transpose(0, 2, 1)
        s_S[0, sq] = r["o_rw_Ss"].transpose(0, 1, 3, 2)
        cp = r["o_ml_Cp"]
        p_C[0, core] = cp[:, :, 0:128].transpose(0, 2, 1)
        p_n[0, core] = cp[:, :, 128]
        cs_ = r["o_ml_Cs"]
        s_C[0, sq] = cs_[:, :, :, 0:128].transpose(0, 1, 3, 2)
        s_n[0, sq] = cs_[:, :, :, 128]
        mo = r["o_ml_m"]
        p_m[0, core] = mo[:, 0]
        s_m[0, sq] = mo[:, 1:].T
        cv = r["o_ml_conv"]
        cvt = cv.transpose(3, 4, 2, 1, 0).reshape(17, 3, D)
        p_cv[0, core] = cvt[0]
        s_cv[0, sq] = cvt[1:]
    return (yp, ys, p_S, p_sh, p_C, p_n, p_m, p_cv, s_S, s_sh, s_C, s_n, s_m, s_cv)


def kernel(**inp):
    b, res = run(inp)
    return assemble(res.results)
```

```python
import numpy as np
from contextlib import ExitStack
import concourse.bass as bass
import concourse.mybir as mybir
from concourse.bass_utils import run_bass_kernel_spmd

F32 = mybir.dt.float32
BF16 = mybir.dt.bfloat16
AF = mybir.ActivationFunctionType
ALU = mybir.AluOpType
AX = mybir.AxisListType

NCORES = 8
D = 1024
NCH = 8
SEQ = 2048
NS = 16
TS = 4
NT = SEQ + NS * TS
DFF = 2816
NFF = DFF // 128
EPS = 1e-6

ENGS = ("pe", "act", "dve", "pool", "sp")
SAME_ENGINE_SYNC = True

VID = {}
_v = 0
for _nm in ("norm_ffa0", "norm_ffa1", "norm_mix0", "norm_mix1", "norm_ffb0", "norm_ffb1", "norm_final",
            "mu0", "mu1", "mu2", "mu3", "mu4", "mu5", "w0", "a0", "k_k", "k_a", "r_k", "gn_w", "gn_b",
            "cw0", "cw1", "cw2", "cw3", "cb", "ml_norm_w"):
    VID[_nm] = _v
    _v += 1
NVEC = _v


class Tok:
    __slots__ = ("name", "w", "r", "excl")

    def __init__(self, name="", excl=False):
        self.name = name
        self.w = []
        self.r = []
        self.excl = excl


class TokMap(dict):
    def __missing__(self, key):
        t = Tok(str(key))
        self[key] = t
        return t


class KB:
    def __init__(self, n_dma_sems=12):
        self.nc = bass.Bass("TRN2", target_bir_lowering=False, dynamic_dma_scratch_size=4096)
        self.es = ExitStack()
        nc = self.nc
        self.sem = {}
        self.count = {}
        self.prog = {e: [] for e in ENGS}
        self.waited = {e: {} for e in ENGS}
        for e in ENGS:
            self.sem[e] = self.es.enter_context(nc.semaphore("s_" + e))
            self.count[e] = 0
        self.dsem = {}
        self.dval = {}
        self.dnext = {}
        for q in ("sp", "pool", "act"):
            self.dsem[q] = []
            for j in range(n_dma_sems):
                key = "d_%s_%d" % (q, j)
                self.sem[key] = self.es.enter_context(nc.semaphore(key))
                self.dsem[q].append(key)
                self.dval[key] = 0
            self.dnext[q] = 0
        self.ninstr = 0
        self.defer = None

    def _wait(self, eng, semkey, value):
        if value <= 0:
            return
        if self.waited[eng].get(semkey, 0) >= value:
            return
        self.waited[eng][semkey] = value
        sem = self.sem[semkey]
        self.prog[eng].append(lambda e, sem=sem, value=value: e.wait_ge(sem, value))

    def _deps(self, eng, reads, writes):
        deps = set()
        for t in reads:
            deps.update(t.w)
            if t.excl:
                deps.update(x for x in t.r if x[0] != eng)
        for t in writes:
            deps.update(t.w)
            deps.update(t.r)
        for (sk, v) in deps:
            if sk == eng and (eng == "pe" or not SAME_ENGINE_SYNC):
                continue
            self._wait(eng, sk, v)

    def emit(self, eng, fn, reads=(), writes=(), signal=True, dur=0.4, after=None):
        if self.defer is not None:
            self.defer.append(dict(kind="op", eng=eng, fn=fn, reads=tuple(reads), writes=tuple(writes), signal=signal,
                                   dur=dur, extra=[after] if after is not None else []))
            self.ninstr += 1
            return len(self.defer) - 1
        self._deps(eng, reads, writes)
        if after is not None:
            self._wait(eng, after[0], after[1])
        self.ninstr += 1
        if signal:
            self.count[eng] += 1
            cid = (eng, self.count[eng])
            sem = self.sem[eng]
            self.prog[eng].append(lambda e, fn=fn, sem=sem: fn(e).then_inc(sem, 1))
        else:
            cid = (eng, self.count[eng] + 1)
            self.prog[eng].append(lambda e, fn=fn: fn(e))
        for t in reads:
            t.r.append(cid)
        for t in writes:
            t.w = [cid]
            t.r = []
        return cid

    def dma(self, q, out, in_, reads=(), writes=()):
        if self.defer is not None:
            self.defer.append(dict(kind="dma", eng=q, out=out, in_=in_, reads=tuple(reads), writes=tuple(writes), signal=True,
                                   dur=0.1, extra=[]))
            self.ninstr += 1
            return len(self.defer) - 1
        self._deps(q, reads, writes)
        return self._dma_issue(q, out, in_, reads, writes)

    def _dma_issue(self, q, out, in_, reads, writes):
        j = self.dnext[q]
        self.dnext[q] = (j + 1) % len(self.dsem[q])
        key = self.dsem[q][j]
        self._wait(q, key, self.dval[key])
        self.dval[key] += 16
        cid = (key, self.dval[key])
        sem = self.sem[key]
        self.ninstr += 1
        self.prog[q].append(lambda e, out=out, in_=in_, sem=sem: e.dma_start(out=out, in_=in_).then_inc(sem, 16))
        for t in reads:
            t.r.append(cid)
        for t in writes:
            t.w = [cid]
            t.r = []
        return cid

    def begin_defer(self):
        self.barrier()
        self.defer = []

    def end_defer(self):
        recs = self.defer
        self.defer = None
        self._schedule(recs)
        self.barrier()

    def _schedule(self, recs):
        n = len(recs)
        lastw, readers = {}, {}
        deps = [None] * n
        for i, r in enumerate(recs):
            e = r["eng"]
            dset = set(r["extra"])
            for t in r["reads"]:
                k = id(t)
                if k in lastw:
                    dset.add(lastw[k])
                if t.excl:
                    dset.update(j for j in readers.get(k, ()) if recs[j]["eng"] != e)
            for t in r["writes"]:
                k = id(t)
                if k in lastw:
                    dset.add(lastw[k])
                dset.update(readers.get(k, ()))
            dset.discard(i)
            deps[i] = dset
            for t in r["reads"]:
                readers.setdefault(id(t), []).append(i)
            for t in r["writes"]:
                lastw[id(t)] = i
                readers[id(t)] = []
        succs = [[] for _ in range(n)]
        indeg = [0] * n
        for i in range(n):
            indeg[i] = len(deps[i])
            for j in deps[i]:
                succs[j].append(i)
        LAT_X, LAT_S = 0.35, 0.12
        ready_t = [0.0] * n
        start = [0.0] * n
        free = {e: 0.0 for e in ENGS}
        ready = {e: [] for e in ENGS}
        for i in range(n):
            if indeg[i] == 0:
                ready[recs[i]["eng"]].append(i)
        done = 0
        while done < n:
            best, bkey = None, None
            for e in ENGS:
                lst = ready[e]
                if not lst:
                    continue
                fe = free[e]
                cand = min(lst, key=lambda i: (max(fe, ready_t[i]), i))
                key = (max(fe, ready_t[cand]), cand)
                if bkey is None or key < bkey:
                    best, bkey = cand, key
            i = best
            r = recs[i]
            e = r["eng"]
            ready[e].remove(i)
            st = bkey[0]
            start[i] = st
            free[e] = st + r["dur"]
            fin = st + r["dur"] + (2.5 if r["kind"] == "dma" else 0.0)
            for j in succs[i]:
                lat = LAT_S if recs[j]["eng"] == e else LAT_X
                if fin + lat > ready_t[j]:
                    ready_t[j] = fin + lat
                indeg[j] -= 1
                if indeg[j] == 0:
                    ready[recs[j]["eng"]].append(j)
            done += 1
        order = sorted(range(n), key=lambda i: (start[i], i))
        cids = [None] * n
        cnt = self.count["pe"]
        pend = []
        for i in order:
            r = recs[i]
            if r["eng"] != "pe" or r["kind"] != "op":
                continue
            if r["signal"]:
                cnt += 1
                cids[i] = ("pe", cnt)
                for j in pend:
                    cids[j] = ("pe", cnt)
                pend = []
            else:
                pend.append(i)
        assert not pend, "deferred PE stream must end with a signalled instruction"
        for i in order:
            r = recs[i]
            e = r["eng"]
            for j in deps[i]:
                sk, v = cids[j]
                if sk == e and j not in r["extra"] and (e == "pe" or not SAME_ENGINE_SYNC):
                    continue
                self._wait(e, sk, v)
            if r["kind"] == "dma":
                cids[i] = self._dma_issue(e, r["out"], r["in_"], (), ())
                continue
            fn = r["fn"]
            if r["signal"]:
                self.count[e] += 1
                cid = (e, self.count[e])
                if e == "pe":
                    assert cid == cids[i], (cid, cids[i])
                cids[i] = cid
                sem = self.sem[e]
                self.prog[e].append(lambda eng_, fn=fn, sem=sem: fn(eng_).then_inc(sem, 1))
            else:
                self.prog[e].append(lambda eng_, fn=fn: fn(eng_))
        self.sched_span = max(start) if n else 0.0

    def barrier(self):
        for e in ENGS:
            for e2 in ENGS:
                if e2 != e:
                    self._wait(e, e2, self.count[e2])
            for q in self.dsem:
                for key in self.dsem[q]:
                    self._wait(e, key, self.dval[key])

    def flush(self):
        self.barrier()
        nc = self.nc
        prog = self.prog
        with nc.Block() as block:
            @block.tensor
            def _(e):
                for f in prog["pe"]:
                    f(e)

            @block.scalar
            def _(e):
                for f in prog["act"]:
                    f(e)

            @block.vector
            def _(e):
                for f in prog["dve"]:
                    f(e)

            @block.gpsimd
            def _(e):
                for f in prog["pool"]:
                    f(e)

            @block.sync
            def _(e):
                for f in prog["sp"]:
                    f(e)
        self.prog = {e: [] for e in ENGS}


class Arena:
    def __init__(self, ap, nwords):
        self.ap = ap
        self.n = nwords
        self.top = 0

    def mark(self):
        return self.top

    def release(self, m):
        self.top = m

    def f32(self, nwords):
        assert self.top + nwords <= self.n, ("arena overflow", self.top, nwords, self.n)
        a = self.ap[:, self.top:self.top + nwords]
        self.top += nwords
        return a

    def bf16(self, nelem):
        nwords = (nelem + 1) // 2
        a = self.f32(nwords).bitcast(BF16)
        return a[:, 0:nelem]


ARENA_WORDS = 56280
XNW = 1 + SEQ + NS * (TS + 1)
SOFF = 1 + SEQ
TILES = [(0, 512), (512, 512), (1024, 512), (1536, 512), (2048, 64)]


class Builder:
    def __init__(self, stages):
        self.stages = stages
        self.kb = KB()
        kb = self.kb
        nc = kb.nc
        self.nc = nc
        es = kb.es
        d = {}

        def din(name, shape):
            d[name] = nc.dram_tensor(name, list(shape), F32, kind="ExternalInput").ap()

        def dout(name, shape):
            d[name] = nc.dram_tensor(name, list(shape), F32, kind="ExternalOutput").ap()

        din("xT", (D, NT))
        din("vecs", (128, NVEC * 8))
        for nm in ("ffa_wg", "ffa_wu", "ffb_wg", "ffb_wu"):
            din(nm, (2, D, DFF))
        for nm in ("ffa_wd", "ffb_wd"):
            din(nm, (2, DFF, D))
        for nm in ("rw_wr", "rw_wk", "rw_wv", "rw_wo"):
            din(nm, (1, D, D))
        din("rw_w1", (1, D, 64)); din("rw_w2", (1, 64, D))
        din("rw_a1", (1, D, 64)); din("rw_a2", (1, 64, D))
        din("rw_g1", (1, D, 160)); din("rw_g2", (1, 160, D))
        din("ml_w_in", (1, D, 3088)); din("ml_w_out", (1, D, D))
        din("mlv", (64, 8, 10)); din("bif", (8, 2)); din("ml_m0T", (8, NS))
        din("ml_C0T", (NS, 8, 64, 129)); din("ml_convT", (8, 2, 64, NS, 3))
        din("shiftT", (128, NCH, NS))
        din("rw_S0T", (NS, 16, 64, 64))
        dout("yT", (D, NT))
        dout("o_ml_m", (8, 17)); dout("o_ml_Cp", (8, 64, 129)); dout("o_ml_Cs", (NS, 8, 64, 129))
        dout("o_ml_conv", (64, 8, 2, 17, 3))
        dout("o_shift", (128, NCH, 17))
        dout("o_rw_Sp", (16, 64, 64))
        dout("o_rw_Ss", (NS, 16, 64, 64))
        self.d = d
        self.out_names = [k for k in d if k == "yT" or k.startswith("o_")]

        arena_t = es.enter_context(nc.sbuf_tensor("arena", [128, ARENA_WORDS], F32))
        self.ar = Arena(arena_t, ARENA_WORDS)
        self.psall = es.enter_context(nc.psum_tensor("psall", [128, 8, 512], F32))
        self.ps = [self.psall[:, i, :] for i in range(8)]
        self.tps = [Tok("ps%d" % i, excl=True) for i in range(8)]
        self.bank_rr = 0
        self.bank_pe = {}

        ar = self.ar
        self.X = ar.f32(NCH * NT).rearrange("p (c n) -> p c n", c=NCH)
        self.tX = TokMap()
        self.VEC = ar.f32(NVEC * 8)
        self.tVEC = Tok("vec")
        self.ONES = ar.bf16(128)
        self.tONES = Tok("ones")
        self.XNraw = ar.bf16(NCH * XNW)
        self.XN = self.XNraw[:, 0:NCH * NT].rearrange("p (c n) -> p c n", c=NCH)
        self.XNS = self.XNraw.rearrange("p (c n) -> p c n", c=NCH)
        self.tXN = TokMap()


    BANK_GROUPS = {"A": (0, 1, 2, 3), "B": (4, 5), "C": (6, 7), "C2": (4, 5)}

    def bank(self, group=None):
        if group is None:
            b = self.bank_rr % 8
            self.bank_rr += 1
            return b
        if not hasattr(self, "_grr"):
            self._grr = {}
        k = self._grr.get(group, 0)
        self._grr[group] = k + 1
        g = self.BANK_GROUPS[group]
        return g[k % len(g)]

    def _dur(self, eng, ap):
        base = 0.22 + 0.0011 * ap.free_size()
        return base * (3.0 if eng == "pool" else 1.0)

    def act(self, out, in_, func, reads, writes, **kw):
        return self.kb.emit("act", lambda e: e.activation(out=out, in_=in_, func=func, **kw), reads, writes, dur=self._dur("act", out))

    def cp(self, eng, out, in_, reads, writes):
        if eng == "act":
            return self.kb.emit("act", lambda e: e.activation(out=out, in_=in_, func=AF.Copy), reads, writes, dur=self._dur("act", out))
        return self.kb.emit(eng, lambda e: e.tensor_copy(out=out, in_=in_), reads, writes, dur=self._dur(eng, out))

    def tt(self, eng, out, in0, in1, op, reads, writes):
        return self.kb.emit(eng, lambda e: e.tensor_tensor(out=out, in0=in0, in1=in1, op=op), reads, writes, dur=self._dur(eng, out))

    def ts(self, eng, out, in0, s1, s2, op0, op1, reads, writes):
        if s2 is None:
            return self.kb.emit(eng, lambda e: e.tensor_scalar(out=out, in0=in0, scalar1=s1, scalar2=None, op0=op0), reads, writes,
                                dur=self._dur(eng, out))
        return self.kb.emit(eng, lambda e: e.tensor_scalar(out=out, in0=in0, scalar1=s1, scalar2=s2, op0=op0, op1=op1), reads, writes,
                            dur=self._dur(eng, out))

    def stt(self, out, in0, scalar, in1, op0, op1, reads, writes):
        return self.kb.emit("dve", lambda e: e.scalar_tensor_tensor(out=out, in0=in0, scalar=scalar, in1=in1, op0=op0, op1=op1),
                            reads, writes, dur=self._dur("dve", out))

    def _pe_rows(self, lhsT, writes):
        K = lhsT.partition_size()
        base = lhsT.base_partition()
        tile = 32 if K <= 32 else (64 if K <= 64 else 128)
        lo, hi = (base // tile) * tile, (base // tile) * tile + tile
        if tile == 128:
            lo, hi = 0, 128
        after = None
        for t in writes:
            for b in range(8):
                if t is self.tps[b]:
                    prev = self.bank_pe.get(b)
                    if prev is not None and prev[2] is not None and (prev[1] <= lo or hi <= prev[0]):
                        after = prev[2]
                    self.bank_pe[b] = [lo, hi, None]
        return tile < 128, after

    def _pe_done(self, writes, cid):
        for t in writes:
            for b in range(8):
                if t is self.tps[b] and self.bank_pe.get(b) is not None:
                    self.bank_pe[b][2] = cid

    def mm(self, out, lhsT, rhs, start, stop, reads, writes, signal=None):
        if signal is None:
            signal = stop
        partial, after = self._pe_rows(lhsT, writes)
        if partial:
            signal = True
        dur = 0.07 + 0.00045 * rhs.free_size()
        cid = self.kb.emit("pe", lambda e: e.matmul(out, lhsT, rhs, start=start, stop=stop), reads, writes, signal=signal,
                           dur=dur, after=after)
        self._pe_done(writes, cid)
        return cid

    def tr(self, out, in_, ident, reads, writes):
        partial, after = self._pe_rows(in_, writes)
        cid = self.kb.emit("pe", lambda e: e.transpose(out, in_, ident), reads, writes, dur=0.12, after=after)
        self._pe_done(writes, cid)
        return cid

    def memset(self, eng, ap, val, writes):
        return self.kb.emit(eng, lambda e: e.memset(ap, val), (), writes)

    def scan(self, out, d0, d1, init, op0, op1, reads, writes):
        return self.kb.emit("dve", lambda e: e.tensor_tensor_scan(out=out, data0=d0, data1=d1, initial=init, op0=op0, op1=op1), reads, writes,
                            dur=0.22 + 0.0022 * out.free_size())

    def recip(self, out, in_, reads, writes):
        return self.kb.emit("dve", lambda e: e.reciprocal(out=out, in_=in_), reads, writes, dur=0.22 + 0.008 * out.free_size())

    def reduce(self, out, in_, op, reads, writes, axis=None):
        axis = AX.X if axis is None else axis
        return self.kb.emit("dve", lambda e: e.tensor_reduce(out=out, in_=in_, axis=axis, op=op), reads, writes, dur=self._dur("dve", in_))

    def vcol(self, name, c):
        j = VID[name] * 8 + c
        return self.VEC[:, j:j + 1]

    def load_inputs(self):
        kb, d = self.kb, self.d
        kb.dma("sp", self.VEC, d["vecs"][:, :], writes=[self.tVEC])
        for c in range(NCH):
            for ti, (t0, n) in enumerate(TILES):
                kb.dma("sp", self.X[:, c, t0:t0 + n], d["xT"][c * 128:(c + 1) * 128, t0:t0 + n],
                       writes=[self.tX[c, ti]])
        kb.emit("dve", lambda e: e.memset(self.ONES, 1.0), writes=[self.tONES])

    def _full_bank(self, b):
        self.bank_pe[b] = [0, 128, ("pe", 0)]

    def rmsnorm_tile(self, ti, gname, out_fn, scratch):
        kb = self.kb
        t0, n = TILES[ti]
        SQ, tSQ, LN, tLN, RS, tRS, bank = scratch
        ps = self.ps[bank][:, :n]
        self._full_bank(bank)
        for c in range(NCH):
            s = c % 2
            kb.emit("act", lambda e, c=c, s=s: e.activation(out=SQ[s][:, :n], in_=self.X[:, c, t0:t0 + n], func=AF.Square),
                    reads=[self.tX[c, ti]], writes=[tSQ[s]])
            kb.emit("pe", lambda e, c=c, s=s: e.matmul(ps, self.ONES, SQ[s][:, :n], start=(c == 0), stop=(c == NCH - 1)),
                    reads=[tSQ[s], self.tONES], writes=[self.tps[bank]], signal=True)
        kb.emit("act", lambda e: e.activation(out=LN[:, :n], in_=ps, func=AF.Ln, scale=1.0 / D, bias=self.EPSC),
                reads=[self.tps[bank], self.tCONST], writes=[tLN])
        kb.emit("act", lambda e: e.activation(out=RS[:, :n], in_=LN[:, :n], func=AF.Exp, scale=-0.5),
                reads=[tLN], writes=[tRS])
        for c in range(NCH):
            out_fn(c, RS[:, :n], tRS)

    def consts(self):
        kb, ar = self.kb, self.ar
        self.CONST = ar.f32(8)
        self.tCONST = Tok("const")
        self.EPSC = self.CONST[:, 0:1]
        self.ONEC = self.CONST[:, 1:2]
        self.NHALFC = self.CONST[:, 2:3]
        self.GNEPSC = self.CONST[:, 3:4]
        for col, val in ((0, EPS), (1, 1.0), (2, -0.5), (3, 64e-5)):
            kb.emit("dve", lambda e, col=col, val=val: e.memset(self.CONST[:, col:col + 1], val), (), [self.tCONST])
        ONESF = ar.f32(128)
        tO = Tok("onesf")
        self.memset("pool", ONESF, 1.0, [tO])
        self.IDENTF = ar.f32(128)
        self.IDENTB = ar.bf16(128)
        self.BONES = ar.bf16(128)
        self.tMASK = Tok("masks")
        kb.emit("pool", lambda e: e.affine_select(out=self.IDENTF, in_=ONESF, pattern=[[-1, 128]], compare_op=ALU.is_equal,
                                                  fill=0.0, base=0, channel_multiplier=1), [tO], [self.tMASK])
        self.cp("pool", self.IDENTB, self.IDENTF, [self.tMASK], [self.tMASK])
        self.memset("pool", self.BONES, 0.0, [self.tMASK])
        self.memset("pool", self.BONES[0:64, 0:64], 1.0, [self.tMASK])
        self.memset("pool", self.BONES[64:128, 64:128], 1.0, [self.tMASK])
        MSU = ar.f32(64)
        MIU = ar.f32(64)
        self.MASKXT = ar.f32(64)
        kb.emit("pool", lambda e: e.affine_select(out=MSU[0:64, :], in_=ONESF[0:64, 0:64], pattern=[[1, 64]], compare_op=ALU.is_gt,
                                                  fill=0.0, base=0, channel_multiplier=-1), [tO], [self.tMASK])
        kb.emit("pool", lambda e: e.affine_select(out=MIU[0:64, :], in_=ONESF[0:64, 0:64], pattern=[[1, 64]], compare_op=ALU.is_ge,
                                                  fill=0.0, base=0, channel_multiplier=-1), [tO], [self.tMASK])
        kb.emit("pool", lambda e: e.affine_select(out=self.MASKXT[0:64, :], in_=ONESF[0:64, 0:64], pattern=[[-1, 64]], compare_op=ALU.is_gt,
                                                  fill=0.0, base=0, channel_multiplier=1), [tO], [self.tMASK])
        self.ts("pool", self.MASKXT[0:64, :], self.MASKXT[0:64, :], -1.0, None, ALU.mult, None, [self.tMASK], [self.tMASK])
        self.MIU = MIU
        self.MASKLL = ar.f32(2 * 4 * 64).rearrange("p (h b t) -> p h b t", h=2, b=4)
        for h in range(2):
            self.cp("pool", self.MASKLL[0:64, h, 0, :], MSU[0:64, :], [self.tMASK], [self.tMASK])
            self.cp("pool", self.MASKLL[0:64, h, 1, :], MIU[0:64, :], [self.tMASK], [self.tMASK])
            self.ts("pool", self.MASKLL[0:64, h, 2, :], MSU[0:64, :], -1.0, None, ALU.mult, None, [self.tMASK], [self.tMASK])
            self.cp("pool", self.MASKLL[0:64, h, 3, :], MIU[0:64, :], [self.tMASK], [self.tMASK])
        self.SM64 = ar.f32(256)
        self.SM4 = ar.f32(16)
        self.memset("pool", self.SM64, 1.0, [self.tMASK])
        self.memset("pool", self.SM64.rearrange("p (j l) -> p j l", l=64)[:, :, 0:1], 0.0, [self.tMASK])
        self.memset("pool", self.SM4, 1.0, [self.tMASK])
        self.memset("pool", self.SM4.rearrange("p (j l) -> p j l", l=4)[:, :, 0:1], 0.0, [self.tMASK])
        self.NEGW0 = ar.f32(8)
        j = VID["w0"] * 8
        self.ts("dve", self.NEGW0, self.VEC[:, j:j + 8], -1.0, None, ALU.mult, None, [self.tVEC], [self.tMASK])

    def rwkv(self):
        kb, d, ar = self.kb, self.d, self.ar
        m0 = ar.mark()
        XNS = self.XNS
        tXNS = TokMap()
        gname = "norm_mix0"
        mu = lambda i, K: self.vcol("mu%d" % i, K)

        HWA = ar.bf16(NT)
        HG1 = ar.bf16(NT)
        HG2 = ar.bf16(NT)
        tHWA, tHG = TokMap(), TokMap()
        W2A2 = ar.bf16(D)
        G2A = ar.bf16(D)
        G2B = ar.bf16(D)
        tW2 = Tok("w2a2g2")
        SHO = ar.f32(NCH * 17).rearrange("p (c j) -> p c j", c=NCH)
        tSHO = Tok("sho")
        SHI = ar.f32(NCH * NS).rearrange("p (c j) -> p c j", c=NCH)
        tSHI = Tok("shi")

        kb.dma("pool", W2A2[0:64, :], d["rw_w2"][0], writes=[tW2])
        kb.dma("pool", W2A2[64:128, :], d["rw_a2"][0], writes=[tW2])
        kb.dma("pool", G2A, d["rw_g2"][0, 0:128, :], writes=[tW2])
        self.memset("pool", G2B, 0.0, [tW2])
        self.memset("pool", HG2, 0.0, [tHG[0]])
        kb.dma("pool", G2B[0:32, :], d["rw_g2"][0, 128:160, :], writes=[tW2])
        kb.dma("sp", SHI, d["shiftT"], writes=[tSHI])

        for c in range(NCH):
            self.memset("pool", XNS[:, c, 0:1], 0.0, [tXNS[c, "init"]])
            sv = XNS[:, c, SOFF:SOFF + NS * 5].rearrange("p (j u) -> p j u", u=5)
            self.cp("pool", sv[:, :, 0], SHI[:, c, :], [tSHI], [tXNS[c, "init"]])

        def xn_aps(K, t0, n):
            if t0 < SEQ:
                return XNS[:, K, 1 + t0:1 + t0 + n], XNS[:, K, t0:t0 + n]
            j0 = (t0 - SEQ) // TS
            nj = n // TS
            sv = XNS[:, K, SOFF + 5 * j0:SOFF + 5 * (j0 + nj)].rearrange("p (j u) -> p j u", u=5)
            return sv[:, :, 1:5], sv[:, :, 0:4]

        def xn_toks(K, t0):
            ti = min(t0 // 512, 4)
            return [tXNS[K, ti], tXNS[K, max(ti - 1, 0)], tXNS[K, "init"]]

        def pview(ps_ap, t0, n):
            if t0 < SEQ:
                return ps_ap
            return ps_ap.rearrange("p (j t) -> p j t", t=TS)

        def mixproj(out, wa, wb, cols, t0, n, wtok, ptok):
            o = pview(out, t0, n)
            for K in range(NCH):
                xa, xb = xn_aps(K, t0, n)
                self.mm(o, wa[:, K, cols], xa, K == 0, False, [wtok] + xn_toks(K, t0), [ptok])
                self.mm(o, wb[:, K, cols], xb, False, K == NCH - 1, [wtok] + xn_toks(K, t0), [ptok])

        def scale_w(raw, wb, Mcols, mu_list, tok):
            for K in range(NCH):
                for (cs, mi) in mu_list:
                    self.ts("pool", wb[:, K, cs], raw[:, K, cs], mu(mi, K), None, ALU.mult, None, [tok, self.tVEC], [tok])
            self.tt("pool", raw, raw, wb, ALU.subtract, [tok], [tok])

        m1 = ar.mark()
        W1A = ar.bf16(NCH * 128).rearrange("p (k m) -> p k m", k=NCH)
        W1B = ar.bf16(NCH * 128).rearrange("p (k m) -> p k m", k=NCH)
        G1A = ar.bf16(NCH * 160).rearrange("p (k m) -> p k m", k=NCH)
        G1B = ar.bf16(NCH * 160).rearrange("p (k m) -> p k m", k=NCH)
        tW1, tG1 = Tok("w1a1"), Tok("g1")
        for K in range(NCH):
            kb.dma("pool", W1A[:, K, 0:64], d["rw_w1"][0, K * 128:(K + 1) * 128, :], writes=[tW1])
            kb.dma("pool", W1A[:, K, 64:128], d["rw_a1"][0, K * 128:(K + 1) * 128, :], writes=[tW1])
            kb.dma("pool", G1A[:, K, :], d["rw_g1"][0, K * 128:(K + 1) * 128, :], writes=[tG1])
        scale_w(W1A, W1B, 128, [(slice(0, 64), 1), (slice(64, 128), 4)], tW1)
        scale_w(G1A, G1B, 160, [(slice(0, 160), 5)], tG1)
        XNF = [ar.f32(512) for _ in range(2)]
        SQ = [ar.bf16(512) for _ in range(2)]
        LN = ar.f32(512)
        RS = ar.f32(512)
        tXNF, tSQ = TokMap(), TokMap()
        tLN, tRS = Tok("ln"), Tok("rs")
        rr = [0]
        for ti, (t0, n) in enumerate(TILES):
            def out_fn(c, rs, trs, ti=ti, t0=t0, n=n):
                s = rr[0] % 2
                rr[0] += 1
                xf = XNF[s][:, :n]
                self.stt(xf, self.X[:, c, t0:t0 + n], self.vcol(gname, c), rs, ALU.mult, ALU.mult,
                         [self.tX[c, ti], trs, self.tVEC], [tXNF[s]])
                if t0 < SEQ:
                    self.cp("act", XNS[:, c, 1 + t0:1 + t0 + n], xf, [tXNF[s]], [tXNS[c, ti]])
                    if t0 + n == SEQ:
                        self.cp("pool", SHO[:, c, 0:1], xf[:, n - 1:n], [tXNF[s]], [tSHO])
                else:
                    sv = XNS[:, c, SOFF:SOFF + NS * 5].rearrange("p (j u) -> p j u", u=5)
                    xv = xf.rearrange("p (j t) -> p j t", t=TS)
                    self.cp("act", sv[:, :, 1:5], xv, [tXNF[s]], [tXNS[c, ti]])
                    self.cp("pool", SHO[:, c, 1:17], xv[:, :, 3], [tXNF[s]], [tSHO])
            self.rmsnorm_tile(ti, gname, out_fn, (SQ, tSQ, LN, tLN, RS, tRS, self.bank()))
            b1, b2, b3 = self.bank(), self.bank(), self.bank()
            mixproj(self.ps[b1][:, :n], W1A, W1B, slice(0, 128), t0, n, tW1, self.tps[b1])
            mixproj(self.ps[b2][:, :n], G1A, G1B, slice(0, 128), t0, n, tG1, self.tps[b2])
            mixproj(self.ps[b3][0:32, :n], G1A, G1B, slice(128, 160), t0, n, tG1, self.tps[b3])
            self.act(HWA[0:64, t0:t0 + n], self.ps[b1][0:64, :n], AF.Tanh, [self.tps[b1]], [tHWA[ti]])
            self.cp("act", HWA[64:128, t0:t0 + n], self.ps[b1][64:128, :n], [self.tps[b1]], [tHWA[ti]])
            self.act(HG1[:, t0:t0 + n], self.ps[b2][:, :n], AF.Sigmoid, [self.tps[b2]], [tHG[ti]])
            self.act(HG2[0:32, t0:t0 + n], self.ps[b3][0:32, :n], AF.Sigmoid, [self.tps[b3]], [tHG[ti]])
        ar.release(m1)
        kb.barrier()
        kb.dma("sp", d["o_shift"], SHO, reads=[tSHO])

        WN = 256

        def f32t():
            return ar.f32(WN)

        def bf16t():
            return ar.bf16(WN)
        WA2 = [{nm: ar.bf16(NCH * 128).rearrange("p (k m) -> p k m", k=NCH) for nm in "rkv"} for _ in range(2)]
        WB2 = [{nm: ar.bf16(NCH * 128).rearrange("p (k m) -> p k m", k=NCH) for nm in "rkv"} for _ in range(2)]
        WO2 = [ar.bf16(D) for _ in range(2)]
        tWc2 = [{nm: Tok("w%s%d" % (nm, i)) for nm in "rkv"} for i in range(2)]
        tWO = [Tok("wo0"), Tok("wo1")]
        Rf, Kf, Vf, A_, EW, CUM, EM, KK, KF, Bv, T1, T2 = [f32t() for _ in range(12)]
        EQ = EW
        Vb, SQb, KTb, BTb, YG = [bf16t() for _ in range(5)]
        RKR = SQb
        S3 = []
        for _ in range(3):
            S3.append(dict(
                KR=ar.bf16(2 * WN).rearrange("p (a n) -> p a n", a=2),
                KTt=ar.bf16(4 * 128).rearrange("p (j m) -> p j m", j=4),
                BTt=ar.bf16(4 * 128).rearrange("p (j m) -> p j m", j=4),
                VTt=ar.bf16(4 * 128).rearrange("p (j m) -> p j m", j=4),
                EP=f32t(), BONUS=f32t(), Gf=f32t(), LLs=ar.bf16(4 * 2 * 4 * 64)))
        S2 = []
        for _ in range(2):
            S2.append(dict(XTs=ar.bf16(4 * 2 * 64), PW=[ar.bf16(8 * 2 * 64) for _ in range(2)]))
        PT = [ar.bf16(8 * 64) for _ in range(2)]
        Gs = ar.bf16(128)
        NU = ar.bf16(128)
        YT = ar.f32(512)
        SQ2 = ar.f32(512)
        YF = SQ2[:, 0:256]
        STAT = ar.f32(32)
        H = ar.f32(64)
        H0d = ar.f32(64)
        Hb = ar.bf16(64)
        HS = ar.f32(4 * 64).rearrange("p (j v) -> p j v", j=4)
        HSb = ar.bf16(4 * 64).rearrange("p (j v) -> p j v", j=4)
        T = TokMap()

        main_tiles = [(t0, 256, 64) for t0 in range(0, SEQ, 256)] + [(SEQ + 16 * q, 16, 4) for q in range(4)]
        import os as _os
        if "KDEBUG" in _os.environ:
            print("rwkv arena top", ar.top, "of", ar.n)
        if "RW_TILES" in _os.environ:
            main_tiles = [main_tiles[int(i)] for i in _os.environ["RW_TILES"].split(",")]
        NCc = int(_os.environ.get("RW_NC", NCH))
        units = []
        for c in range(NCc):
            for k_, (t0, n, L) in enumerate(main_tiles):
                u = len(units)
                units.append(dict(u=u, c=c, t0=t0, n=n, L=L, first=(k_ == 0), last=(k_ == len(main_tiles) - 1),
                                  lastprompt=(t0 < SEQ and (k_ + 1 == len(main_tiles) or main_tiles[k_ + 1][0] >= SEQ))))

        def load_weights(c):
            ccols = slice(c * 128, (c + 1) * 128)
            WA, WB, tWc = WA2[c % 2], WB2[c % 2], tWc2[c % 2]
            for nm, key, mi in (("r", "rw_wr", 0), ("k", "rw_wk", 2), ("v", "rw_wv", 3)):
                for K in range(NCH):
                    kb.dma("pool", WA[nm][:, K, :], d[key][0, K * 128:(K + 1) * 128, ccols], writes=[tWc[nm]])
                scale_w(WA[nm], WB[nm], 128, [(slice(0, 128), mi)], tWc[nm])

        def stageA(U):
            u, c, t0, n, L = U["u"], U["c"], U["t0"], U["n"], U["L"]
            s3, s2 = S3[u % 3], S2[u % 2]
            k3, k2 = u % 3, u % 2
            sample = t0 >= SEQ
            NCk = 4
            ti5 = min(t0 // 512, 4)
            tsl = slice(t0, t0 + n)
            cs = lambda j: slice(j * L, (j + 1) * L)
            ccols = slice(c * 128, (c + 1) * 128)
            WA, WB, tWc = WA2[c % 2], WB2[c % 2], tWc2[c % 2]
            if U["first"]:
                if c == 0:
                    load_weights(0)
                if c + 1 < NCc:
                    load_weights(c + 1)
                yield
            KR, KTt, BTt, VTt, EP, BONUS, Gf = s3["KR"], s3["KTt"], s3["BTt"], s3["VTt"], s3["EP"], s3["BONUS"], s3["Gf"]
            tKR, tKTt, tBTt, tVTt, tEP, tBONUS, tGf, tLL = (T["KR", k3], T["KTt", k3], T["BTt", k3], T["VTt", k3], T["EP", k3],
                                                          T["BONUS", k3], T["Gf", k3], T["LLs", k3])
            tXT = T["XTs", k2]
            bA, bB, bC, bD = self.bank("A"), self.bank("A"), self.bank("A"), self.bank("A")
            PR, PK = self.ps[bA][:, 0:n], self.ps[bA][:, 256:256 + n]
            PV, PGt = self.ps[bB][:, 0:n], self.ps[bB][:, 256:256 + n]
            PWL, PAL = self.ps[bC][:, 0:n], self.ps[bC][:, 256:256 + n]
            PKK, PSm = self.ps[bD][:, 0:n], self.ps[bD][:, 256:256 + n]
            for K0 in range(0, NCH, 2):
                pass
            mixproj(PR, WA["r"], WB["r"], slice(0, 128), t0, n, tWc["r"], self.tps[bA])
            yield
            mixproj(PK, WA["k"], WB["k"], slice(0, 128), t0, n, tWc["k"], self.tps[bA])
            yield
            mixproj(PV, WA["v"], WB["v"], slice(0, 128), t0, n, tWc["v"], self.tps[bB])
            self.mm(PGt, G2A[:, ccols], HG1[:, tsl], True, False, [tW2, tHG[ti5]], [self.tps[bB]])
            self.mm(PGt, G2B[:, ccols], HG2[:, tsl], False, True, [tW2, tHG[ti5], tHG[0]], [self.tps[bB]])
            self.mm(PWL, W2A2[0:64, ccols], HWA[0:64, tsl], True, True, [tW2, tHWA[ti5]], [self.tps[bC]])
            self.mm(PAL, W2A2[64:128, ccols], HWA[64:128, tsl], True, True, [tW2, tHWA[ti5]], [self.tps[bC]])
            yield
            w = lambda a: a[:, 0:n]
            tV = self.tVEC
            self.cp("act", w(Rf), PR, [self.tps[bA]], [T["Rf"]])
            self.cp("act", w(Kf), PK, [self.tps[bA]], [T["Kf"]])
            yield
            self.cp("act", w(Vf), PV, [self.tps[bB]], [T["Vf"]])
            self.cp("act", w(Gf), PGt, [self.tps[bB]], [tGf])
            self.cp("dve", w(Vb), w(Vf), [T["Vf"]], [T["Vb"]])
            yield
            self.act(w(A_), PAL, AF.Sigmoid, [self.tps[bC], tV], [T["A"]], bias=self.vcol("a0", c))
            self.act(w(T1), PWL, AF.Exp, [self.tps[bC], self.tMASK], [T["T1"]], scale=-1.0, bias=self.NEGW0[:, c:c + 1])
            self.ts("dve", w(KK), w(Kf), self.vcol("k_k", c), None, ALU.mult, None, [T["Kf"], tV], [T["KK"]])
            yield
            self.act(w(T1), w(T1), AF.Ln, [T["T1"], self.tCONST], [T["T1"]], bias=self.ONEC)
            self.act(w(SQb), w(KK), AF.Square, [T["KK"]], [T["SQb"]])
            self.mm(PKK, self.BONES, w(SQb), True, True, [T["SQb"], self.tMASK], [self.tps[bD]], signal=True)
            yield
            self.act(w(EW), w(T1), AF.Exp, [T["T1"], self.tCONST], [T["EW"]], scale=-1.0, bias=self.NHALFC)
            self.ts("dve", w(T1), w(A_), -1.0, self.vcol("k_a", c), ALU.add, ALU.mult, [T["A"], tV], [T["T1"]])
            self.stt(w(KF), w(T1), 1.0, w(Kf), ALU.add, ALU.mult, [T["T1"], T["Kf"]], [T["KF"]])
            yield
            SM = self.SM4[:, 0:n] if sample else self.SM64[:, 0:n]
            self.scan(w(CUM), SM, w(EW), 0.0, ALU.mult, ALU.subtract, [T["EW"], self.tMASK], [T["CUM"]])
            self.act(w(T2), PKK, AF.Sqrt, [self.tps[bD]], [T["T2"]])
            yield
            self.act(w(EP), w(CUM), AF.Exp, [T["CUM"]], [tEP])
            self.act(w(EM), w(CUM), AF.Exp, [T["CUM"]], [T["EM"]], scale=-1.0)
            self.ts("dve", w(T2), w(T2), 1e-12, None, ALU.max, None, [T["T2"]], [T["T2"]])
            self.recip(w(T2), w(T2), [T["T2"]], [T["T2"]])
            yield
            self.tt("dve", w(KK), w(KK), w(T2), ALU.mult, [T["KK"], T["T2"]], [T["KK"]])
            self.tt("dve", w(T2), w(CUM), w(EW), ALU.add, [T["CUM"], T["EW"], T["KK"]], [T["T2"]])
            self.act(w(EQ), w(T2), AF.Exp, [T["T2"]], [T["EW"]])
            yield
            self.stt(w(RKR), w(Rf), self.vcol("r_k", c), w(KF), ALU.mult, ALU.mult, [T["Rf"], T["KF"], tV], [T["SQb"]])
            self.mm(PSm, self.BONES, w(RKR), True, True, [T["SQb"], self.tMASK], [self.tps[bD]], signal=True)
            self.tt("dve", w(Bv), w(KK), w(A_), ALU.mult, [T["KK"], T["A"]], [T["Bv"]])
            self.tt("dve", KR[:, 1, 0:n], w(Rf), w(EP), ALU.mult, [T["Rf"], tEP], [tKR])
            yield
            self.tt("dve", KR[:, 0, 0:n], w(KK), w(EQ), ALU.mult, [T["KK"], T["EW"]], [tKR])
            self.tt("dve", w(KTb), w(KF), w(EM), ALU.mult, [T["KF"], T["EM"]], [T["KTb"]])
            self.tt("dve", w(BTb), w(Bv), w(EM), ALU.mult, [T["Bv"], T["EM"]], [T["BTb"]])
            self.tt("dve", w(BONUS), PSm, w(Vf), ALU.mult, [self.tps[bD], T["Vf"]], [tBONUS])
            yield
            bT = self.bank("A")
            PTr = self.ps[bT].bitcast(BF16)
            for (src, ts_, off) in ((KTb, "KTb", 0), (BTb, "BTb", 1)):
                for j in range(NCk):
                    self.tr(PTr[0:L, off * 512 + j * 128:off * 512 + (j + 1) * 128], src[:, cs(j)], self.IDENTB,
                            [T[ts_], self.tMASK], [self.tps[bT]])
            self.cp("act", KTt[0:L, :, :], PTr[0:L, 0:512].rearrange("p (j m) -> p j m", j=4), [self.tps[bT]], [tKTt])
            self.cp("act", BTt[0:L, :, :], PTr[0:L, 512:1024].rearrange("p (j m) -> p j m", j=4), [self.tps[bT]], [tBTt])
            yield
            bT2 = self.bank("A")
            PTr2 = self.ps[bT2].bitcast(BF16)
            for j in range(NCk):
                self.tr(PTr2[0:L, j * 128:(j + 1) * 128], Vb[:, cs(j)], self.IDENTB, [T["Vb"], self.tMASK], [self.tps[bT2]])
            self.cp("act", VTt[0:L, :, :], PTr2[0:L, 0:512].rearrange("p (j m) -> p j m", j=4), [self.tps[bT2]], [tVTt])
            yield
            LLv = s3["LLs"][:, 0:4 * 2 * 4 * L].rearrange("p (j h b t) -> p j h b t", j=4, h=2, b=4)
            XTv = s2["XTs"][:, 0:4 * 2 * L].rearrange("p (j h t) -> p j h t", j=4, h=2)
            mk = self.MASKLL[0:L, :, :, 0:L]
            for h in range(2):
                hs = slice(64 * h, 64 * h + 64)
                bX = self.bank("A")
                PXT = self.ps[bX][:, 0:4 * L].rearrange("p (j t) -> p j t", j=4)
                for g0 in (0, 2):
                    bL = self.bank("A")
                    PLL = self.ps[bL][:, 0:2 * 4 * L].rearrange("p (j b t) -> p j b t", j=2, b=4)
                    for jj in range(2):
                        j = g0 + jj
                        self.mm(PLL[0:L, jj, 0:2, :], KTb[hs, cs(j)], KR[hs, :, cs(j)], True, True, [T["KTb"], tKR], [self.tps[bL]])
                        self.mm(PLL[0:L, jj, 2:4, :], BTb[hs, cs(j)], KR[hs, :, cs(j)], True, True, [T["BTb"], tKR], [self.tps[bL]])
                        self.mm(PXT[0:L, j, :], KR[hs, 0, cs(j)], BTb[hs, cs(j)], True, True, [T["BTb"], tKR], [self.tps[bX]])
                    self.tt("dve", LLv[0:L, g0:g0 + 2, h], PLL[0:L], mk, ALU.mult, [self.tps[bL], self.tMASK], [tLL])
                    yield
                self.tt("dve", XTv[0:L, :, h, :], PXT[0:L], self.MASKXT[0:L, 0:L].unsqueeze(1).to_broadcast([L, 4, L]), ALU.mult,
                        [self.tps[bX], self.tMASK], [tXT])
                yield

        def stageB(U):
            u, L = U["u"], U["L"]
            s3, s2 = S3[u % 3], S2[u % 2]
            k3, k2 = u % 3, u % 2
            NCk, NM = 4, 8
            LLv = s3["LLs"][:, 0:4 * 2 * 4 * L].rearrange("p (j h b t) -> p j h b t", j=4, h=2, b=4)
            XTv = s2["XTs"][:, 0:4 * 2 * L].rearrange("p (j h t) -> p j h t", j=4, h=2)
            tLL, tXT = T["LLs", k3], T["XTs", k2]
            PWv = [p[:, 0:NM * 2 * L].rearrange("p (i a t) -> p i a t", i=NM, a=2) for p in s2["PW"]]
            PTv = [p[:, 0:NM * L].rearrange("p (i t) -> p i t", i=NM) for p in PT]
            tPW = [T["PW", k2, 0], T["PW", k2, 1]]
            PW4 = PWv[0].rearrange("p (j h) a t -> p j h a t", h=2)
            self.cp("dve", PW4[0:L, :, :, 0, :], LLv[0:L, :, :, 2, :], [tLL], [tPW[0]])
            self.cp("pool", PWv[0][0:L, :, 1, :], self.IDENTB[0:L, 0:L].unsqueeze(1).to_broadcast([L, NM, L]), [self.tMASK], [tPW[0]])
            self.cp("act", PTv[0][0:L].rearrange("p (j h) t -> p j h t", h=2), XTv[0:L], [tXT], [T["PT", 0]])
            yield
            nlev = 6 if L == 64 else 2
            cur = 0
            mpb = 4 if L == 64 else 8
            for lev in range(nlev):
                nxt = 1 - cur
                last = lev == nlev - 1
                for i0 in range(0, NM, mpb):
                    bI = self.bank("B")
                    PA = self.ps[bI][:, 0:mpb * 2 * L].rearrange("p (i a t) -> p i a t", i=mpb, a=2)
                    for ii in range(mpb):
                        i = i0 + ii
                        self.mm(PA[0:L, ii], PTv[cur][0:L, i, :], PWv[cur][0:L, i], True, True,
                                [T["PT", cur], tPW[cur]], [self.tps[bI]], signal=(ii == mpb - 1))
                    if not last:
                        self.cp("act", PWv[nxt][0:L, i0:i0 + mpb, 0, :], PA[0:L, :, 0, :], [self.tps[bI]], [tPW[nxt]])
                    self.tt("dve", PWv[nxt][0:L, i0:i0 + mpb, 1, :], PA[0:L, :, 1, :], PWv[cur][0:L, i0:i0 + mpb, 1, :], ALU.add,
                            [self.tps[bI], tPW[cur]], [tPW[nxt]])
                    yield
                if not last:
                    bJ = self.bank("B")
                    PB = self.ps[bJ][:, 0:NM * L].rearrange("p (i t) -> p i t", i=NM)
                    for i in range(NM):
                        self.mm(PB[0:L, i, :], PWv[cur][0:L, i, 0, :], PTv[cur][0:L, i, :], True, True,
                                [T["PT", cur], tPW[cur]], [self.tps[bJ]], signal=(i == NM - 1))
                    self.cp("act", PTv[nxt][0:L], PB[0:L], [self.tps[bJ]], [T["PT", nxt]])
                    yield
                cur = nxt
            U["Wv"] = PWv[cur]
            U["tWv"] = tPW[cur]

        def stageC(U):
            u, c, t0, n, L = U["u"], U["c"], U["t0"], U["n"], U["L"]
            s3 = S3[u % 3]
            k3 = u % 3
            sample = t0 >= SEQ
            NCk = 4
            ti5 = min(t0 // 512, 4)
            tsl = slice(t0, t0 + n)
            cs = lambda j: slice(j * L, (j + 1) * L)
            w = lambda a: a[:, 0:n]
            tV = self.tVEC
            KR, KTt, BTt, VTt, EP, BONUS, Gf = s3["KR"], s3["KTt"], s3["BTt"], s3["VTt"], s3["EP"], s3["BONUS"], s3["Gf"]
            tKR, tKTt, tBTt, tVTt, tEP, tBONUS, tGf, tLL = (T["KR", k3], T["KTt", k3], T["BTt", k3], T["VTt", k3], T["EP", k3],
                                                          T["BONUS", k3], T["Gf", k3], T["LLs", k3])
            LLv = s3["LLs"][:, 0:4 * 2 * 4 * L].rearrange("p (j h b t) -> p j h b t", j=4, h=2, b=4)
            Wv, tWv = U["Wv"], U["tWv"]
            WO = WO2[c % 2]
            if U["first"]:
                kb.dma("pool", WO, d["rw_wo"][0, c * 128:(c + 1) * 128, :], writes=[tWO[c % 2]])
                self.memset("pool", H, 0.0, [T["H"]])
                self.memset("pool", Hb, 0.0, [T["Hb"]])
            if sample:
                q = (t0 - SEQ) // 16
                for jj in range(4):
                    kb.dma("sp", HS[:, jj, :], d["rw_S0T"][4 * q + jj, 2 * c:2 * c + 2].rearrange("h k v -> (h k) v"),
                           writes=[T["HS", jj]])
                    self.cp("act", HSb[:, jj, :], HS[:, jj, :], [T["HS", jj]], [T["HSb", jj]])
                yield
            YTv = YT[:, 0:4 * 128].rearrange("p (j m) -> p j m", j=4)
            for j in range(NCk):
                if sample:
                    Hc, Hbc, Hdc = HS[:, j, :], HSb[:, j, :], H0d
                    tH, tHb, tHd = T["HS", j], T["HSb", j], T["H0d"]
                else:
                    Hc, Hbc, Hdc = H, Hb, H0d
                    tH, tHb, tHd = T["H"], T["Hb"], T["H0d"]
                DL = EP[:, (j + 1) * L - 1:(j + 1) * L]
                bS = self.bank("C")
                PG = self.ps[bS][0:L, 0:128]
                PU = self.ps[bS][0:L, 128:256]
                PY = self.ps[bS][0:L, 256:384]
                bH = self.bank("C")
                PH = self.ps[bH][:, 0:64]
                tS = self.tps[bS]
                tSH = self.tps[bH]
                for h in range(2):
                    hs = slice(64 * h, 64 * h + 64)
                    self.mm(PG[:, hs], LLv[0:L, j, h, 0, :], VTt[0:L, j, hs], True, False, [tLL, tVTt], [tS])
                    self.mm(PG[:, hs], KR[hs, 0, cs(j)], Hbc[hs, :], False, True, [tKR, tHb], [tS], signal=True)
                self.act(Hdc, Hc, AF.Identity, [tH, tEP], [tHd], scale=DL)
                yield
                self.cp("act", Gs[0:L, :], PG, [tS], [T["Gs"]])
                yield
                for h in range(2):
                    hs = slice(64 * h, 64 * h + 64)
                    self.mm(PU[:, hs], Wv[0:L, 2 * j + h, 1, :], Gs[0:L, hs], True, True, [tWv, T["Gs"]], [tS], signal=True)
                yield
                self.act(NU[0:L, :], PU, AF.Identity, [tS], [T["NU"]], scale=-1.0)
                yield
                for h in range(2):
                    hs = slice(64 * h, 64 * h + 64)
                    self.mm(PH[hs, :], KTt[0:L, j, hs], VTt[0:L, j, hs], True, False, [tKTt, tVTt], [tSH])
                    self.mm(PH[hs, :], BTt[0:L, j, hs], NU[0:L, hs], False, True, [tBTt, T["NU"]], [tSH], signal=True)
                for h in range(2):
                    hs = slice(64 * h, 64 * h + 64)
                    self.mm(PY[:, hs], LLv[0:L, j, h, 1, :], VTt[0:L, j, hs], True, False, [tLL, tVTt], [tS])
                    self.mm(PY[:, hs], LLv[0:L, j, h, 3, :], NU[0:L, hs], False, False, [tLL, T["NU"]], [tS])
                    self.mm(PY[:, hs], KR[hs, 1, cs(j)], Hbc[hs, :], False, True, [tKR, tHb], [tS], signal=True)
                yield
                self.stt(Hbc, PH, DL, Hdc, ALU.mult, ALU.add, [tSH, tEP, tHd], [tHb])
                self.stt(Hc, PH, DL, Hdc, ALU.mult, ALU.add, [tSH, tEP, tHd], [tH])
                self.cp("act", YTv[0:L, j, :], PY, [tS], [T["YT"]])
                yield
            if sample:
                for jj in range(4):
                    kb.dma("sp", d["o_rw_Ss"][4 * q + jj, 2 * c:2 * c + 2].rearrange("h k v -> (h k) v"), HS[:, jj, :],
                           reads=[T["HS", jj]])
            if U["lastprompt"]:
                kb.dma("sp", d["o_rw_Sp"][2 * c:2 * c + 2].rearrange("h k v -> (h k) v"), H, reads=[T["H"]])
            G8 = 8
            YT3 = YT[:, 0:512].rearrange("p (g v) -> p g v", g=G8)
            SQ3 = SQ2[:, 0:512].rearrange("p (g v) -> p g v", g=G8)
            SUMv, VARv, RSTv = STAT[:, 0:8], STAT[:, 8:16], STAT[:, 16:24]
            self.reduce(SUMv[0:L, :], YT3[0:L], ALU.add, [T["YT"]], [T["SUM"]])
            self.ts("dve", SUMv[0:L, :], SUMv[0:L, :], 1.0 / 64, None, ALU.mult, None, [T["SUM"]], [T["SUM"]])
            yield
            self.tt("dve", YT3[0:L], YT3[0:L], SUMv[0:L, :].unsqueeze(2).to_broadcast([L, G8, 64]), ALU.subtract,
                    [T["YT"], T["SUM"]], [T["YT"]])
            yield
            self.act(SQ2[0:L, 0:512], YT[0:L, 0:512], AF.Square, [T["YT"]], [T["SQ2"]])
            yield
            self.reduce(VARv[0:L, :], SQ3[0:L], ALU.add, [T["SQ2"]], [T["VAR"]])
            yield
            self.act(RSTv[0:L, :], VARv[0:L, :], AF.Ln, [T["VAR"], self.tCONST], [T["RST"]], scale=1.0 / 64, bias=self.GNEPSC[0:L, :])
            yield
            self.act(RSTv[0:L, :], RSTv[0:L, :], AF.Exp, [T["RST"]], [T["RST"]], scale=-0.5)
            yield
            self.tt("dve", YT3[0:L], YT3[0:L], RSTv[0:L, :].unsqueeze(2).to_broadcast([L, G8, 64]), ALU.mult,
                    [T["YT"], T["RST"]], [T["YT"]])
            yield
            bY = self.bank("C")
            PYF = self.ps[bY][:, 0:n]
            for j in range(NCk):
                self.tr(PYF[:, cs(j)], YTv[0:L, j, :], self.IDENTF[0:L, 0:L], [T["YT"], self.tMASK], [self.tps[bY]])
            yield
            self.act(w(YF), PYF, AF.Identity, [self.tps[bY], tV], [T["SQ2"]], scale=self.vcol("gn_w", c), bias=self.vcol("gn_b", c))
            yield
            self.tt("pool", w(YF), w(YF), w(BONUS), ALU.add, [T["SQ2"], tBONUS], [T["SQ2"]])
            yield
            self.tt("dve", w(YG), w(YF), w(Gf), ALU.mult, [T["SQ2"], tGf], [T["YG"]])
            yield
            for dc0 in range(0, NCH, 2):
                bO = self.bank("C")
                for k2_ in range(2):
                    dc = dc0 + k2_
                    PO = self.ps[bO][:, 256 * k2_:256 * k2_ + n]
                    self.mm(PO, WO[:, dc * 128:(dc + 1) * 128], w(YG), True, True, [tWO[c % 2], T["YG"]], [self.tps[bO]], signal=True)
                for k2_ in range(2):
                    dc = dc0 + k2_
                    PO = self.ps[bO][:, 256 * k2_:256 * k2_ + n]
                    self.tt("dve", self.X[:, dc, tsl], PO, self.X[:, dc, tsl], ALU.add, [self.tps[bO], self.tX[dc, ti5]], [self.tX[dc, ti5]])
                yield

        STEPS = [int(v) for v in _os.environ.get("RW_STEP", "1,1,1").split(",")]

        def drain(gens):
            gens = [(g, k) for g, k in zip(gens, STEPS) if g is not None]
            while gens:
                for item in list(gens):
                    g, k = item
                    try:
                        for _ in range(k):
                            next(g)
                    except StopIteration:
                        gens.remove(item)

        PIPE = int(_os.environ.get("RW_PIPE", "1"))
        NU_ = len(units)
        SCHED = int(_os.environ.get("K_SCHED", "1"))
        if SCHED:
            self.bank_pe = {}
            kb.begin_defer()
        if PIPE:
            for s in range(NU_ + 2):
                gA = stageA(units[s]) if s < NU_ else None
                gB = stageB(units[s - 1]) if 0 <= s - 1 < NU_ else None
                gC = stageC(units[s - 2]) if 0 <= s - 2 < NU_ else None
                drain([gC, gB, gA])
        else:
            for U in units:
                drain([stageA(U)])
                drain([stageB(U)])
                drain([stageC(U)])
        if SCHED:
            kb.end_defer()
            self.bank_pe = {}
        ar.release(m0)
        kb.barrier()

    def mlstm(self):
        kb, d, ar = self.kb, self.d, self.ar
        m0 = ar.mark()
        gname = "norm_mix1"
        XN, tXN = self.XN, self.tXN
        T = TokMap()
        NEG = -1.0e30
        EKA = ar.f32(NT)
        EQA = ar.f32(NT)
        EMTT = ar.f32(48 * 8).rearrange("p (j h) -> p j h", h=8)
        self.EMTP = EMTT[:, 0:32, :]
        EMTS = EMTT[:, 32:48, :]
        self.BBP = ar.f32(2)
        self.ABP = ar.f32(2)
        MLV = ar.f32(8 * 10).rearrange("p (h k) -> p h k", h=8)
        BIF = ar.f32(4)
        M0T = ar.f32(NS)
        MOUT = ar.f32(17)
        SEL = ar.f32(8 * 64).rearrange("p (h m) -> p h m", h=8)
        CONVO = ar.f32(8 * 2 * 17 * 3).rearrange("p (h w s k) -> p h w s k", h=8, w=2, s=17)
        tEK, tEQ = TokMap(), TokMap()
        kb.dma("sp", MLV[0:64], d["mlv"], writes=[T["MLV"]])
        kb.dma("sp", BIF[0:8, 0:2], d["bif"], writes=[T["BIF"]])
        kb.dma("sp", M0T[0:8, :], d["ml_m0T"], writes=[T["M0T"]])
        self.ts("dve", BIF[0:8, 2:3], BIF[0:8, 1:2], -1.0, None, ALU.mult, None, [T["BIF"]], [T["BIF"]])
        self.cp("pool", SEL[0:8], self.IDENTF[0:8, 0:8].unsqueeze(2).to_broadcast([8, 8, 64]), [self.tMASK], [T["SEL"]])

        m1 = ar.mark()
        WIF = ar.bf16(NCH * 16).rearrange("p (k m) -> p k m", k=NCH)
        for K in range(NCH):
            kb.dma("pool", WIF[:, K, :], d["ml_w_in"][0, K * 128:(K + 1) * 128, 3072:3088], writes=[T["WIF"]])
        SQ = [ar.bf16(512) for _ in range(2)]
        LN = ar.f32(512)
        RS = ar.f32(512)
        tSQ = TokMap()
        tLN, tRS = Tok("ln"), Tok("rs")
        LI, LF, BB, AA = [ar.f32(512) for _ in range(4)]
        ABX = ar.f32(513)
        D0, D1, TMPg, MTg, EMg = [ar.f32(512) for _ in range(5)]
        for ti, (t0, n) in enumerate(TILES):
            sample = t0 >= SEQ
            Lc = 4 if sample else 64
            nck = n // Lc

            def out_fn(c, rs, trs, ti=ti, t0=t0, n=n):
                self.stt(XN[:, c, t0:t0 + n], self.X[:, c, t0:t0 + n], self.vcol(gname, c), rs, ALU.mult, ALU.mult,
                         [self.tX[c, ti], trs, self.tVEC], [tXN[c, ti]])
            self.rmsnorm_tile(ti, gname, out_fn, (SQ, tSQ, LN, tLN, RS, tRS, self.bank()))
            bI, bF = self.bank(), self.bank()
            PI, PF = self.ps[bI][0:8, :n], self.ps[bF][0:8, :n]
            for K in range(NCH):
                self.mm(PI, WIF[:, K, 0:8], XN[:, K, t0:t0 + n], K == 0, K == NCH - 1, [T["WIF"], tXN[K, ti]], [self.tps[bI]])
            for K in range(NCH):
                self.mm(PF, WIF[:, K, 8:16], XN[:, K, t0:t0 + n], K == 0, K == NCH - 1, [T["WIF"], tXN[K, ti]], [self.tps[bF]])
            g = lambda a: a[0:8, 0:n]
            self.act(g(LI), PI, AF.Identity, [self.tps[bI], T["BIF"]], [T["LI"]], bias=BIF[0:8, 0:1])
            self.act(g(TMPg), PF, AF.Exp, [self.tps[bF], T["BIF"]], [T["TMP"]], scale=-1.0, bias=BIF[0:8, 2:3])
            self.act(g(LF), g(TMPg), AF.Ln, [T["TMP"], self.tCONST], [T["LF"]], bias=self.ONEC[0:8, :])
            self.memset("dve", g(D0), 1.0, [T["D0"]])
            init = 0.0
            rd = []
            if sample:
                self.memset("dve", g(D0).rearrange("p (s t) -> p s t", t=TS)[:, :, 0:1], 0.0, [T["D0"]])
            elif ti > 0:
                init = self.BBP[0:8, 0:1]
                rd = [T["BBprev"]]
            self.scan(g(BB), g(D0), g(LF), init, ALU.mult, ALU.subtract, [T["D0"], T["LF"]] + rd, [T["BB"]])
            self.tt("dve", g(AA), g(LI), g(BB), ALU.subtract, [T["LI"], T["BB"]], [T["AA"]])
            ab = ABX[0:8, 1:1 + n]
            if sample:
                self.memset("dve", g(D1), 0.0, [T["D1"]])
                self.memset("dve", g(D1).rearrange("p (s t) -> p s t", t=TS)[:, :, 0:1], NEG, [T["D1"]])
                a3 = g(AA).rearrange("p (s t) -> p s t", t=TS)
                self.tt("dve", a3[:, :, 0], a3[:, :, 0], M0T[0:8, :], ALU.max, [T["AA"], T["M0T"]], [T["AA"]])
                self.scan(ab, g(D1), g(AA), 0.0, ALU.add, ALU.max, [T["D1"], T["AA"]], [T["ABX"]])
                rho = M0T[0:8, :].unsqueeze(2).to_broadcast([8, NS, TS])
                rtok = [T["M0T"]]
                a_v = g(AA).rearrange("p (s t) -> p s t", t=TS)
                ab_v = ab.rearrange("p (s t) -> p s t", t=TS)
                ek_v = g(TMPg).rearrange("p (s t) -> p s t", t=TS)
                eq_v = g(D0).rearrange("p (s t) -> p s t", t=TS)
                self.tt("dve", g(AA), g(LI), g(BB), ALU.subtract, [T["LI"], T["BB"], T["ABX"]], [T["AA"]])
            else:
                self.memset("dve", g(D1), 0.0, [T["D1"]])
                if ti == 0:
                    self.memset("dve", ABX[0:8, 0:1], 0.0, [T["ABX"]])
                    ainit = 0.0
                else:
                    self.cp("dve", ABX[0:8, 0:1], self.ABP[0:8, 0:1], [T["ABprev"]], [T["ABX"]])
                    ainit = self.ABP[0:8, 0:1]
                self.scan(ab, g(D1), g(AA), ainit, ALU.add, ALU.max, [T["D1"], T["AA"], T["ABX"]] + ([T["ABprev"]] if ti else []), [T["ABX"]])
                rho = ABX[0:8, 0:n].rearrange("p (j l) -> p j l", l=64)[:, :, 0:1].to_broadcast([8, nck, 64])
                rtok = [T["ABX"]]
                a_v = g(AA).rearrange("p (j l) -> p j l", l=64)
                ab_v = ab.rearrange("p (j l) -> p j l", l=64)
                ek_v = g(TMPg).rearrange("p (j l) -> p j l", l=64)
                eq_v = g(D0).rearrange("p (j l) -> p j l", l=64)
            self.tt("dve", ek_v, a_v, rho, ALU.subtract, [T["AA"]] + rtok, [T["TMP"]])
            self.act(EKA[0:8, t0:t0 + n], g(TMPg), AF.Exp, [T["TMP"]], [tEK[ti]])
            self.tt("dve", eq_v, rho, ab_v, ALU.subtract, [T["ABX"], T["D0"]] + rtok, [T["D0"]])
            self.act(EQA[0:8, t0:t0 + n], g(D0), AF.Exp, [T["D0"]], [tEQ[ti]])
            self.tt("dve", g(MTg), g(BB), ab, ALU.add, [T["BB"], T["ABX"]], [T["MT"]])
            self.act(g(EMg), g(MTg), AF.Exp, [T["MT"]], [T["EM"]], scale=-1.0)
            bT = self.bank()
            for j in range(nck):
                self.tr(self.ps[bT][0:Lc, j * 8:(j + 1) * 8], EMg[0:8, j * Lc:(j + 1) * Lc], self.IDENTF[0:8, 0:8],
                        [T["EM"], self.tMASK], [self.tps[bT]])
            cb0 = t0 // 64 if not sample else 32
            if sample:
                self.cp("act", EMTS[0:Lc, 0:nck, :], self.ps[bT][0:Lc, 0:nck * 8].rearrange("p (j h) -> p j h", h=8),
                        [self.tps[bT]], [T["EMTS"]])
                self.cp("pool", MOUT[0:8, 1:17], g(MTg).rearrange("p (s t) -> p s t", t=TS)[:, :, 3], [T["MT"]], [T["MOUT"]])
            else:
                self.cp("act", self.EMTP[0:Lc, cb0:cb0 + nck, :], self.ps[bT][0:Lc, 0:nck * 8].rearrange("p (j h) -> p j h", h=8),
                        [self.tps[bT]], [T["EMTP"]])
                if ti == 3:
                    self.cp("pool", MOUT[0:8, 0:1], MTg[0:8, n - 1:n], [T["MT"]], [T["MOUT"]])
                self.cp("pool", self.BBP[0:8, 0:1], BB[0:8, n - 1:n], [T["BB"]], [T["BBprev"]])
                self.cp("pool", self.ABP[0:8, 0:1], ABX[0:8, n:n + 1], [T["ABX"]], [T["ABprev"]])
        ar.release(m1)
        kb.barrier()
        kb.dma("sp", d["o_ml_m"], MOUT[0:8, :], reads=[T["MOUT"]])

        WIN = ar.bf16(NCH * 384).rearrange("p (k m) -> p k m", k=NCH)
        WO2 = [ar.bf16(D) for _ in range(2)]
        tWO = [Tok("mwo0"), Tok("mwo1")]
        RAW = [ar.f32(520) for _ in range(2)]
        ACC = [ar.f32(512) for _ in range(2)]
        SIL = [ar.f32(512) for _ in range(2)]
        SA = [dict(QP=ar.bf16(512), KP=ar.bf16(512), VA=ar.bf16(8 * 130).rearrange("p (j m) -> p j m", j=8),
                   KTt=ar.bf16(8 * 64).rearrange("p (j m) -> p j m", j=8), LAMB=ar.f32(512)) for _ in range(2)]
        SO = [ar.f32(512) for _ in range(3)]
        SH = [ar.f32(8 * 128).rearrange("p (j m) -> p j m", j=8) for _ in range(2)]
        STall = ar.bf16(8 * 64)
        SQH = ar.f32(8 * 128)
        STATH = ar.f32(32)
        DEN = ar.f32(16)
        HG = ar.bf16(512)
        C = ar.f32(130)
        Cd = ar.f32(130)
        Cb = ar.bf16(130)
        CS = ar.f32(4 * 130).rearrange("p (s m) -> p s m", s=4)
        CSd = ar.f32(4 * 130).rearrange("p (s m) -> p s m", s=4)
        CSb = ar.bf16(4 * 130).rearrange("p (s m) -> p s m", s=4)
        PCLb = ar.f32(8 * 130).rearrange("p (j m) -> p j m", j=8)
        CbAll = ar.bf16(9 * 130).rearrange("p (j m) -> p j m", j=9)
        main_tiles = [(t0, 512, 64, 8) for t0 in range(0, SEQ, 512)] + [(SEQ + 16 * q, 16, 4, 4) for q in range(4)]
        import os as _os
        if "ML_TILES" in _os.environ:
            main_tiles = [main_tiles[int(i)] for i in _os.environ["ML_TILES"].split(",")]
        units = []
        for h in range(int(_os.environ.get("ML_NH", 8))):
            for k_, (t0, n, L, NCk) in enumerate(main_tiles):
                units.append(dict(u=len(units), h=h, t0=t0, n=n, L=L, NCk=NCk, first=(k_ == 0),
                                  lastprompt=(t0 < SEQ and (k_ + 1 == len(main_tiles) or main_tiles[k_ + 1][0] >= SEQ))))
        for sa in SA:
            self.memset("pool", sa["VA"][0:64, :, 128:129], 1.0, [T["VAinit"]])

        def stageA(U):
            u, h, t0, n, L, NCk = U["u"], U["h"], U["t0"], U["n"], U["L"], U["NCk"]
            sa, k2, k3 = SA[u % 2], u % 2, u % 3
            QP, KP, VA, KTt, LAMB, Osig = sa["QP"], sa["KP"], sa["VA"], sa["KTt"], sa["LAMB"], SO[k3]
            tQP, tKP, tVA, tKTt, tLAM, tO = T["QP", k2], T["KP", k2], T["VA", k2], T["KTt", k2], T["LAM", k2], T["O", k3]
            sample = t0 >= SEQ
            ti5 = min(t0 // 512, 4)
            tsl = slice(t0, t0 + n)
            cs = lambda j: slice(j * L, (j + 1) * L)
            xt = [tXN[K, ti5] for K in range(NCH)]
            if U["first"]:
                for K in range(NCH):
                    rows = slice(K * 128, (K + 1) * 128)
                    kb.dma("pool", WIN[:, K, 0:64], d["ml_w_in"][0, rows, h * 64:(h + 1) * 64], writes=[T["WIN"]])
                    kb.dma("pool", WIN[:, K, 64:128], d["ml_w_in"][0, rows, 512 + h * 64:512 + (h + 1) * 64], writes=[T["WIN"]])
                    kb.dma("pool", WIN[:, K, 128:256], d["ml_w_in"][0, rows, 1024 + h * 128:1024 + (h + 1) * 128], writes=[T["WIN"]])
                    kb.dma("pool", WIN[:, K, 256:384], d["ml_w_in"][0, rows, 2048 + h * 128:2048 + (h + 1) * 128], writes=[T["WIN"]])
                kb.dma("pool", WO2[h % 2], d["ml_w_out"][0, h * 128:(h + 1) * 128, :], writes=[tWO[h % 2]])
                for w_ in range(2):
                    self.memset("pool", RAW[w_][0:64, 0:3], 0.0, [T["RAW", w_]])
                yield
            if sample:
                q4 = (t0 - SEQ) // 16
                for w_ in range(2):
                    rv = RAW[w_][0:64, 0:28].rearrange("p (s u) -> p s u", u=7)
                    kb.dma("sp", rv[:, :, 0:3], d["ml_convT"][h, w_, :, 4 * q4:4 * q4 + 4, :], writes=[T["RAW", w_]])
            bQ, bK, bO = self.bank("A"), self.bank("A"), self.bank("A")
            PQ, PK, PO_ = self.ps[bQ][0:64, :n], self.ps[bK][0:64, :n], self.ps[bO][:, :n]
            for (P_, cols, bb) in ((PQ, slice(0, 64), bQ), (PK, slice(64, 128), bK), (PO_, slice(256, 384), bO)):
                for K in range(NCH):
                    self.mm(P_, WIN[:, K, cols], XN[:, K, tsl], K == 0, K == NCH - 1, [T["WIN"], xt[K]], [self.tps[bb]])
                yield
            self.act(Osig[:, :n], PO_, AF.Sigmoid, [self.tps[bO]], [tO])
            for w_, (P_, bb) in enumerate(((PQ, bQ), (PK, bK))):
                mv = lambda k_: MLV[0:64, h, 5 * w_ + k_:5 * w_ + k_ + 1]
                R_ = RAW[w_]
                if sample:
                    rv = R_[0:64, 0:28].rearrange("p (s u) -> p s u", u=7)
                    self.cp("act", rv[:, :, 3:7], P_.rearrange("p (s t) -> p s t", t=TS), [self.tps[bb]], [T["RAW", w_]])
                    taps = [rv[:, :, k_:k_ + 4] for k_ in range(4)]
                    acc = ACC[w_][0:64, 0:n].rearrange("p (s t) -> p s t", t=TS)
                    self.cp("pool", CONVO[0:64, h, w_, 1 + 4 * q4:5 + 4 * q4, :], rv[:, :, 4:7], [T["RAW", w_]], [T["CONVO"]])
                else:
                    self.cp("act", R_[0:64, 3:3 + n], P_, [self.tps[bb]], [T["RAW", w_]])
                    taps = [R_[0:64, k_:k_ + n] for k_ in range(4)]
                    acc = ACC[w_][0:64, 0:n]
                yield
                self.ts("dve", acc, taps[0], mv(0), mv(4), ALU.mult, ALU.add, [T["RAW", w_], T["MLV"]], [T["ACC", w_]])
                for k_ in range(1, 4):
                    self.stt(acc, taps[k_], mv(k_), acc, ALU.mult, ALU.add, [T["RAW", w_], T["MLV"], T["ACC", w_]], [T["ACC", w_]])
                yield
                self.act(SIL[w_][0:64, 0:n], ACC[w_][0:64, 0:n], AF.Silu, [T["ACC", w_]], [T["SIL", w_]])
                if not sample:
                    if t0 + n == SEQ:
                        self.cp("pool", CONVO[0:64, h, w_, 0, :], R_[0:64, n:n + 3], [T["RAW", w_]], [T["CONVO"]])
                    self.cp("pool", R_[0:64, 0:3], R_[0:64, n:n + 3], [T["RAW", w_], T["ACC", w_]], [T["RAW", w_]])
                yield
            for j0 in range(0, NCk, 2):
                bV = self.bank("A")
                nj = min(2, NCk - j0)
                for jj in range(nj):
                    j = j0 + jj
                    PVt = self.ps[bV][0:L, jj * 128:(jj + 1) * 128]
                    for K in range(NCH):
                        self.mm(PVt, XN[:, K, t0 + j * L:t0 + (j + 1) * L], WIN[:, K, 128:256], K == 0, K == NCH - 1,
                                [T["WIN"], xt[K]], [self.tps[bV]])
                self.cp("act", VA[0:L, j0:j0 + nj, 0:128], self.ps[bV][0:L, 0:nj * 128].rearrange("p (j m) -> p j m", m=128),
                        [self.tps[bV], T["VAinit"]], [tVA])
                yield
            bM, bM2 = self.bank("A"), self.bank("A")
            PBK, PBQ = self.ps[bM][0:64, 0:n], self.ps[bM2][0:64, 0:n]
            tBQ = self.tps[bM2]
            self.mm(PBK, SEL[0:8, h, :], EKA[0:8, tsl], True, True, [T["SEL"], tEK[ti5]], [self.tps[bM]])
            self.mm(PBQ, SEL[0:8, h, :], EQA[0:8, tsl], True, True, [T["SEL"], tEQ[ti5]], [tBQ])
            yield
            self.tt("dve", KP[0:64, 0:n], SIL[1][0:64, 0:n], PBK, ALU.mult, [T["SIL", 1], self.tps[bM]], [tKP])
            self.stt(QP[0:64, 0:n], SIL[0][0:64, 0:n], 0.125, PBQ, ALU.mult, ALU.mult, [T["SIL", 0], tBQ], [tQP])
            self.cp("act", LAMB[0:64, 0:n], PBQ, [tBQ], [tLAM])
            yield
            bT = self.bank("A")
            PTr = self.ps[bT].bitcast(BF16)
            for j in range(NCk):
                self.tr(PTr[0:L, j * 64:(j + 1) * 64], KP[0:64, cs(j)], self.IDENTB[0:64, 0:64], [tKP, self.tMASK], [self.tps[bT]])
            self.cp("act", KTt[0:L, 0:NCk, :], PTr[0:L, 0:NCk * 64].rearrange("p (j m) -> p j m", m=64), [self.tps[bT]], [tKTt])
            yield

        def stageC(U):
            u, h, t0, n, L, NCk = U["u"], U["h"], U["t0"], U["n"], U["L"], U["NCk"]
            sa, k2 = SA[u % 2], u % 2
            QP, KP, VA, KTt, LAM = sa["QP"], sa["KP"], sa["VA"], sa["KTt"], sa["LAMB"]
            tQP, tKP, tVA, tKTt, tLAM = T["QP", k2], T["KP", k2], T["VA", k2], T["KTt", k2], T["LAM", k2]
            HT, tHT = SH[k2], T["HT", k2]
            sample = t0 >= SEQ
            cs = lambda j: slice(j * L, (j + 1) * L)
            if U["first"]:
                self.memset("pool", C[0:64], 0.0, [T["C"]])
                self.memset("pool", Cb[0:64], 0.0, [T["Cb"]])
            if sample:
                q4 = (t0 - SEQ) // 16
                for s in range(4):
                    kb.dma("sp", CS[0:64, s, 0:129], d["ml_C0T"][4 * q4 + s, h], writes=[T["CS", s]])
                yield
            PCL = PCLb[0:64, 0:NCk, 0:129]
            for j0 in range(0, NCk, 2):
                bP = self.bank("C")
                for jj in range(2):
                    j = j0 + jj
                    PCS = self.ps[bP][0:64, 256 * jj:256 * jj + 129]
                    self.mm(PCS, KTt[0:L, j, :], VA[0:L, j, 0:129], True, True, [tKTt, tVA], [self.tps[bP]])
                for jj in range(2):
                    j = j0 + jj
                    PCS = self.ps[bP][0:64, 256 * jj:256 * jj + 129]
                    lam = LAM[0:64, (j + 1) * L - 1:(j + 1) * L]
                    self.act(PCL[:, j, :], PCS, AF.Identity, [self.tps[bP], tLAM], [T["PCL"]], scale=lam)
                yield
            CbA = CbAll[0:64, 0:NCk + 1, 0:129]
            for j in range(NCk):
                lam = LAM[0:64, (j + 1) * L - 1:(j + 1) * L]
                if sample:
                    Cc, tC = CS[0:64, j, 0:129], T["CS", j]
                    self.cp("act", CbA[:, j, :], Cc, [tC], [T["CbA"]])
                    self.stt(Cc, Cc, lam, PCL[:, j, :], ALU.mult, ALU.add, [tC, tLAM, T["PCL"]], [tC])
                else:
                    Cc, tC = C[0:64, 0:129], T["C"]
                    if j == 0:
                        self.cp("act", CbA[:, 0, :], Cc, [tC], [T["CbA"]])
                    self.stt(CbA[:, j + 1, :], Cc, lam, PCL[:, j, :], ALU.mult, ALU.add, [tC, tLAM, T["PCL"]], [T["CbA"]])
                    self.stt(Cc, Cc, lam, PCL[:, j, :], ALU.mult, ALU.add, [tC, tLAM, T["PCL"]], [tC])
                yield
            bS = self.bank("C")
            PSTv = self.ps[bS][0:L, 0:NCk * L].rearrange("p (j t) -> p j t", j=NCk)
            for j in range(NCk):
                self.mm(PSTv[:, j, :], KP[0:64, cs(j)], QP[0:64, cs(j)], True, True, [tKP, tQP], [self.tps[bS]])
            yield
            STv = STall[0:L, 0:NCk * L].rearrange("p (j t) -> p j t", j=NCk)
            self.tt("dve", STv, PSTv, self.MIU[0:L, 0:L].unsqueeze(1).to_broadcast([L, NCk, L]), ALU.mult,
                    [self.tps[bS], self.tMASK], [T["ST"]])
            yield
            if sample:
                emall, tem = EMTS[0:L, 4 * q4:4 * q4 + NCk, h], T["EMTS"]
            else:
                emall, tem = self.EMTP[0:L, t0 // 64:t0 // 64 + NCk, h], T["EMTP"]
            for g0 in range(0, NCk, 3):
                g = min(3, NCk - g0)
                bN = self.bank("C")
                tN = self.tps[bN]
                PNDv = self.ps[bN][0:L, 0:g * 129].rearrange("p (j m) -> p j m", j=g)
                for jj in range(g):
                    j = g0 + jj
                    self.mm(PNDv[:, jj, :], QP[0:64, cs(j)], CbA[:, j, :], True, False, [tQP, T["CbA"]], [tN])
                    self.mm(PNDv[:, jj, :], STv[:, j, :], VA[0:L, j, 0:129], False, True, [T["ST"], tVA], [tN])
                yield
                dn = DEN[0:L, g0:g0 + g]
                self.act(dn, PNDv[:, :, 128], AF.Abs, [tN], [T["DEN"]])
                yield
                self.tt("dve", dn, dn, emall[:, g0:g0 + g], ALU.max, [T["DEN"], tem], [T["DEN"]])
                self.recip(dn, dn, [T["DEN"]], [T["DEN"]])
                yield
                self.tt("dve", HT[0:L, g0:g0 + g, :], PNDv[:, :, 0:128], dn.unsqueeze(2).to_broadcast([L, g, 128]), ALU.mult,
                        [tN, T["DEN"]], [tHT])
                yield
            if sample:
                for s in range(4):
                    kb.dma("sp", d["o_ml_Cs"][4 * q4 + s, h], CS[0:64, s, 0:129], reads=[T["CS", s]])
            if U["lastprompt"]:
                kb.dma("sp", d["o_ml_Cp"][h], C[0:64, 0:129], reads=[T["C"]])

        def stageD(U):
            u, h, t0, n, L, NCk = U["u"], U["h"], U["t0"], U["n"], U["L"], U["NCk"]
            k2, k3 = u % 2, u % 3
            HT, tHT = SH[k2], T["HT", k2]
            Osig, tO = SO[k3], T["O", k3]
            WOh = WO2[h % 2]
            ti5 = min(t0 // 512, 4)
            tsl = slice(t0, t0 + n)
            cs = lambda j: slice(j * L, (j + 1) * L)
            G = NCk
            HTf = HT[0:L, 0:G, :]
            SQv = SQH[0:L, 0:G * 128].rearrange("p (j m) -> p j m", m=128)
            self.act(SQv, HTf, AF.Square, [tHT], [T["SQH"]])
            yield
            self.reduce(STATH[0:L, 0:G], SQv, ALU.add, [T["SQH"]], [T["STATH"]])
            yield
            self.act(STATH[0:L, 0:G], STATH[0:L, 0:G], AF.Ln, [T["STATH"], self.tCONST], [T["STATH"]], scale=1.0 / 128, bias=self.EPSC[0:L, :])
            yield
            self.act(STATH[0:L, 0:G], STATH[0:L, 0:G], AF.Exp, [T["STATH"]], [T["STATH"]], scale=-0.5)
            yield
            self.tt("dve", HTf, HTf, STATH[0:L, 0:G].unsqueeze(2).to_broadcast([L, G, 128]), ALU.mult, [tHT, T["STATH"]], [tHT])
            yield
            bY = self.bank("C2")
            PYF = self.ps[bY][:, 0:n]
            for j in range(NCk):
                self.tr(PYF[:, cs(j)], HT[0:L, j, :], self.IDENTF[0:L, 0:L], [tHT, self.tMASK], [self.tps[bY]])
            yield
            self.stt(HG[:, 0:n], PYF, self.vcol("ml_norm_w", h), Osig[:, 0:n], ALU.mult, ALU.mult, [self.tps[bY], tO, self.tVEC], [T["HG"]])
            yield
            for dc in range(NCH):
                bO2 = self.bank("C2")
                PO2 = self.ps[bO2][:, 0:n]
                self.mm(PO2, WOh[:, dc * 128:(dc + 1) * 128], HG[:, 0:n], True, True, [tWO[h % 2], T["HG"]], [self.tps[bO2]])
                self.tt("dve", self.X[:, dc, tsl], PO2, self.X[:, dc, tsl], ALU.add, [self.tps[bO2], self.tX[dc, ti5]], [self.tX[dc, ti5]])
                yield

        def drain(gens):
            gens = [g for g in gens if g is not None]
            while gens:
                for g in list(gens):
                    try:
                        next(g)
                    except StopIteration:
                        gens.remove(g)

        NU_ = len(units)
        SCHED = int(_os.environ.get("K_SCHED", "1"))
        if SCHED:
            self.bank_pe = {}
            kb.begin_defer()
        for s in range(NU_ + 2):
            gA = stageA(units[s]) if s < NU_ else None
            gC = stageC(units[s - 1]) if 0 <= s - 1 < NU_ else None
            gD = stageD(units[s - 2]) if 0 <= s - 2 < NU_ else None
            drain([gC, gD, gA])
        if SCHED:
            kb.end_defer()
            self.bank_pe = {}
        kb.dma("sp", d["o_ml_conv"], CONVO[0:64], reads=[T["CONVO"]])
        ar.release(m0)
        kb.barrier()

    def ffn(self, L, which):
        kb, d, ar = self.kb, self.d, self.ar
        m = ar.mark()
        wg = d["ff%s_wg" % which][L]
        wu = d["ff%s_wu" % which][L]
        wd = d["ff%s_wd" % which][L]
        gname = "norm_ff%s%d" % (which, L)
        G = 4
        groups = [(f0, min(G, NFF - f0)) for f0 in range(0, NFF, G)]
        WG = [ar.bf16(NCH * 512).rearrange("p (c f) -> p c f", c=NCH) for _ in range(2)]
        WU = [ar.bf16(NCH * 512).rearrange("p (c f) -> p c f", c=NCH) for _ in range(2)]
        WD = [ar.bf16(G * D).rearrange("p (f o) -> p f o", f=G) for _ in range(2)]
        H = [ar.bf16(G * 512).rearrange("p (f n) -> p f n", f=G) for _ in range(2)]
        SG = [ar.f32(512) for _ in range(2)]
        SQ = [ar.bf16(512) for _ in range(2)]
        LN = ar.f32(512)
        RS = ar.f32(512)
        tW = TokMap()
        tH = TokMap()
        tSG = TokMap()
        tSQ = TokMap()
        tLN, tRS = Tok("ln"), Tok("rs")

        def load_group(gi):
            f0, nf = groups[gi]
            s = gi % 2
            for c in range(NCH):
                kb.dma("pool", WG[s][:, c, 0:nf * 128], wg[c * 128:(c + 1) * 128, f0 * 128:(f0 + nf) * 128],
                       writes=[tW["g", s, c]])
                kb.dma("pool", WU[s][:, c, 0:nf * 128], wu[c * 128:(c + 1) * 128, f0 * 128:(f0 + nf) * 128],
                       writes=[tW["u", s, c]])
            for fi in range(nf):
                kb.dma("pool", WD[s][:, fi, :], wd[(f0 + fi) * 128:(f0 + fi + 1) * 128, :], writes=[tW["d", s, fi]])

        load_group(0)
        load_group(1)

        for ti, (t0, n) in enumerate(TILES):
            def out_fn(c, rs, trs, ti=ti, t0=t0, n=n):
                kb.emit("dve", lambda e: e.scalar_tensor_tensor(out=self.XN[:, c, t0:t0 + n], in0=self.X[:, c, t0:t0 + n],
                                                                scalar=self.vcol(gname, c), in1=rs,
                                                                op0=ALU.mult, op1=ALU.mult),
                        reads=[self.tX[c, ti], trs, self.tVEC], writes=[self.tXN[c, ti]])
            self.rmsnorm_tile(ti, gname, out_fn, (SQ, tSQ, LN, tLN, RS, tRS, 4 + ti % 4))

        items = [(gi, ti) for gi in range(len(groups)) for ti in range(len(TILES))]
        po_rr = [0]

        def up(idx):
            gi, ti = items[idx]
            f0, nf = groups[gi]
            s = gi % 2
            hs = idx % 2
            t0, n = TILES[ti]
            for fi in range(nf):
                b = fi % 2
                pg = self.ps[b][:, :n]
                pu = self.ps[2 + b][:, :n]
                for c in range(NCH):
                    kb.emit("pe", lambda e, c=c, fi=fi, pg=pg: e.matmul(pg, WG[s][:, c, fi * 128:(fi + 1) * 128],
                                                                     self.XN[:, c, t0:t0 + n], start=(c == 0), stop=(c == NCH - 1)),
                            reads=[tW["g", s, c], self.tXN[c, ti]], writes=[self.tps[b]], signal=(c == NCH - 1))
                for c in range(NCH):
                    kb.emit("pe", lambda e, c=c, fi=fi, pu=pu: e.matmul(pu, WU[s][:, c, fi * 128:(fi + 1) * 128],
                                                                     self.XN[:, c, t0:t0 + n], start=(c == 0), stop=(c == NCH - 1)),
                            reads=[tW["u", s, c], self.tXN[c, ti]], writes=[self.tps[2 + b]], signal=(c == NCH - 1))
                kb.emit("act", lambda e, b=b, pg=pg: e.activation(out=SG[b][:, :n], in_=pg, func=AF.Silu),
                        reads=[self.tps[b]], writes=[tSG[b]])
                kb.emit("dve", lambda e, b=b, fi=fi, pu=pu: e.tensor_tensor(out=H[hs][:, fi, :n], in0=SG[b][:, :n], in1=pu, op=ALU.mult),
                        reads=[tSG[b], self.tps[2 + b]], writes=[tH[hs, fi]])

        def down(idx):
            gi, ti = items[idx]
            f0, nf = groups[gi]
            s = gi % 2
            hs = idx % 2
            t0, n = TILES[ti]
            for dc in range(NCH):
                bank = 4 + po_rr[0] % 4
                po_rr[0] += 1
                po = self.ps[bank][:, :n]
                for fi in range(nf):
                    kb.emit("pe", lambda e, fi=fi, dc=dc, po=po: e.matmul(po, WD[s][:, fi, dc * 128:(dc + 1) * 128], H[hs][:, fi, :n],
                                                                       start=(fi == 0), stop=(fi == nf - 1)),
                            reads=[tW["d", s, fi], tH[hs, fi]], writes=[self.tps[bank]], signal=(fi == nf - 1))
                kb.emit("dve", lambda e, dc=dc, po=po: e.scalar_tensor_tensor(out=self.X[:, dc, t0:t0 + n], in0=po, scalar=0.5,
                                                                           in1=self.X[:, dc, t0:t0 + n], op0=ALU.mult, op1=ALU.add),
                        reads=[self.tps[bank], self.tX[dc, ti]], writes=[self.tX[dc, ti]])

        ntile = len(TILES)
        for idx in range(len(items)):
            up(idx)
            if idx > 0:
                down(idx - 1)
                gi_prev, ti_prev = items[idx - 1]
                if ti_prev == ntile - 1 and gi_prev + 2 < len(groups):
                    load_group(gi_prev + 2)
        down(len(items) - 1)
        ar.release(m)
        kb.barrier()

    def final_norm(self):
        kb, d, ar = self.kb, self.d, self.ar
        m = ar.mark()
        SQ = [ar.bf16(512) for _ in range(2)]
        LN = ar.f32(512)
        RS = ar.f32(512)
        Y = [ar.f32(512) for _ in range(4)]
        tSQ, tY = TokMap(), TokMap()
        tLN, tRS = Tok("ln"), Tok("rs")
        rr = [0]
        for ti, (t0, n) in enumerate(TILES):
            def out_fn(c, rs, trs, ti=ti, t0=t0, n=n):
                s = rr[0] % 4
                rr[0] += 1
                kb.emit("dve", lambda e: e.scalar_tensor_tensor(out=Y[s][:, :n], in0=self.X[:, c, t0:t0 + n],
                                                                scalar=self.vcol("norm_final", c), in1=rs,
                                                                op0=ALU.mult, op1=ALU.mult),
                        reads=[self.tX[c, ti], trs, self.tVEC], writes=[tY[s]])
                kb.dma("sp", d["yT"][c * 128:(c + 1) * 128, t0:t0 + n], Y[s][:, :n], reads=[tY[s]])
            self.rmsnorm_tile(ti, "norm_final", out_fn, (SQ, tSQ, LN, tLN, RS, tRS, 4 + ti % 4))
        ar.release(m)
        kb.barrier()

    def dump_x(self):
        kb, d = self.kb, self.d
        for c in range(NCH):
            for ti, (t0, n) in enumerate(TILES):
                kb.dma("sp", d["yT"][c * 128:(c + 1) * 128, t0:t0 + n], self.X[:, c, t0:t0 + n], reads=[self.tX[c, ti]])

    def build(self):
        st = self.stages
        self.load_inputs()
        self.consts()
        for L in range(2):
            if "ffa%d" % L in st:
                self.ffn(L, "a")
            if "mix%d" % L in st and L == 0:
                self.rwkv()
            if "mix%d" % L in st and L == 1:
                self.mlstm()
            if "ffb%d" % L in st:
                self.ffn(L, "b")
        if "final" in st:
            self.final_norm()
        else:
            self.dump_x()
        self.kb.flush()
        return self.nc


ALL_STAGES = ("ffa0", "mix0", "ffb0", "ffa1", "mix1", "ffb1", "final")


def pack_vecs(inp):
    vecs = np.zeros((NVEC, D), np.float32)

    def put(name, v):
        vecs[VID[name]] = np.asarray(v, np.float32).reshape(D)

    for L in range(2):
        put("norm_ffa%d" % L, inp["norm_ffa"][L])
        put("norm_mix%d" % L, inp["norm_mix"][L])
        put("norm_ffb%d" % L, inp["norm_ffb"][L])
    put("norm_final", inp["norm_final"])
    for i in range(6):
        put("mu%d" % i, inp["rw_mu"][0, i])
    put("w0", inp["rw_w0"][0])
    put("a0", inp["rw_a0"][0])
    put("k_k", inp["rw_k_k"][0])
    put("k_a", inp["rw_k_a"][0])
    put("r_k", inp["rw_r_k"][0])
    put("gn_w", inp["rw_gn_w"][0])
    put("gn_b", inp["rw_gn_b"][0])
    for j in range(4):
        put("cw%d" % j, inp["ml_conv_w"][0, j])
    put("cb", inp["ml_conv_b"][0])
    put("ml_norm_w", inp["ml_norm_w"][0])
    return np.ascontiguousarray(vecs.reshape(NVEC, NCH, 128).transpose(2, 0, 1).reshape(128, NVEC * NCH))


def make_in_maps(inp):
    vecs = pack_vecs(inp)
    shared = {"vecs": vecs}
    for nm in ("ffa_wg", "ffa_wu", "ffb_wg", "ffb_wu", "ffa_wd", "ffb_wd"):
        shared[nm] = np.ascontiguousarray(inp[nm], dtype=np.float32)
    cw = np.asarray(inp["ml_conv_w"][0], np.float32)
    cbv = np.asarray(inp["ml_conv_b"][0], np.float32)
    mlv = np.zeros((64, 8, 10), np.float32)
    for w in range(2):
        for k in range(4):
            mlv[:, :, 5 * w + k] = cw[k, w * 512:(w + 1) * 512].reshape(8, 64).T
        mlv[:, :, 5 * w + 4] = cbv[w * 512:(w + 1) * 512].reshape(8, 64).T
    shared["mlv"] = mlv
    shared["bif"] = np.ascontiguousarray(np.asarray(inp["ml_b_if"][0], np.float32).reshape(2, 8).T)
    for nm in ("rw_wr", "rw_wk", "rw_wv", "rw_wo", "rw_w1", "rw_w2", "rw_a1", "rw_a2", "rw_g1", "rw_g2", "ml_w_in", "ml_w_out"):
        shared[nm] = np.ascontiguousarray(inp[nm], dtype=np.float32)
    maps = []
    for core in range(NCORES):
        xs = np.concatenate([inp["x_prompt"][core], inp["x_sample"][core * NS:(core + 1) * NS].reshape(NS * TS, D)], axis=0)
        m = dict(shared)
        m["xT"] = np.ascontiguousarray(xs.T.astype(np.float32))
        sq = slice(core * NS, (core + 1) * NS)
        sh = inp["state_rwkv_shift"][0, sq]
        m["shiftT"] = np.ascontiguousarray(sh.reshape(NS, NCH, 128).transpose(2, 1, 0).astype(np.float32))
        m["rw_S0T"] = np.ascontiguousarray(inp["state_rwkv_S"][0, sq].transpose(0, 1, 3, 2).astype(np.float32))
        m["ml_m0T"] = np.ascontiguousarray(inp["state_mlstm_m"][0, sq].T.astype(np.float32))
        c0 = np.concatenate([inp["state_mlstm_C"][0, sq].transpose(0, 1, 3, 2), inp["state_mlstm_n"][0, sq][..., None]], axis=-1)
        m["ml_C0T"] = np.ascontiguousarray(c0.astype(np.float32))
        cv = inp["state_mlstm_conv"][0, sq]
        m["ml_convT"] = np.ascontiguousarray(cv.reshape(NS, 3, 2, 8, 64).transpose(3, 2, 4, 0, 1).astype(np.float32))
        maps.append(m)
    return maps


def run(inp, stages=ALL_STAGES, trace=False):
    b = Builder(stages)
    nc = b.build()
    maps = make_in_maps(inp)
    res = run_bass_kernel_spmd(nc, maps, core_ids=list(range(NCORES)), trace=trace)
    return b, res


def assemble(results):
    f = np.float32
    yp = np.zeros((8, SEQ, D), f); ys = np.zeros((128, TS, D), f)
    p_S = np.zeros((1, 8, 16, 64, 64), f); p_sh = np.zeros((1, 8, D), f)
    p_C = np.zeros((1, 8, 8, 128, 64), f); p_n = np.zeros((1, 8, 8, 64), f); p_m = np.zeros((1, 8, 8), f)
    p_cv = np.zeros((1, 8, 3, D), f)
    s_S = np.zeros((1, 128, 16, 64, 64), f); s_sh = np.zeros((1, 128, D), f)
    s_C = np.zeros((1, 128, 8, 128, 64), f); s_n = np.zeros((1, 128, 8, 64), f); s_m = np.zeros((1, 128, 8), f)
    s_cv = np.zeros((1, 128, 3, D), f)
    for core in range(NCORES):
        r = results[core]
        sq = slice(core * NS, (core + 1) * NS)
        y = r["yT"].T
        yp[core] = y[:SEQ]
        ys[sq] = y[SEQ:].reshape(NS, TS, D)
        sho = r["o_shift"]
        p_sh[0, core] = sho[:, :, 0].T.reshape(D)
        s_sh[0, sq] = sho[:, :, 1:].transpose(2, 1, 0).reshape(NS, D)
        p_S[0, core] = r["o_rw_Sp"].transpose(0, 2, 1)
        s_S[0, sq] = r["o_rw_Ss"].transpose(0, 1, 3, 2)
        cp = r["o_ml_Cp"]
        p_C[0, core] = cp[:, :, 0:128].transpose(0, 2, 1)
        p_n[0, core] = cp[:, :, 128]
        cs_ = r["o_ml_Cs"]
        s_C[0, sq] = cs_[:, :, :, 0:128].transpose(0, 1, 3, 2)
        s_n[0, sq] = cs_[:, :, :, 128]
        mo = r["o_ml_m"]
        p_m[0, core] = mo[:, 0]
        s_m[0, sq] = mo[:, 1:].T
        cv = r["o_ml_conv"]
        cvt = cv.transpose(3, 4, 2, 1, 0).reshape(17, 3, D)
        p_cv[0, core] = cvt[0]
        s_cv[0, sq] = cvt[1:]
    return (yp, ys, p_S, p_sh, p_C, p_n, p_m, p_cv, s_S, s_sh, s_C, s_n, s_m, s_cv)


def kernel(**inp):
    b, res = run(inp)
    return assemble(res.results)
```

```python
import numpy as np
from contextlib import ExitStack
import concourse.bass as bass
import concourse.mybir as mybir
from concourse.bass_utils import run_bass_kernel_spmd

F32 = mybir.dt.float32
BF16 = mybir.dt.bfloat16
AF = mybir.ActivationFunctionType
ALU = mybir.AluOpType
AX = mybir.AxisListType

NCORES = 8
D = 1024
NCH = 8
SEQ = 2048
NS = 16
TS = 4
NT = SEQ + NS * TS
DFF = 2816
NFF = DFF // 128
EPS = 1e-6

ENGS = ("pe", "act", "dve", "pool", "sp")
SAME_ENGINE_SYNC = True

VID = {}
_v = 0
for _nm in ("norm_ffa0", "norm_ffa1", "norm_mix0", "norm_mix1", "norm_ffb0", "norm_ffb1", "norm_final",
            "mu0", "mu1", "mu2", "mu3", "mu4", "mu5", "w0", "a0", "k_k", "k_a", "r_k", "gn_w", "gn_b",
            "cw0", "cw1", "cw2", "cw3", "cb", "ml_norm_w"):
    VID[_nm] = _v
    _v += 1
NVEC = _v


class Tok:
    __slots__ = ("name", "w", "r", "excl")

    def __init__(self, name="", excl=False):
        self.name = name
        self.w = []
        self.r = []
        self.excl = excl


class TokMap(dict):
    def __missing__(self, key):
        t = Tok(str(key))
        self[key] = t
        return t


class KB:
    def __init__(self, n_dma_sems=12):
        self.nc = bass.Bass("TRN2", target_bir_lowering=False, dynamic_dma_scratch_size=4096)
        self.es = ExitStack()
        nc = self.nc
        self.sem = {}
        self.count = {}
        self.prog = {e: [] for e in ENGS}
        self.waited = {e: {} for e in ENGS}
        for e in ENGS:
            self.sem[e] = self.es.enter_context(nc.semaphore("s_" + e))
            self.count[e] = 0
        self.dsem = {}
        self.dval = {}
        self.dnext = {}
        for q in ("sp", "pool", "act"):
            self.dsem[q] = []
            for j in range(n_dma_sems):
                key = "d_%s_%d" % (q, j)
                self.sem[key] = self.es.enter_context(nc.semaphore(key))
                self.dsem[q].append(key)
                self.dval[key] = 0
            self.dnext[q] = 0
        self.ninstr = 0
        self.defer = None

    def _wait(self, eng, semkey, value):
        if value <= 0:
            return
        if self.waited[eng].get(semkey, 0) >= value:
            return
        self.waited[eng][semkey] = value
        sem = self.sem[semkey]
        self.prog[eng].append(lambda e, sem=sem, value=value: e.wait_ge(sem, value))

    def _deps(self, eng, reads, writes):
        deps = set()
        for t in reads:
            deps.update(t.w)
            if t.excl:
                deps.update(x for x in t.r if x[0] != eng)
        for t in writes:
            deps.update(t.w)
            deps.update(t.r)
        for (sk, v) in deps:
            if sk == eng and (eng == "pe" or not SAME_ENGINE_SYNC):
                continue
            self._wait(eng, sk, v)

    def emit(self, eng, fn, reads=(), writes=(), signal=True, dur=0.4, after=None):
        if self.defer is not None:
            self.defer.append(dict(kind="op", eng=eng, fn=fn, reads=tuple(reads), writes=tuple(writes), signal=signal,
                                   dur=dur, extra=[after] if after is not None else []))
            self.ninstr += 1
            return len(self.defer) - 1
        self._deps(eng, reads, writes)
        if after is not None:
            self._wait(eng, after[0], after[1])
        self.ninstr += 1
        if signal:
            self.count[eng] += 1
            cid = (eng, self.count[eng])
            sem = self.sem[eng]
            self.prog[eng].append(lambda e, fn=fn, sem=sem: fn(e).then_inc(sem, 1))
        else:
            cid = (eng, self.count[eng] + 1)
            self.prog[eng].append(lambda e, fn=fn: fn(e))
        for t in reads:
            t.r.append(cid)
        for t in writes:
            t.w = [cid]
            t.r = []
        return cid

    def dma(self, q, out, in_, reads=(), writes=()):
        if self.defer is not None:
            self.defer.append(dict(kind="dma", eng=q, out=out, in_=in_, reads=tuple(reads), writes=tuple(writes), signal=True,
                                   dur=0.1, extra=[]))
            self.ninstr += 1
            return len(self.defer) - 1
        self._deps(q, reads, writes)
        return self._dma_issue(q, out, in_, reads, writes)

    def _dma_issue(self, q, out, in_, reads, writes):
        j = self.dnext[q]
        self.dnext[q] = (j + 1) % len(self.dsem[q])
        key = self.dsem[q][j]
        self._wait(q, key, self.dval[key])
        self.dval[key] += 16
        cid = (key, self.dval[key])
        sem = self.sem[key]
        self.ninstr += 1
        self.prog[q].append(lambda e, out=out, in_=in_, sem=sem: e.dma_start(out=out, in_=in_).then_inc(sem, 16))
        for t in reads:
            t.r.append(cid)
        for t in writes:
            t.w = [cid]
            t.r = []
        return cid

    def begin_defer(self):
        self.barrier()
        self.defer = []

    def end_defer(self):
        recs = self.defer
        self.defer = None
        self._schedule(recs)
        self.barrier()

    def _schedule(self, recs):
        n = len(recs)
        lastw, readers = {}, {}
        deps = [None] * n
        for i, r in enumerate(recs):
            e = r["eng"]
            dset = set(r["extra"])
            for t in r["reads"]:
                k = id(t)
                if k in lastw:
                    dset.add(lastw[k])
                if t.excl:
                    dset.update(j for j in readers.get(k, ()) if recs[j]["eng"] != e)
            for t in r["writes"]:
                k = id(t)
                if k in lastw:
                    dset.add(lastw[k])
                dset.update(readers.get(k, ()))
            dset.discard(i)
            deps[i] = dset
            for t in r["reads"]:
                readers.setdefault(id(t), []).append(i)
            for t in r["writes"]:
                lastw[id(t)] = i
                readers[id(t)] = []
        succs = [[] for _ in range(n)]
        indeg = [0] * n
        for i in range(n):
            indeg[i] = len(deps[i])
            for j in deps[i]:
                succs[j].append(i)
        import os as _os
        LAT_X = float(_os.environ.get("K_LATX", "0.15"))
        LAT_S = float(_os.environ.get("K_LATS", "0.05"))
        DSC = float(_os.environ.get("K_DSC", "1.5"))
        ready_t = [0.0] * n
        start = [0.0] * n
        free = {e: 0.0 for e in ENGS}
        ready = {e: [] for e in ENGS}
        for i in range(n):
            if indeg[i] == 0:
                ready[recs[i]["eng"]].append(i)
        done = 0
        while done < n:
            best, bkey = None, None
            for e in ENGS:
                lst = ready[e]
                if not lst:
                    continue
                fe = free[e]
                cand = min(lst, key=lambda i: (max(fe, ready_t[i]), i))
                key = (max(fe, ready_t[cand]), cand)
                if bkey is None or key < bkey:
                    best, bkey = cand, key
            i = best
            r = recs[i]
            e = r["eng"]
            ready[e].remove(i)
            st = bkey[0]
            start[i] = st
            du = r["dur"] * DSC
            free[e] = st + du
            fin = st + du + (2.5 if r["kind"] == "dma" else 0.0)
            for j in succs[i]:
                lat = LAT_S if recs[j]["eng"] == e else LAT_X
                if fin + lat > ready_t[j]:
                    ready_t[j] = fin + lat
                indeg[j] -= 1
                if indeg[j] == 0:
                    ready[recs[j]["eng"]].append(j)
            done += 1
        order = sorted(range(n), key=lambda i: (start[i], i))
        cids = [None] * n
        cnt = self.count["pe"]
        pend = []
        for i in order:
            r = recs[i]
            if r["eng"] != "pe" or r["kind"] != "op":
                continue
            if r["signal"]:
                cnt += 1
                cids[i] = ("pe", cnt)
                for j in pend:
                    cids[j] = ("pe", cnt)
                pend = []
            else:
                pend.append(i)
        assert not pend, "deferred PE stream must end with a signalled instruction"
        for i in order:
            r = recs[i]
            e = r["eng"]
            for j in deps[i]:
                sk, v = cids[j]
                if sk == e and j not in r["extra"] and (e == "pe" or not SAME_ENGINE_SYNC):
                    continue
                self._wait(e, sk, v)
            if r["kind"] == "dma":
                cids[i] = self._dma_issue(e, r["out"], r["in_"], (), ())
                continue
            fn = r["fn"]
            if r["signal"]:
                self.count[e] += 1
                cid = (e, self.count[e])
                if e == "pe":
                    assert cid == cids[i], (cid, cids[i])
                cids[i] = cid
                sem = self.sem[e]
                self.prog[e].append(lambda eng_, fn=fn, sem=sem: fn(eng_).then_inc(sem, 1))
            else:
                self.prog[e].append(lambda eng_, fn=fn: fn(eng_))
        self.sched_span = max(start) if n else 0.0

    def barrier(self):
        for e in ENGS:
            for e2 in ENGS:
                if e2 != e:
                    self._wait(e, e2, self.count[e2])
            for q in self.dsem:
                for key in self.dsem[q]:
                    self._wait(e, key, self.dval[key])

    def flush(self):
        self.barrier()
        nc = self.nc
        prog = self.prog
        with nc.Block() as block:
            @block.tensor
            def _(e):
                for f in prog["pe"]:
                    f(e)

            @block.scalar
            def _(e):
                for f in prog["act"]:
                    f(e)

            @block.vector
            def _(e):
                for f in prog["dve"]:
                    f(e)

            @block.gpsimd
            def _(e):
                for f in prog["pool"]:
                    f(e)

            @block.sync
            def _(e):
                for f in prog["sp"]:
                    f(e)
        self.prog = {e: [] for e in ENGS}


class Arena:
    def __init__(self, ap, nwords):
        self.ap = ap
        self.n = nwords
        self.top = 0

    def mark(self):
        return self.top

    def release(self, m):
        self.top = m

    def f32(self, nwords):
        assert self.top + nwords <= self.n, ("arena overflow", self.top, nwords, self.n)
        a = self.ap[:, self.top:self.top + nwords]
        self.top += nwords
        return a

    def bf16(self, nelem):
        nwords = (nelem + 1) // 2
        a = self.f32(nwords).bitcast(BF16)
        return a[:, 0:nelem]


ARENA_WORDS = 56280
XNW = 1 + SEQ + NS * (TS + 1)
SOFF = 1 + SEQ
TILES = [(0, 512), (512, 512), (1024, 512), (1536, 512), (2048, 64)]


class Builder:
    def __init__(self, stages):
        self.stages = stages
        self.kb = KB()
        kb = self.kb
        nc = kb.nc
        self.nc = nc
        es = kb.es
        d = {}

        def din(name, shape):
            d[name] = nc.dram_tensor(name, list(shape), F32, kind="ExternalInput").ap()

        def dout(name, shape):
            d[name] = nc.dram_tensor(name, list(shape), F32, kind="ExternalOutput").ap()

        din("xT", (D, NT))
        din("vecs", (128, NVEC * 8))
        for nm in ("ffa_wg", "ffa_wu", "ffb_wg", "ffb_wu"):
            din(nm, (2, D, DFF))
        for nm in ("ffa_wd", "ffb_wd"):
            din(nm, (2, DFF, D))
        for nm in ("rw_wr", "rw_wk", "rw_wv", "rw_wo"):
            din(nm, (1, D, D))
        din("rw_w1", (1, D, 64)); din("rw_w2", (1, 64, D))
        din("rw_a1", (1, D, 64)); din("rw_a2", (1, 64, D))
        din("rw_g1", (1, D, 160)); din("rw_g2", (1, 160, D))
        din("ml_w_in", (1, D, 3088)); din("ml_w_out", (1, D, D))
        din("mlv", (64, 8, 10)); din("bif", (8, 2)); din("ml_m0T", (8, NS))
        din("ml_C0T", (NS, 8, 64, 129)); din("ml_convT", (8, 2, 64, NS, 3))
        din("shiftT", (128, NCH, NS))
        din("rw_S0T", (NS, 16, 64, 64))
        dout("yT", (D, NT))
        dout("o_ml_m", (8, 17)); dout("o_ml_Cp", (8, 64, 129)); dout("o_ml_Cs", (NS, 8, 64, 129))
        dout("o_ml_conv", (64, 8, 2, 17, 3))
        dout("o_shift", (128, NCH, 17))
        dout("o_rw_Sp", (16, 64, 64))
        dout("o_rw_Ss", (NS, 16, 64, 64))
        self.d = d
        self.out_names = [k for k in d if k == "yT" or k.startswith("o_")]

        arena_t = es.enter_context(nc.sbuf_tensor("arena", [128, ARENA_WORDS], F32))
        self.ar = Arena(arena_t, ARENA_WORDS)
        self.psall = es.enter_context(nc.psum_tensor("psall", [128, 8, 512], F32))
        self.ps = [self.psall[:, i, :] for i in range(8)]
        self.tps = [Tok("ps%d" % i, excl=True) for i in range(8)]
        self.bank_rr = 0
        self.bank_pe = {}

        ar = self.ar
        self.X = ar.f32(NCH * NT).rearrange("p (c n) -> p c n", c=NCH)
        self.tX = TokMap()
        self.VEC = ar.f32(NVEC * 8)
        self.tVEC = Tok("vec")
        self.ONES = ar.bf16(128)
        self.tONES = Tok("ones")
        self.XNraw = ar.bf16(NCH * XNW)
        self.XN = self.XNraw[:, 0:NCH * NT].rearrange("p (c n) -> p c n", c=NCH)
        self.XNS = self.XNraw.rearrange("p (c n) -> p c n", c=NCH)
        self.tXN = TokMap()


    BANK_GROUPS = {"A": (0, 1, 2, 3), "B": (4, 5), "C": (6, 7), "C2": (4, 5)}

    def bank(self, group=None):
        if group is None:
            b = self.bank_rr % 8
            self.bank_rr += 1
            return b
        if not hasattr(self, "_grr"):
            self._grr = {}
        k = self._grr.get(group, 0)
        self._grr[group] = k + 1
        g = self.BANK_GROUPS[group]
        return g[k % len(g)]

    def _dur(self, eng, ap):
        base = 0.22 + 0.0011 * ap.free_size()
        return base * (3.0 if eng == "pool" else 1.0)

    def act(self, out, in_, func, reads, writes, **kw):
        return self.kb.emit("act", lambda e: e.activation(out=out, in_=in_, func=func, **kw), reads, writes, dur=self._dur("act", out))

    def cp(self, eng, out, in_, reads, writes):
        if eng == "act":
            return self.kb.emit("act", lambda e: e.activation(out=out, in_=in_, func=AF.Copy), reads, writes, dur=self._dur("act", out))
        return self.kb.emit(eng, lambda e: e.tensor_copy(out=out, in_=in_), reads, writes, dur=self._dur(eng, out))

    def tt(self, eng, out, in0, in1, op, reads, writes):
        return self.kb.emit(eng, lambda e: e.tensor_tensor(out=out, in0=in0, in1=in1, op=op), reads, writes, dur=self._dur(eng, out))

    def ts(self, eng, out, in0, s1, s2, op0, op1, reads, writes):
        if s2 is None:
            return self.kb.emit(eng, lambda e: e.tensor_scalar(out=out, in0=in0, scalar1=s1, scalar2=None, op0=op0), reads, writes,
                                dur=self._dur(eng, out))
        return self.kb.emit(eng, lambda e: e.tensor_scalar(out=out, in0=in0, scalar1=s1, scalar2=s2, op0=op0, op1=op1), reads, writes,
                            dur=self._dur(eng, out))

    def stt(self, out, in0, scalar, in1, op0, op1, reads, writes):
        return self.kb.emit("dve", lambda e: e.scalar_tensor_tensor(out=out, in0=in0, scalar=scalar, in1=in1, op0=op0, op1=op1),
                            reads, writes, dur=self._dur("dve", out))

    def _pe_rows(self, lhsT, writes):
        K = lhsT.partition_size()
        base = lhsT.base_partition()
        tile = 32 if K <= 32 else (64 if K <= 64 else 128)
        lo, hi = (base // tile) * tile, (base // tile) * tile + tile
        if tile == 128:
            lo, hi = 0, 128
        after = None
        for t in writes:
            for b in range(8):
                if t is self.tps[b]:
                    prev = self.bank_pe.get(b)
                    if prev is not None and prev[2] is not None and (prev[1] <= lo or hi <= prev[0]):
                        after = prev[2]
                    self.bank_pe[b] = [lo, hi, None]
        return tile < 128, after

    def _pe_done(self, writes, cid):
        for t in writes:
            for b in range(8):
                if t is self.tps[b] and self.bank_pe.get(b) is not None:
                    self.bank_pe[b][2] = cid

    def mm(self, out, lhsT, rhs, start, stop, reads, writes, signal=None):
        if signal is None:
            signal = stop
        partial, after = self._pe_rows(lhsT, writes)
        if partial:
            signal = True
        dur = 0.07 + 0.00045 * rhs.free_size()
        cid = self.kb.emit("pe", lambda e: e.matmul(out, lhsT, rhs, start=start, stop=stop), reads, writes, signal=signal,
                           dur=dur, after=after)
        self._pe_done(writes, cid)
        return cid

    def tr(self, out, in_, ident, reads, writes):
        partial, after = self._pe_rows(in_, writes)
        cid = self.kb.emit("pe", lambda e: e.transpose(out, in_, ident), reads, writes, dur=0.12, after=after)
        self._pe_done(writes, cid)
        return cid

    def memset(self, eng, ap, val, writes):
        return self.kb.emit(eng, lambda e: e.memset(ap, val), (), writes)

    def scan(self, out, d0, d1, init, op0, op1, reads, writes):
        return self.kb.emit("dve", lambda e: e.tensor_tensor_scan(out=out, data0=d0, data1=d1, initial=init, op0=op0, op1=op1), reads, writes,
                            dur=0.22 + 0.0022 * out.free_size())

    def recip(self, out, in_, reads, writes):
        return self.kb.emit("dve", lambda e: e.reciprocal(out=out, in_=in_), reads, writes, dur=0.22 + 0.008 * out.free_size())

    def reduce(self, out, in_, op, reads, writes, axis=None):
        axis = AX.X if axis is None else axis
        return self.kb.emit("dve", lambda e: e.tensor_reduce(out=out, in_=in_, axis=axis, op=op), reads, writes, dur=self._dur("dve", in_))

    def vcol(self, name, c):
        j = VID[name] * 8 + c
        return self.VEC[:, j:j + 1]

    def load_inputs(self):
        kb, d = self.kb, self.d
        kb.dma("sp", self.VEC, d["vecs"][:, :], writes=[self.tVEC])
        for c in range(NCH):
            for ti, (t0, n) in enumerate(TILES):
                kb.dma("sp", self.X[:, c, t0:t0 + n], d["xT"][c * 128:(c + 1) * 128, t0:t0 + n],
                       writes=[self.tX[c, ti]])
        kb.emit("dve", lambda e: e.memset(self.ONES, 1.0), writes=[self.tONES])

    def _full_bank(self, b):
        self.bank_pe[b] = [0, 128, None]

    def rmsnorm_tile(self, ti, gname, out_fn, scratch):
        kb = self.kb
        t0, n = TILES[ti]
        SQ, tSQ, LN, tLN, RS, tRS, bank = scratch
        ps = self.ps[bank][:, :n]
        self._full_bank(bank)
        for c in range(NCH):
            s = c % 2
            kb.emit("act", lambda e, c=c, s=s: e.activation(out=SQ[s][:, :n], in_=self.X[:, c, t0:t0 + n], func=AF.Square),
                    reads=[self.tX[c, ti]], writes=[tSQ[s]])
            kb.emit("pe", lambda e, c=c, s=s: e.matmul(ps, self.ONES, SQ[s][:, :n], start=(c == 0), stop=(c == NCH - 1)),
                    reads=[tSQ[s], self.tONES], writes=[self.tps[bank]], signal=True)
        kb.emit("act", lambda e: e.activation(out=LN[:, :n], in_=ps, func=AF.Ln, scale=1.0 / D, bias=self.EPSC),
                reads=[self.tps[bank], self.tCONST], writes=[tLN])
        kb.emit("act", lambda e: e.activation(out=RS[:, :n], in_=LN[:, :n], func=AF.Exp, scale=-0.5),
                reads=[tLN], writes=[tRS])
        for c in range(NCH):
            out_fn(c, RS[:, :n], tRS)

    def consts(self):
        kb, ar = self.kb, self.ar
        self.CONST = ar.f32(8)
        self.tCONST = Tok("const")
        self.EPSC = self.CONST[:, 0:1]
        self.ONEC = self.CONST[:, 1:2]
        self.NHALFC = self.CONST[:, 2:3]
        self.GNEPSC = self.CONST[:, 3:4]
        for col, val in ((0, EPS), (1, 1.0), (2, -0.5), (3, 64e-5)):
            kb.emit("dve", lambda e, col=col, val=val: e.memset(self.CONST[:, col:col + 1], val), (), [self.tCONST])
        ONESF = ar.f32(128)
        tO = Tok("onesf")
        self.memset("pool", ONESF, 1.0, [tO])
        self.IDENTF = ar.f32(128)
        self.IDENTB = ar.bf16(128)
        self.BONES = ar.bf16(128)
        self.tMASK = Tok("masks")
        kb.emit("pool", lambda e: e.affine_select(out=self.IDENTF, in_=ONESF, pattern=[[-1, 128]], compare_op=ALU.is_equal,
                                                  fill=0.0, base=0, channel_multiplier=1), [tO], [self.tMASK])
        self.cp("pool", self.IDENTB, self.IDENTF, [self.tMASK], [self.tMASK])
        self.memset("pool", self.BONES, 0.0, [self.tMASK])
        self.memset("pool", self.BONES[0:64, 0:64], 1.0, [self.tMASK])
        self.memset("pool", self.BONES[64:128, 64:128], 1.0, [self.tMASK])
        MSU = ar.f32(64)
        MIU = ar.f32(64)
        self.MASKXT = ar.f32(64)
        kb.emit("pool", lambda e: e.affine_select(out=MSU[0:64, :], in_=ONESF[0:64, 0:64], pattern=[[1, 64]], compare_op=ALU.is_gt,
                                                  fill=0.0, base=0, channel_multiplier=-1), [tO], [self.tMASK])
        kb.emit("pool", lambda e: e.affine_select(out=MIU[0:64, :], in_=ONESF[0:64, 0:64], pattern=[[1, 64]], compare_op=ALU.is_ge,
                                                  fill=0.0, base=0, channel_multiplier=-1), [tO], [self.tMASK])
        kb.emit("pool", lambda e: e.affine_select(out=self.MASKXT[0:64, :], in_=ONESF[0:64, 0:64], pattern=[[-1, 64]], compare_op=ALU.is_gt,
                                                  fill=0.0, base=0, channel_multiplier=1), [tO], [self.tMASK])
        self.ts("pool", self.MASKXT[0:64, :], self.MASKXT[0:64, :], -1.0, None, ALU.mult, None, [self.tMASK], [self.tMASK])
        self.MIU = MIU
        self.MASKLL = ar.f32(2 * 4 * 64).rearrange("p (h b t) -> p h b t", h=2, b=4)
        for h in range(2):
            self.cp("pool", self.MASKLL[0:64, h, 0, :], MSU[0:64, :], [self.tMASK], [self.tMASK])
            self.cp("pool", self.MASKLL[0:64, h, 1, :], MIU[0:64, :], [self.tMASK], [self.tMASK])
            self.ts("pool", self.MASKLL[0:64, h, 2, :], MSU[0:64, :], -1.0, None, ALU.mult, None, [self.tMASK], [self.tMASK])
            self.cp("pool", self.MASKLL[0:64, h, 3, :], MIU[0:64, :], [self.tMASK], [self.tMASK])
        self.SM64 = ar.f32(256)
        self.SM4 = ar.f32(16)
        self.memset("pool", self.SM64, 1.0, [self.tMASK])
        self.memset("pool", self.SM64.rearrange("p (j l) -> p j l", l=64)[:, :, 0:1], 0.0, [self.tMASK])
        self.memset("pool", self.SM4, 1.0, [self.tMASK])
        self.memset("pool", self.SM4.rearrange("p (j l) -> p j l", l=4)[:, :, 0:1], 0.0, [self.tMASK])
        self.NEGW0 = ar.f32(8)
        j = VID["w0"] * 8
        self.ts("dve", self.NEGW0, self.VEC[:, j:j + 8], -1.0, None, ALU.mult, None, [self.tVEC], [self.tMASK])

    def rwkv(self):
        kb, d, ar = self.kb, self.d, self.ar
        m0 = ar.mark()
        XNS = self.XNS
        tXNS = TokMap()
        gname = "norm_mix0"
        mu = lambda i, K: self.vcol("mu%d" % i, K)

        HWA = ar.bf16(NT)
        HG1 = ar.bf16(NT)
        HG2 = ar.bf16(NT)
        tHWA, tHG = TokMap(), TokMap()
        W2A2 = ar.bf16(D)
        G2A = ar.bf16(D)
        G2B = ar.bf16(D)
        tW2 = Tok("w2a2g2")
        SHO = ar.f32(NCH * 17).rearrange("p (c j) -> p c j", c=NCH)
        tSHO = Tok("sho")
        SHI = ar.f32(NCH * NS).rearrange("p (c j) -> p c j", c=NCH)
        tSHI = Tok("shi")

        kb.dma("pool", W2A2[0:64, :], d["rw_w2"][0], writes=[tW2])
        kb.dma("pool", W2A2[64:128, :], d["rw_a2"][0], writes=[tW2])
        kb.dma("pool", G2A, d["rw_g2"][0, 0:128, :], writes=[tW2])
        self.memset("pool", G2B, 0.0, [tW2])
        self.memset("pool", HG2, 0.0, [tHG[0]])
        kb.dma("pool", G2B[0:32, :], d["rw_g2"][0, 128:160, :], writes=[tW2])
        kb.dma("sp", SHI, d["shiftT"], writes=[tSHI])

        for c in range(NCH):
            self.memset("pool", XNS[:, c, 0:1], 0.0, [tXNS[c, "init"]])
            sv = XNS[:, c, SOFF:SOFF + NS * 5].rearrange("p (j u) -> p j u", u=5)
            self.cp("pool", sv[:, :, 0], SHI[:, c, :], [tSHI], [tXNS[c, "init"]])

        def xn_aps(K, t0, n):
            if t0 < SEQ:
                return XNS[:, K, 1 + t0:1 + t0 + n], XNS[:, K, t0:t0 + n]
            j0 = (t0 - SEQ) // TS
            nj = n // TS
            sv = XNS[:, K, SOFF + 5 * j0:SOFF + 5 * (j0 + nj)].rearrange("p (j u) -> p j u", u=5)
            return sv[:, :, 1:5], sv[:, :, 0:4]

        def xn_toks(K, t0):
            ti = min(t0 // 512, 4)
            return [tXNS[K, ti], tXNS[K, max(ti - 1, 0)], tXNS[K, "init"]]

        def pview(ps_ap, t0, n):
            if t0 < SEQ:
                return ps_ap
            return ps_ap.rearrange("p (j t) -> p j t", t=TS)

        def mixproj(out, wa, wb, cols, t0, n, wtok, ptok):
            o = pview(out, t0, n)
            for K in range(NCH):
                xa, xb = xn_aps(K, t0, n)
                self.mm(o, wa[:, K, cols], xa, K == 0, False, [wtok] + xn_toks(K, t0), [ptok])
                self.mm(o, wb[:, K, cols], xb, False, K == NCH - 1, [wtok] + xn_toks(K, t0), [ptok])

        def scale_w(raw, wb, Mcols, mu_list, tok):
            for K in range(NCH):
                for (cs, mi) in mu_list:
                    self.ts("pool", wb[:, K, cs], raw[:, K, cs], mu(mi, K), None, ALU.mult, None, [tok, self.tVEC], [tok])
            self.tt("pool", raw, raw, wb, ALU.subtract, [tok], [tok])

        m1 = ar.mark()
        W1A = ar.bf16(NCH * 128).rearrange("p (k m) -> p k m", k=NCH)
        W1B = ar.bf16(NCH * 128).rearrange("p (k m) -> p k m", k=NCH)
        G1A = ar.bf16(NCH * 160).rearrange("p (k m) -> p k m", k=NCH)
        G1B = ar.bf16(NCH * 160).rearrange("p (k m) -> p k m", k=NCH)
        tW1, tG1 = Tok("w1a1"), Tok("g1")
        for K in range(NCH):
            kb.dma("pool", W1A[:, K, 0:64], d["rw_w1"][0, K * 128:(K + 1) * 128, :], writes=[tW1])
            kb.dma("pool", W1A[:, K, 64:128], d["rw_a1"][0, K * 128:(K + 1) * 128, :], writes=[tW1])
            kb.dma("pool", G1A[:, K, :], d["rw_g1"][0, K * 128:(K + 1) * 128, :], writes=[tG1])
        scale_w(W1A, W1B, 128, [(slice(0, 64), 1), (slice(64, 128), 4)], tW1)
        scale_w(G1A, G1B, 160, [(slice(0, 160), 5)], tG1)
        XNF = [ar.f32(512) for _ in range(2)]
        SQ = [ar.bf16(512) for _ in range(2)]
        LN = ar.f32(512)
        RS = ar.f32(512)
        tXNF, tSQ = TokMap(), TokMap()
        tLN, tRS = Tok("ln"), Tok("rs")
        rr = [0]
        import os as _os
        PRE = int(_os.environ.get("K_PRE", "1"))
        if PRE:
            self.bank_pe = {}
            kb.begin_defer()
        for ti, (t0, n) in enumerate(TILES):
            def out_fn(c, rs, trs, ti=ti, t0=t0, n=n):
                s = rr[0] % 2
                rr[0] += 1
                xf = XNF[s][:, :n]
                self.stt(xf, self.X[:, c, t0:t0 + n], self.vcol(gname, c), rs, ALU.mult, ALU.mult,
                         [self.tX[c, ti], trs, self.tVEC], [tXNF[s]])
                if t0 < SEQ:
                    self.cp("act", XNS[:, c, 1 + t0:1 + t0 + n], xf, [tXNF[s]], [tXNS[c, ti]])
                    if t0 + n == SEQ:
                        self.cp("pool", SHO[:, c, 0:1], xf[:, n - 1:n], [tXNF[s]], [tSHO])
                else:
                    sv = XNS[:, c, SOFF:SOFF + NS * 5].rearrange("p (j u) -> p j u", u=5)
                    xv = xf.rearrange("p (j t) -> p j t", t=TS)
                    self.cp("act", sv[:, :, 1:5], xv, [tXNF[s]], [tXNS[c, ti]])
                    self.cp("pool", SHO[:, c, 1:17], xv[:, :, 3], [tXNF[s]], [tSHO])
            self.rmsnorm_tile(ti, gname, out_fn, (SQ, tSQ, LN, tLN, RS, tRS, self.bank()))
            b1, b2, b3 = self.bank(), self.bank(), self.bank()
            mixproj(self.ps[b1][:, :n], W1A, W1B, slice(0, 128), t0, n, tW1, self.tps[b1])
            mixproj(self.ps[b2][:, :n], G1A, G1B, slice(0, 128), t0, n, tG1, self.tps[b2])
            mixproj(self.ps[b3][0:32, :n], G1A, G1B, slice(128, 160), t0, n, tG1, self.tps[b3])
            self.act(HWA[0:64, t0:t0 + n], self.ps[b1][0:64, :n], AF.Tanh, [self.tps[b1]], [tHWA[ti]])
            self.cp("act", HWA[64:128, t0:t0 + n], self.ps[b1][64:128, :n], [self.tps[b1]], [tHWA[ti]])
            self.act(HG1[:, t0:t0 + n], self.ps[b2][:, :n], AF.Sigmoid, [self.tps[b2]], [tHG[ti]])
            self.act(HG2[0:32, t0:t0 + n], self.ps[b3][0:32, :n], AF.Sigmoid, [self.tps[b3]], [tHG[ti]])
        if PRE:
            kb.end_defer()
            self.bank_pe = {}
        ar.release(m1)
        kb.barrier()
        kb.dma("sp", d["o_shift"], SHO, reads=[tSHO])

        WN = 256

        def f32t():
            return ar.f32(WN)

        def bf16t():
            return ar.bf16(WN)
        WA2 = [{nm: ar.bf16(NCH * 128).rearrange("p (k m) -> p k m", k=NCH) for nm in "rkv"} for _ in range(2)]
        WB2 = [{nm: ar.bf16(NCH * 128).rearrange("p (k m) -> p k m", k=NCH) for nm in "rkv"} for _ in range(2)]
        WO2 = [ar.bf16(D) for _ in range(2)]
        tWc2 = [{nm: Tok("w%s%d" % (nm, i)) for nm in "rkv"} for i in range(2)]
        tWO = [Tok("wo0"), Tok("wo1")]
        Rf, Kf, Vf, A_, EW, CUM, EM, KK, KF, Bv, T1, T2 = [f32t() for _ in range(12)]
        EQ = EW
        Vb, SQb, KTb, BTb, YG = [bf16t() for _ in range(5)]
        RKR = SQb
        S3 = []
        for _ in range(3):
            S3.append(dict(
                KR=ar.bf16(2 * WN).rearrange("p (a n) -> p a n", a=2),
                KTt=ar.bf16(4 * 128).rearrange("p (j m) -> p j m", j=4),
                BTt=ar.bf16(4 * 128).rearrange("p (j m) -> p j m", j=4),
                VTt=ar.bf16(4 * 128).rearrange("p (j m) -> p j m", j=4),
                EP=f32t(), BONUS=f32t(), Gf=f32t(), LLs=ar.bf16(4 * 2 * 4 * 64)))
        S2 = []
        for _ in range(2):
            S2.append(dict(XTs=ar.bf16(4 * 2 * 64), PW=[ar.bf16(8 * 2 * 64) for _ in range(2)]))
        PT = [ar.bf16(8 * 64) for _ in range(2)]
        Gs = ar.bf16(128)
        NU = ar.bf16(128)
        YT = ar.f32(512)
        SQ2 = ar.f32(512)
        YF = SQ2[:, 0:256]
        STAT = ar.f32(32)
        H = ar.f32(64)
        H0d = ar.f32(64)
        Hb = ar.bf16(64)
        HS = ar.f32(4 * 64).rearrange("p (j v) -> p j v", j=4)
        HSb = ar.bf16(4 * 64).rearrange("p (j v) -> p j v", j=4)
        T = TokMap()

        main_tiles = [(t0, 256, 64) for t0 in range(0, SEQ, 256)] + [(SEQ + 16 * q, 16, 4) for q in range(4)]
        import os as _os
        if "KDEBUG" in _os.environ:
            print("rwkv arena top", ar.top, "of", ar.n)
        if "RW_TILES" in _os.environ:
            main_tiles = [main_tiles[int(i)] for i in _os.environ["RW_TILES"].split(",")]
        NCc = int(_os.environ.get("RW_NC", NCH))
        units = []
        for c in range(NCc):
            for k_, (t0, n, L) in enumerate(main_tiles):
                u = len(units)
                units.append(dict(u=u, c=c, t0=t0, n=n, L=L, first=(k_ == 0), last=(k_ == len(main_tiles) - 1),
                                  lastprompt=(t0 < SEQ and (k_ + 1 == len(main_tiles) or main_tiles[k_ + 1][0] >= SEQ))))

        def load_weights(c):
            ccols = slice(c * 128, (c + 1) * 128)
            WA, WB, tWc = WA2[c % 2], WB2[c % 2], tWc2[c % 2]
            for nm, key, mi in (("r", "rw_wr", 0), ("k", "rw_wk", 2), ("v", "rw_wv", 3)):
                for K in range(NCH):
                    kb.dma("pool", WA[nm][:, K, :], d[key][0, K * 128:(K + 1) * 128, ccols], writes=[tWc[nm]])
                scale_w(WA[nm], WB[nm], 128, [(slice(0, 128), mi)], tWc[nm])

        def stageA(U):
            u, c, t0, n, L = U["u"], U["c"], U["t0"], U["n"], U["L"]
            s3, s2 = S3[u % 3], S2[u % 2]
            k3, k2 = u % 3, u % 2
            sample = t0 >= SEQ
            NCk = 4
            ti5 = min(t0 // 512, 4)
            tsl = slice(t0, t0 + n)
            cs = lambda j: slice(j * L, (j + 1) * L)
            ccols = slice(c * 128, (c + 1) * 128)
            WA, WB, tWc = WA2[c % 2], WB2[c % 2], tWc2[c % 2]
            if U["first"]:
                if c == 0:
                    load_weights(0)
                if c + 1 < NCc:
                    load_weights(c + 1)
                yield
            KR, KTt, BTt, VTt, EP, BONUS, Gf = s3["KR"], s3["KTt"], s3["BTt"], s3["VTt"], s3["EP"], s3["BONUS"], s3["Gf"]
            tKR, tKTt, tBTt, tVTt, tEP, tBONUS, tGf, tLL = (T["KR", k3], T["KTt", k3], T["BTt", k3], T["VTt", k3], T["EP", k3],
                                                          T["BONUS", k3], T["Gf", k3], T["LLs", k3])
            tXT = T["XTs", k2]
            bA, bB, bC, bD = self.bank("A"), self.bank("A"), self.bank("A"), self.bank("A")
            PR, PK = self.ps[bA][:, 0:n], self.ps[bA][:, 256:256 + n]
            PV, PGt = self.ps[bB][:, 0:n], self.ps[bB][:, 256:256 + n]
            PWL, PAL = self.ps[bC][:, 0:n], self.ps[bC][:, 256:256 + n]
            PKK, PSm = self.ps[bD][:, 0:n], self.ps[bD][:, 256:256 + n]
            for K0 in range(0, NCH, 2):
                pass
            mixproj(PR, WA["r"], WB["r"], slice(0, 128), t0, n, tWc["r"], self.tps[bA])
            yield
            mixproj(PK, WA["k"], WB["k"], slice(0, 128), t0, n, tWc["k"], self.tps[bA])
            yield
            mixproj(PV, WA["v"], WB["v"], slice(0, 128), t0, n, tWc["v"], self.tps[bB])
            self.mm(PGt, G2A[:, ccols], HG1[:, tsl], True, False, [tW2, tHG[ti5]], [self.tps[bB]])
            self.mm(PGt, G2B[:, ccols], HG2[:, tsl], False, True, [tW2, tHG[ti5], tHG[0]], [self.tps[bB]])
            self.mm(PWL, W2A2[0:64, ccols], HWA[0:64, tsl], True, True, [tW2, tHWA[ti5]], [self.tps[bC]])
            self.mm(PAL, W2A2[64:128, ccols], HWA[64:128, tsl], True, True, [tW2, tHWA[ti5]], [self.tps[bC]])
            yield
            w = lambda a: a[:, 0:n]
            tV = self.tVEC
            self.cp("act", w(Rf), PR, [self.tps[bA]], [T["Rf"]])
            self.cp("act", w(Kf), PK, [self.tps[bA]], [T["Kf"]])
            yield
            self.cp("act", w(Vf), PV, [self.tps[bB]], [T["Vf"]])
            self.cp("act", w(Gf), PGt, [self.tps[bB]], [tGf])
            self.cp("dve", w(Vb), w(Vf), [T["Vf"]], [T["Vb"]])
            yield
            self.act(w(A_), PAL, AF.Sigmoid, [self.tps[bC], tV], [T["A"]], bias=self.vcol("a0", c))
            self.act(w(T1), PWL, AF.Exp, [self.tps[bC], self.tMASK], [T["T1"]], scale=-1.0, bias=self.NEGW0[:, c:c + 1])
            self.ts("dve", w(KK), w(Kf), self.vcol("k_k", c), None, ALU.mult, None, [T["Kf"], tV], [T["KK"]])
            yield
            self.act(w(T1), w(T1), AF.Ln, [T["T1"], self.tCONST], [T["T1"]], bias=self.ONEC)
            self.act(w(SQb), w(KK), AF.Square, [T["KK"]], [T["SQb"]])
            self.mm(PKK, self.BONES, w(SQb), True, True, [T["SQb"], self.tMASK], [self.tps[bD]], signal=True)
            yield
            self.act(w(EW), w(T1), AF.Exp, [T["T1"], self.tCONST], [T["EW"]], scale=-1.0, bias=self.NHALFC)
            self.ts("dve", w(T1), w(A_), -1.0, self.vcol("k_a", c), ALU.add, ALU.mult, [T["A"], tV], [T["T1"]])
            self.stt(w(KF), w(T1), 1.0, w(Kf), ALU.add, ALU.mult, [T["T1"], T["Kf"]], [T["KF"]])
            yield
            SM = self.SM4[:, 0:n] if sample else self.SM64[:, 0:n]
            self.scan(w(CUM), SM, w(EW), 0.0, ALU.mult, ALU.subtract, [T["EW"], self.tMASK], [T["CUM"]])
            self.act(w(T2), PKK, AF.Sqrt, [self.tps[bD]], [T["T2"]])
            yield
            self.act(w(EP), w(CUM), AF.Exp, [T["CUM"]], [tEP])
            self.act(w(EM), w(CUM), AF.Exp, [T["CUM"]], [T["EM"]], scale=-1.0)
            self.ts("dve", w(T2), w(T2), 1e-12, None, ALU.max, None, [T["T2"]], [T["T2"]])
            self.recip(w(T2), w(T2), [T["T2"]], [T["T2"]])
            yield
            self.tt("dve", w(KK), w(KK), w(T2), ALU.mult, [T["KK"], T["T2"]], [T["KK"]])
            self.tt("dve", w(T2), w(CUM), w(EW), ALU.add, [T["CUM"], T["EW"], T["KK"]], [T["T2"]])
            self.act(w(EQ), w(T2), AF.Exp, [T["T2"]], [T["EW"]])
            yield
            self.stt(w(RKR), w(Rf), self.vcol("r_k", c), w(KF), ALU.mult, ALU.mult, [T["Rf"], T["KF"], tV], [T["SQb"]])
            self.mm(PSm, self.BONES, w(RKR), True, True, [T["SQb"], self.tMASK], [self.tps[bD]], signal=True)
            self.tt("dve", w(Bv), w(KK), w(A_), ALU.mult, [T["KK"], T["A"]], [T["Bv"]])
            self.tt("dve", KR[:, 1, 0:n], w(Rf), w(EP), ALU.mult, [T["Rf"], tEP], [tKR])
            yield
            self.tt("dve", KR[:, 0, 0:n], w(KK), w(EQ), ALU.mult, [T["KK"], T["EW"]], [tKR])
            self.tt("dve", w(KTb), w(KF), w(EM), ALU.mult, [T["KF"], T["EM"]], [T["KTb"]])
            self.tt("dve", w(BTb), w(Bv), w(EM), ALU.mult, [T["Bv"], T["EM"]], [T["BTb"]])
            self.tt("dve", w(BONUS), PSm, w(Vf), ALU.mult, [self.tps[bD], T["Vf"]], [tBONUS])
            yield
            bT = self.bank("A")
            PTr = self.ps[bT].bitcast(BF16)
            for (src, ts_, off) in ((KTb, "KTb", 0), (BTb, "BTb", 1)):
                for j in range(NCk):
                    self.tr(PTr[0:L, off * 512 + j * 128:off * 512 + (j + 1) * 128], src[:, cs(j)], self.IDENTB,
                            [T[ts_], self.tMASK], [self.tps[bT]])
            self.cp("act", KTt[0:L, :, :], PTr[0:L, 0:512].rearrange("p (j m) -> p j m", j=4), [self.tps[bT]], [tKTt])
            self.cp("act", BTt[0:L, :, :], PTr[0:L, 512:1024].rearrange("p (j m) -> p j m", j=4), [self.tps[bT]], [tBTt])
            yield
            bT2 = self.bank("A")
            PTr2 = self.ps[bT2].bitcast(BF16)
            for j in range(NCk):
                self.tr(PTr2[0:L, j * 128:(j + 1) * 128], Vb[:, cs(j)], self.IDENTB, [T["Vb"], self.tMASK], [self.tps[bT2]])
            self.cp("act", VTt[0:L, :, :], PTr2[0:L, 0:512].rearrange("p (j m) -> p j m", j=4), [self.tps[bT2]], [tVTt])
            yield
            LLv = s3["LLs"][:, 0:4 * 2 * 4 * L].rearrange("p (j h b t) -> p j h b t", j=4, h=2, b=4)
            XTv = s2["XTs"][:, 0:4 * 2 * L].rearrange("p (j h t) -> p j h t", j=4, h=2)
            mk = self.MASKLL[0:L, :, :, 0:L]
            for h in range(2):
                hs = slice(64 * h, 64 * h + 64)
                bX = self.bank("A")
                PXT = self.ps[bX][:, 0:4 * L].rearrange("p (j t) -> p j t", j=4)
                for g0 in (0, 2):
                    bL = self.bank("A")
                    PLL = self.ps[bL][:, 0:2 * 4 * L].rearrange("p (j b t) -> p j b t", j=2, b=4)
                    for jj in range(2):
                        j = g0 + jj
                        self.mm(PLL[0:L, jj, 0:2, :], KTb[hs, cs(j)], KR[hs, :, cs(j)], True, True, [T["KTb"], tKR], [self.tps[bL]])
                        self.mm(PLL[0:L, jj, 2:4, :], BTb[hs, cs(j)], KR[hs, :, cs(j)], True, True, [T["BTb"], tKR], [self.tps[bL]])
                        self.mm(PXT[0:L, j, :], KR[hs, 0, cs(j)], BTb[hs, cs(j)], True, True, [T["BTb"], tKR], [self.tps[bX]])
                    self.tt("dve", LLv[0:L, g0:g0 + 2, h], PLL[0:L], mk, ALU.mult, [self.tps[bL], self.tMASK], [tLL])
                    yield
                self.tt("dve", XTv[0:L, :, h, :], PXT[0:L], self.MASKXT[0:L, 0:L].unsqueeze(1).to_broadcast([L, 4, L]), ALU.mult,
                        [self.tps[bX], self.tMASK], [tXT])
                yield

        def stageB(U):
            u, L = U["u"], U["L"]
            s3, s2 = S3[u % 3], S2[u % 2]
            k3, k2 = u % 3, u % 2
            NCk, NM = 4, 8
            LLv = s3["LLs"][:, 0:4 * 2 * 4 * L].rearrange("p (j h b t) -> p j h b t", j=4, h=2, b=4)
            XTv = s2["XTs"][:, 0:4 * 2 * L].rearrange("p (j h t) -> p j h t", j=4, h=2)
            tLL, tXT = T["LLs", k3], T["XTs", k2]
            PWv = [p[:, 0:NM * 2 * L].rearrange("p (i a t) -> p i a t", i=NM, a=2) for p in s2["PW"]]
            PTv = [p[:, 0:NM * L].rearrange("p (i t) -> p i t", i=NM) for p in PT]
            tPW = [T["PW", k2, 0], T["PW", k2, 1]]
            PW4 = PWv[0].rearrange("p (j h) a t -> p j h a t", h=2)
            self.cp("dve", PW4[0:L, :, :, 0, :], LLv[0:L, :, :, 2, :], [tLL], [tPW[0]])
            self.cp("pool", PWv[0][0:L, :, 1, :], self.IDENTB[0:L, 0:L].unsqueeze(1).to_broadcast([L, NM, L]), [self.tMASK], [tPW[0]])
            self.cp("act", PTv[0][0:L].rearrange("p (j h) t -> p j h t", h=2), XTv[0:L], [tXT], [T["PT", 0]])
            yield
            nlev = 6 if L == 64 else 2
            cur = 0
            mpb = 4 if L == 64 else 8
            for lev in range(nlev):
                nxt = 1 - cur
                last = lev == nlev - 1
                for i0 in range(0, NM, mpb):
                    bI = self.bank("B")
                    PA = self.ps[bI][:, 0:mpb * 2 * L].rearrange("p (i a t) -> p i a t", i=mpb, a=2)
                    for ii in range(mpb):
                        i = i0 + ii
                        self.mm(PA[0:L, ii], PTv[cur][0:L, i, :], PWv[cur][0:L, i], True, True,
                                [T["PT", cur], tPW[cur]], [self.tps[bI]], signal=(ii == mpb - 1))
                    if not last:
                        self.cp("act", PWv[nxt][0:L, i0:i0 + mpb, 0, :], PA[0:L, :, 0, :], [self.tps[bI]], [tPW[nxt]])
                    self.tt("dve", PWv[nxt][0:L, i0:i0 + mpb, 1, :], PA[0:L, :, 1, :], PWv[cur][0:L, i0:i0 + mpb, 1, :], ALU.add,
                            [self.tps[bI], tPW[cur]], [tPW[nxt]])
                    yield
                if not last:
                    bJ = self.bank("B")
                    PB = self.ps[bJ][:, 0:NM * L].rearrange("p (i t) -> p i t", i=NM)
                    for i in range(NM):
                        self.mm(PB[0:L, i, :], PWv[cur][0:L, i, 0, :], PTv[cur][0:L, i, :], True, True,
                                [T["PT", cur], tPW[cur]], [self.tps[bJ]], signal=(i == NM - 1))
                    self.cp("act", PTv[nxt][0:L], PB[0:L], [self.tps[bJ]], [T["PT", nxt]])
                    yield
                cur = nxt
            U["Wv"] = PWv[cur]
            U["tWv"] = tPW[cur]

        def stageC(U):
            u, c, t0, n, L = U["u"], U["c"], U["t0"], U["n"], U["L"]
            s3 = S3[u % 3]
            k3 = u % 3
            sample = t0 >= SEQ
            NCk = 4
            ti5 = min(t0 // 512, 4)
            tsl = slice(t0, t0 + n)
            cs = lambda j: slice(j * L, (j + 1) * L)
            w = lambda a: a[:, 0:n]
            tV = self.tVEC
            KR, KTt, BTt, VTt, EP, BONUS, Gf = s3["KR"], s3["KTt"], s3["BTt"], s3["VTt"], s3["EP"], s3["BONUS"], s3["Gf"]
            tKR, tKTt, tBTt, tVTt, tEP, tBONUS, tGf, tLL = (T["KR", k3], T["KTt", k3], T["BTt", k3], T["VTt", k3], T["EP", k3],
                                                          T["BONUS", k3], T["Gf", k3], T["LLs", k3])
            LLv = s3["LLs"][:, 0:4 * 2 * 4 * L].rearrange("p (j h b t) -> p j h b t", j=4, h=2, b=4)
            Wv, tWv = U["Wv"], U["tWv"]
            WO = WO2[c % 2]
            if U["first"]:
                kb.dma("pool", WO, d["rw_wo"][0, c * 128:(c + 1) * 128, :], writes=[tWO[c % 2]])
                self.memset("pool", H, 0.0, [T["H"]])
                self.memset("pool", Hb, 0.0, [T["Hb"]])
            if sample:
                q = (t0 - SEQ) // 16
                for jj in range(4):
                    kb.dma("sp", HS[:, jj, :], d["rw_S0T"][4 * q + jj, 2 * c:2 * c + 2].rearrange("h k v -> (h k) v"),
                           writes=[T["HS", jj]])
                    self.cp("act", HSb[:, jj, :], HS[:, jj, :], [T["HS", jj]], [T["HSb", jj]])
                yield
            YTv = YT[:, 0:4 * 128].rearrange("p (j m) -> p j m", j=4)
            for j in range(NCk):
                if sample:
                    Hc, Hbc, Hdc = HS[:, j, :], HSb[:, j, :], H0d
                    tH, tHb, tHd = T["HS", j], T["HSb", j], T["H0d"]
                else:
                    Hc, Hbc, Hdc = H, Hb, H0d
                    tH, tHb, tHd = T["H"], T["Hb"], T["H0d"]
                DL = EP[:, (j + 1) * L - 1:(j + 1) * L]
                bS = self.bank("C")
                PG = self.ps[bS][0:L, 0:128]
                PU = self.ps[bS][0:L, 128:256]
                PY = self.ps[bS][0:L, 256:384]
                bH = self.bank("C")
                PH = self.ps[bH][:, 0:64]
                tS = self.tps[bS]
                tSH = self.tps[bH]
                for h in range(2):
                    hs = slice(64 * h, 64 * h + 64)
                    self.mm(PG[:, hs], LLv[0:L, j, h, 0, :], VTt[0:L, j, hs], True, False, [tLL, tVTt], [tS])
                    self.mm(PG[:, hs], KR[hs, 0, cs(j)], Hbc[hs, :], False, True, [tKR, tHb], [tS], signal=True)
                self.act(Hdc, Hc, AF.Identity, [tH, tEP], [tHd], scale=DL)
                yield
                self.cp("act", Gs[0:L, :], PG, [tS], [T["Gs"]])
                yield
                for h in range(2):
                    hs = slice(64 * h, 64 * h + 64)
                    self.mm(PU[:, hs], Wv[0:L, 2 * j + h, 1, :], Gs[0:L, hs], True, True, [tWv, T["Gs"]], [tS], signal=True)
                yield
                self.act(NU[0:L, :], PU, AF.Identity, [tS], [T["NU"]], scale=-1.0)
                yield
                for h in range(2):
                    hs = slice(64 * h, 64 * h + 64)
                    self.mm(PH[hs, :], KTt[0:L, j, hs], VTt[0:L, j, hs], True, False, [tKTt, tVTt], [tSH])
                    self.mm(PH[hs, :], BTt[0:L, j, hs], NU[0:L, hs], False, True, [tBTt, T["NU"]], [tSH], signal=True)
                for h in range(2):
                    hs = slice(64 * h, 64 * h + 64)
                    self.mm(PY[:, hs], LLv[0:L, j, h, 1, :], VTt[0:L, j, hs], True, False, [tLL, tVTt], [tS])
                    self.mm(PY[:, hs], LLv[0:L, j, h, 3, :], NU[0:L, hs], False, False, [tLL, T["NU"]], [tS])
                    self.mm(PY[:, hs], KR[hs, 1, cs(j)], Hbc[hs, :], False, True, [tKR, tHb], [tS], signal=True)
                yield
                self.stt(Hbc, PH, DL, Hdc, ALU.mult, ALU.add, [tSH, tEP, tHd], [tHb])
                self.stt(Hc, PH, DL, Hdc, ALU.mult, ALU.add, [tSH, tEP, tHd], [tH])
                self.cp("act", YTv[0:L, j, :], PY, [tS], [T["YT"]])
                yield
            if sample:
                for jj in range(4):
                    kb.dma("sp", d["o_rw_Ss"][4 * q + jj, 2 * c:2 * c + 2].rearrange("h k v -> (h k) v"), HS[:, jj, :],
                           reads=[T["HS", jj]])
            if U["lastprompt"]:
                kb.dma("sp", d["o_rw_Sp"][2 * c:2 * c + 2].rearrange("h k v -> (h k) v"), H, reads=[T["H"]])
            G8 = 8
            YT3 = YT[:, 0:512].rearrange("p (g v) -> p g v", g=G8)
            SQ3 = SQ2[:, 0:512].rearrange("p (g v) -> p g v", g=G8)
            SUMv, VARv, RSTv = STAT[:, 0:8], STAT[:, 8:16], STAT[:, 16:24]
            self.reduce(SUMv[0:L, :], YT3[0:L], ALU.add, [T["YT"]], [T["SUM"]])
            self.ts("dve", SUMv[0:L, :], SUMv[0:L, :], 1.0 / 64, None, ALU.mult, None, [T["SUM"]], [T["SUM"]])
            yield
            self.tt("dve", YT3[0:L], YT3[0:L], SUMv[0:L, :].unsqueeze(2).to_broadcast([L, G8, 64]), ALU.subtract,
                    [T["YT"], T["SUM"]], [T["YT"]])
            yield
            self.act(SQ2[0:L, 0:512], YT[0:L, 0:512], AF.Square, [T["YT"]], [T["SQ2"]])
            yield
            self.reduce(VARv[0:L, :], SQ3[0:L], ALU.add, [T["SQ2"]], [T["VAR"]])
            yield
            self.act(RSTv[0:L, :], VARv[0:L, :], AF.Ln, [T["VAR"], self.tCONST], [T["RST"]], scale=1.0 / 64, bias=self.GNEPSC[0:L, :])
            yield
            self.act(RSTv[0:L, :], RSTv[0:L, :], AF.Exp, [T["RST"]], [T["RST"]], scale=-0.5)
            yield
            self.tt("dve", YT3[0:L], YT3[0:L], RSTv[0:L, :].unsqueeze(2).to_broadcast([L, G8, 64]), ALU.mult,
                    [T["YT"], T["RST"]], [T["YT"]])
            yield
            bY = self.bank("C")
            PYF = self.ps[bY][:, 0:n]
            for j in range(NCk):
                self.tr(PYF[:, cs(j)], YTv[0:L, j, :], self.IDENTF[0:L, 0:L], [T["YT"], self.tMASK], [self.tps[bY]])
            yield
            self.act(w(YF), PYF, AF.Identity, [self.tps[bY], tV], [T["SQ2"]], scale=self.vcol("gn_w", c), bias=self.vcol("gn_b", c))
            yield
            self.tt("pool", w(YF), w(YF), w(BONUS), ALU.add, [T["SQ2"], tBONUS], [T["SQ2"]])
            yield
            self.tt("dve", w(YG), w(YF), w(Gf), ALU.mult, [T["SQ2"], tGf], [T["YG"]])
            yield
            for dc0 in range(0, NCH, 2):
                bO = self.bank("C")
                for k2_ in range(2):
                    dc = dc0 + k2_
                    PO = self.ps[bO][:, 256 * k2_:256 * k2_ + n]
                    self.mm(PO, WO[:, dc * 128:(dc + 1) * 128], w(YG), True, True, [tWO[c % 2], T["YG"]], [self.tps[bO]], signal=True)
                for k2_ in range(2):
                    dc = dc0 + k2_
                    PO = self.ps[bO][:, 256 * k2_:256 * k2_ + n]
                    self.tt("dve", self.X[:, dc, tsl], PO, self.X[:, dc, tsl], ALU.add, [self.tps[bO], self.tX[dc, ti5]], [self.tX[dc, ti5]])
                yield

        STEPS = [int(v) for v in _os.environ.get("RW_STEP", "1,1,1").split(",")]

        def drain(gens):
            gens = [(g, k) for g, k in zip(gens, STEPS) if g is not None]
            while gens:
                for item in list(gens):
                    g, k = item
                    try:
                        for _ in range(k):
                            next(g)
                    except StopIteration:
                        gens.remove(item)

        PIPE = int(_os.environ.get("RW_PIPE", "1"))
        NU_ = len(units)
        SCHED = int(_os.environ.get("K_SCHED", "1"))
        if SCHED:
            self.bank_pe = {}
            kb.begin_defer()
        if PIPE:
            for s in range(NU_ + 2):
                gA = stageA(units[s]) if s < NU_ else None
                gB = stageB(units[s - 1]) if 0 <= s - 1 < NU_ else None
                gC = stageC(units[s - 2]) if 0 <= s - 2 < NU_ else None
                drain([gC, gB, gA])
        else:
            for U in units:
                drain([stageA(U)])
                drain([stageB(U)])
                drain([stageC(U)])
        if SCHED:
            kb.end_defer()
            self.bank_pe = {}
        ar.release(m0)
        kb.barrier()

    def mlstm(self):
        kb, d, ar = self.kb, self.d, self.ar
        m0 = ar.mark()
        gname = "norm_mix1"
        XN, tXN = self.XN, self.tXN
        T = TokMap()
        NEG = -1.0e30
        EKA = ar.f32(NT)
        EQA = ar.f32(NT)
        EMTT = ar.f32(48 * 8).rearrange("p (j h) -> p j h", h=8)
        self.EMTP = EMTT[:, 0:32, :]
        EMTS = EMTT[:, 32:48, :]
        self.BBP = ar.f32(2)
        self.ABP = ar.f32(2)
        MLV = ar.f32(8 * 10).rearrange("p (h k) -> p h k", h=8)
        BIF = ar.f32(4)
        M0T = ar.f32(NS)
        MOUT = ar.f32(17)
        SEL = ar.f32(8 * 64).rearrange("p (h m) -> p h m", h=8)
        CONVO = ar.f32(8 * 2 * 17 * 3).rearrange("p (h w s k) -> p h w s k", h=8, w=2, s=17)
        tEK, tEQ = TokMap(), TokMap()
        kb.dma("sp", MLV[0:64], d["mlv"], writes=[T["MLV"]])
        kb.dma("sp", BIF[0:8, 0:2], d["bif"], writes=[T["BIF"]])
        kb.dma("sp", M0T[0:8, :], d["ml_m0T"], writes=[T["M0T"]])
        self.ts("dve", BIF[0:8, 2:3], BIF[0:8, 1:2], -1.0, None, ALU.mult, None, [T["BIF"]], [T["BIF"]])
        self.cp("pool", SEL[0:8], self.IDENTF[0:8, 0:8].unsqueeze(2).to_broadcast([8, 8, 64]), [self.tMASK], [T["SEL"]])

        m1 = ar.mark()
        WIF = ar.bf16(NCH * 16).rearrange("p (k m) -> p k m", k=NCH)
        for K in range(NCH):
            kb.dma("pool", WIF[:, K, :], d["ml_w_in"][0, K * 128:(K + 1) * 128, 3072:3088], writes=[T["WIF"]])
        SQ = [ar.bf16(512) for _ in range(2)]
        LN = ar.f32(512)
        RS = ar.f32(512)
        tSQ = TokMap()
        tLN, tRS = Tok("ln"), Tok("rs")
        LI, LF, BB, AA = [ar.f32(512) for _ in range(4)]
        ABX = ar.f32(513)
        D0, D1, TMPg, MTg, EMg = [ar.f32(512) for _ in range(5)]
        import os as _os
        PRE = int(_os.environ.get("K_PRE", "1"))
        if PRE:
            self.bank_pe = {}
            kb.begin_defer()
        for ti, (t0, n) in enumerate(TILES):
            sample = t0 >= SEQ
            Lc = 4 if sample else 64
            nck = n // Lc

            def out_fn(c, rs, trs, ti=ti, t0=t0, n=n):
                self.stt(XN[:, c, t0:t0 + n], self.X[:, c, t0:t0 + n], self.vcol(gname, c), rs, ALU.mult, ALU.mult,
                         [self.tX[c, ti], trs, self.tVEC], [tXN[c, ti]])
            self.rmsnorm_tile(ti, gname, out_fn, (SQ, tSQ, LN, tLN, RS, tRS, self.bank()))
            bI, bF = self.bank(), self.bank()
            PI, PF = self.ps[bI][0:8, :n], self.ps[bF][0:8, :n]
            for K in range(NCH):
                self.mm(PI, WIF[:, K, 0:8], XN[:, K, t0:t0 + n], K == 0, K == NCH - 1, [T["WIF"], tXN[K, ti]], [self.tps[bI]])
            for K in range(NCH):
                self.mm(PF, WIF[:, K, 8:16], XN[:, K, t0:t0 + n], K == 0, K == NCH - 1, [T["WIF"], tXN[K, ti]], [self.tps[bF]])
            g = lambda a: a[0:8, 0:n]
            self.act(g(LI), PI, AF.Identity, [self.tps[bI], T["BIF"]], [T["LI"]], bias=BIF[0:8, 0:1])
            self.act(g(TMPg), PF, AF.Exp, [self.tps[bF], T["BIF"]], [T["TMP"]], scale=-1.0, bias=BIF[0:8, 2:3])
            self.act(g(LF), g(TMPg), AF.Ln, [T["TMP"], self.tCONST], [T["LF"]], bias=self.ONEC[0:8, :])
            self.memset("dve", g(D0), 1.0, [T["D0"]])
            init = 0.0
            rd = []
            if sample:
                self.memset("dve", g(D0).rearrange("p (s t) -> p s t", t=TS)[:, :, 0:1], 0.0, [T["D0"]])
            elif ti > 0:
                init = self.BBP[0:8, 0:1]
                rd = [T["BBprev"]]
            self.scan(g(BB), g(D0), g(LF), init, ALU.mult, ALU.subtract, [T["D0"], T["LF"]] + rd, [T["BB"]])
            self.tt("dve", g(AA), g(LI), g(BB), ALU.subtract, [T["LI"], T["BB"]], [T["AA"]])
            ab = ABX[0:8, 1:1 + n]
            if sample:
                self.memset("dve", g(D1), 0.0, [T["D1"]])
                self.memset("dve", g(D1).rearrange("p (s t) -> p s t", t=TS)[:, :, 0:1], NEG, [T["D1"]])
                a3 = g(AA).rearrange("p (s t) -> p s t", t=TS)
                self.tt("dve", a3[:, :, 0], a3[:, :, 0], M0T[0:8, :], ALU.max, [T["AA"], T["M0T"]], [T["AA"]])
                self.scan(ab, g(D1), g(AA), 0.0, ALU.add, ALU.max, [T["D1"], T["AA"]], [T["ABX"]])
                rho = M0T[0:8, :].unsqueeze(2).to_broadcast([8, NS, TS])
                rtok = [T["M0T"]]
                a_v = g(AA).rearrange("p (s t) -> p s t", t=TS)
                ab_v = ab.rearrange("p (s t) -> p s t", t=TS)
                ek_v = g(TMPg).rearrange("p (s t) -> p s t", t=TS)
                eq_v = g(D0).rearrange("p (s t) -> p s t", t=TS)
                self.tt("dve", g(AA), g(LI), g(BB), ALU.subtract, [T["LI"], T["BB"], T["ABX"]], [T["AA"]])
            else:
                self.memset("dve", g(D1), 0.0, [T["D1"]])
                if ti == 0:
                    self.memset("dve", ABX[0:8, 0:1], 0.0, [T["ABX"]])
                    ainit = 0.0
                else:
                    self.cp("dve", ABX[0:8, 0:1], self.ABP[0:8, 0:1], [T["ABprev"]], [T["ABX"]])
                    ainit = self.ABP[0:8, 0:1]
                self.scan(ab, g(D1), g(AA), ainit, ALU.add, ALU.max, [T["D1"], T["AA"], T["ABX"]] + ([T["ABprev"]] if ti else []), [T["ABX"]])
                rho = ABX[0:8, 0:n].rearrange("p (j l) -> p j l", l=64)[:, :, 0:1].to_broadcast([8, nck, 64])
                rtok = [T["ABX"]]
                a_v = g(AA).rearrange("p (j l) -> p j l", l=64)
                ab_v = ab.rearrange("p (j l) -> p j l", l=64)
                ek_v = g(TMPg).rearrange("p (j l) -> p j l", l=64)
                eq_v = g(D0).rearrange("p (j l) -> p j l", l=64)
            self.tt("dve", ek_v, a_v, rho, ALU.subtract, [T["AA"]] + rtok, [T["TMP"]])
            self.act(EKA[0:8, t0:t0 + n], g(TMPg), AF.Exp, [T["TMP"]], [tEK[ti]])
            self.tt("dve", eq_v, rho, ab_v, ALU.subtract, [T["ABX"], T["D0"]] + rtok, [T["D0"]])
            self.act(EQA[0:8, t0:t0 + n], g(D0), AF.Exp, [T["D0"]], [tEQ[ti]])
            self.tt("dve", g(MTg), g(BB), ab, ALU.add, [T["BB"], T["ABX"]], [T["MT"]])
            self.act(g(EMg), g(MTg), AF.Exp, [T["MT"]], [T["EM"]], scale=-1.0)
            bT = self.bank()
            for j in range(nck):
                self.tr(self.ps[bT][0:Lc, j * 8:(j + 1) * 8], EMg[0:8, j * Lc:(j + 1) * Lc], self.IDENTF[0:8, 0:8],
                        [T["EM"], self.tMASK], [self.tps[bT]])
            cb0 = t0 // 64 if not sample else 32
            if sample:
                self.cp("act", EMTS[0:Lc, 0:nck, :], self.ps[bT][0:Lc, 0:nck * 8].rearrange("p (j h) -> p j h", h=8),
                        [self.tps[bT]], [T["EMTS"]])
                self.cp("pool", MOUT[0:8, 1:17], g(MTg).rearrange("p (s t) -> p s t", t=TS)[:, :, 3], [T["MT"]], [T["MOUT"]])
            else:
                self.cp("act", self.EMTP[0:Lc, cb0:cb0 + nck, :], self.ps[bT][0:Lc, 0:nck * 8].rearrange("p (j h) -> p j h", h=8),
                        [self.tps[bT]], [T["EMTP"]])
                if ti == 3:
                    self.cp("pool", MOUT[0:8, 0:1], MTg[0:8, n - 1:n], [T["MT"]], [T["MOUT"]])
                self.cp("pool", self.BBP[0:8, 0:1], BB[0:8, n - 1:n], [T["BB"]], [T["BBprev"]])
                self.cp("pool", self.ABP[0:8, 0:1], ABX[0:8, n:n + 1], [T["ABX"]], [T["ABprev"]])
        if PRE:
            kb.end_defer()
            self.bank_pe = {}
        ar.release(m1)
        kb.barrier()
        kb.dma("sp", d["o_ml_m"], MOUT[0:8, :], reads=[T["MOUT"]])

        WIN = ar.bf16(NCH * 384).rearrange("p (k m) -> p k m", k=NCH)
        WO2 = [ar.bf16(D) for _ in range(2)]
        tWO = [Tok("mwo0"), Tok("mwo1")]
        RAW = [ar.f32(520) for _ in range(2)]
        ACC = [ar.f32(512) for _ in range(2)]
        SIL = [ar.f32(512) for _ in range(2)]
        SA = [dict(QP=ar.bf16(512), KP=ar.bf16(512), VA=ar.bf16(8 * 130).rearrange("p (j m) -> p j m", j=8),
                   KTt=ar.bf16(8 * 64).rearrange("p (j m) -> p j m", j=8), LAMB=ar.f32(512)) for _ in range(2)]
        SO = [ar.f32(512) for _ in range(3)]
        SH = [ar.f32(8 * 128).rearrange("p (j m) -> p j m", j=8) for _ in range(2)]
        STall = ar.bf16(8 * 64)
        SQH = ar.f32(8 * 128)
        STATH = ar.f32(32)
        DEN = ar.f32(16)
        HG = ar.bf16(512)
        C = ar.f32(130)
        Cd = ar.f32(130)
        Cb = ar.bf16(130)
        CS = ar.f32(4 * 130).rearrange("p (s m) -> p s m", s=4)
        CSd = ar.f32(4 * 130).rearrange("p (s m) -> p s m", s=4)
        CSb = ar.bf16(4 * 130).rearrange("p (s m) -> p s m", s=4)
        PCLb = ar.f32(8 * 130).rearrange("p (j m) -> p j m", j=8)
        CbAll = ar.bf16(9 * 130).rearrange("p (j m) -> p j m", j=9)
        main_tiles = [(t0, 512, 64, 8) for t0 in range(0, SEQ, 512)] + [(SEQ + 16 * q, 16, 4, 4) for q in range(4)]
        import os as _os
        if "ML_TILES" in _os.environ:
            main_tiles = [main_tiles[int(i)] for i in _os.environ["ML_TILES"].split(",")]
        units = []
        for h in range(int(_os.environ.get("ML_NH", 8))):
            for k_, (t0, n, L, NCk) in enumerate(main_tiles):
                units.append(dict(u=len(units), h=h, t0=t0, n=n, L=L, NCk=NCk, first=(k_ == 0),
                                  lastprompt=(t0 < SEQ and (k_ + 1 == len(main_tiles) or main_tiles[k_ + 1][0] >= SEQ))))
        for sa in SA:
            self.memset("pool", sa["VA"][0:64, :, 128:129], 1.0, [T["VAinit"]])

        def stageA(U):
            u, h, t0, n, L, NCk = U["u"], U["h"], U["t0"], U["n"], U["L"], U["NCk"]
            sa, k2, k3 = SA[u % 2], u % 2, u % 3
            QP, KP, VA, KTt, LAMB, Osig = sa["QP"], sa["KP"], sa["VA"], sa["KTt"], sa["LAMB"], SO[k3]
            tQP, tKP, tVA, tKTt, tLAM, tO = T["QP", k2], T["KP", k2], T["VA", k2], T["KTt", k2], T["LAM", k2], T["O", k3]
            sample = t0 >= SEQ
            ti5 = min(t0 // 512, 4)
            tsl = slice(t0, t0 + n)
            cs = lambda j: slice(j * L, (j + 1) * L)
            xt = [tXN[K, ti5] for K in range(NCH)]
            if U["first"]:
                for K in range(NCH):
                    rows = slice(K * 128, (K + 1) * 128)
                    kb.dma("pool", WIN[:, K, 0:64], d["ml_w_in"][0, rows, h * 64:(h + 1) * 64], writes=[T["WIN"]])
                    kb.dma("pool", WIN[:, K, 64:128], d["ml_w_in"][0, rows, 512 + h * 64:512 + (h + 1) * 64], writes=[T["WIN"]])
                    kb.dma("pool", WIN[:, K, 128:256], d["ml_w_in"][0, rows, 1024 + h * 128:1024 + (h + 1) * 128], writes=[T["WIN"]])
                    kb.dma("pool", WIN[:, K, 256:384], d["ml_w_in"][0, rows, 2048 + h * 128:2048 + (h + 1) * 128], writes=[T["WIN"]])
                kb.dma("pool", WO2[h % 2], d["ml_w_out"][0, h * 128:(h + 1) * 128, :], writes=[tWO[h % 2]])
                for w_ in range(2):
                    self.memset("pool", RAW[w_][0:64, 0:3], 0.0, [T["RAW", w_]])
                yield
            if sample:
                q4 = (t0 - SEQ) // 16
                for w_ in range(2):
                    rv = RAW[w_][0:64, 0:28].rearrange("p (s u) -> p s u", u=7)
                    kb.dma("sp", rv[:, :, 0:3], d["ml_convT"][h, w_, :, 4 * q4:4 * q4 + 4, :], writes=[T["RAW", w_]])
            bQ, bK, bO = self.bank("A"), self.bank("A"), self.bank("A")
            PQ, PK, PO_ = self.ps[bQ][0:64, :n], self.ps[bK][0:64, :n], self.ps[bO][:, :n]
            for (P_, cols, bb) in ((PQ, slice(0, 64), bQ), (PK, slice(64, 128), bK), (PO_, slice(256, 384), bO)):
                for K in range(NCH):
                    self.mm(P_, WIN[:, K, cols], XN[:, K, tsl], K == 0, K == NCH - 1, [T["WIN"], xt[K]], [self.tps[bb]])
                yield
            self.act(Osig[:, :n], PO_, AF.Sigmoid, [self.tps[bO]], [tO])
            for w_, (P_, bb) in enumerate(((PQ, bQ), (PK, bK))):
                mv = lambda k_: MLV[0:64, h, 5 * w_ + k_:5 * w_ + k_ + 1]
                R_ = RAW[w_]
                if sample:
                    rv = R_[0:64, 0:28].rearrange("p (s u) -> p s u", u=7)
                    self.cp("act", rv[:, :, 3:7], P_.rearrange("p (s t) -> p s t", t=TS), [self.tps[bb]], [T["RAW", w_]])
                    taps = [rv[:, :, k_:k_ + 4] for k_ in range(4)]
                    acc = ACC[w_][0:64, 0:n].rearrange("p (s t) -> p s t", t=TS)
                    self.cp("pool", CONVO[0:64, h, w_, 1 + 4 * q4:5 + 4 * q4, :], rv[:, :, 4:7], [T["RAW", w_]], [T["CONVO"]])
                else:
                    self.cp("act", R_[0:64, 3:3 + n], P_, [self.tps[bb]], [T["RAW", w_]])
                    taps = [R_[0:64, k_:k_ + n] for k_ in range(4)]
                    acc = ACC[w_][0:64, 0:n]
                yield
                self.ts("dve", acc, taps[0], mv(0), mv(4), ALU.mult, ALU.add, [T["RAW", w_], T["MLV"]], [T["ACC", w_]])
                for k_ in range(1, 4):
                    self.stt(acc, taps[k_], mv(k_), acc, ALU.mult, ALU.add, [T["RAW", w_], T["MLV"], T["ACC", w_]], [T["ACC", w_]])
                yield
                self.act(SIL[w_][0:64, 0:n], ACC[w_][0:64, 0:n], AF.Silu, [T["ACC", w_]], [T["SIL", w_]])
                if not sample:
                    if t0 + n == SEQ:
                        self.cp("pool", CONVO[0:64, h, w_, 0, :], R_[0:64, n:n + 3], [T["RAW", w_]], [T["CONVO"]])
                    self.cp("pool", R_[0:64, 0:3], R_[0:64, n:n + 3], [T["RAW", w_], T["ACC", w_]], [T["RAW", w_]])
                yield
            for j0 in range(0, NCk, 2):
                bV = self.bank("A")
                nj = min(2, NCk - j0)
                for jj in range(nj):
                    j = j0 + jj
                    PVt = self.ps[bV][0:L, jj * 128:(jj + 1) * 128]
                    for K in range(NCH):
                        self.mm(PVt, XN[:, K, t0 + j * L:t0 + (j + 1) * L], WIN[:, K, 128:256], K == 0, K == NCH - 1,
                                [T["WIN"], xt[K]], [self.tps[bV]])
                self.cp("act", VA[0:L, j0:j0 + nj, 0:128], self.ps[bV][0:L, 0:nj * 128].rearrange("p (j m) -> p j m", m=128),
                        [self.tps[bV], T["VAinit"]], [tVA])
                yield
            bM, bM2 = self.bank("A"), self.bank("A")
            PBK, PBQ = self.ps[bM][0:64, 0:n], self.ps[bM2][0:64, 0:n]
            tBQ = self.tps[bM2]
            self.mm(PBK, SEL[0:8, h, :], EKA[0:8, tsl], True, True, [T["SEL"], tEK[ti5]], [self.tps[bM]])
            self.mm(PBQ, SEL[0:8, h, :], EQA[0:8, tsl], True, True, [T["SEL"], tEQ[ti5]], [tBQ])
            yield
            self.tt("dve", KP[0:64, 0:n], SIL[1][0:64, 0:n], PBK, ALU.mult, [T["SIL", 1], self.tps[bM]], [tKP])
            self.stt(QP[0:64, 0:n], SIL[0][0:64, 0:n], 0.125, PBQ, ALU.mult, ALU.mult, [T["SIL", 0], tBQ], [tQP])
            self.cp("act", LAMB[0:64, 0:n], PBQ, [tBQ], [tLAM])
            yield
            bT = self.bank("A")
            PTr = self.ps[bT].bitcast(BF16)
            for j in range(NCk):
                self.tr(PTr[0:L, j * 64:(j + 1) * 64], KP[0:64, cs(j)], self.IDENTB[0:64, 0:64], [tKP, self.tMASK], [self.tps[bT]])
            self.cp("act", KTt[0:L, 0:NCk, :], PTr[0:L, 0:NCk * 64].rearrange("p (j m) -> p j m", m=64), [self.tps[bT]], [tKTt])
            yield

        def stageC(U):
            u, h, t0, n, L, NCk = U["u"], U["h"], U["t0"], U["n"], U["L"], U["NCk"]
            sa, k2 = SA[u % 2], u % 2
            QP, KP, VA, KTt, LAM = sa["QP"], sa["KP"], sa["VA"], sa["KTt"], sa["LAMB"]
            tQP, tKP, tVA, tKTt, tLAM = T["QP", k2], T["KP", k2], T["VA", k2], T["KTt", k2], T["LAM", k2]
            HT, tHT = SH[k2], T["HT", k2]
            sample = t0 >= SEQ
            cs = lambda j: slice(j * L, (j + 1) * L)
            if U["first"]:
                self.memset("pool", C[0:64], 0.0, [T["C"]])
                self.memset("pool", Cb[0:64], 0.0, [T["Cb"]])
            if sample:
                q4 = (t0 - SEQ) // 16
                for s in range(4):
                    kb.dma("sp", CS[0:64, s, 0:129], d["ml_C0T"][4 * q4 + s, h], writes=[T["CS", s]])
                yield
            PCL = PCLb[0:64, 0:NCk, 0:129]
            for j0 in range(0, NCk, 2):
                bP = self.bank("C")
                for jj in range(2):
                    j = j0 + jj
                    PCS = self.ps[bP][0:64, 256 * jj:256 * jj + 129]
                    self.mm(PCS, KTt[0:L, j, :], VA[0:L, j, 0:129], True, True, [tKTt, tVA], [self.tps[bP]])
                for jj in range(2):
                    j = j0 + jj
                    PCS = self.ps[bP][0:64, 256 * jj:256 * jj + 129]
                    lam = LAM[0:64, (j + 1) * L - 1:(j + 1) * L]
                    self.act(PCL[:, j, :], PCS, AF.Identity, [self.tps[bP], tLAM], [T["PCL"]], scale=lam)
                yield
            CbA = CbAll[0:64, 0:NCk + 1, 0:129]
            for j in range(NCk):
                lam = LAM[0:64, (j + 1) * L - 1:(j + 1) * L]
                if sample:
                    Cc, tC = CS[0:64, j, 0:129], T["CS", j]
                    self.cp("act", CbA[:, j, :], Cc, [tC], [T["CbA"]])
                    self.stt(Cc, Cc, lam, PCL[:, j, :], ALU.mult, ALU.add, [tC, tLAM, T["PCL"]], [tC])
                else:
                    Cc, tC = C[0:64, 0:129], T["C"]
                    if j == 0:
                        self.cp("act", CbA[:, 0, :], Cc, [tC], [T["CbA"]])
                    self.stt(CbA[:, j + 1, :], Cc, lam, PCL[:, j, :], ALU.mult, ALU.add, [tC, tLAM, T["PCL"]], [T["CbA"]])
                    self.stt(Cc, Cc, lam, PCL[:, j, :], ALU.mult, ALU.add, [tC, tLAM, T["PCL"]], [tC])
                yield
            bS = self.bank("C")
            PSTv = self.ps[bS][0:L, 0:NCk * L].rearrange("p (j t) -> p j t", j=NCk)
            for j in range(NCk):
                self.mm(PSTv[:, j, :], KP[0:64, cs(j)], QP[0:64, cs(j)], True, True, [tKP, tQP], [self.tps[bS]])
            yield
            STv = STall[0:L, 0:NCk * L].rearrange("p (j t) -> p j t", j=NCk)
            self.tt("dve", STv, PSTv, self.MIU[0:L, 0:L].unsqueeze(1).to_broadcast([L, NCk, L]), ALU.mult,
                    [self.tps[bS], self.tMASK], [T["ST"]])
            yield
            if sample:
                emall, tem = EMTS[0:L, 4 * q4:4 * q4 + NCk, h], T["EMTS"]
            else:
                emall, tem = self.EMTP[0:L, t0 // 64:t0 // 64 + NCk, h], T["EMTP"]
            for g0 in range(0, NCk, 3):
                g = min(3, NCk - g0)
                bN = self.bank("C")
                tN = self.tps[bN]
                PNDv = self.ps[bN][0:L, 0:g * 129].rearrange("p (j m) -> p j m", j=g)
                for jj in range(g):
                    j = g0 + jj
                    self.mm(PNDv[:, jj, :], QP[0:64, cs(j)], CbA[:, j, :], True, False, [tQP, T["CbA"]], [tN])
                    self.mm(PNDv[:, jj, :], STv[:, j, :], VA[0:L, j, 0:129], False, True, [T["ST"], tVA], [tN])
                yield
                dn = DEN[0:L, g0:g0 + g]
                self.act(dn, PNDv[:, :, 128], AF.Abs, [tN], [T["DEN"]])
                yield
                self.tt("dve", dn, dn, emall[:, g0:g0 + g], ALU.max, [T["DEN"], tem], [T["DEN"]])
                self.recip(dn, dn, [T["DEN"]], [T["DEN"]])
                yield
                self.tt("dve", HT[0:L, g0:g0 + g, :], PNDv[:, :, 0:128], dn.unsqueeze(2).to_broadcast([L, g, 128]), ALU.mult,
                        [tN, T["DEN"]], [tHT])
                yield
            if sample:
                for s in range(4):
                    kb.dma("sp", d["o_ml_Cs"][4 * q4 + s, h], CS[0:64, s, 0:129], reads=[T["CS", s]])
            if U["lastprompt"]:
                kb.dma("sp", d["o_ml_Cp"][h], C[0:64, 0:129], reads=[T["C"]])

        def stageD(U):
            u, h, t0, n, L, NCk = U["u"], U["h"], U["t0"], U["n"], U["L"], U["NCk"]
            k2, k3 = u % 2, u % 3
            HT, tHT = SH[k2], T["HT", k2]
            Osig, tO = SO[k3], T["O", k3]
            WOh = WO2[h % 2]
            ti5 = min(t0 // 512, 4)
            tsl = slice(t0, t0 + n)
            cs = lambda j: slice(j * L, (j + 1) * L)
            G = NCk
            HTf = HT[0:L, 0:G, :]
            SQv = SQH[0:L, 0:G * 128].rearrange("p (j m) -> p j m", m=128)
            self.act(SQv, HTf, AF.Square, [tHT], [T["SQH"]])
            yield
            self.reduce(STATH[0:L, 0:G], SQv, ALU.add, [T["SQH"]], [T["STATH"]])
            yield
            self.act(STATH[0:L, 0:G], STATH[0:L, 0:G], AF.Ln, [T["STATH"], self.tCONST], [T["STATH"]], scale=1.0 / 128, bias=self.EPSC[0:L, :])
            yield
            self.act(STATH[0:L, 0:G], STATH[0:L, 0:G], AF.Exp, [T["STATH"]], [T["STATH"]], scale=-0.5)
            yield
            self.tt("dve", HTf, HTf, STATH[0:L, 0:G].unsqueeze(2).to_broadcast([L, G, 128]), ALU.mult, [tHT, T["STATH"]], [tHT])
            yield
            bY = self.bank("C2")
            PYF = self.ps[bY][:, 0:n]
            for j in range(NCk):
                self.tr(PYF[:, cs(j)], HT[0:L, j, :], self.IDENTF[0:L, 0:L], [tHT, self.tMASK], [self.tps[bY]])
            yield
            self.stt(HG[:, 0:n], PYF, self.vcol("ml_norm_w", h), Osig[:, 0:n], ALU.mult, ALU.mult, [self.tps[bY], tO, self.tVEC], [T["HG"]])
            yield
            for dc in range(NCH):
                bO2 = self.bank("C2")
                PO2 = self.ps[bO2][:, 0:n]
                self.mm(PO2, WOh[:, dc * 128:(dc + 1) * 128], HG[:, 0:n], True, True, [tWO[h % 2], T["HG"]], [self.tps[bO2]])
                self.tt("dve", self.X[:, dc, tsl], PO2, self.X[:, dc, tsl], ALU.add, [self.tps[bO2], self.tX[dc, ti5]], [self.tX[dc, ti5]])
                yield

        def drain(gens):
            gens = [g for g in gens if g is not None]
            while gens:
                for g in list(gens):
                    try:
                        next(g)
                    except StopIteration:
                        gens.remove(g)

        NU_ = len(units)
        SCHED = int(_os.environ.get("K_SCHED", "1"))
        if SCHED:
            self.bank_pe = {}
            kb.begin_defer()
        for s in range(NU_ + 2):
            gA = stageA(units[s]) if s < NU_ else None
            gC = stageC(units[s - 1]) if 0 <= s - 1 < NU_ else None
            gD = stageD(units[s - 2]) if 0 <= s - 2 < NU_ else None
            drain([gC, gD, gA])
        if SCHED:
            kb.end_defer()
            self.bank_pe = {}
        kb.dma("sp", d["o_ml_conv"], CONVO[0:64], reads=[T["CONVO"]])
        ar.release(m0)
        kb.barrier()

    def ffn(self, L, which):
        kb, d, ar = self.kb, self.d, self.ar
        m = ar.mark()
        wg = d["ff%s_wg" % which][L]
        wu = d["ff%s_wu" % which][L]
        wd = d["ff%s_wd" % which][L]
        gname = "norm_ff%s%d" % (which, L)
        G = 4
        groups = [(f0, min(G, NFF - f0)) for f0 in range(0, NFF, G)]
        WG = [ar.bf16(NCH * 512).rearrange("p (c f) -> p c f", c=NCH) for _ in range(2)]
        WU = [ar.bf16(NCH * 512).rearrange("p (c f) -> p c f", c=NCH) for _ in range(2)]
        WD = [ar.bf16(G * D).rearrange("p (f o) -> p f o", f=G) for _ in range(2)]
        H = [ar.bf16(G * 512).rearrange("p (f n) -> p f n", f=G) for _ in range(2)]
        SG = [ar.f32(512) for _ in range(2)]
        SQ = [ar.bf16(512) for _ in range(2)]
        LN = ar.f32(512)
        RS = ar.f32(512)
        tW = TokMap()
        tH = TokMap()
        tSG = TokMap()
        tSQ = TokMap()
        tLN, tRS = Tok("ln"), Tok("rs")

        def load_group(gi):
            f0, nf = groups[gi]
            s = gi % 2
            for c in range(NCH):
                kb.dma("pool", WG[s][:, c, 0:nf * 128], wg[c * 128:(c + 1) * 128, f0 * 128:(f0 + nf) * 128],
                       writes=[tW["g", s, c]])
                kb.dma("pool", WU[s][:, c, 0:nf * 128], wu[c * 128:(c + 1) * 128, f0 * 128:(f0 + nf) * 128],
                       writes=[tW["u", s, c]])
            for fi in range(nf):
                kb.dma("pool", WD[s][:, fi, :], wd[(f0 + fi) * 128:(f0 + fi + 1) * 128, :], writes=[tW["d", s, fi]])

        load_group(0)
        load_group(1)

        for ti, (t0, n) in enumerate(TILES):
            def out_fn(c, rs, trs, ti=ti, t0=t0, n=n):
                kb.emit("dve", lambda e: e.scalar_tensor_tensor(out=self.XN[:, c, t0:t0 + n], in0=self.X[:, c, t0:t0 + n],
                                                                scalar=self.vcol(gname, c), in1=rs,
                                                                op0=ALU.mult, op1=ALU.mult),
                        reads=[self.tX[c, ti], trs, self.tVEC], writes=[self.tXN[c, ti]])
            self.rmsnorm_tile(ti, gname, out_fn, (SQ, tSQ, LN, tLN, RS, tRS, 4 + ti % 4))

        items = [(gi, ti) for gi in range(len(groups)) for ti in range(len(TILES))]
        po_rr = [0]

        def up(idx):
            gi, ti = items[idx]
            f0, nf = groups[gi]
            s = gi % 2
            hs = idx % 2
            t0, n = TILES[ti]
            for fi in range(nf):
                b = fi % 2
                pg = self.ps[b][:, :n]
                pu = self.ps[2 + b][:, :n]
                for c in range(NCH):
                    kb.emit("pe", lambda e, c=c, fi=fi, pg=pg: e.matmul(pg, WG[s][:, c, fi * 128:(fi + 1) * 128],
                                                                     self.XN[:, c, t0:t0 + n], start=(c == 0), stop=(c == NCH - 1)),
                            reads=[tW["g", s, c], self.tXN[c, ti]], writes=[self.tps[b]], signal=(c == NCH - 1))
                for c in range(NCH):
                    kb.emit("pe", lambda e, c=c, fi=fi, pu=pu: e.matmul(pu, WU[s][:, c, fi * 128:(fi + 1) * 128],
                                                                     self.XN[:, c, t0:t0 + n], start=(c == 0), stop=(c == NCH - 1)),
                            reads=[tW["u", s, c], self.tXN[c, ti]], writes=[self.tps[2 + b]], signal=(c == NCH - 1))
                kb.emit("act", lambda e, b=b, pg=pg: e.activation(out=SG[b][:, :n], in_=pg, func=AF.Silu),
                        reads=[self.tps[b]], writes=[tSG[b]])
                kb.emit("dve", lambda e, b=b, fi=fi, pu=pu: e.tensor_tensor(out=H[hs][:, fi, :n], in0=SG[b][:, :n], in1=pu, op=ALU.mult),
                        reads=[tSG[b], self.tps[2 + b]], writes=[tH[hs, fi]])

        def down(idx):
            gi, ti = items[idx]
            f0, nf = groups[gi]
            s = gi % 2
            hs = idx % 2
            t0, n = TILES[ti]
            for dc in range(NCH):
                bank = 4 + po_rr[0] % 4
                po_rr[0] += 1
                po = self.ps[bank][:, :n]
                for fi in range(nf):
                    kb.emit("pe", lambda e, fi=fi, dc=dc, po=po: e.matmul(po, WD[s][:, fi, dc * 128:(dc + 1) * 128], H[hs][:, fi, :n],
                                                                       start=(fi == 0), stop=(fi == nf - 1)),
                            reads=[tW["d", s, fi], tH[hs, fi]], writes=[self.tps[bank]], signal=(fi == nf - 1))
                kb.emit("dve", lambda e, dc=dc, po=po: e.scalar_tensor_tensor(out=self.X[:, dc, t0:t0 + n], in0=po, scalar=0.5,
                                                                           in1=self.X[:, dc, t0:t0 + n], op0=ALU.mult, op1=ALU.add),
                        reads=[self.tps[bank], self.tX[dc, ti]], writes=[self.tX[dc, ti]])

        ntile = len(TILES)
        for idx in range(len(items)):
            up(idx)
            if idx > 0:
                down(idx - 1)
                gi_prev, ti_prev = items[idx - 1]
                if ti_prev == ntile - 1 and gi_prev + 2 < len(groups):
                    load_group(gi_prev + 2)
        down(len(items) - 1)
        ar.release(m)
        kb.barrier()

    def final_norm(self):
        kb, d, ar = self.kb, self.d, self.ar
        m = ar.mark()
        SQ = [ar.bf16(512) for _ in range(2)]
        LN = ar.f32(512)
        RS = ar.f32(512)
        Y = [ar.f32(512) for _ in range(4)]
        tSQ, tY = TokMap(), TokMap()
        tLN, tRS = Tok("ln"), Tok("rs")
        rr = [0]
        for ti, (t0, n) in enumerate(TILES):
            def out_fn(c, rs, trs, ti=ti, t0=t0, n=n):
                s = rr[0] % 4
                rr[0] += 1
                kb.emit("dve", lambda e: e.scalar_tensor_tensor(out=Y[s][:, :n], in0=self.X[:, c, t0:t0 + n],
                                                                scalar=self.vcol("norm_final", c), in1=rs,
                                                                op0=ALU.mult, op1=ALU.mult),
                        reads=[self.tX[c, ti], trs, self.tVEC], writes=[tY[s]])
                kb.dma("sp", d["yT"][c * 128:(c + 1) * 128, t0:t0 + n], Y[s][:, :n], reads=[tY[s]])
            self.rmsnorm_tile(ti, "norm_final", out_fn, (SQ, tSQ, LN, tLN, RS, tRS, 4 + ti % 4))
        ar.release(m)
        kb.barrier()

    def dump_x(self):
        kb, d = self.kb, self.d
        for c in range(NCH):
            for ti, (t0, n) in enumerate(TILES):
                kb.dma("sp", d["yT"][c * 128:(c + 1) * 128, t0:t0 + n], self.X[:, c, t0:t0 + n], reads=[self.tX[c, ti]])

    def build(self):
        st = self.stages
        self.load_inputs()
        self.consts()
        for L in range(2):
            if "ffa%d" % L in st:
                self.ffn(L, "a")
            if "mix%d" % L in st and L == 0:
                self.rwkv()
            if "mix%d" % L in st and L == 1:
                self.mlstm()
            if "ffb%d" % L in st:
                self.ffn(L, "b")
        if "final" in st:
            self.final_norm()
        else:
            self.dump_x()
        self.kb.flush()
        return self.nc


ALL_STAGES = ("ffa0", "mix0", "ffb0", "ffa1", "mix1", "ffb1", "final")


def pack_vecs(inp):
    vecs = np.zeros((NVEC, D), np.float32)

    def put(name, v):
        vecs[VID[name]] = np.asarray(v, np.float32).reshape(D)

    for L in range(2):
        put("norm_ffa%d" % L, inp["norm_ffa"][L])
        put("norm_mix%d" % L, inp["norm_mix"][L])
        put("norm_ffb%d" % L, inp["norm_ffb"][L])
    put("norm_final", inp["norm_final"])
    for i in range(6):
        put("mu%d" % i, inp["rw_mu"][0, i])
    put("w0", inp["rw_w0"][0])
    put("a0", inp["rw_a0"][0])
    put("k_k", inp["rw_k_k"][0])
    put("k_a", inp["rw_k_a"][0])
    put("r_k", inp["rw_r_k"][0])
    put("gn_w", inp["rw_gn_w"][0])
    put("gn_b", inp["rw_gn_b"][0])
    for j in range(4):
        put("cw%d" % j, inp["ml_conv_w"][0, j])
    put("cb", inp["ml_conv_b"][0])
    put("ml_norm_w", inp["ml_norm_w"][0])
    return np.ascontiguousarray(vecs.reshape(NVEC, NCH, 128).transpose(2, 0, 1).reshape(128, NVEC * NCH))


def make_in_maps(inp):
    vecs = pack_vecs(inp)
    shared = {"vecs": vecs}
    for nm in ("ffa_wg", "ffa_wu", "ffb_wg", "ffb_wu", "ffa_wd", "ffb_wd"):
        shared[nm] = np.ascontiguousarray(inp[nm], dtype=np.float32)
    cw = np.asarray(inp["ml_conv_w"][0], np.float32)
    cbv = np.asarray(inp["ml_conv_b"][0], np.float32)
    mlv = np.zeros((64, 8, 10), np.float32)
    for w in range(2):
        for k in range(4):
            mlv[:, :, 5 * w + k] = cw[k, w * 512:(w + 1) * 512].reshape(8, 64).T
        mlv[:, :, 5 * w + 4] = cbv[w * 512:(w + 1) * 512].reshape(8, 64).T
    shared["mlv"] = mlv
    shared["bif"] = np.ascontiguousarray(np.asarray(inp["ml_b_if"][0], np.float32).reshape(2, 8).T)
    for nm in ("rw_wr", "rw_wk", "rw_wv", "rw_wo", "rw_w1", "rw_w2", "rw_a1", "rw_a2", "rw_g1", "rw_g2", "ml_w_in", "ml_w_out"):
        shared[nm] = np.ascontiguousarray(inp[nm], dtype=np.float32)
    maps = []
    for core in range(NCORES):
        xs = np.concatenate([inp["x_prompt"][core], inp["x_sample"][core * NS:(core + 1) * NS].reshape(NS * TS, D)], axis=0)
        m = dict(shared)
        m["xT"] = np.ascontiguousarray(xs.T.astype(np.float32))
        sq = slice(core * NS, (core + 1) * NS)
        sh = inp["state_rwkv_shift"][0, sq]
        m["shiftT"] = np.ascontiguousarray(sh.reshape(NS, NCH, 128).transpose(2, 1, 0).astype(np.float32))
        m["rw_S0T"] = np.ascontiguousarray(inp["state_rwkv_S"][0, sq].transpose(0, 1, 3, 2).astype(np.float32))
        m["ml_m0T"] = np.ascontiguousarray(inp["state_mlstm_m"][0, sq].T.astype(np.float32))
        c0 = np.concatenate([inp["state_mlstm_C"][0, sq].transpose(0, 1, 3, 2), inp["state_mlstm_n"][0, sq][..., None]], axis=-1)
        m["ml_C0T"] = np.ascontiguousarray(c0.astype(np.float32))
        cv = inp["state_mlstm_conv"][0, sq]
        m["ml_convT"] = np.ascontiguousarray(cv.reshape(NS, 3, 2, 8, 64).transpose(3, 2, 4, 0, 1).astype(np.float32))
        maps.append(m)
    return maps


def run(inp, stages=ALL_STAGES, trace=False):
    b = Builder(stages)
    nc = b.build()
    maps = make_in_maps(inp)
    res = run_bass_kernel_spmd(nc, maps, core_ids=list(range(NCORES)), trace=trace)
    return b, res


def assemble(results):
    f = np.float32
    yp = np.zeros((8, SEQ, D), f); ys = np.zeros((128, TS, D), f)
    p_S = np.zeros((1, 8, 16, 64, 64), f); p_sh = np.zeros((1, 8, D), f)
    p_C = np.zeros((1, 8, 8, 128, 64), f); p_n = np.zeros((1, 8, 8, 64), f); p_m = np.zeros((1, 8, 8), f)
    p_cv = np.zeros((1, 8, 3, D), f)
    s_S = np.zeros((1, 128, 16, 64, 64), f); s_sh = np.zeros((1, 128, D), f)
    s_C = np.zeros((1, 128, 8, 128, 64), f); s_n = np.zeros((1, 128, 8, 64), f); s_m = np.zeros((1, 128, 8), f)
    s_cv = np.zeros((1, 128, 3, D), f)
    for core in range(NCORES):
        r = results[core]
        sq = slice(core * NS, (core + 1) * NS)
        y = r["yT"].T
        yp[core] = y[:SEQ]
        ys[sq] = y[SEQ:].reshape(NS, TS, D)
        sho = r["o_shift"]
        p_sh[0, core] = sho[:, :, 0].T.reshape(D)
        s_sh[0, sq] = sho[:, :, 1:].transpose(2, 1, 0).reshape(NS, D)
        p_S[0, core] = r["o_rw_Sp"].transpose(0, 2, 1)
        s_S[0, sq] = r["o_rw_Ss"].transpose(0, 1, 3, 2)
        cp = r["o_ml_Cp"]
        p_C[0, core] = cp[:, :, 0:128].transpose(0, 2, 1)
        p_n[0, core] = cp[:, :, 128]
        cs_ = r["o_ml_Cs"]
        s_C[0, sq] = cs_[:, :, :, 0:128].transpose(0, 1, 3, 2)
        s_n[0, sq] = cs_[:, :, :, 128]
        mo = r["o_ml_m"]
        p_m[0, core] = mo[:, 0]
        s_m[0, sq] = mo[:, 1:].T
        cv = r["o_ml_conv"]
        cvt = cv.transpose(3, 4, 2, 1, 0).reshape(17, 3, D)
        p_cv[0, core] = cvt[0]
        s_cv[0, sq] = cvt[1:]
    return (yp, ys, p_S, p_sh, p_C, p_n, p_m, p_cv, s_S, s_sh, s_C, s_n, s_m, s_cv)


def kernel(**inp):
    b, res = run(inp)
    return assemble(res.results)
```

```python
import numpy as np
from contextlib import ExitStack
import concourse.bass as bass
import concourse.mybir as mybir
from concourse.bass_utils import run_bass_kernel_spmd

F32 = mybir.dt.float32
BF16 = mybir.dt.bfloat16
AF = mybir.ActivationFunctionType
ALU = mybir.AluOpType
AX = mybir.AxisListType

NCORES = 8
D = 1024
NCH = 8
SEQ = 2048
NS = 16
TS = 4
NT = SEQ + NS * TS
DFF = 2816
NFF = DFF // 128
EPS = 1e-6

ENGS = ("pe", "act", "dve", "pool", "sp")
SAME_ENGINE_SYNC = True

VID = {}
_v = 0
for _nm in ("norm_ffa0", "norm_ffa1", "norm_mix0", "norm_mix1", "norm_ffb0", "norm_ffb1", "norm_final",
            "mu0", "mu1", "mu2", "mu3", "mu4", "mu5", "w0", "a0", "k_k", "k_a", "r_k", "gn_w", "gn_b",
            "cw0", "cw1", "cw2", "cw3", "cb", "ml_norm_w"):
    VID[_nm] = _v
    _v += 1
NVEC = _v


class Tok:
    __slots__ = ("name", "w", "r", "excl")

    def __init__(self, name="", excl=False):
        self.name = name
        self.w = []
        self.r = []
        self.excl = excl


class TokMap(dict):
    def __missing__(self, key):
        t = Tok(str(key))
        self[key] = t
        return t


class KB:
    def __init__(self, n_dma_sems=12):
        self.nc = bass.Bass("TRN2", target_bir_lowering=False, dynamic_dma_scratch_size=4096)
        self.es = ExitStack()
        nc = self.nc
        self.sem = {}
        self.count = {}
        self.prog = {e: [] for e in ENGS}
        self.waited = {e: {} for e in ENGS}
        for e in ENGS:
            self.sem[e] = self.es.enter_context(nc.semaphore("s_" + e))
            self.count[e] = 0
        self.dsem = {}
        self.dval = {}
        self.dnext = {}
        for q in ("sp", "pool", "act"):
            self.dsem[q] = []
            for j in range(n_dma_sems):
                key = "d_%s_%d" % (q, j)
                self.sem[key] = self.es.enter_context(nc.semaphore(key))
                self.dsem[q].append(key)
                self.dval[key] = 0
            self.dnext[q] = 0
        self.ninstr = 0
        self.defer = None

    def _wait(self, eng, semkey, value):
        if value <= 0:
            return
        if self.waited[eng].get(semkey, 0) >= value:
            return
        self.waited[eng][semkey] = value
        sem = self.sem[semkey]
        self.prog[eng].append(lambda e, sem=sem, value=value: e.wait_ge(sem, value))

    def _deps(self, eng, reads, writes):
        deps = set()
        for t in reads:
            deps.update(t.w)
            if t.excl:
                deps.update(x for x in t.r if x[0] != eng)
        for t in writes:
            deps.update(t.w)
            deps.update(t.r)
        for (sk, v) in deps:
            if sk == eng and (eng == "pe" or not SAME_ENGINE_SYNC):
                continue
            self._wait(eng, sk, v)

    def emit(self, eng, fn, reads=(), writes=(), signal=True, dur=0.4, after=None):
        if self.defer is not None:
            self.defer.append(dict(kind="op", eng=eng, fn=fn, reads=tuple(reads), writes=tuple(writes), signal=signal,
                                   dur=dur, extra=[after] if after is not None else []))
            self.ninstr += 1
            return len(self.defer) - 1
        self._deps(eng, reads, writes)
        if after is not None:
            self._wait(eng, after[0], after[1])
        self.ninstr += 1
        if signal:
            self.count[eng] += 1
            cid = (eng, self.count[eng])
            sem = self.sem[eng]
            self.prog[eng].append(lambda e, fn=fn, sem=sem: fn(e).then_inc(sem, 1))
        else:
            cid = (eng, self.count[eng] + 1)
            self.prog[eng].append(lambda e, fn=fn: fn(e))
        for t in reads:
            t.r.append(cid)
        for t in writes:
            t.w = [cid]
            t.r = []
        return cid

    def dma(self, q, out, in_, reads=(), writes=()):
        if self.defer is not None:
            self.defer.append(dict(kind="dma", eng=q, out=out, in_=in_, reads=tuple(reads), writes=tuple(writes), signal=True,
                                   dur=0.1, extra=[]))
            self.ninstr += 1
            return len(self.defer) - 1
        self._deps(q, reads, writes)
        return self._dma_issue(q, out, in_, reads, writes)

    def _dma_issue(self, q, out, in_, reads, writes):
        j = self.dnext[q]
        self.dnext[q] = (j + 1) % len(self.dsem[q])
        key = self.dsem[q][j]
        self._wait(q, key, self.dval[key])
        self.dval[key] += 16
        cid = (key, self.dval[key])
        sem = self.sem[key]
        self.ninstr += 1
        self.prog[q].append(lambda e, out=out, in_=in_, sem=sem: e.dma_start(out=out, in_=in_).then_inc(sem, 16))
        for t in reads:
            t.r.append(cid)
        for t in writes:
            t.w = [cid]
            t.r = []
        return cid

    def begin_defer(self):
        self.barrier()
        self.defer = []

    def end_defer(self):
        recs = self.defer
        self.defer = None
        self._schedule(recs)
        self.barrier()

    def _schedule(self, recs):
        n = len(recs)
        lastw, readers = {}, {}
        deps = [None] * n
        for i, r in enumerate(recs):
            e = r["eng"]
            dset = set(r["extra"])
            for t in r["reads"]:
                k = id(t)
                if k in lastw:
                    dset.add(lastw[k])
                if t.excl:
                    dset.update(j for j in readers.get(k, ()) if recs[j]["eng"] != e)
            for t in r["writes"]:
                k = id(t)
                if k in lastw:
                    dset.add(lastw[k])
                dset.update(readers.get(k, ()))
            dset.discard(i)
            deps[i] = dset
            for t in r["reads"]:
                readers.setdefault(id(t), []).append(i)
            for t in r["writes"]:
                lastw[id(t)] = i
                readers[id(t)] = []
        succs = [[] for _ in range(n)]
        indeg = [0] * n
        for i in range(n):
            indeg[i] = len(deps[i])
            for j in deps[i]:
                succs[j].append(i)
        import os as _os
        LAT_X = float(_os.environ.get("K_LATX", "0.15"))
        LAT_S = float(_os.environ.get("K_LATS", "0.05"))
        DSC = float(_os.environ.get("K_DSC", "1.5"))
        CP = int(_os.environ.get("K_CP", "1"))
        tail = [0.0] * n
        for i in range(n - 1, -1, -1):
            ri = recs[i]
            t_ = 0.0
            for j in succs[i]:
                lat = LAT_S if recs[j]["eng"] == ri["eng"] else LAT_X
                if tail[j] + lat > t_:
                    t_ = tail[j] + lat
            tail[i] = ri["dur"] * DSC + t_
        ready_t = [0.0] * n
        start = [0.0] * n
        free = {e: 0.0 for e in ENGS}
        ready = {e: [] for e in ENGS}
        for i in range(n):
            if indeg[i] == 0:
                ready[recs[i]["eng"]].append(i)
        done = 0
        while done < n:
            best, bkey = None, None
            for e in ENGS:
                lst = ready[e]
                if not lst:
                    continue
                fe = free[e]
                if CP:
                    cand = min(lst, key=lambda i: (max(fe, ready_t[i]), -tail[i], i))
                    key = (max(fe, ready_t[cand]), -tail[cand], cand)
                else:
                    cand = min(lst, key=lambda i: (max(fe, ready_t[i]), i))
                    key = (max(fe, ready_t[cand]), 0.0, cand)
                if bkey is None or key < bkey:
                    best, bkey = cand, key
            i = best
            r = recs[i]
            e = r["eng"]
            ready[e].remove(i)
            st = bkey[0]
            start[i] = st
            du = r["dur"] * DSC
            free[e] = st + du
            fin = st + du + (2.5 if r["kind"] == "dma" else 0.0)
            for j in succs[i]:
                lat = LAT_S if recs[j]["eng"] == e else LAT_X
                if fin + lat > ready_t[j]:
                    ready_t[j] = fin + lat
                indeg[j] -= 1
                if indeg[j] == 0:
                    ready[recs[j]["eng"]].append(j)
            done += 1
        order = sorted(range(n), key=lambda i: (start[i], i))
        cids = [None] * n
        cnt = self.count["pe"]
        for i in order:
            r = recs[i]
            if r["eng"] == "pe" and r["kind"] == "op" and r["signal"]:
                cnt += 1
                cids[i] = ("pe", cnt)
        nxt_sig = None
        for i in range(n - 1, -1, -1):
            r = recs[i]
            if r["eng"] != "pe" or r["kind"] != "op":
                continue
            if r["signal"]:
                nxt_sig = i
            else:
                assert nxt_sig is not None, "deferred PE stream must end with a signalled instruction"
                cids[i] = cids[nxt_sig]
        for i in order:
            r = recs[i]
            e = r["eng"]
            for j in deps[i]:
                sk, v = cids[j]
                if sk == e and j not in r["extra"] and (e == "pe" or not SAME_ENGINE_SYNC):
                    continue
                self._wait(e, sk, v)
            if r["kind"] == "dma":
                cids[i] = self._dma_issue(e, r["out"], r["in_"], (), ())
                continue
            fn = r["fn"]
            if r["signal"]:
                self.count[e] += 1
                cid = (e, self.count[e])
                if e == "pe":
                    assert cid == cids[i], (cid, cids[i])
                cids[i] = cid
                sem = self.sem[e]
                self.prog[e].append(lambda eng_, fn=fn, sem=sem: fn(eng_).then_inc(sem, 1))
            else:
                self.prog[e].append(lambda eng_, fn=fn: fn(eng_))
        self.sched_span = max(start) if n else 0.0

    def barrier(self):
        for e in ENGS:
            for e2 in ENGS:
                if e2 != e:
                    self._wait(e, e2, self.count[e2])
            for q in self.dsem:
                for key in self.dsem[q]:
                    self._wait(e, key, self.dval[key])

    def flush(self):
        self.barrier()
        nc = self.nc
        prog = self.prog
        with nc.Block() as block:
            @block.tensor
            def _(e):
                for f in prog["pe"]:
                    f(e)

            @block.scalar
            def _(e):
                for f in prog["act"]:
                    f(e)

            @block.vector
            def _(e):
                for f in prog["dve"]:
                    f(e)

            @block.gpsimd
            def _(e):
                for f in prog["pool"]:
                    f(e)

            @block.sync
            def _(e):
                for f in prog["sp"]:
                    f(e)
        self.prog = {e: [] for e in ENGS}


class Arena:
    def __init__(self, ap, nwords):
        self.ap = ap
        self.n = nwords
        self.top = 0

    def mark(self):
        return self.top

    def release(self, m):
        self.top = m

    def f32(self, nwords):
        assert self.top + nwords <= self.n, ("arena overflow", self.top, nwords, self.n)
        a = self.ap[:, self.top:self.top + nwords]
        self.top += nwords
        return a

    def bf16(self, nelem):
        nwords = (nelem + 1) // 2
        a = self.f32(nwords).bitcast(BF16)
        return a[:, 0:nelem]


ARENA_WORDS = 56280
XNW = 1 + SEQ + NS * (TS + 1)
SOFF = 1 + SEQ
TILES = [(0, 512), (512, 512), (1024, 512), (1536, 512), (2048, 64)]


class Builder:
    def __init__(self, stages):
        self.stages = stages
        self.kb = KB()
        kb = self.kb
        nc = kb.nc
        self.nc = nc
        es = kb.es
        d = {}

        def din(name, shape):
            d[name] = nc.dram_tensor(name, list(shape), F32, kind="ExternalInput").ap()

        def dout(name, shape):
            d[name] = nc.dram_tensor(name, list(shape), F32, kind="ExternalOutput").ap()

        din("xT", (D, NT))
        din("vecs", (128, NVEC * 8))
        for nm in ("ffa_wg", "ffa_wu", "ffb_wg", "ffb_wu"):
            din(nm, (2, D, DFF))
        for nm in ("ffa_wd", "ffb_wd"):
            din(nm, (2, DFF, D))
        for nm in ("rw_wr", "rw_wk", "rw_wv", "rw_wo"):
            din(nm, (1, D, D))
        din("rw_w1", (1, D, 64)); din("rw_w2", (1, 64, D))
        din("rw_a1", (1, D, 64)); din("rw_a2", (1, 64, D))
        din("rw_g1", (1, D, 160)); din("rw_g2", (1, 160, D))
        din("ml_w_in", (1, D, 3088)); din("ml_w_out", (1, D, D))
        din("mlv", (64, 8, 10)); din("bif", (8, 2)); din("ml_m0T", (8, NS))
        din("ml_C0T", (NS, 8, 64, 129)); din("ml_convT", (8, 2, 64, NS, 3))
        din("shiftT", (128, NCH, NS))
        din("rw_S0T", (NS, 16, 64, 64))
        dout("yT", (D, NT))
        dout("o_ml_m", (8, 17)); dout("o_ml_Cp", (8, 64, 129)); dout("o_ml_Cs", (NS, 8, 64, 129))
        dout("o_ml_conv", (64, 8, 2, 17, 3))
        dout("o_shift", (128, NCH, 17))
        dout("o_rw_Sp", (16, 64, 64))
        dout("o_rw_Ss", (NS, 16, 64, 64))
        self.d = d
        self.out_names = [k for k in d if k == "yT" or k.startswith("o_")]

        arena_t = es.enter_context(nc.sbuf_tensor("arena", [128, ARENA_WORDS], F32))
        self.ar = Arena(arena_t, ARENA_WORDS)
        self.psall = es.enter_context(nc.psum_tensor("psall", [128, 8, 512], F32))
        self.ps = [self.psall[:, i, :] for i in range(8)]
        self.tps = [Tok("ps%d" % i, excl=True) for i in range(8)]
        self.bank_rr = 0
        self.bank_pe = {}

        ar = self.ar
        self.X = ar.f32(NCH * NT).rearrange("p (c n) -> p c n", c=NCH)
        self.tX = TokMap()
        self.VEC = ar.f32(NVEC * 8)
        self.tVEC = Tok("vec")
        self.ONES = ar.bf16(128)
        self.tONES = Tok("ones")
        self.XNraw = ar.bf16(NCH * XNW)
        self.XN = self.XNraw[:, 0:NCH * NT].rearrange("p (c n) -> p c n", c=NCH)
        self.XNS = self.XNraw.rearrange("p (c n) -> p c n", c=NCH)
        self.tXN = TokMap()


    BANK_GROUPS = {"A": (0, 1, 2, 3), "B": (4, 5), "C": (6, 7), "C2": (4, 5)}

    def bank(self, group=None):
        if group is None:
            b = self.bank_rr % 8
            self.bank_rr += 1
            return b
        if not hasattr(self, "_grr"):
            self._grr = {}
        k = self._grr.get(group, 0)
        self._grr[group] = k + 1
        g = self.BANK_GROUPS[group]
        return g[k % len(g)]

    def _dur(self, eng, ap):
        base = 0.22 + 0.0011 * ap.free_size()
        return base * (3.0 if eng == "pool" else 1.0)

    def act(self, out, in_, func, reads, writes, **kw):
        return self.kb.emit("act", lambda e: e.activation(out=out, in_=in_, func=func, **kw), reads, writes, dur=self._dur("act", out))

    def cp(self, eng, out, in_, reads, writes):
        if eng == "act":
            return self.kb.emit("act", lambda e: e.activation(out=out, in_=in_, func=AF.Copy), reads, writes, dur=self._dur("act", out))
        return self.kb.emit(eng, lambda e: e.tensor_copy(out=out, in_=in_), reads, writes, dur=self._dur(eng, out))

    def tt(self, eng, out, in0, in1, op, reads, writes):
        return self.kb.emit(eng, lambda e: e.tensor_tensor(out=out, in0=in0, in1=in1, op=op), reads, writes, dur=self._dur(eng, out))

    def ts(self, eng, out, in0, s1, s2, op0, op1, reads, writes):
        if s2 is None:
            return self.kb.emit(eng, lambda e: e.tensor_scalar(out=out, in0=in0, scalar1=s1, scalar2=None, op0=op0), reads, writes,
                                dur=self._dur(eng, out))
        return self.kb.emit(eng, lambda e: e.tensor_scalar(out=out, in0=in0, scalar1=s1, scalar2=s2, op0=op0, op1=op1), reads, writes,
                            dur=self._dur(eng, out))

    def stt(self, out, in0, scalar, in1, op0, op1, reads, writes):
        return self.kb.emit("dve", lambda e: e.scalar_tensor_tensor(out=out, in0=in0, scalar=scalar, in1=in1, op0=op0, op1=op1),
                            reads, writes, dur=self._dur("dve", out))

    def _pe_rows(self, lhsT, writes):
        K = lhsT.partition_size()
        base = lhsT.base_partition()
        tile = 32 if K <= 32 else (64 if K <= 64 else 128)
        lo, hi = (base // tile) * tile, (base // tile) * tile + tile
        if tile == 128:
            lo, hi = 0, 128
        after = None
        for t in writes:
            for b in range(8):
                if t is self.tps[b]:
                    prev = self.bank_pe.get(b)
                    if prev is not None and prev[2] is not None and (prev[1] <= lo or hi <= prev[0]):
                        after = prev[2]
                    self.bank_pe[b] = [lo, hi, None]
        return tile < 128, after

    def _pe_done(self, writes, cid):
        for t in writes:
            for b in range(8):
                if t is self.tps[b] and self.bank_pe.get(b) is not None:
                    self.bank_pe[b][2] = cid

    def mm(self, out, lhsT, rhs, start, stop, reads, writes, signal=None):
        if signal is None:
            signal = stop
        partial, after = self._pe_rows(lhsT, writes)
        if partial:
            signal = True
        dur = 0.07 + 0.00045 * rhs.free_size()
        cid = self.kb.emit("pe", lambda e: e.matmul(out, lhsT, rhs, start=start, stop=stop), reads, writes, signal=signal,
                           dur=dur, after=after)
        self._pe_done(writes, cid)
        return cid

    def tr(self, out, in_, ident, reads, writes):
        partial, after = self._pe_rows(in_, writes)
        cid = self.kb.emit("pe", lambda e: e.transpose(out, in_, ident), reads, writes, dur=0.12, after=after)
        self._pe_done(writes, cid)
        return cid

    def memset(self, eng, ap, val, writes):
        return self.kb.emit(eng, lambda e: e.memset(ap, val), (), writes)

    def scan(self, out, d0, d1, init, op0, op1, reads, writes):
        return self.kb.emit("dve", lambda e: e.tensor_tensor_scan(out=out, data0=d0, data1=d1, initial=init, op0=op0, op1=op1), reads, writes,
                            dur=0.22 + 0.0022 * out.free_size())

    def recip(self, out, in_, reads, writes):
        return self.kb.emit("dve", lambda e: e.reciprocal(out=out, in_=in_), reads, writes, dur=0.22 + 0.008 * out.free_size())

    def reduce(self, out, in_, op, reads, writes, axis=None):
        axis = AX.X if axis is None else axis
        return self.kb.emit("dve", lambda e: e.tensor_reduce(out=out, in_=in_, axis=axis, op=op), reads, writes, dur=self._dur("dve", in_))

    def vcol(self, name, c):
        j = VID[name] * 8 + c
        return self.VEC[:, j:j + 1]

    def load_inputs(self):
        kb, d = self.kb, self.d
        kb.dma("sp", self.VEC, d["vecs"][:, :], writes=[self.tVEC])
        for c in range(NCH):
            for ti, (t0, n) in enumerate(TILES):
                kb.dma("sp", self.X[:, c, t0:t0 + n], d["xT"][c * 128:(c + 1) * 128, t0:t0 + n],
                       writes=[self.tX[c, ti]])
        kb.emit("dve", lambda e: e.memset(self.ONES, 1.0), writes=[self.tONES])

    def _full_bank(self, b):
        self.bank_pe[b] = [0, 128, None]

    def rmsnorm_tile(self, ti, gname, out_fn, scratch):
        kb = self.kb
        t0, n = TILES[ti]
        SQ, tSQ, LN, tLN, RS, tRS, bank = scratch
        ps = self.ps[bank][:, :n]
        self._full_bank(bank)
        for c in range(NCH):
            s = c % 2
            kb.emit("act", lambda e, c=c, s=s: e.activation(out=SQ[s][:, :n], in_=self.X[:, c, t0:t0 + n], func=AF.Square),
                    reads=[self.tX[c, ti]], writes=[tSQ[s]])
            kb.emit("pe", lambda e, c=c, s=s: e.matmul(ps, self.ONES, SQ[s][:, :n], start=(c == 0), stop=(c == NCH - 1)),
                    reads=[tSQ[s], self.tONES], writes=[self.tps[bank]], signal=True)
        kb.emit("act", lambda e: e.activation(out=LN[:, :n], in_=ps, func=AF.Ln, scale=1.0 / D, bias=self.EPSC),
                reads=[self.tps[bank], self.tCONST], writes=[tLN])
        kb.emit("act", lambda e: e.activation(out=RS[:, :n], in_=LN[:, :n], func=AF.Exp, scale=-0.5),
                reads=[tLN], writes=[tRS])
        for c in range(NCH):
            out_fn(c, RS[:, :n], tRS)

    def consts(self):
        kb, ar = self.kb, self.ar
        self.CONST = ar.f32(8)
        self.tCONST = Tok("const")
        self.EPSC = self.CONST[:, 0:1]
        self.ONEC = self.CONST[:, 1:2]
        self.NHALFC = self.CONST[:, 2:3]
        self.GNEPSC = self.CONST[:, 3:4]
        for col, val in ((0, EPS), (1, 1.0), (2, -0.5), (3, 64e-5)):
            kb.emit("dve", lambda e, col=col, val=val: e.memset(self.CONST[:, col:col + 1], val), (), [self.tCONST])
        ONESF = ar.f32(128)
        tO = Tok("onesf")
        self.memset("pool", ONESF, 1.0, [tO])
        self.IDENTF = ar.f32(128)
        self.IDENTB = ar.bf16(128)
        self.BONES = ar.bf16(128)
        self.tMASK = Tok("masks")
        kb.emit("pool", lambda e: e.affine_select(out=self.IDENTF, in_=ONESF, pattern=[[-1, 128]], compare_op=ALU.is_equal,
                                                  fill=0.0, base=0, channel_multiplier=1), [tO], [self.tMASK])
        self.cp("pool", self.IDENTB, self.IDENTF, [self.tMASK], [self.tMASK])
        self.memset("pool", self.BONES, 0.0, [self.tMASK])
        self.memset("pool", self.BONES[0:64, 0:64], 1.0, [self.tMASK])
        self.memset("pool", self.BONES[64:128, 64:128], 1.0, [self.tMASK])
        MSU = ar.f32(64)
        MIU = ar.f32(64)
        self.MASKXT = ar.f32(64)
        kb.emit("pool", lambda e: e.affine_select(out=MSU[0:64, :], in_=ONESF[0:64, 0:64], pattern=[[1, 64]], compare_op=ALU.is_gt,
                                                  fill=0.0, base=0, channel_multiplier=-1), [tO], [self.tMASK])
        kb.emit("pool", lambda e: e.affine_select(out=MIU[0:64, :], in_=ONESF[0:64, 0:64], pattern=[[1, 64]], compare_op=ALU.is_ge,
                                                  fill=0.0, base=0, channel_multiplier=-1), [tO], [self.tMASK])
        kb.emit("pool", lambda e: e.affine_select(out=self.MASKXT[0:64, :], in_=ONESF[0:64, 0:64], pattern=[[-1, 64]], compare_op=ALU.is_gt,
                                                  fill=0.0, base=0, channel_multiplier=1), [tO], [self.tMASK])
        self.ts("pool", self.MASKXT[0:64, :], self.MASKXT[0:64, :], -1.0, None, ALU.mult, None, [self.tMASK], [self.tMASK])
        self.MIU = MIU
        self.MASKLL = ar.f32(2 * 4 * 64).rearrange("p (h b t) -> p h b t", h=2, b=4)
        for h in range(2):
            self.cp("pool", self.MASKLL[0:64, h, 0, :], MSU[0:64, :], [self.tMASK], [self.tMASK])
            self.cp("pool", self.MASKLL[0:64, h, 1, :], MIU[0:64, :], [self.tMASK], [self.tMASK])
            self.ts("pool", self.MASKLL[0:64, h, 2, :], MSU[0:64, :], -1.0, None, ALU.mult, None, [self.tMASK], [self.tMASK])
            self.cp("pool", self.MASKLL[0:64, h, 3, :], MIU[0:64, :], [self.tMASK], [self.tMASK])
        self.SM64 = ar.f32(256)
        self.SM4 = ar.f32(16)
        self.memset("pool", self.SM64, 1.0, [self.tMASK])
        self.memset("pool", self.SM64.rearrange("p (j l) -> p j l", l=64)[:, :, 0:1], 0.0, [self.tMASK])
        self.memset("pool", self.SM4, 1.0, [self.tMASK])
        self.memset("pool", self.SM4.rearrange("p (j l) -> p j l", l=4)[:, :, 0:1], 0.0, [self.tMASK])
        self.NEGW0 = ar.f32(8)
        j = VID["w0"] * 8
        self.ts("dve", self.NEGW0, self.VEC[:, j:j + 8], -1.0, None, ALU.mult, None, [self.tVEC], [self.tMASK])

    def rwkv(self):
        kb, d, ar = self.kb, self.d, self.ar
        m0 = ar.mark()
        XNS = self.XNS
        tXNS = TokMap()
        gname = "norm_mix0"
        mu = lambda i, K: self.vcol("mu%d" % i, K)

        HWA = ar.bf16(NT)
        HG1 = ar.bf16(NT)
        HG2 = ar.bf16(NT)
        tHWA, tHG = TokMap(), TokMap()
        W2A2 = ar.bf16(D)
        G2A = ar.bf16(D)
        G2B = ar.bf16(D)
        tW2 = Tok("w2a2g2")
        SHO = ar.f32(NCH * 17).rearrange("p (c j) -> p c j", c=NCH)
        tSHO = Tok("sho")
        SHI = ar.f32(NCH * NS).rearrange("p (c j) -> p c j", c=NCH)
        tSHI = Tok("shi")

        kb.dma("pool", W2A2[0:64, :], d["rw_w2"][0], writes=[tW2])
        kb.dma("pool", W2A2[64:128, :], d["rw_a2"][0], writes=[tW2])
        kb.dma("pool", G2A, d["rw_g2"][0, 0:128, :], writes=[tW2])
        self.memset("pool", G2B, 0.0, [tW2])
        self.memset("pool", HG2, 0.0, [tHG[0]])
        kb.dma("pool", G2B[0:32, :], d["rw_g2"][0, 128:160, :], writes=[tW2])
        kb.dma("sp", SHI, d["shiftT"], writes=[tSHI])

        for c in range(NCH):
            self.memset("pool", XNS[:, c, 0:1], 0.0, [tXNS[c, "init"]])
            sv = XNS[:, c, SOFF:SOFF + NS * 5].rearrange("p (j u) -> p j u", u=5)
            self.cp("pool", sv[:, :, 0], SHI[:, c, :], [tSHI], [tXNS[c, "init"]])

        def xn_aps(K, t0, n):
            if t0 < SEQ:
                return XNS[:, K, 1 + t0:1 + t0 + n], XNS[:, K, t0:t0 + n]
            j0 = (t0 - SEQ) // TS
            nj = n // TS
            sv = XNS[:, K, SOFF + 5 * j0:SOFF + 5 * (j0 + nj)].rearrange("p (j u) -> p j u", u=5)
            return sv[:, :, 1:5], sv[:, :, 0:4]

        def xn_toks(K, t0):
            ti = min(t0 // 512, 4)
            return [tXNS[K, ti], tXNS[K, max(ti - 1, 0)], tXNS[K, "init"]]

        def pview(ps_ap, t0, n):
            if t0 < SEQ:
                return ps_ap
            return ps_ap.rearrange("p (j t) -> p j t", t=TS)

        def mixproj(out, wa, wb, cols, t0, n, wtok, ptok):
            o = pview(out, t0, n)
            for K in range(NCH):
                xa, xb = xn_aps(K, t0, n)
                self.mm(o, wa[:, K, cols], xa, K == 0, False, [wtok] + xn_toks(K, t0), [ptok])
                self.mm(o, wb[:, K, cols], xb, False, K == NCH - 1, [wtok] + xn_toks(K, t0), [ptok])

        def scale_w(raw, wb, Mcols, mu_list, tok):
            for K in range(NCH):
                for (cs, mi) in mu_list:
                    self.ts("pool", wb[:, K, cs], raw[:, K, cs], mu(mi, K), None, ALU.mult, None, [tok, self.tVEC], [tok])
            self.tt("pool", raw, raw, wb, ALU.subtract, [tok], [tok])

        m1 = ar.mark()
        W1A = ar.bf16(NCH * 128).rearrange("p (k m) -> p k m", k=NCH)
        W1B = ar.bf16(NCH * 128).rearrange("p (k m) -> p k m", k=NCH)
        G1A = ar.bf16(NCH * 160).rearrange("p (k m) -> p k m", k=NCH)
        G1B = ar.bf16(NCH * 160).rearrange("p (k m) -> p k m", k=NCH)
        tW1, tG1 = Tok("w1a1"), Tok("g1")
        for K in range(NCH):
            kb.dma("pool", W1A[:, K, 0:64], d["rw_w1"][0, K * 128:(K + 1) * 128, :], writes=[tW1])
            kb.dma("pool", W1A[:, K, 64:128], d["rw_a1"][0, K * 128:(K + 1) * 128, :], writes=[tW1])
            kb.dma("pool", G1A[:, K, :], d["rw_g1"][0, K * 128:(K + 1) * 128, :], writes=[tG1])
        scale_w(W1A, W1B, 128, [(slice(0, 64), 1), (slice(64, 128), 4)], tW1)
        scale_w(G1A, G1B, 160, [(slice(0, 160), 5)], tG1)
        XNF = [ar.f32(512) for _ in range(2)]
        SQ = [ar.bf16(512) for _ in range(2)]
        LN = ar.f32(512)
        RS = ar.f32(512)
        tXNF, tSQ = TokMap(), TokMap()
        tLN, tRS = Tok("ln"), Tok("rs")
        rr = [0]
        import os as _os
        PRE = int(_os.environ.get("K_PRE", "1"))
        if PRE:
            self.bank_pe = {}
            kb.begin_defer()
        for ti, (t0, n) in enumerate(TILES):
            def out_fn(c, rs, trs, ti=ti, t0=t0, n=n):
                s = rr[0] % 2
                rr[0] += 1
                xf = XNF[s][:, :n]
                self.stt(xf, self.X[:, c, t0:t0 + n], self.vcol(gname, c), rs, ALU.mult, ALU.mult,
                         [self.tX[c, ti], trs, self.tVEC], [tXNF[s]])
                if t0 < SEQ:
                    self.cp("act", XNS[:, c, 1 + t0:1 + t0 + n], xf, [tXNF[s]], [tXNS[c, ti]])
                    if t0 + n == SEQ:
                        self.cp("pool", SHO[:, c, 0:1], xf[:, n - 1:n], [tXNF[s]], [tSHO])
                else:
                    sv = XNS[:, c, SOFF:SOFF + NS * 5].rearrange("p (j u) -> p j u", u=5)
                    xv = xf.rearrange("p (j t) -> p j t", t=TS)
                    self.cp("act", sv[:, :, 1:5], xv, [tXNF[s]], [tXNS[c, ti]])
                    self.cp("pool", SHO[:, c, 1:17], xv[:, :, 3], [tXNF[s]], [tSHO])
            self.rmsnorm_tile(ti, gname, out_fn, (SQ, tSQ, LN, tLN, RS, tRS, self.bank()))
            b1, b2, b3 = self.bank(), self.bank(), self.bank()
            mixproj(self.ps[b1][:, :n], W1A, W1B, slice(0, 128), t0, n, tW1, self.tps[b1])
            mixproj(self.ps[b2][:, :n], G1A, G1B, slice(0, 128), t0, n, tG1, self.tps[b2])
            mixproj(self.ps[b3][0:32, :n], G1A, G1B, slice(128, 160), t0, n, tG1, self.tps[b3])
            self.act(HWA[0:64, t0:t0 + n], self.ps[b1][0:64, :n], AF.Tanh, [self.tps[b1]], [tHWA[ti]])
            self.cp("act", HWA[64:128, t0:t0 + n], self.ps[b1][64:128, :n], [self.tps[b1]], [tHWA[ti]])
            self.act(HG1[:, t0:t0 + n], self.ps[b2][:, :n], AF.Sigmoid, [self.tps[b2]], [tHG[ti]])
            self.act(HG2[0:32, t0:t0 + n], self.ps[b3][0:32, :n], AF.Sigmoid, [self.tps[b3]], [tHG[ti]])
        if PRE:
            kb.end_defer()
            self.bank_pe = {}
        ar.release(m1)
        kb.barrier()
        kb.dma("sp", d["o_shift"], SHO, reads=[tSHO])

        WN = 256

        def f32t():
            return ar.f32(WN)

        def bf16t():
            return ar.bf16(WN)
        WA2 = [{nm: ar.bf16(NCH * 128).rearrange("p (k m) -> p k m", k=NCH) for nm in "rkv"} for _ in range(2)]
        WB2 = [{nm: ar.bf16(NCH * 128).rearrange("p (k m) -> p k m", k=NCH) for nm in "rkv"} for _ in range(2)]
        WO2 = [ar.bf16(D) for _ in range(2)]
        tWc2 = [{nm: Tok("w%s%d" % (nm, i)) for nm in "rkv"} for i in range(2)]
        tWO = [Tok("wo0"), Tok("wo1")]
        Rf, Kf, Vf, A_, EW, CUM, EM, KK, KF, Bv, T1, T2 = [f32t() for _ in range(12)]
        EQ = EW
        Vb, SQb, KTb, BTb, YG = [bf16t() for _ in range(5)]
        RKR = SQb
        S3 = []
        for _ in range(3):
            S3.append(dict(
                KR=ar.bf16(2 * WN).rearrange("p (a n) -> p a n", a=2),
                KTt=ar.bf16(4 * 128).rearrange("p (j m) -> p j m", j=4),
                BTt=ar.bf16(4 * 128).rearrange("p (j m) -> p j m", j=4),
                VTt=ar.bf16(4 * 128).rearrange("p (j m) -> p j m", j=4),
                EP=f32t(), BONUS=f32t(), Gf=f32t(), LLs=ar.bf16(4 * 2 * 4 * 64)))
        S2 = []
        for _ in range(2):
            S2.append(dict(XTs=ar.bf16(4 * 2 * 64), PW=[ar.bf16(8 * 2 * 64) for _ in range(2)]))
        PT = [ar.bf16(8 * 64) for _ in range(2)]
        Gs = ar.bf16(128)
        NU = ar.bf16(128)
        YT = ar.f32(512)
        SQ2 = ar.f32(512)
        YF = SQ2[:, 0:256]
        STAT = ar.f32(32)
        H = ar.f32(64)
        H0d = ar.f32(64)
        Hb = ar.bf16(64)
        HS = ar.f32(4 * 64).rearrange("p (j v) -> p j v", j=4)
        HSb = ar.bf16(4 * 64).rearrange("p (j v) -> p j v", j=4)
        T = TokMap()

        main_tiles = [(t0, 256, 64) for t0 in range(0, SEQ, 256)] + [(SEQ + 16 * q, 16, 4) for q in range(4)]
        import os as _os
        if "KDEBUG" in _os.environ:
            print("rwkv arena top", ar.top, "of", ar.n)
        if "RW_TILES" in _os.environ:
            main_tiles = [main_tiles[int(i)] for i in _os.environ["RW_TILES"].split(",")]
        NCc = int(_os.environ.get("RW_NC", NCH))
        units = []
        for c in range(NCc):
            for k_, (t0, n, L) in enumerate(main_tiles):
                u = len(units)
                units.append(dict(u=u, c=c, t0=t0, n=n, L=L, first=(k_ == 0), last=(k_ == len(main_tiles) - 1),
                                  lastprompt=(t0 < SEQ and (k_ + 1 == len(main_tiles) or main_tiles[k_ + 1][0] >= SEQ))))

        def load_weights(c):
            ccols = slice(c * 128, (c + 1) * 128)
            WA, WB, tWc = WA2[c % 2], WB2[c % 2], tWc2[c % 2]
            for nm, key, mi in (("r", "rw_wr", 0), ("k", "rw_wk", 2), ("v", "rw_wv", 3)):
                for K in range(NCH):
                    kb.dma("pool", WA[nm][:, K, :], d[key][0, K * 128:(K + 1) * 128, ccols], writes=[tWc[nm]])
                scale_w(WA[nm], WB[nm], 128, [(slice(0, 128), mi)], tWc[nm])

        def stageA(U):
            u, c, t0, n, L = U["u"], U["c"], U["t0"], U["n"], U["L"]
            s3, s2 = S3[u % 3], S2[u % 2]
            k3, k2 = u % 3, u % 2
            sample = t0 >= SEQ
            NCk = 4
            ti5 = min(t0 // 512, 4)
            tsl = slice(t0, t0 + n)
            cs = lambda j: slice(j * L, (j + 1) * L)
            ccols = slice(c * 128, (c + 1) * 128)
            WA, WB, tWc = WA2[c % 2], WB2[c % 2], tWc2[c % 2]
            if U["first"]:
                if c == 0:
                    load_weights(0)
                if c + 1 < NCc:
                    load_weights(c + 1)
                yield
            KR, KTt, BTt, VTt, EP, BONUS, Gf = s3["KR"], s3["KTt"], s3["BTt"], s3["VTt"], s3["EP"], s3["BONUS"], s3["Gf"]
            tKR, tKTt, tBTt, tVTt, tEP, tBONUS, tGf, tLL = (T["KR", k3], T["KTt", k3], T["BTt", k3], T["VTt", k3], T["EP", k3],
                                                          T["BONUS", k3], T["Gf", k3], T["LLs", k3])
            tXT = T["XTs", k2]
            bA, bB, bC, bD = self.bank("A"), self.bank("A"), self.bank("A"), self.bank("A")
            PR, PK = self.ps[bA][:, 0:n], self.ps[bA][:, 256:256 + n]
            PV, PGt = self.ps[bB][:, 0:n], self.ps[bB][:, 256:256 + n]
            PWL, PAL = self.ps[bC][:, 0:n], self.ps[bC][:, 256:256 + n]
            PKK, PSm = self.ps[bD][:, 0:n], self.ps[bD][:, 256:256 + n]
            for K0 in range(0, NCH, 2):
                pass
            mixproj(PR, WA["r"], WB["r"], slice(0, 128), t0, n, tWc["r"], self.tps[bA])
            yield
            mixproj(PK, WA["k"], WB["k"], slice(0, 128), t0, n, tWc["k"], self.tps[bA])
            yield
            mixproj(PV, WA["v"], WB["v"], slice(0, 128), t0, n, tWc["v"], self.tps[bB])
            self.mm(PGt, G2A[:, ccols], HG1[:, tsl], True, False, [tW2, tHG[ti5]], [self.tps[bB]])
            self.mm(PGt, G2B[:, ccols], HG2[:, tsl], False, True, [tW2, tHG[ti5], tHG[0]], [self.tps[bB]])
            self.mm(PWL, W2A2[0:64, ccols], HWA[0:64, tsl], True, True, [tW2, tHWA[ti5]], [self.tps[bC]])
            self.mm(PAL, W2A2[64:128, ccols], HWA[64:128, tsl], True, True, [tW2, tHWA[ti5]], [self.tps[bC]])
            yield
            w = lambda a: a[:, 0:n]
            tV = self.tVEC
            self.cp("act", w(Rf), PR, [self.tps[bA]], [T["Rf"]])
            self.cp("act", w(Kf), PK, [self.tps[bA]], [T["Kf"]])
            yield
            self.cp("act", w(Vf), PV, [self.tps[bB]], [T["Vf"]])
            self.cp("act", w(Gf), PGt, [self.tps[bB]], [tGf])
            self.cp("dve", w(Vb), w(Vf), [T["Vf"]], [T["Vb"]])
            yield
            self.act(w(A_), PAL, AF.Sigmoid, [self.tps[bC], tV], [T["A"]], bias=self.vcol("a0", c))
            self.act(w(T1), PWL, AF.Exp, [self.tps[bC], self.tMASK], [T["T1"]], scale=-1.0, bias=self.NEGW0[:, c:c + 1])
            self.ts("dve", w(KK), w(Kf), self.vcol("k_k", c), None, ALU.mult, None, [T["Kf"], tV], [T["KK"]])
            yield
            self.act(w(T1), w(T1), AF.Ln, [T["T1"], self.tCONST], [T["T1"]], bias=self.ONEC)
            self.act(w(SQb), w(KK), AF.Square, [T["KK"]], [T["SQb"]])
            self.mm(PKK, self.BONES, w(SQb), True, True, [T["SQb"], self.tMASK], [self.tps[bD]], signal=True)
            yield
            self.act(w(EW), w(T1), AF.Exp, [T["T1"], self.tCONST], [T["EW"]], scale=-1.0, bias=self.NHALFC)
            self.ts("dve", w(T1), w(A_), -1.0, self.vcol("k_a", c), ALU.add, ALU.mult, [T["A"], tV], [T["T1"]])
            self.stt(w(KF), w(T1), 1.0, w(Kf), ALU.add, ALU.mult, [T["T1"], T["Kf"]], [T["KF"]])
            yield
            SM = self.SM4[:, 0:n] if sample else self.SM64[:, 0:n]
            self.scan(w(CUM), SM, w(EW), 0.0, ALU.mult, ALU.subtract, [T["EW"], self.tMASK], [T["CUM"]])
            self.act(w(T2), PKK, AF.Sqrt, [self.tps[bD]], [T["T2"]])
            yield
            self.act(w(EP), w(CUM), AF.Exp, [T["CUM"]], [tEP])
            self.act(w(EM), w(CUM), AF.Exp, [T["CUM"]], [T["EM"]], scale=-1.0)
            self.ts("dve", w(T2), w(T2), 1e-12, None, ALU.max, None, [T["T2"]], [T["T2"]])
            self.recip(w(T2), w(T2), [T["T2"]], [T["T2"]])
            yield
            self.tt("dve", w(KK), w(KK), w(T2), ALU.mult, [T["KK"], T["T2"]], [T["KK"]])
            self.tt("dve", w(T2), w(CUM), w(EW), ALU.add, [T["CUM"], T["EW"], T["KK"]], [T["T2"]])
            self.act(w(EQ), w(T2), AF.Exp, [T["T2"]], [T["EW"]])
            yield
            self.stt(w(RKR), w(Rf), self.vcol("r_k", c), w(KF), ALU.mult, ALU.mult, [T["Rf"], T["KF"], tV], [T["SQb"]])
            self.mm(PSm, self.BONES, w(RKR), True, True, [T["SQb"], self.tMASK], [self.tps[bD]], signal=True)
            self.tt("dve", w(Bv), w(KK), w(A_), ALU.mult, [T["KK"], T["A"]], [T["Bv"]])
            self.tt("dve", KR[:, 1, 0:n], w(Rf), w(EP), ALU.mult, [T["Rf"], tEP], [tKR])
            yield
            self.tt("dve", KR[:, 0, 0:n], w(KK), w(EQ), ALU.mult, [T["KK"], T["EW"]], [tKR])
            self.tt("dve", w(KTb), w(KF), w(EM), ALU.mult, [T["KF"], T["EM"]], [T["KTb"]])
            self.tt("dve", w(BTb), w(Bv), w(EM), ALU.mult, [T["Bv"], T["EM"]], [T["BTb"]])
            self.tt("dve", w(BONUS), PSm, w(Vf), ALU.mult, [self.tps[bD], T["Vf"]], [tBONUS])
            yield
            bT = self.bank("A")
            PTr = self.ps[bT].bitcast(BF16)
            for (src, ts_, off) in ((KTb, "KTb", 0), (BTb, "BTb", 1)):
                for j in range(NCk):
                    self.tr(PTr[0:L, off * 512 + j * 128:off * 512 + (j + 1) * 128], src[:, cs(j)], self.IDENTB,
                            [T[ts_], self.tMASK], [self.tps[bT]])
            self.cp("act", KTt[0:L, :, :], PTr[0:L, 0:512].rearrange("p (j m) -> p j m", j=4), [self.tps[bT]], [tKTt])
            self.cp("act", BTt[0:L, :, :], PTr[0:L, 512:1024].rearrange("p (j m) -> p j m", j=4), [self.tps[bT]], [tBTt])
            yield
            bT2 = self.bank("A")
            PTr2 = self.ps[bT2].bitcast(BF16)
            for j in range(NCk):
                self.tr(PTr2[0:L, j * 128:(j + 1) * 128], Vb[:, cs(j)], self.IDENTB, [T["Vb"], self.tMASK], [self.tps[bT2]])
            self.cp("act", VTt[0:L, :, :], PTr2[0:L, 0:512].rearrange("p (j m) -> p j m", j=4), [self.tps[bT2]], [tVTt])
            yield
            LLv = s3["LLs"][:, 0:4 * 2 * 4 * L].rearrange("p (j h b t) -> p j h b t", j=4, h=2, b=4)
            XTv = s2["XTs"][:, 0:4 * 2 * L].rearrange("p (j h t) -> p j h t", j=4, h=2)
            mk = self.MASKLL[0:L, :, :, 0:L]
            for h in range(2):
                hs = slice(64 * h, 64 * h + 64)
                bX = self.bank("A")
                PXT = self.ps[bX][:, 0:4 * L].rearrange("p (j t) -> p j t", j=4)
                for g0 in (0, 2):
                    bL = self.bank("A")
                    PLL = self.ps[bL][:, 0:2 * 4 * L].rearrange("p (j b t) -> p j b t", j=2, b=4)
                    for jj in range(2):
                        j = g0 + jj
                        self.mm(PLL[0:L, jj, 0:2, :], KTb[hs, cs(j)], KR[hs, :, cs(j)], True, True, [T["KTb"], tKR], [self.tps[bL]])
                        self.mm(PLL[0:L, jj, 2:4, :], BTb[hs, cs(j)], KR[hs, :, cs(j)], True, True, [T["BTb"], tKR], [self.tps[bL]])
                        self.mm(PXT[0:L, j, :], KR[hs, 0, cs(j)], BTb[hs, cs(j)], True, True, [T["BTb"], tKR], [self.tps[bX]])
                    self.tt("dve", LLv[0:L, g0:g0 + 2, h], PLL[0:L], mk, ALU.mult, [self.tps[bL], self.tMASK], [tLL])
                    yield
                self.tt("dve", XTv[0:L, :, h, :], PXT[0:L], self.MASKXT[0:L, 0:L].unsqueeze(1).to_broadcast([L, 4, L]), ALU.mult,
                        [self.tps[bX], self.tMASK], [tXT])
                yield

        def stageB(U):
            u, L = U["u"], U["L"]
            s3, s2 = S3[u % 3], S2[u % 2]
            k3, k2 = u % 3, u % 2
            NCk, NM = 4, 8
            LLv = s3["LLs"][:, 0:4 * 2 * 4 * L].rearrange("p (j h b t) -> p j h b t", j=4, h=2, b=4)
            XTv = s2["XTs"][:, 0:4 * 2 * L].rearrange("p (j h t) -> p j h t", j=4, h=2)
            tLL, tXT = T["LLs", k3], T["XTs", k2]
            PWv = [p[:, 0:NM * 2 * L].rearrange("p (i a t) -> p i a t", i=NM, a=2) for p in s2["PW"]]
            PTv = [p[:, 0:NM * L].rearrange("p (i t) -> p i t", i=NM) for p in PT]
            tPW = [T["PW", k2, 0], T["PW", k2, 1]]
            PW4 = PWv[0].rearrange("p (j h) a t -> p j h a t", h=2)
            self.cp("dve", PW4[0:L, :, :, 0, :], LLv[0:L, :, :, 2, :], [tLL], [tPW[0]])
            self.cp("pool", PWv[0][0:L, :, 1, :], self.IDENTB[0:L, 0:L].unsqueeze(1).to_broadcast([L, NM, L]), [self.tMASK], [tPW[0]])
            self.cp("act", PTv[0][0:L].rearrange("p (j h) t -> p j h t", h=2), XTv[0:L], [tXT], [T["PT", 0]])
            yield
            nlev = 6 if L == 64 else 2
            cur = 0
            mpb = 4 if L == 64 else 8
            for lev in range(nlev):
                nxt = 1 - cur
                last = lev == nlev - 1
                for i0 in range(0, NM, mpb):
                    bI = self.bank("B")
                    PA = self.ps[bI][:, 0:mpb * 2 * L].rearrange("p (i a t) -> p i a t", i=mpb, a=2)
                    for ii in range(mpb):
                        i = i0 + ii
                        self.mm(PA[0:L, ii], PTv[cur][0:L, i, :], PWv[cur][0:L, i], True, True,
                                [T["PT", cur], tPW[cur]], [self.tps[bI]], signal=(ii == mpb - 1))
                    if not last:
                        self.cp("act", PWv[nxt][0:L, i0:i0 + mpb, 0, :], PA[0:L, :, 0, :], [self.tps[bI]], [tPW[nxt]])
                    self.tt("dve", PWv[nxt][0:L, i0:i0 + mpb, 1, :], PA[0:L, :, 1, :], PWv[cur][0:L, i0:i0 + mpb, 1, :], ALU.add,
                            [self.tps[bI], tPW[cur]], [tPW[nxt]])
                    yield
                if not last:
                    bJ = self.bank("B")
                    PB = self.ps[bJ][:, 0:NM * L].rearrange("p (i t) -> p i t", i=NM)
                    for i in range(NM):
                        self.mm(PB[0:L, i, :], PWv[cur][0:L, i, 0, :], PTv[cur][0:L, i, :], True, True,
                                [T["PT", cur], tPW[cur]], [self.tps[bJ]], signal=(i == NM - 1))
                    self.cp("act", PTv[nxt][0:L], PB[0:L], [self.tps[bJ]], [T["PT", nxt]])
                    yield
                cur = nxt
            U["Wv"] = PWv[cur]
            U["tWv"] = tPW[cur]

        def stageC(U):
            u, c, t0, n, L = U["u"], U["c"], U["t0"], U["n"], U["L"]
            s3 = S3[u % 3]
            k3 = u % 3
            sample = t0 >= SEQ
            NCk = 4
            ti5 = min(t0 // 512, 4)
            tsl = slice(t0, t0 + n)
            cs = lambda j: slice(j * L, (j + 1) * L)
            w = lambda a: a[:, 0:n]
            tV = self.tVEC
            KR, KTt, BTt, VTt, EP, BONUS, Gf = s3["KR"], s3["KTt"], s3["BTt"], s3["VTt"], s3["EP"], s3["BONUS"], s3["Gf"]
            tKR, tKTt, tBTt, tVTt, tEP, tBONUS, tGf, tLL = (T["KR", k3], T["KTt", k3], T["BTt", k3], T["VTt", k3], T["EP", k3],
                                                          T["BONUS", k3], T["Gf", k3], T["LLs", k3])
            LLv = s3["LLs"][:, 0:4 * 2 * 4 * L].rearrange("p (j h b t) -> p j h b t", j=4, h=2, b=4)
            Wv, tWv = U["Wv"], U["tWv"]
            WO = WO2[c % 2]
            if U["first"]:
                kb.dma("pool", WO, d["rw_wo"][0, c * 128:(c + 1) * 128, :], writes=[tWO[c % 2]])
                self.memset("pool", H, 0.0, [T["H"]])
                self.memset("pool", Hb, 0.0, [T["Hb"]])
            if sample:
                q = (t0 - SEQ) // 16
                for jj in range(4):
                    kb.dma("sp", HS[:, jj, :], d["rw_S0T"][4 * q + jj, 2 * c:2 * c + 2].rearrange("h k v -> (h k) v"),
                           writes=[T["HS", jj]])
                    self.cp("act", HSb[:, jj, :], HS[:, jj, :], [T["HS", jj]], [T["HSb", jj]])
                yield
            YTv = YT[:, 0:4 * 128].rearrange("p (j m) -> p j m", j=4)
            for j in range(NCk):
                if sample:
                    Hc, Hbc, Hdc = HS[:, j, :], HSb[:, j, :], H0d
                    tH, tHb, tHd = T["HS", j], T["HSb", j], T["H0d"]
                else:
                    Hc, Hbc, Hdc = H, Hb, H0d
                    tH, tHb, tHd = T["H"], T["Hb"], T["H0d"]
                DL = EP[:, (j + 1) * L - 1:(j + 1) * L]
                bS = self.bank("C")
                PG = self.ps[bS][0:L, 0:128]
                PU = self.ps[bS][0:L, 128:256]
                PY = self.ps[bS][0:L, 256:384]
                bH = self.bank("C")
                PH = self.ps[bH][:, 0:64]
                tS = self.tps[bS]
                tSH = self.tps[bH]
                for h in range(2):
                    hs = slice(64 * h, 64 * h + 64)
                    self.mm(PG[:, hs], LLv[0:L, j, h, 0, :], VTt[0:L, j, hs], True, False, [tLL, tVTt], [tS])
                    self.mm(PG[:, hs], KR[hs, 0, cs(j)], Hbc[hs, :], False, True, [tKR, tHb], [tS], signal=True)
                self.act(Hdc, Hc, AF.Identity, [tH, tEP], [tHd], scale=DL)
                yield
                self.cp("act", Gs[0:L, :], PG, [tS], [T["Gs"]])
                yield
                for h in range(2):
                    hs = slice(64 * h, 64 * h + 64)
                    self.mm(PU[:, hs], Wv[0:L, 2 * j + h, 1, :], Gs[0:L, hs], True, True, [tWv, T["Gs"]], [tS], signal=True)
                yield
                self.act(NU[0:L, :], PU, AF.Identity, [tS], [T["NU"]], scale=-1.0)
                yield
                for h in range(2):
                    hs = slice(64 * h, 64 * h + 64)
                    self.mm(PH[hs, :], KTt[0:L, j, hs], VTt[0:L, j, hs], True, False, [tKTt, tVTt], [tSH])
                    self.mm(PH[hs, :], BTt[0:L, j, hs], NU[0:L, hs], False, True, [tBTt, T["NU"]], [tSH], signal=True)
                for h in range(2):
                    hs = slice(64 * h, 64 * h + 64)
                    self.mm(PY[:, hs], LLv[0:L, j, h, 1, :], VTt[0:L, j, hs], True, False, [tLL, tVTt], [tS])
                    self.mm(PY[:, hs], LLv[0:L, j, h, 3, :], NU[0:L, hs], False, False, [tLL, T["NU"]], [tS])
                    self.mm(PY[:, hs], KR[hs, 1, cs(j)], Hbc[hs, :], False, True, [tKR, tHb], [tS], signal=True)
                yield
                self.stt(Hbc, PH, DL, Hdc, ALU.mult, ALU.add, [tSH, tEP, tHd], [tHb])
                self.stt(Hc, PH, DL, Hdc, ALU.mult, ALU.add, [tSH, tEP, tHd], [tH])
                self.cp("act", YTv[0:L, j, :], PY, [tS], [T["YT"]])
                yield
            if sample:
                for jj in range(4):
                    kb.dma("sp", d["o_rw_Ss"][4 * q + jj, 2 * c:2 * c + 2].rearrange("h k v -> (h k) v"), HS[:, jj, :],
                           reads=[T["HS", jj]])
            if U["lastprompt"]:
                kb.dma("sp", d["o_rw_Sp"][2 * c:2 * c + 2].rearrange("h k v -> (h k) v"), H, reads=[T["H"]])
            G8 = 8
            YT3 = YT[:, 0:512].rearrange("p (g v) -> p g v", g=G8)
            SQ3 = SQ2[:, 0:512].rearrange("p (g v) -> p g v", g=G8)
            SUMv, VARv, RSTv = STAT[:, 0:8], STAT[:, 8:16], STAT[:, 16:24]
            self.reduce(SUMv[0:L, :], YT3[0:L], ALU.add, [T["YT"]], [T["SUM"]])
            self.ts("dve", SUMv[0:L, :], SUMv[0:L, :], 1.0 / 64, None, ALU.mult, None, [T["SUM"]], [T["SUM"]])
            yield
            self.tt("dve", YT3[0:L], YT3[0:L], SUMv[0:L, :].unsqueeze(2).to_broadcast([L, G8, 64]), ALU.subtract,
                    [T["YT"], T["SUM"]], [T["YT"]])
            yield
            self.act(SQ2[0:L, 0:512], YT[0:L, 0:512], AF.Square, [T["YT"]], [T["SQ2"]])
            yield
            self.reduce(VARv[0:L, :], SQ3[0:L], ALU.add, [T["SQ2"]], [T["VAR"]])
            yield
            self.act(RSTv[0:L, :], VARv[0:L, :], AF.Ln, [T["VAR"], self.tCONST], [T["RST"]], scale=1.0 / 64, bias=self.GNEPSC[0:L, :])
            yield
            self.act(RSTv[0:L, :], RSTv[0:L, :], AF.Exp, [T["RST"]], [T["RST"]], scale=-0.5)
            yield
            self.tt("dve", YT3[0:L], YT3[0:L], RSTv[0:L, :].unsqueeze(2).to_broadcast([L, G8, 64]), ALU.mult,
                    [T["YT"], T["RST"]], [T["YT"]])
            yield
            bY = self.bank("C")
            PYF = self.ps[bY][:, 0:n]
            for j in range(NCk):
                self.tr(PYF[:, cs(j)], YTv[0:L, j, :], self.IDENTF[0:L, 0:L], [T["YT"], self.tMASK], [self.tps[bY]])
            yield
            self.act(w(YF), PYF, AF.Identity, [self.tps[bY], tV], [T["SQ2"]], scale=self.vcol("gn_w", c), bias=self.vcol("gn_b", c))
            yield
            self.tt("pool", w(YF), w(YF), w(BONUS), ALU.add, [T["SQ2"], tBONUS], [T["SQ2"]])
            yield
            self.tt("dve", w(YG), w(YF), w(Gf), ALU.mult, [T["SQ2"], tGf], [T["YG"]])
            yield
            for dc0 in range(0, NCH, 2):
                bO = self.bank("C")
                for k2_ in range(2):
                    dc = dc0 + k2_
                    PO = self.ps[bO][:, 256 * k2_:256 * k2_ + n]
                    self.mm(PO, WO[:, dc * 128:(dc + 1) * 128], w(YG), True, True, [tWO[c % 2], T["YG"]], [self.tps[bO]], signal=True)
                for k2_ in range(2):
                    dc = dc0 + k2_
                    PO = self.ps[bO][:, 256 * k2_:256 * k2_ + n]
                    self.tt("dve", self.X[:, dc, tsl], PO, self.X[:, dc, tsl], ALU.add, [self.tps[bO], self.tX[dc, ti5]], [self.tX[dc, ti5]])
                yield

        STEPS = [int(v) for v in _os.environ.get("RW_STEP", "1,1,1").split(",")]

        def drain(gens):
            gens = [(g, k) for g, k in zip(gens, STEPS) if g is not None]
            while gens:
                for item in list(gens):
                    g, k = item
                    try:
                        for _ in range(k):
                            next(g)
                    except StopIteration:
                        gens.remove(item)

        PIPE = int(_os.environ.get("RW_PIPE", "1"))
        NU_ = len(units)
        SCHED = int(_os.environ.get("K_SCHED", "1"))
        if SCHED:
            self.bank_pe = {}
            kb.begin_defer()
        if PIPE:
            for s in range(NU_ + 2):
                gA = stageA(units[s]) if s < NU_ else None
                gB = stageB(units[s - 1]) if 0 <= s - 1 < NU_ else None
                gC = stageC(units[s - 2]) if 0 <= s - 2 < NU_ else None
                drain([gC, gB, gA])
        else:
            for U in units:
                drain([stageA(U)])
                drain([stageB(U)])
                drain([stageC(U)])
        if SCHED:
            kb.end_defer()
            self.bank_pe = {}
        ar.release(m0)
        kb.barrier()

    def mlstm(self):
        kb, d, ar = self.kb, self.d, self.ar
        m0 = ar.mark()
        gname = "norm_mix1"
        XN, tXN = self.XN, self.tXN
        T = TokMap()
        NEG = -1.0e30
        EKA = ar.f32(NT)
        EQA = ar.f32(NT)
        EMTT = ar.f32(48 * 8).rearrange("p (j h) -> p j h", h=8)
        self.EMTP = EMTT[:, 0:32, :]
        EMTS = EMTT[:, 32:48, :]
        self.BBP = ar.f32(2)
        self.ABP = ar.f32(2)
        MLV = ar.f32(8 * 10).rearrange("p (h k) -> p h k", h=8)
        BIF = ar.f32(4)
        M0T = ar.f32(NS)
        MOUT = ar.f32(17)
        SEL = ar.f32(8 * 64).rearrange("p (h m) -> p h m", h=8)
        CONVO = ar.f32(8 * 2 * 17 * 3).rearrange("p (h w s k) -> p h w s k", h=8, w=2, s=17)
        tEK, tEQ = TokMap(), TokMap()
        kb.dma("sp", MLV[0:64], d["mlv"], writes=[T["MLV"]])
        kb.dma("sp", BIF[0:8, 0:2], d["bif"], writes=[T["BIF"]])
        kb.dma("sp", M0T[0:8, :], d["ml_m0T"], writes=[T["M0T"]])
        self.ts("dve", BIF[0:8, 2:3], BIF[0:8, 1:2], -1.0, None, ALU.mult, None, [T["BIF"]], [T["BIF"]])
        self.cp("pool", SEL[0:8], self.IDENTF[0:8, 0:8].unsqueeze(2).to_broadcast([8, 8, 64]), [self.tMASK], [T["SEL"]])

        m1 = ar.mark()
        WIF = ar.bf16(NCH * 16).rearrange("p (k m) -> p k m", k=NCH)
        for K in range(NCH):
            kb.dma("pool", WIF[:, K, :], d["ml_w_in"][0, K * 128:(K + 1) * 128, 3072:3088], writes=[T["WIF"]])
        SQ = [ar.bf16(512) for _ in range(2)]
        LN = ar.f32(512)
        RS = ar.f32(512)
        tSQ = TokMap()
        tLN, tRS = Tok("ln"), Tok("rs")
        LI, LF, BB, AA = [ar.f32(512) for _ in range(4)]
        ABX = ar.f32(513)
        D0, D1, TMPg, MTg, EMg = [ar.f32(512) for _ in range(5)]
        import os as _os
        PRE = int(_os.environ.get("K_PRE", "1"))
        if PRE:
            self.bank_pe = {}
            kb.begin_defer()
        for ti, (t0, n) in enumerate(TILES):
            sample = t0 >= SEQ
            Lc = 4 if sample else 64
            nck = n // Lc

            def out_fn(c, rs, trs, ti=ti, t0=t0, n=n):
                self.stt(XN[:, c, t0:t0 + n], self.X[:, c, t0:t0 + n], self.vcol(gname, c), rs, ALU.mult, ALU.mult,
                         [self.tX[c, ti], trs, self.tVEC], [tXN[c, ti]])
            self.rmsnorm_tile(ti, gname, out_fn, (SQ, tSQ, LN, tLN, RS, tRS, self.bank()))
            bI, bF = self.bank(), self.bank()
            PI, PF = self.ps[bI][0:8, :n], self.ps[bF][0:8, :n]
            for K in range(NCH):
                self.mm(PI, WIF[:, K, 0:8], XN[:, K, t0:t0 + n], K == 0, K == NCH - 1, [T["WIF"], tXN[K, ti]], [self.tps[bI]])
            for K in range(NCH):
                self.mm(PF, WIF[:, K, 8:16], XN[:, K, t0:t0 + n], K == 0, K == NCH - 1, [T["WIF"], tXN[K, ti]], [self.tps[bF]])
            g = lambda a: a[0:8, 0:n]
            self.act(g(LI), PI, AF.Identity, [self.tps[bI], T["BIF"]], [T["LI"]], bias=BIF[0:8, 0:1])
            self.act(g(TMPg), PF, AF.Exp, [self.tps[bF], T["BIF"]], [T["TMP"]], scale=-1.0, bias=BIF[0:8, 2:3])
            self.act(g(LF), g(TMPg), AF.Ln, [T["TMP"], self.tCONST], [T["LF"]], bias=self.ONEC[0:8, :])
            self.memset("dve", g(D0), 1.0, [T["D0"]])
            init = 0.0
            rd = []
            if sample:
                self.memset("dve", g(D0).rearrange("p (s t) -> p s t", t=TS)[:, :, 0:1], 0.0, [T["D0"]])
            elif ti > 0:
                init = self.BBP[0:8, 0:1]
                rd = [T["BBprev"]]
            self.scan(g(BB), g(D0), g(LF), init, ALU.mult, ALU.subtract, [T["D0"], T["LF"]] + rd, [T["BB"]])
            self.tt("dve", g(AA), g(LI), g(BB), ALU.subtract, [T["LI"], T["BB"]], [T["AA"]])
            ab = ABX[0:8, 1:1 + n]
            if sample:
                self.memset("dve", g(D1), 0.0, [T["D1"]])
                self.memset("dve", g(D1).rearrange("p (s t) -> p s t", t=TS)[:, :, 0:1], NEG, [T["D1"]])
                a3 = g(AA).rearrange("p (s t) -> p s t", t=TS)
                self.tt("dve", a3[:, :, 0], a3[:, :, 0], M0T[0:8, :], ALU.max, [T["AA"], T["M0T"]], [T["AA"]])
                self.scan(ab, g(D1), g(AA), 0.0, ALU.add, ALU.max, [T["D1"], T["AA"]], [T["ABX"]])
                rho = M0T[0:8, :].unsqueeze(2).to_broadcast([8, NS, TS])
                rtok = [T["M0T"]]
                a_v = g(AA).rearrange("p (s t) -> p s t", t=TS)
                ab_v = ab.rearrange("p (s t) -> p s t", t=TS)
                ek_v = g(TMPg).rearrange("p (s t) -> p s t", t=TS)
                eq_v = g(D0).rearrange("p (s t) -> p s t", t=TS)
                self.tt("dve", g(AA), g(LI), g(BB), ALU.subtract, [T["LI"], T["BB"], T["ABX"]], [T["AA"]])
            else:
                self.memset("dve", g(D1), 0.0, [T["D1"]])
                if ti == 0:
                    self.memset("dve", ABX[0:8, 0:1], 0.0, [T["ABX"]])
                    ainit = 0.0
                else:
                    self.cp("dve", ABX[0:8, 0:1], self.ABP[0:8, 0:1], [T["ABprev"]], [T["ABX"]])
                    ainit = self.ABP[0:8, 0:1]
                self.scan(ab, g(D1), g(AA), ainit, ALU.add, ALU.max, [T["D1"], T["AA"], T["ABX"]] + ([T["ABprev"]] if ti else []), [T["ABX"]])
                rho = ABX[0:8, 0:n].rearrange("p (j l) -> p j l", l=64)[:, :, 0:1].to_broadcast([8, nck, 64])
                rtok = [T["ABX"]]
                a_v = g(AA).rearrange("p (j l) -> p j l", l=64)
                ab_v = ab.rearrange("p (j l) -> p j l", l=64)
                ek_v = g(TMPg).rearrange("p (j l) -> p j l", l=64)
                eq_v = g(D0).rearrange("p (j l) -> p j l", l=64)
            self.tt("dve", ek_v, a_v, rho, ALU.subtract, [T["AA"]] + rtok, [T["TMP"]])
            self.act(EKA[0:8, t0:t0 + n], g(TMPg), AF.Exp, [T["TMP"]], [tEK[ti]])
            self.tt("dve", eq_v, rho, ab_v, ALU.subtract, [T["ABX"], T["D0"]] + rtok, [T["D0"]])
            self.act(EQA[0:8, t0:t0 + n], g(D0), AF.Exp, [T["D0"]], [tEQ[ti]])
            self.tt("dve", g(MTg), g(BB), ab, ALU.add, [T["BB"], T["ABX"]], [T["MT"]])
            self.act(g(EMg), g(MTg), AF.Exp, [T["MT"]], [T["EM"]], scale=-1.0)
            bT = self.bank()
            for j in range(nck):
                self.tr(self.ps[bT][0:Lc, j * 8:(j + 1) * 8], EMg[0:8, j * Lc:(j + 1) * Lc], self.IDENTF[0:8, 0:8],
                        [T["EM"], self.tMASK], [self.tps[bT]])
            cb0 = t0 // 64 if not sample else 32
            if sample:
                self.cp("act", EMTS[0:Lc, 0:nck, :], self.ps[bT][0:Lc, 0:nck * 8].rearrange("p (j h) -> p j h", h=8),
                        [self.tps[bT]], [T["EMTS"]])
                self.cp("pool", MOUT[0:8, 1:17], g(MTg).rearrange("p (s t) -> p s t", t=TS)[:, :, 3], [T["MT"]], [T["MOUT"]])
            else:
                self.cp("act", self.EMTP[0:Lc, cb0:cb0 + nck, :], self.ps[bT][0:Lc, 0:nck * 8].rearrange("p (j h) -> p j h", h=8),
                        [self.tps[bT]], [T["EMTP"]])
                if ti == 3:
                    self.cp("pool", MOUT[0:8, 0:1], MTg[0:8, n - 1:n], [T["MT"]], [T["MOUT"]])
                self.cp("pool", self.BBP[0:8, 0:1], BB[0:8, n - 1:n], [T["BB"]], [T["BBprev"]])
                self.cp("pool", self.ABP[0:8, 0:1], ABX[0:8, n:n + 1], [T["ABX"]], [T["ABprev"]])
        if PRE:
            kb.end_defer()
            self.bank_pe = {}
        ar.release(m1)
        kb.barrier()
        kb.dma("sp", d["o_ml_m"], MOUT[0:8, :], reads=[T["MOUT"]])

        WIN = ar.bf16(NCH * 384).rearrange("p (k m) -> p k m", k=NCH)
        WO2 = [ar.bf16(D) for _ in range(2)]
        tWO = [Tok("mwo0"), Tok("mwo1")]
        RAW = [ar.f32(520) for _ in range(2)]
        ACC = [ar.f32(512) for _ in range(2)]
        SIL = [ar.f32(512) for _ in range(2)]
        SA = [dict(QP=ar.bf16(512), KP=ar.bf16(512), VA=ar.bf16(8 * 130).rearrange("p (j m) -> p j m", j=8),
                   KTt=ar.bf16(8 * 64).rearrange("p (j m) -> p j m", j=8), LAMB=ar.f32(512)) for _ in range(2)]
        SO = [ar.f32(512) for _ in range(3)]
        SH = [ar.f32(8 * 128).rearrange("p (j m) -> p j m", j=8) for _ in range(2)]
        STall = ar.bf16(8 * 64)
        SQH = ar.f32(8 * 128)
        STATH = ar.f32(32)
        DEN = ar.f32(16)
        HG = ar.bf16(512)
        C = ar.f32(130)
        Cd = ar.f32(130)
        Cb = ar.bf16(130)
        CS = ar.f32(4 * 130).rearrange("p (s m) -> p s m", s=4)
        CSd = ar.f32(4 * 130).rearrange("p (s m) -> p s m", s=4)
        CSb = ar.bf16(4 * 130).rearrange("p (s m) -> p s m", s=4)
        PCLb = ar.f32(8 * 130).rearrange("p (j m) -> p j m", j=8)
        CbAll = ar.bf16(9 * 130).rearrange("p (j m) -> p j m", j=9)
        main_tiles = [(t0, 512, 64, 8) for t0 in range(0, SEQ, 512)] + [(SEQ + 16 * q, 16, 4, 4) for q in range(4)]
        import os as _os
        if "ML_TILES" in _os.environ:
            main_tiles = [main_tiles[int(i)] for i in _os.environ["ML_TILES"].split(",")]
        units = []
        for h in range(int(_os.environ.get("ML_NH", 8))):
            for k_, (t0, n, L, NCk) in enumerate(main_tiles):
                units.append(dict(u=len(units), h=h, t0=t0, n=n, L=L, NCk=NCk, first=(k_ == 0),
                                  lastprompt=(t0 < SEQ and (k_ + 1 == len(main_tiles) or main_tiles[k_ + 1][0] >= SEQ))))
        for sa in SA:
            self.memset("pool", sa["VA"][0:64, :, 128:129], 1.0, [T["VAinit"]])

        def stageA(U):
            u, h, t0, n, L, NCk = U["u"], U["h"], U["t0"], U["n"], U["L"], U["NCk"]
            sa, k2, k3 = SA[u % 2], u % 2, u % 3
            QP, KP, VA, KTt, LAMB, Osig = sa["QP"], sa["KP"], sa["VA"], sa["KTt"], sa["LAMB"], SO[k3]
            tQP, tKP, tVA, tKTt, tLAM, tO = T["QP", k2], T["KP", k2], T["VA", k2], T["KTt", k2], T["LAM", k2], T["O", k3]
            sample = t0 >= SEQ
            ti5 = min(t0 // 512, 4)
            tsl = slice(t0, t0 + n)
            cs = lambda j: slice(j * L, (j + 1) * L)
            xt = [tXN[K, ti5] for K in range(NCH)]
            if U["first"]:
                for K in range(NCH):
                    rows = slice(K * 128, (K + 1) * 128)
                    kb.dma("pool", WIN[:, K, 0:64], d["ml_w_in"][0, rows, h * 64:(h + 1) * 64], writes=[T["WIN"]])
                    kb.dma("pool", WIN[:, K, 64:128], d["ml_w_in"][0, rows, 512 + h * 64:512 + (h + 1) * 64], writes=[T["WIN"]])
                    kb.dma("pool", WIN[:, K, 128:256], d["ml_w_in"][0, rows, 1024 + h * 128:1024 + (h + 1) * 128], writes=[T["WIN"]])
                    kb.dma("pool", WIN[:, K, 256:384], d["ml_w_in"][0, rows, 2048 + h * 128:2048 + (h + 1) * 128], writes=[T["WIN"]])
                kb.dma("pool", WO2[h % 2], d["ml_w_out"][0, h * 128:(h + 1) * 128, :], writes=[tWO[h % 2]])
                for w_ in range(2):
                    self.memset("pool", RAW[w_][0:64, 0:3], 0.0, [T["RAW", w_]])
                yield
            if sample:
                q4 = (t0 - SEQ) // 16
                for w_ in range(2):
                    rv = RAW[w_][0:64, 0:28].rearrange("p (s u) -> p s u", u=7)
                    kb.dma("sp", rv[:, :, 0:3], d["ml_convT"][h, w_, :, 4 * q4:4 * q4 + 4, :], writes=[T["RAW", w_]])
            bQ, bK, bO = self.bank("A"), self.bank("A"), self.bank("A")
            PQ, PK, PO_ = self.ps[bQ][0:64, :n], self.ps[bK][0:64, :n], self.ps[bO][:, :n]
            for (P_, cols, bb) in ((PQ, slice(0, 64), bQ), (PK, slice(64, 128), bK), (PO_, slice(256, 384), bO)):
                for K in range(NCH):
                    self.mm(P_, WIN[:, K, cols], XN[:, K, tsl], K == 0, K == NCH - 1, [T["WIN"], xt[K]], [self.tps[bb]])
                yield
            self.act(Osig[:, :n], PO_, AF.Sigmoid, [self.tps[bO]], [tO])
            for w_, (P_, bb) in enumerate(((PQ, bQ), (PK, bK))):
                mv = lambda k_: MLV[0:64, h, 5 * w_ + k_:5 * w_ + k_ + 1]
                R_ = RAW[w_]
                if sample:
                    rv = R_[0:64, 0:28].rearrange("p (s u) -> p s u", u=7)
                    self.cp("act", rv[:, :, 3:7], P_.rearrange("p (s t) -> p s t", t=TS), [self.tps[bb]], [T["RAW", w_]])
                    taps = [rv[:, :, k_:k_ + 4] for k_ in range(4)]
                    acc = ACC[w_][0:64, 0:n].rearrange("p (s t) -> p s t", t=TS)
                    self.cp("pool", CONVO[0:64, h, w_, 1 + 4 * q4:5 + 4 * q4, :], rv[:, :, 4:7], [T["RAW", w_]], [T["CONVO"]])
                else:
                    self.cp("act", R_[0:64, 3:3 + n], P_, [self.tps[bb]], [T["RAW", w_]])
                    taps = [R_[0:64, k_:k_ + n] for k_ in range(4)]
                    acc = ACC[w_][0:64, 0:n]
                yield
                self.ts("dve", acc, taps[0], mv(0), mv(4), ALU.mult, ALU.add, [T["RAW", w_], T["MLV"]], [T["ACC", w_]])
                for k_ in range(1, 4):
                    self.stt(acc, taps[k_], mv(k_), acc, ALU.mult, ALU.add, [T["RAW", w_], T["MLV"], T["ACC", w_]], [T["ACC", w_]])
                yield
                self.act(SIL[w_][0:64, 0:n], ACC[w_][0:64, 0:n], AF.Silu, [T["ACC", w_]], [T["SIL", w_]])
                if not sample:
                    if t0 + n == SEQ:
                        self.cp("pool", CONVO[0:64, h, w_, 0, :], R_[0:64, n:n + 3], [T["RAW", w_]], [T["CONVO"]])
                    self.cp("pool", R_[0:64, 0:3], R_[0:64, n:n + 3], [T["RAW", w_], T["ACC", w_]], [T["RAW", w_]])
                yield
            for j0 in range(0, NCk, 2):
                bV = self.bank("A")
                nj = min(2, NCk - j0)
                for jj in range(nj):
                    j = j0 + jj
                    PVt = self.ps[bV][0:L, jj * 128:(jj + 1) * 128]
                    for K in range(NCH):
                        self.mm(PVt, XN[:, K, t0 + j * L:t0 + (j + 1) * L], WIN[:, K, 128:256], K == 0, K == NCH - 1,
                                [T["WIN"], xt[K]], [self.tps[bV]])
                self.cp("act", VA[0:L, j0:j0 + nj, 0:128], self.ps[bV][0:L, 0:nj * 128].rearrange("p (j m) -> p j m", m=128),
                        [self.tps[bV], T["VAinit"]], [tVA])
                yield
            bM, bM2 = self.bank("A"), self.bank("A")
            PBK, PBQ = self.ps[bM][0:64, 0:n], self.ps[bM2][0:64, 0:n]
            tBQ = self.tps[bM2]
            self.mm(PBK, SEL[0:8, h, :], EKA[0:8, tsl], True, True, [T["SEL"], tEK[ti5]], [self.tps[bM]])
            self.mm(PBQ, SEL[0:8, h, :], EQA[0:8, tsl], True, True, [T["SEL"], tEQ[ti5]], [tBQ])
            yield
            self.tt("dve", KP[0:64, 0:n], SIL[1][0:64, 0:n], PBK, ALU.mult, [T["SIL", 1], self.tps[bM]], [tKP])
            self.stt(QP[0:64, 0:n], SIL[0][0:64, 0:n], 0.125, PBQ, ALU.mult, ALU.mult, [T["SIL", 0], tBQ], [tQP])
            self.cp("act", LAMB[0:64, 0:n], PBQ, [tBQ], [tLAM])
            yield
            bT = self.bank("A")
            PTr = self.ps[bT].bitcast(BF16)
            for j in range(NCk):
                self.tr(PTr[0:L, j * 64:(j + 1) * 64], KP[0:64, cs(j)], self.IDENTB[0:64, 0:64], [tKP, self.tMASK], [self.tps[bT]])
            self.cp("act", KTt[0:L, 0:NCk, :], PTr[0:L, 0:NCk * 64].rearrange("p (j m) -> p j m", m=64), [self.tps[bT]], [tKTt])
            yield

        def stageC(U):
            u, h, t0, n, L, NCk = U["u"], U["h"], U["t0"], U["n"], U["L"], U["NCk"]
            sa, k2 = SA[u % 2], u % 2
            QP, KP, VA, KTt, LAM = sa["QP"], sa["KP"], sa["VA"], sa["KTt"], sa["LAMB"]
            tQP, tKP, tVA, tKTt, tLAM = T["QP", k2], T["KP", k2], T["VA", k2], T["KTt", k2], T["LAM", k2]
            HT, tHT = SH[k2], T["HT", k2]
            sample = t0 >= SEQ
            cs = lambda j: slice(j * L, (j + 1) * L)
            if U["first"]:
                self.memset("pool", C[0:64], 0.0, [T["C"]])
                self.memset("pool", Cb[0:64], 0.0, [T["Cb"]])
            if sample:
                q4 = (t0 - SEQ) // 16
                for s in range(4):
                    kb.dma("sp", CS[0:64, s, 0:129], d["ml_C0T"][4 * q4 + s, h], writes=[T["CS", s]])
                yield
            PCL = PCLb[0:64, 0:NCk, 0:129]
            for j0 in range(0, NCk, 2):
                bP = self.bank("C")
                for jj in range(2):
                    j = j0 + jj
                    PCS = self.ps[bP][0:64, 256 * jj:256 * jj + 129]
                    self.mm(PCS, KTt[0:L, j, :], VA[0:L, j, 0:129], True, True, [tKTt, tVA], [self.tps[bP]])
                for jj in range(2):
                    j = j0 + jj
                    PCS = self.ps[bP][0:64, 256 * jj:256 * jj + 129]
                    lam = LAM[0:64, (j + 1) * L - 1:(j + 1) * L]
                    self.act(PCL[:, j, :], PCS, AF.Identity, [self.tps[bP], tLAM], [T["PCL"]], scale=lam)
                yield
            CbA = CbAll[0:64, 0:NCk + 1, 0:129]
            for j in range(NCk):
                lam = LAM[0:64, (j + 1) * L - 1:(j + 1) * L]
                if sample:
                    Cc, tC = CS[0:64, j, 0:129], T["CS", j]
                    self.cp("act", CbA[:, j, :], Cc, [tC], [T["CbA"]])
                    self.stt(Cc, Cc, lam, PCL[:, j, :], ALU.mult, ALU.add, [tC, tLAM, T["PCL"]], [tC])
                else:
                    Cc, tC = C[0:64, 0:129], T["C"]
                    if j == 0:
                        self.cp("act", CbA[:, 0, :], Cc, [tC], [T["CbA"]])
                    self.stt(CbA[:, j + 1, :], Cc, lam, PCL[:, j, :], ALU.mult, ALU.add, [tC, tLAM, T["PCL"]], [T["CbA"]])
                    self.stt(Cc, Cc, lam, PCL[:, j, :], ALU.mult, ALU.add, [tC, tLAM, T["PCL"]], [tC])
                yield
            bS = self.bank("C")
            PSTv = self.ps[bS][0:L, 0:NCk * L].rearrange("p (j t) -> p j t", j=NCk)
            for j in range(NCk):
                self.mm(PSTv[:, j, :], KP[0:64, cs(j)], QP[0:64, cs(j)], True, True, [tKP, tQP], [self.tps[bS]])
            yield
            STv = STall[0:L, 0:NCk * L].rearrange("p (j t) -> p j t", j=NCk)
            self.tt("dve", STv, PSTv, self.MIU[0:L, 0:L].unsqueeze(1).to_broadcast([L, NCk, L]), ALU.mult,
                    [self.tps[bS], self.tMASK], [T["ST"]])
            yield
            if sample:
                emall, tem = EMTS[0:L, 4 * q4:4 * q4 + NCk, h], T["EMTS"]
            else:
                emall, tem = self.EMTP[0:L, t0 // 64:t0 // 64 + NCk, h], T["EMTP"]
            for g0 in range(0, NCk, 3):
                g = min(3, NCk - g0)
                bN = self.bank("C")
                tN = self.tps[bN]
                PNDv = self.ps[bN][0:L, 0:g * 129].rearrange("p (j m) -> p j m", j=g)
                for jj in range(g):
                    j = g0 + jj
                    self.mm(PNDv[:, jj, :], QP[0:64, cs(j)], CbA[:, j, :], True, False, [tQP, T["CbA"]], [tN])
                    self.mm(PNDv[:, jj, :], STv[:, j, :], VA[0:L, j, 0:129], False, True, [T["ST"], tVA], [tN])
                yield
                dn = DEN[0:L, g0:g0 + g]
                self.act(dn, PNDv[:, :, 128], AF.Abs, [tN], [T["DEN"]])
                yield
                self.tt("dve", dn, dn, emall[:, g0:g0 + g], ALU.max, [T["DEN"], tem], [T["DEN"]])
                self.recip(dn, dn, [T["DEN"]], [T["DEN"]])
                yield
                self.tt("dve", HT[0:L, g0:g0 + g, :], PNDv[:, :, 0:128], dn.unsqueeze(2).to_broadcast([L, g, 128]), ALU.mult,
                        [tN, T["DEN"]], [tHT])
                yield
            if sample:
                for s in range(4):
                    kb.dma("sp", d["o_ml_Cs"][4 * q4 + s, h], CS[0:64, s, 0:129], reads=[T["CS", s]])
            if U["lastprompt"]:
                kb.dma("sp", d["o_ml_Cp"][h], C[0:64, 0:129], reads=[T["C"]])

        def stageD(U):
            u, h, t0, n, L, NCk = U["u"], U["h"], U["t0"], U["n"], U["L"], U["NCk"]
            k2, k3 = u % 2, u % 3
            HT, tHT = SH[k2], T["HT", k2]
            Osig, tO = SO[k3], T["O", k3]
            WOh = WO2[h % 2]
            ti5 = min(t0 // 512, 4)
            tsl = slice(t0, t0 + n)
            cs = lambda j: slice(j * L, (j + 1) * L)
            G = NCk
            HTf = HT[0:L, 0:G, :]
            SQv = SQH[0:L, 0:G * 128].rearrange("p (j m) -> p j m", m=128)
            self.act(SQv, HTf, AF.Square, [tHT], [T["SQH"]])
            yield
            self.reduce(STATH[0:L, 0:G], SQv, ALU.add, [T["SQH"]], [T["STATH"]])
            yield
            self.act(STATH[0:L, 0:G], STATH[0:L, 0:G], AF.Ln, [T["STATH"], self.tCONST], [T["STATH"]], scale=1.0 / 128, bias=self.EPSC[0:L, :])
            yield
            self.act(STATH[0:L, 0:G], STATH[0:L, 0:G], AF.Exp, [T["STATH"]], [T["STATH"]], scale=-0.5)
            yield
            self.tt("dve", HTf, HTf, STATH[0:L, 0:G].unsqueeze(2).to_broadcast([L, G, 128]), ALU.mult, [tHT, T["STATH"]], [tHT])
            yield
            bY = self.bank("C2")
            PYF = self.ps[bY][:, 0:n]
            for j in range(NCk):
                self.tr(PYF[:, cs(j)], HT[0:L, j, :], self.IDENTF[0:L, 0:L], [tHT, self.tMASK], [self.tps[bY]])
            yield
            self.stt(HG[:, 0:n], PYF, self.vcol("ml_norm_w", h), Osig[:, 0:n], ALU.mult, ALU.mult, [self.tps[bY], tO, self.tVEC], [T["HG"]])
            yield
            for dc in range(NCH):
                bO2 = self.bank("C2")
                PO2 = self.ps[bO2][:, 0:n]
                self.mm(PO2, WOh[:, dc * 128:(dc + 1) * 128], HG[:, 0:n], True, True, [tWO[h % 2], T["HG"]], [self.tps[bO2]])
                self.tt("dve", self.X[:, dc, tsl], PO2, self.X[:, dc, tsl], ALU.add, [self.tps[bO2], self.tX[dc, ti5]], [self.tX[dc, ti5]])
                yield

        def drain(gens):
            gens = [g for g in gens if g is not None]
            while gens:
                for g in list(gens):
                    try:
                        next(g)
                    except StopIteration:
                        gens.remove(g)

        NU_ = len(units)
        SCHED = int(_os.environ.get("K_SCHED", "1"))
        if SCHED:
            self.bank_pe = {}
            kb.begin_defer()
        for s in range(NU_ + 2):
            gA = stageA(units[s]) if s < NU_ else None
            gC = stageC(units[s - 1]) if 0 <= s - 1 < NU_ else None
            gD = stageD(units[s - 2]) if 0 <= s - 2 < NU_ else None
            drain([gC, gD, gA])
        if SCHED:
            kb.end_defer()
            self.bank_pe = {}
        kb.dma("sp", d["o_ml_conv"], CONVO[0:64], reads=[T["CONVO"]])
        ar.release(m0)
        kb.barrier()

    def ffn(self, L, which):
        kb, d, ar = self.kb, self.d, self.ar
        m = ar.mark()
        wg = d["ff%s_wg" % which][L]
        wu = d["ff%s_wu" % which][L]
        wd = d["ff%s_wd" % which][L]
        gname = "norm_ff%s%d" % (which, L)
        G = 4
        groups = [(f0, min(G, NFF - f0)) for f0 in range(0, NFF, G)]
        WG = [ar.bf16(NCH * 512).rearrange("p (c f) -> p c f", c=NCH) for _ in range(2)]
        WU = [ar.bf16(NCH * 512).rearrange("p (c f) -> p c f", c=NCH) for _ in range(2)]
        WD = [ar.bf16(G * D).rearrange("p (f o) -> p f o", f=G) for _ in range(2)]
        H = [ar.bf16(G * 512).rearrange("p (f n) -> p f n", f=G) for _ in range(2)]
        SG = [ar.f32(512) for _ in range(2)]
        SQ = [ar.bf16(512) for _ in range(2)]
        LN = ar.f32(512)
        RS = ar.f32(512)
        tW = TokMap()
        tH = TokMap()
        tSG = TokMap()
        tSQ = TokMap()
        tLN, tRS = Tok("ln"), Tok("rs")

        def load_group(gi):
            f0, nf = groups[gi]
            s = gi % 2
            for c in range(NCH):
                kb.dma("pool", WG[s][:, c, 0:nf * 128], wg[c * 128:(c + 1) * 128, f0 * 128:(f0 + nf) * 128],
                       writes=[tW["g", s, c]])
                kb.dma("pool", WU[s][:, c, 0:nf * 128], wu[c * 128:(c + 1) * 128, f0 * 128:(f0 + nf) * 128],
                       writes=[tW["u", s, c]])
            for fi in range(nf):
                kb.dma("pool", WD[s][:, fi, :], wd[(f0 + fi) * 128:(f0 + fi + 1) * 128, :], writes=[tW["d", s, fi]])

        load_group(0)
        load_group(1)

        for ti, (t0, n) in enumerate(TILES):
            def out_fn(c, rs, trs, ti=ti, t0=t0, n=n):
                kb.emit("dve", lambda e: e.scalar_tensor_tensor(out=self.XN[:, c, t0:t0 + n], in0=self.X[:, c, t0:t0 + n],
                                                                scalar=self.vcol(gname, c), in1=rs,
                                                                op0=ALU.mult, op1=ALU.mult),
                        reads=[self.tX[c, ti], trs, self.tVEC], writes=[self.tXN[c, ti]])
            self.rmsnorm_tile(ti, gname, out_fn, (SQ, tSQ, LN, tLN, RS, tRS, 4 + ti % 4))

        items = [(gi, ti) for gi in range(len(groups)) for ti in range(len(TILES))]
        po_rr = [0]

        def up(idx):
            gi, ti = items[idx]
            f0, nf = groups[gi]
            s = gi % 2
            hs = idx % 2
            t0, n = TILES[ti]
            for fi in range(nf):
                b = fi % 2
                pg = self.ps[b][:, :n]
                pu = self.ps[2 + b][:, :n]
                for c in range(NCH):
                    kb.emit("pe", lambda e, c=c, fi=fi, pg=pg: e.matmul(pg, WG[s][:, c, fi * 128:(fi + 1) * 128],
                                                                     self.XN[:, c, t0:t0 + n], start=(c == 0), stop=(c == NCH - 1)),
                            reads=[tW["g", s, c], self.tXN[c, ti]], writes=[self.tps[b]], signal=(c == NCH - 1))
                for c in range(NCH):
                    kb.emit("pe", lambda e, c=c, fi=fi, pu=pu: e.matmul(pu, WU[s][:, c, fi * 128:(fi + 1) * 128],
                                                                     self.XN[:, c, t0:t0 + n], start=(c == 0), stop=(c == NCH - 1)),
                            reads=[tW["u", s, c], self.tXN[c, ti]], writes=[self.tps[2 + b]], signal=(c == NCH - 1))
                kb.emit("act", lambda e, b=b, pg=pg: e.activation(out=SG[b][:, :n], in_=pg, func=AF.Silu),
                        reads=[self.tps[b]], writes=[tSG[b]])
                kb.emit("dve", lambda e, b=b, fi=fi, pu=pu: e.tensor_tensor(out=H[hs][:, fi, :n], in0=SG[b][:, :n], in1=pu, op=ALU.mult),
                        reads=[tSG[b], self.tps[2 + b]], writes=[tH[hs, fi]])

        def down(idx):
            gi, ti = items[idx]
            f0, nf = groups[gi]
            s = gi % 2
            hs = idx % 2
            t0, n = TILES[ti]
            for dc in range(NCH):
                bank = 4 + po_rr[0] % 4
                po_rr[0] += 1
                po = self.ps[bank][:, :n]
                for fi in range(nf):
                    kb.emit("pe", lambda e, fi=fi, dc=dc, po=po: e.matmul(po, WD[s][:, fi, dc * 128:(dc + 1) * 128], H[hs][:, fi, :n],
                                                                       start=(fi == 0), stop=(fi == nf - 1)),
                            reads=[tW["d", s, fi], tH[hs, fi]], writes=[self.tps[bank]], signal=(fi == nf - 1))
                kb.emit("dve", lambda e, dc=dc, po=po: e.scalar_tensor_tensor(out=self.X[:, dc, t0:t0 + n], in0=po, scalar=0.5,
                                                                           in1=self.X[:, dc, t0:t0 + n], op0=ALU.mult, op1=ALU.add),
                        reads=[self.tps[bank], self.tX[dc, ti]], writes=[self.tX[dc, ti]])

        ntile = len(TILES)
        for idx in range(len(items)):
            up(idx)
            if idx > 0:
                down(idx - 1)
                gi_prev, ti_prev = items[idx - 1]
                if ti_prev == ntile - 1 and gi_prev + 2 < len(groups):
                    load_group(gi_prev + 2)
        down(len(items) - 1)
        ar.release(m)
        kb.barrier()

    def final_norm(self):
        kb, d, ar = self.kb, self.d, self.ar
        m = ar.mark()
        SQ = [ar.bf16(512) for _ in range(2)]
        LN = ar.f32(512)
        RS = ar.f32(512)
        Y = [ar.f32(512) for _ in range(4)]
        tSQ, tY = TokMap(), TokMap()
        tLN, tRS = Tok("ln"), Tok("rs")
        rr = [0]
        for ti, (t0, n) in enumerate(TILES):
            def out_fn(c, rs, trs, ti=ti, t0=t0, n=n):
                s = rr[0] % 4
                rr[0] += 1
                kb.emit("dve", lambda e: e.scalar_tensor_tensor(out=Y[s][:, :n], in0=self.X[:, c, t0:t0 + n],
                                                                scalar=self.vcol("norm_final", c), in1=rs,
                                                                op0=ALU.mult, op1=ALU.mult),
                        reads=[self.tX[c, ti], trs, self.tVEC], writes=[tY[s]])
                kb.dma("sp", d["yT"][c * 128:(c + 1) * 128, t0:t0 + n], Y[s][:, :n], reads=[tY[s]])
            self.rmsnorm_tile(ti, "norm_final", out_fn, (SQ, tSQ, LN, tLN, RS, tRS, 4 + ti % 4))
        ar.release(m)
        kb.barrier()

    def dump_x(self):
        kb, d = self.kb, self.d
        for c in range(NCH):
            for ti, (t0, n) in enumerate(TILES):
                kb.dma("sp", d["yT"][c * 128:(c + 1) * 128, t0:t0 + n], self.X[:, c, t0:t0 + n], reads=[self.tX[c, ti]])

    def build(self):
        st = self.stages
        self.load_inputs()
        self.consts()
        for L in range(2):
            if "ffa%d" % L in st:
                self.ffn(L, "a")
            if "mix%d" % L in st and L == 0:
                self.rwkv()
            if "mix%d" % L in st and L == 1:
                self.mlstm()
            if "ffb%d" % L in st:
                self.ffn(L, "b")
        if "final" in st:
            self.final_norm()
        else:
            self.dump_x()
        self.kb.flush()
        return self.nc


ALL_STAGES = ("ffa0", "mix0", "ffb0", "ffa1", "mix1", "ffb1", "final")


def pack_vecs(inp):
    vecs = np.zeros((NVEC, D), np.float32)

    def put(name, v):
        vecs[VID[name]] = np.asarray(v, np.float32).reshape(D)

    for L in range(2):
        put("norm_ffa%d" % L, inp["norm_ffa"][L])
        put("norm_mix%d" % L, inp["norm_mix"][L])
        put("norm_ffb%d" % L, inp["norm_ffb"][L])
    put("norm_final", inp["norm_final"])
    for i in range(6):
        put("mu%d" % i, inp["rw_mu"][0, i])
    put("w0", inp["rw_w0"][0])
    put("a0", inp["rw_a0"][0])
    put("k_k", inp["rw_k_k"][0])
    put("k_a", inp["rw_k_a"][0])
    put("r_k", inp["rw_r_k"][0])
    put("gn_w", inp["rw_gn_w"][0])
    put("gn_b", inp["rw_gn_b"][0])
    for j in range(4):
        put("cw%d" % j, inp["ml_conv_w"][0, j])
    put("cb", inp["ml_conv_b"][0])
    put("ml_norm_w", inp["ml_norm_w"][0])
    return np.ascontiguousarray(vecs.reshape(NVEC, NCH, 128).transpose(2, 0, 1).reshape(128, NVEC * NCH))


def make_in_maps(inp):
    vecs = pack_vecs(inp)
    shared = {"vecs": vecs}
    for nm in ("ffa_wg", "ffa_wu", "ffb_wg", "ffb_wu", "ffa_wd", "ffb_wd"):
        shared[nm] = np.ascontiguousarray(inp[nm], dtype=np.float32)
    cw = np.asarray(inp["ml_conv_w"][0], np.float32)
    cbv = np.asarray(inp["ml_conv_b"][0], np.float32)
    mlv = np.zeros((64, 8, 10), np.float32)
    for w in range(2):
        for k in range(4):
            mlv[:, :, 5 * w + k] = cw[k, w * 512:(w + 1) * 512].reshape(8, 64).T
        mlv[:, :, 5 * w + 4] = cbv[w * 512:(w + 1) * 512].reshape(8, 64).T
    shared["mlv"] = mlv
    shared["bif"] = np.ascontiguousarray(np.asarray(inp["ml_b_if"][0], np.float32).reshape(2, 8).T)
    for nm in ("rw_wr", "rw_wk", "rw_wv", "rw_wo", "rw_w1", "rw_w2", "rw_a1", "rw_a2", "rw_g1", "rw_g2", "ml_w_in", "ml_w_out"):
        shared[nm] = np.ascontiguousarray(inp[nm], dtype=np.float32)
    maps = []
    for core in range(NCORES):
        xs = np.concatenate([inp["x_prompt"][core], inp["x_sample"][core * NS:(core + 1) * NS].reshape(NS * TS, D)], axis=0)
        m = dict(shared)
        m["xT"] = np.ascontiguousarray(xs.T.astype(np.float32))
        sq = slice(core * NS, (core + 1) * NS)
        sh = inp["state_rwkv_shift"][0, sq]
        m["shiftT"] = np.ascontiguousarray(sh.reshape(NS, NCH, 128).transpose(2, 1, 0).astype(np.float32))
        m["rw_S0T"] = np.ascontiguousarray(inp["state_rwkv_S"][0, sq].transpose(0, 1, 3, 2).astype(np.float32))
        m["ml_m0T"] = np.ascontiguousarray(inp["state_mlstm_m"][0, sq].T.astype(np.float32))
        c0 = np.concatenate([inp["state_mlstm_C"][0, sq].transpose(0, 1, 3, 2), inp["state_mlstm_n"][0, sq][..., None]], axis=-1)
        m["ml_C0T"] = np.ascontiguousarray(c0.astype(np.float32))
        cv = inp["state_mlstm_conv"][0, sq]
        m["ml_convT"] = np.ascontiguousarray(cv.reshape(NS, 3, 2, 8, 64).transpose(3, 2, 4, 0, 1).astype(np.float32))
        maps.append(m)
    return maps


def run(inp, stages=ALL_STAGES, trace=False):
    b = Builder(stages)
    nc = b.build()
    maps = make_in_maps(inp)
    res = run_bass_kernel_spmd(nc, maps, core_ids=list(range(NCORES)), trace=trace)
    return b, res


def assemble(results):
    f = np.float32
    yp = np.zeros((8, SEQ, D), f); ys = np.zeros((128, TS, D), f)
    p_S = np.zeros((1, 8, 16, 64, 64), f); p_sh = np.zeros((1, 8, D), f)
    p_C = np.zeros((1, 8, 8, 128, 64), f); p_n = np.zeros((1, 8, 8, 64), f); p_m = np.zeros((1, 8, 8), f)
    p_cv = np.zeros((1, 8, 3, D), f)
    s_S = np.zeros((1, 128, 16, 64, 64), f); s_sh = np.zeros((1, 128, D), f)
    s_C = np.zeros((1, 128, 8, 128, 64), f); s_n = np.zeros((1, 128, 8, 64), f); s_m = np.zeros((1, 128, 8), f)
    s_cv = np.zeros((1, 128, 3, D), f)
    for core in range(NCORES):
        r = results[core]
        sq = slice(core * NS, (core + 1) * NS)
        y = r["yT"].T
        yp[core] = y[:SEQ]
        ys[sq] = y[SEQ:].reshape(NS, TS, D)
        sho = r["o_shift"]
        p_sh[0, core] = sho[:, :, 0].T.reshape(D)
        s_sh[0, sq] = sho[:, :, 1:].transpose(2, 1, 0).reshape(NS, D)
        p_S[0, core] = r["o_rw_Sp"].transpose(0, 2, 1)
        s_S[0, sq] = r["o_rw_Ss"].transpose(0, 1, 3, 2)
        cp = r["o_ml_Cp"]
        p_C[0, core] = cp[:, :, 0:128].transpose(0, 2, 1)
        p_n[0, core] = cp[:, :, 128]
        cs_ = r["o_ml_Cs"]
        s_C[0, sq] = cs_[:, :, :, 0:128].transpose(0, 1, 3, 2)
        s_n[0, sq] = cs_[:, :, :, 128]
        mo = r["o_ml_m"]
        p_m[0, core] = mo[:, 0]
        s_m[0, sq] = mo[:, 1:].T
        cv = r["o_ml_conv"]
        cvt = cv.transpose(3, 4, 2, 1, 0).reshape(17, 3, D)
        p_cv[0, core] = cvt[0]
        s_cv[0, sq] = cvt[1:]
    return (yp, ys, p_S, p_sh, p_C, p_n, p_m, p_cv, s_S, s_sh, s_C, s_n, s_m, s_cv)


def kernel(**inp):
    b, res = run(inp)
    return assemble(res.results)
```

```python
import numpy as np
from contextlib import ExitStack
import concourse.bass as bass
import concourse.mybir as mybir
from concourse.bass_utils import run_bass_kernel_spmd

F32 = mybir.dt.float32
BF16 = mybir.dt.bfloat16
AF = mybir.ActivationFunctionType
ALU = mybir.AluOpType
AX = mybir.AxisListType

NCORES = 8
D = 1024
NCH = 8
SEQ = 2048
NS = 16
TS = 4
NT = SEQ + NS * TS
DFF = 2816
NFF = DFF // 128
EPS = 1e-6

ENGS = ("pe", "act", "dve", "pool", "sp")
SAME_ENGINE_SYNC = True

VID = {}
_v = 0
for _nm in ("norm_ffa0", "norm_ffa1", "norm_mix0", "norm_mix1", "norm_ffb0", "norm_ffb1", "norm_final",
            "mu0", "mu1", "mu2", "mu3", "mu4", "mu5", "w0", "a0", "k_k", "k_a", "r_k", "gn_w", "gn_b",
            "cw0", "cw1", "cw2", "cw3", "cb", "ml_norm_w"):
    VID[_nm] = _v
    _v += 1
NVEC = _v


class Tok:
    __slots__ = ("name", "w", "r", "excl")

    def __init__(self, name="", excl=False):
        self.name = name
        self.w = []
        self.r = []
        self.excl = excl


class TokMap(dict):
    def __missing__(self, key):
        t = Tok(str(key))
        self[key] = t
        return t


class KB:
    def __init__(self, n_dma_sems=12):
        self.nc = bass.Bass("TRN2", target_bir_lowering=False, dynamic_dma_scratch_size=4096)
        self.es = ExitStack()
        nc = self.nc
        self.sem = {}
        self.count = {}
        self.prog = {e: [] for e in ENGS}
        self.waited = {e: {} for e in ENGS}
        for e in ENGS:
            self.sem[e] = self.es.enter_context(nc.semaphore("s_" + e))
            self.count[e] = 0
        self.dsem = {}
        self.dval = {}
        self.dnext = {}
        for q in ("sp", "pool", "act"):
            self.dsem[q] = []
            for j in range(n_dma_sems):
                key = "d_%s_%d" % (q, j)
                self.sem[key] = self.es.enter_context(nc.semaphore(key))
                self.dsem[q].append(key)
                self.dval[key] = 0
            self.dnext[q] = 0
        self.ninstr = 0
        self.defer = None

    def _wait(self, eng, semkey, value):
        if value <= 0:
            return
        if self.waited[eng].get(semkey, 0) >= value:
            return
        self.waited[eng][semkey] = value
        sem = self.sem[semkey]
        self.prog[eng].append(lambda e, sem=sem, value=value: e.wait_ge(sem, value))

    def _deps(self, eng, reads, writes):
        deps = set()
        for t in reads:
            deps.update(t.w)
            if t.excl:
                deps.update(x for x in t.r if x[0] != eng)
        for t in writes:
            deps.update(t.w)
            deps.update(t.r)
        for (sk, v) in deps:
            if sk == eng and (eng == "pe" or not SAME_ENGINE_SYNC):
                continue
            self._wait(eng, sk, v)

    def emit(self, eng, fn, reads=(), writes=(), signal=True, dur=0.4, after=None):
        if self.defer is not None:
            self.defer.append(dict(kind="op", eng=eng, fn=fn, reads=tuple(reads), writes=tuple(writes), signal=signal,
                                   dur=dur, extra=[after] if after is not None else []))
            self.ninstr += 1
            return len(self.defer) - 1
        self._deps(eng, reads, writes)
        if after is not None:
            self._wait(eng, after[0], after[1])
        self.ninstr += 1
        if signal:
            self.count[eng] += 1
            cid = (eng, self.count[eng])
            sem = self.sem[eng]
            self.prog[eng].append(lambda e, fn=fn, sem=sem: fn(e).then_inc(sem, 1))
        else:
            cid = (eng, self.count[eng] + 1)
            self.prog[eng].append(lambda e, fn=fn: fn(e))
        for t in reads:
            t.r.append(cid)
        for t in writes:
            t.w = [cid]
            t.r = []
        return cid

    def dma(self, q, out, in_, reads=(), writes=()):
        if self.defer is not None:
            self.defer.append(dict(kind="dma", eng=q, out=out, in_=in_, reads=tuple(reads), writes=tuple(writes), signal=True,
                                   dur=0.1, extra=[]))
            self.ninstr += 1
            return len(self.defer) - 1
        self._deps(q, reads, writes)
        return self._dma_issue(q, out, in_, reads, writes)

    def _dma_issue(self, q, out, in_, reads, writes):
        j = self.dnext[q]
        self.dnext[q] = (j + 1) % len(self.dsem[q])
        key = self.dsem[q][j]
        self._wait(q, key, self.dval[key])
        self.dval[key] += 16
        cid = (key, self.dval[key])
        sem = self.sem[key]
        self.ninstr += 1
        self.prog[q].append(lambda e, out=out, in_=in_, sem=sem: e.dma_start(out=out, in_=in_).then_inc(sem, 16))
        for t in reads:
            t.r.append(cid)
        for t in writes:
            t.w = [cid]
            t.r = []
        return cid

    def begin_defer(self):
        self.barrier()
        self.defer = []

    def end_defer(self):
        recs = self.defer
        self.defer = None
        self._schedule(recs)
        self.barrier()

    def _schedule(self, recs):
        n = len(recs)
        lastw, readers = {}, {}
        deps = [None] * n
        for i, r in enumerate(recs):
            e = r["eng"]
            dset = set(r["extra"])
            for t in r["reads"]:
                k = id(t)
                if k in lastw:
                    dset.add(lastw[k])
                if t.excl:
                    dset.update(j for j in readers.get(k, ()) if recs[j]["eng"] != e)
            for t in r["writes"]:
                k = id(t)
                if k in lastw:
                    dset.add(lastw[k])
                dset.update(readers.get(k, ()))
            dset.discard(i)
            deps[i] = dset
            for t in r["reads"]:
                readers.setdefault(id(t), []).append(i)
            for t in r["writes"]:
                lastw[id(t)] = i
                readers[id(t)] = []
        succs = [[] for _ in range(n)]
        indeg = [0] * n
        for i in range(n):
            indeg[i] = len(deps[i])
            for j in deps[i]:
                succs[j].append(i)
        import os as _os
        LAT_X = float(_os.environ.get("K_LATX", "0.15"))
        LAT_S = float(_os.environ.get("K_LATS", "0.05"))
        DSC = float(_os.environ.get("K_DSC", "1.5"))
        CP = int(_os.environ.get("K_CP", "1"))
        CPW = float(_os.environ.get("K_CPW", "0.2"))
        tail = [0.0] * n
        for i in range(n - 1, -1, -1):
            ri = recs[i]
            t_ = 0.0
            for j in succs[i]:
                lat = LAT_S if recs[j]["eng"] == ri["eng"] else LAT_X
                if tail[j] + lat > t_:
                    t_ = tail[j] + lat
            tail[i] = ri["dur"] * DSC + t_
        ready_t = [0.0] * n
        start = [0.0] * n
        free = {e: 0.0 for e in ENGS}
        ready = {e: [] for e in ENGS}
        for i in range(n):
            if indeg[i] == 0:
                ready[recs[i]["eng"]].append(i)
        done = 0
        while done < n:
            best, bkey = None, None
            for e in ENGS:
                lst = ready[e]
                if not lst:
                    continue
                fe = free[e]
                if CP:
                    cand = min(lst, key=lambda i: (int(max(fe, ready_t[i]) / CPW), -tail[i], i))
                    key = (int(max(fe, ready_t[cand]) / CPW), -tail[cand], cand, max(fe, ready_t[cand]))
                else:
                    cand = min(lst, key=lambda i: (max(fe, ready_t[i]), i))
                    key = (max(fe, ready_t[cand]), 0.0, cand)
                if bkey is None or key < bkey:
                    best, bkey = cand, key
            i = best
            r = recs[i]
            e = r["eng"]
            ready[e].remove(i)
            st = bkey[3] if CP else bkey[0]
            start[i] = st
            du = r["dur"] * DSC
            free[e] = st + du
            fin = st + du + (2.5 if r["kind"] == "dma" else 0.0)
            for j in succs[i]:
                lat = LAT_S if recs[j]["eng"] == e else LAT_X
                if fin + lat > ready_t[j]:
                    ready_t[j] = fin + lat
                indeg[j] -= 1
                if indeg[j] == 0:
                    ready[recs[j]["eng"]].append(j)
            done += 1
        order = sorted(range(n), key=lambda i: (start[i], i))
        cids = [None] * n
        cnt = self.count["pe"]
        for i in order:
            r = recs[i]
            if r["eng"] == "pe" and r["kind"] == "op" and r["signal"]:
                cnt += 1
                cids[i] = ("pe", cnt)
        nxt_sig = None
        for i in range(n - 1, -1, -1):
            r = recs[i]
            if r["eng"] != "pe" or r["kind"] != "op":
                continue
            if r["signal"]:
                nxt_sig = i
            else:
                assert nxt_sig is not None, "deferred PE stream must end with a signalled instruction"
                cids[i] = cids[nxt_sig]
        for i in order:
            r = recs[i]
            e = r["eng"]
            for j in deps[i]:
                sk, v = cids[j]
                if sk == e and j not in r["extra"] and (e == "pe" or not SAME_ENGINE_SYNC):
                    continue
                self._wait(e, sk, v)
            if r["kind"] == "dma":
                cids[i] = self._dma_issue(e, r["out"], r["in_"], (), ())
                continue
            fn = r["fn"]
            if r["signal"]:
                self.count[e] += 1
                cid = (e, self.count[e])
                if e == "pe":
                    assert cid == cids[i], (cid, cids[i])
                cids[i] = cid
                sem = self.sem[e]
                self.prog[e].append(lambda eng_, fn=fn, sem=sem: fn(eng_).then_inc(sem, 1))
            else:
                self.prog[e].append(lambda eng_, fn=fn: fn(eng_))
        self.sched_span = max(start) if n else 0.0

    def barrier(self):
        for e in ENGS:
            for e2 in ENGS:
                if e2 != e:
                    self._wait(e, e2, self.count[e2])
            for q in self.dsem:
                for key in self.dsem[q]:
                    self._wait(e, key, self.dval[key])

    def flush(self):
        self.barrier()
        nc = self.nc
        prog = self.prog
        with nc.Block() as block:
            @block.tensor
            def _(e):
                for f in prog["pe"]:
                    f(e)

            @block.scalar
            def _(e):
                for f in prog["act"]:
                    f(e)

            @block.vector
            def _(e):
                for f in prog["dve"]:
                    f(e)

            @block.gpsimd
            def _(e):
                for f in prog["pool"]:
                    f(e)

            @block.sync
            def _(e):
                for f in prog["sp"]:
                    f(e)
        self.prog = {e: [] for e in ENGS}


class Arena:
    def __init__(self, ap, nwords):
        self.ap = ap
        self.n = nwords
        self.top = 0

    def mark(self):
        return self.top

    def release(self, m):
        self.top = m

    def f32(self, nwords):
        assert self.top + nwords <= self.n, ("arena overflow", self.top, nwords, self.n)
        a = self.ap[:, self.top:self.top + nwords]
        self.top += nwords
        return a

    def bf16(self, nelem):
        nwords = (nelem + 1) // 2
        a = self.f32(nwords).bitcast(BF16)
        return a[:, 0:nelem]


ARENA_WORDS = 56280
XNW = 1 + SEQ + NS * (TS + 1)
SOFF = 1 + SEQ
TILES = [(0, 512), (512, 512), (1024, 512), (1536, 512), (2048, 64)]


class Builder:
    def __init__(self, stages):
        self.stages = stages
        self.kb = KB()
        kb = self.kb
        nc = kb.nc
        self.nc = nc
        es = kb.es
        d = {}

        def din(name, shape):
            d[name] = nc.dram_tensor(name, list(shape), F32, kind="ExternalInput").ap()

        def dout(name, shape):
            d[name] = nc.dram_tensor(name, list(shape), F32, kind="ExternalOutput").ap()

        din("xT", (D, NT))
        din("vecs", (128, NVEC * 8))
        for nm in ("ffa_wg", "ffa_wu", "ffb_wg", "ffb_wu"):
            din(nm, (2, D, DFF))
        for nm in ("ffa_wd", "ffb_wd"):
            din(nm, (2, DFF, D))
        for nm in ("rw_wr", "rw_wk", "rw_wv", "rw_wo"):
            din(nm, (1, D, D))
        din("rw_w1", (1, D, 64)); din("rw_w2", (1, 64, D))
        din("rw_a1", (1, D, 64)); din("rw_a2", (1, 64, D))
        din("rw_g1", (1, D, 160)); din("rw_g2", (1, 160, D))
        din("ml_w_in", (1, D, 3088)); din("ml_w_out", (1, D, D))
        din("mlv", (64, 8, 10)); din("bif", (8, 2)); din("ml_m0T", (8, NS))
        din("ml_C0T", (NS, 8, 64, 129)); din("ml_convT", (8, 2, 64, NS, 3))
        din("shiftT", (128, NCH, NS))
        din("rw_S0T", (NS, 16, 64, 64))
        dout("yT", (D, NT))
        dout("o_ml_m", (8, 17)); dout("o_ml_Cp", (8, 64, 129)); dout("o_ml_Cs", (NS, 8, 64, 129))
        dout("o_ml_conv", (64, 8, 2, 17, 3))
        dout("o_shift", (128, NCH, 17))
        dout("o_rw_Sp", (16, 64, 64))
        dout("o_rw_Ss", (NS, 16, 64, 64))
        self.d = d
        self.out_names = [k for k in d if k == "yT" or k.startswith("o_")]

        arena_t = es.enter_context(nc.sbuf_tensor("arena", [128, ARENA_WORDS], F32))
        self.ar = Arena(arena_t, ARENA_WORDS)
        self.psall = es.enter_context(nc.psum_tensor("psall", [128, 8, 512], F32))
        self.ps = [self.psall[:, i, :] for i in range(8)]
        self.tps = [Tok("ps%d" % i, excl=True) for i in range(8)]
        self.bank_rr = 0
        self.bank_pe = {}

        ar = self.ar
        self.X = ar.f32(NCH * NT).rearrange("p (c n) -> p c n", c=NCH)
        self.tX = TokMap()
        self.VEC = ar.f32(NVEC * 8)
        self.tVEC = Tok("vec")
        self.ONES = ar.bf16(128)
        self.tONES = Tok("ones")
        self.XNraw = ar.bf16(NCH * XNW)
        self.XN = self.XNraw[:, 0:NCH * NT].rearrange("p (c n) -> p c n", c=NCH)
        self.XNS = self.XNraw.rearrange("p (c n) -> p c n", c=NCH)
        self.tXN = TokMap()


    BANK_GROUPS = {"A": (0, 1, 2, 3), "B": (4, 5), "C": (6, 7), "C2": (4, 5)}

    def bank(self, group=None):
        if group is None:
            b = self.bank_rr % 8
            self.bank_rr += 1
            return b
        if not hasattr(self, "_grr"):
            self._grr = {}
        k = self._grr.get(group, 0)
        self._grr[group] = k + 1
        g = self.BANK_GROUPS[group]
        return g[k % len(g)]

    def _dur(self, eng, ap):
        base = 0.22 + 0.0011 * ap.free_size()
        return base * (3.0 if eng == "pool" else 1.0)

    def act(self, out, in_, func, reads, writes, **kw):
        return self.kb.emit("act", lambda e: e.activation(out=out, in_=in_, func=func, **kw), reads, writes, dur=self._dur("act", out))

    def cp(self, eng, out, in_, reads, writes):
        if eng == "act":
            return self.kb.emit("act", lambda e: e.activation(out=out, in_=in_, func=AF.Copy), reads, writes, dur=self._dur("act", out))
        return self.kb.emit(eng, lambda e: e.tensor_copy(out=out, in_=in_), reads, writes, dur=self._dur(eng, out))

    def tt(self, eng, out, in0, in1, op, reads, writes):
        return self.kb.emit(eng, lambda e: e.tensor_tensor(out=out, in0=in0, in1=in1, op=op), reads, writes, dur=self._dur(eng, out))

    def ts(self, eng, out, in0, s1, s2, op0, op1, reads, writes):
        if s2 is None:
            return self.kb.emit(eng, lambda e: e.tensor_scalar(out=out, in0=in0, scalar1=s1, scalar2=None, op0=op0), reads, writes,
                                dur=self._dur(eng, out))
        return self.kb.emit(eng, lambda e: e.tensor_scalar(out=out, in0=in0, scalar1=s1, scalar2=s2, op0=op0, op1=op1), reads, writes,
                            dur=self._dur(eng, out))

    def stt(self, out, in0, scalar, in1, op0, op1, reads, writes):
        return self.kb.emit("dve", lambda e: e.scalar_tensor_tensor(out=out, in0=in0, scalar=scalar, in1=in1, op0=op0, op1=op1),
                            reads, writes, dur=self._dur("dve", out))

    def _pe_rows(self, lhsT, writes):
        K = lhsT.partition_size()
        base = lhsT.base_partition()
        tile = 32 if K <= 32 else (64 if K <= 64 else 128)
        lo, hi = (base // tile) * tile, (base // tile) * tile + tile
        if tile == 128:
            lo, hi = 0, 128
        after = None
        for t in writes:
            for b in range(8):
                if t is self.tps[b]:
                    prev = self.bank_pe.get(b)
                    if prev is not None and prev[2] is not None and (prev[1] <= lo or hi <= prev[0]):
                        after = prev[2]
                    self.bank_pe[b] = [lo, hi, None]
        return tile < 128, after

    def _pe_done(self, writes, cid):
        for t in writes:
            for b in range(8):
                if t is self.tps[b] and self.bank_pe.get(b) is not None:
                    self.bank_pe[b][2] = cid

    def mm(self, out, lhsT, rhs, start, stop, reads, writes, signal=None):
        if signal is None:
            signal = stop
        partial, after = self._pe_rows(lhsT, writes)
        if partial:
            signal = True
        dur = 0.07 + 0.00045 * rhs.free_size()
        cid = self.kb.emit("pe", lambda e: e.matmul(out, lhsT, rhs, start=start, stop=stop), reads, writes, signal=signal,
                           dur=dur, after=after)
        self._pe_done(writes, cid)
        return cid

    def tr(self, out, in_, ident, reads, writes):
        partial, after = self._pe_rows(in_, writes)
        cid = self.kb.emit("pe", lambda e: e.transpose(out, in_, ident), reads, writes, dur=0.12, after=after)
        self._pe_done(writes, cid)
        return cid

    def memset(self, eng, ap, val, writes):
        return self.kb.emit(eng, lambda e: e.memset(ap, val), (), writes)

    def scan(self, out, d0, d1, init, op0, op1, reads, writes):
        return self.kb.emit("dve", lambda e: e.tensor_tensor_scan(out=out, data0=d0, data1=d1, initial=init, op0=op0, op1=op1), reads, writes,
                            dur=0.22 + 0.0022 * out.free_size())

    def recip(self, out, in_, reads, writes):
        return self.kb.emit("dve", lambda e: e.reciprocal(out=out, in_=in_), reads, writes, dur=0.22 + 0.008 * out.free_size())

    def reduce(self, out, in_, op, reads, writes, axis=None):
        axis = AX.X if axis is None else axis
        return self.kb.emit("dve", lambda e: e.tensor_reduce(out=out, in_=in_, axis=axis, op=op), reads, writes, dur=self._dur("dve", in_))

    def vcol(self, name, c):
        j = VID[name] * 8 + c
        return self.VEC[:, j:j + 1]

    def load_inputs(self):
        kb, d = self.kb, self.d
        kb.dma("sp", self.VEC, d["vecs"][:, :], writes=[self.tVEC])
        for c in range(NCH):
            for ti, (t0, n) in enumerate(TILES):
                kb.dma("sp", self.X[:, c, t0:t0 + n], d["xT"][c * 128:(c + 1) * 128, t0:t0 + n],
                       writes=[self.tX[c, ti]])
        kb.emit("dve", lambda e: e.memset(self.ONES, 1.0), writes=[self.tONES])

    def _full_bank(self, b):
        self.bank_pe[b] = [0, 128, None]

    def rmsnorm_tile(self, ti, gname, out_fn, scratch):
        kb = self.kb
        t0, n = TILES[ti]
        SQ, tSQ, LN, tLN, RS, tRS, bank = scratch
        ps = self.ps[bank][:, :n]
        self._full_bank(bank)
        for c in range(NCH):
            s = c % 2
            kb.emit("act", lambda e, c=c, s=s: e.activation(out=SQ[s][:, :n], in_=self.X[:, c, t0:t0 + n], func=AF.Square),
                    reads=[self.tX[c, ti]], writes=[tSQ[s]])
            kb.emit("pe", lambda e, c=c, s=s: e.matmul(ps, self.ONES, SQ[s][:, :n], start=(c == 0), stop=(c == NCH - 1)),
                    reads=[tSQ[s], self.tONES], writes=[self.tps[bank]], signal=True)
        kb.emit("act", lambda e: e.activation(out=LN[:, :n], in_=ps, func=AF.Ln, scale=1.0 / D, bias=self.EPSC),
                reads=[self.tps[bank], self.tCONST], writes=[tLN])
        kb.emit("act", lambda e: e.activation(out=RS[:, :n], in_=LN[:, :n], func=AF.Exp, scale=-0.5),
                reads=[tLN], writes=[tRS])
        for c in range(NCH):
            out_fn(c, RS[:, :n], tRS)

    def consts(self):
        kb, ar = self.kb, self.ar
        self.CONST = ar.f32(8)
        self.tCONST = Tok("const")
        self.EPSC = self.CONST[:, 0:1]
        self.ONEC = self.CONST[:, 1:2]
        self.NHALFC = self.CONST[:, 2:3]
        self.GNEPSC = self.CONST[:, 3:4]
        for col, val in ((0, EPS), (1, 1.0), (2, -0.5), (3, 64e-5)):
            kb.emit("dve", lambda e, col=col, val=val: e.memset(self.CONST[:, col:col + 1], val), (), [self.tCONST])
        ONESF = ar.f32(128)
        tO = Tok("onesf")
        self.memset("pool", ONESF, 1.0, [tO])
        self.IDENTF = ar.f32(128)
        self.IDENTB = ar.bf16(128)
        self.BONES = ar.bf16(128)
        self.tMASK = Tok("masks")
        kb.emit("pool", lambda e: e.affine_select(out=self.IDENTF, in_=ONESF, pattern=[[-1, 128]], compare_op=ALU.is_equal,
                                                  fill=0.0, base=0, channel_multiplier=1), [tO], [self.tMASK])
        self.cp("pool", self.IDENTB, self.IDENTF, [self.tMASK], [self.tMASK])
        self.memset("pool", self.BONES, 0.0, [self.tMASK])
        self.memset("pool", self.BONES[0:64, 0:64], 1.0, [self.tMASK])
        self.memset("pool", self.BONES[64:128, 64:128], 1.0, [self.tMASK])
        MSU = ar.f32(64)
        MIU = ar.f32(64)
        self.MASKXT = ar.f32(64)
        kb.emit("pool", lambda e: e.affine_select(out=MSU[0:64, :], in_=ONESF[0:64, 0:64], pattern=[[1, 64]], compare_op=ALU.is_gt,
                                                  fill=0.0, base=0, channel_multiplier=-1), [tO], [self.tMASK])
        kb.emit("pool", lambda e: e.affine_select(out=MIU[0:64, :], in_=ONESF[0:64, 0:64], pattern=[[1, 64]], compare_op=ALU.is_ge,
                                                  fill=0.0, base=0, channel_multiplier=-1), [tO], [self.tMASK])
        kb.emit("pool", lambda e: e.affine_select(out=self.MASKXT[0:64, :], in_=ONESF[0:64, 0:64], pattern=[[-1, 64]], compare_op=ALU.is_gt,
                                                  fill=0.0, base=0, channel_multiplier=1), [tO], [self.tMASK])
        self.ts("pool", self.MASKXT[0:64, :], self.MASKXT[0:64, :], -1.0, None, ALU.mult, None, [self.tMASK], [self.tMASK])
        self.MIU = MIU
        self.MASKLL = ar.f32(2 * 4 * 64).rearrange("p (h b t) -> p h b t", h=2, b=4)
        for h in range(2):
            self.cp("pool", self.MASKLL[0:64, h, 0, :], MSU[0:64, :], [self.tMASK], [self.tMASK])
            self.cp("pool", self.MASKLL[0:64, h, 1, :], MIU[0:64, :], [self.tMASK], [self.tMASK])
            self.ts("pool", self.MASKLL[0:64, h, 2, :], MSU[0:64, :], -1.0, None, ALU.mult, None, [self.tMASK], [self.tMASK])
            self.cp("pool", self.MASKLL[0:64, h, 3, :], MIU[0:64, :], [self.tMASK], [self.tMASK])
        self.SM64 = ar.f32(256)
        self.SM4 = ar.f32(16)
        self.memset("pool", self.SM64, 1.0, [self.tMASK])
        self.memset("pool", self.SM64.rearrange("p (j l) -> p j l", l=64)[:, :, 0:1], 0.0, [self.tMASK])
        self.memset("pool", self.SM4, 1.0, [self.tMASK])
        self.memset("pool", self.SM4.rearrange("p (j l) -> p j l", l=4)[:, :, 0:1], 0.0, [self.tMASK])
        self.NEGW0 = ar.f32(8)
        j = VID["w0"] * 8
        self.ts("dve", self.NEGW0, self.VEC[:, j:j + 8], -1.0, None, ALU.mult, None, [self.tVEC], [self.tMASK])

    def rwkv(self):
        kb, d, ar = self.kb, self.d, self.ar
        m0 = ar.mark()
        XNS = self.XNS
        tXNS = TokMap()
        gname = "norm_mix0"
        mu = lambda i, K: self.vcol("mu%d" % i, K)

        HWA = ar.bf16(NT)
        HG1 = ar.bf16(NT)
        HG2 = ar.bf16(NT)
        tHWA, tHG = TokMap(), TokMap()
        W2A2 = ar.bf16(D)
        G2A = ar.bf16(D)
        G2B = ar.bf16(D)
        tW2 = Tok("w2a2g2")
        SHO = ar.f32(NCH * 17).rearrange("p (c j) -> p c j", c=NCH)
        tSHO = Tok("sho")
        SHI = ar.f32(NCH * NS).rearrange("p (c j) -> p c j", c=NCH)
        tSHI = Tok("shi")

        kb.dma("pool", W2A2[0:64, :], d["rw_w2"][0], writes=[tW2])
        kb.dma("pool", W2A2[64:128, :], d["rw_a2"][0], writes=[tW2])
        kb.dma("pool", G2A, d["rw_g2"][0, 0:128, :], writes=[tW2])
        self.memset("pool", G2B, 0.0, [tW2])
        self.memset("pool", HG2, 0.0, [tHG[0]])
        kb.dma("pool", G2B[0:32, :], d["rw_g2"][0, 128:160, :], writes=[tW2])
        kb.dma("sp", SHI, d["shiftT"], writes=[tSHI])

        for c in range(NCH):
            self.memset("pool", XNS[:, c, 0:1], 0.0, [tXNS[c, "init"]])
            sv = XNS[:, c, SOFF:SOFF + NS * 5].rearrange("p (j u) -> p j u", u=5)
            self.cp("pool", sv[:, :, 0], SHI[:, c, :], [tSHI], [tXNS[c, "init"]])

        def xn_aps(K, t0, n):
            if t0 < SEQ:
                return XNS[:, K, 1 + t0:1 + t0 + n], XNS[:, K, t0:t0 + n]
            j0 = (t0 - SEQ) // TS
            nj = n // TS
            sv = XNS[:, K, SOFF + 5 * j0:SOFF + 5 * (j0 + nj)].rearrange("p (j u) -> p j u", u=5)
            return sv[:, :, 1:5], sv[:, :, 0:4]

        def xn_toks(K, t0):
            ti = min(t0 // 512, 4)
            return [tXNS[K, ti], tXNS[K, max(ti - 1, 0)], tXNS[K, "init"]]

        def pview(ps_ap, t0, n):
            if t0 < SEQ:
                return ps_ap
            return ps_ap.rearrange("p (j t) -> p j t", t=TS)

        def mixproj(out, wa, wb, cols, t0, n, wtok, ptok):
            o = pview(out, t0, n)
            for K in range(NCH):
                xa, xb = xn_aps(K, t0, n)
                self.mm(o, wa[:, K, cols], xa, K == 0, False, [wtok] + xn_toks(K, t0), [ptok])
                self.mm(o, wb[:, K, cols], xb, False, K == NCH - 1, [wtok] + xn_toks(K, t0), [ptok])

        def scale_w(raw, wb, Mcols, mu_list, tok):
            for K in range(NCH):
                for (cs, mi) in mu_list:
                    self.ts("pool", wb[:, K, cs], raw[:, K, cs], mu(mi, K), None, ALU.mult, None, [tok, self.tVEC], [tok])
            self.tt("pool", raw, raw, wb, ALU.subtract, [tok], [tok])

        m1 = ar.mark()
        W1A = ar.bf16(NCH * 128).rearrange("p (k m) -> p k m", k=NCH)
        W1B = ar.bf16(NCH * 128).rearrange("p (k m) -> p k m", k=NCH)
        G1A = ar.bf16(NCH * 160).rearrange("p (k m) -> p k m", k=NCH)
        G1B = ar.bf16(NCH * 160).rearrange("p (k m) -> p k m", k=NCH)
        tW1, tG1 = Tok("w1a1"), Tok("g1")
        for K in range(NCH):
            kb.dma("pool", W1A[:, K, 0:64], d["rw_w1"][0, K * 128:(K + 1) * 128, :], writes=[tW1])
            kb.dma("pool", W1A[:, K, 64:128], d["rw_a1"][0, K * 128:(K + 1) * 128, :], writes=[tW1])
            kb.dma("pool", G1A[:, K, :], d["rw_g1"][0, K * 128:(K + 1) * 128, :], writes=[tG1])
        scale_w(W1A, W1B, 128, [(slice(0, 64), 1), (slice(64, 128), 4)], tW1)
        scale_w(G1A, G1B, 160, [(slice(0, 160), 5)], tG1)
        XNF = [ar.f32(512) for _ in range(2)]
        SQ = [ar.bf16(512) for _ in range(2)]
        LN = ar.f32(512)
        RS = ar.f32(512)
        tXNF, tSQ = TokMap(), TokMap()
        tLN, tRS = Tok("ln"), Tok("rs")
        rr = [0]
        import os as _os
        PRE = int(_os.environ.get("K_PRE", "1"))
        if PRE:
            self.bank_pe = {}
            kb.begin_defer()
        for ti, (t0, n) in enumerate(TILES):
            def out_fn(c, rs, trs, ti=ti, t0=t0, n=n):
                s = rr[0] % 2
                rr[0] += 1
                xf = XNF[s][:, :n]
                self.stt(xf, self.X[:, c, t0:t0 + n], self.vcol(gname, c), rs, ALU.mult, ALU.mult,
                         [self.tX[c, ti], trs, self.tVEC], [tXNF[s]])
                if t0 < SEQ:
                    self.cp("act", XNS[:, c, 1 + t0:1 + t0 + n], xf, [tXNF[s]], [tXNS[c, ti]])
                    if t0 + n == SEQ:
                        self.cp("pool", SHO[:, c, 0:1], xf[:, n - 1:n], [tXNF[s]], [tSHO])
                else:
                    sv = XNS[:, c, SOFF:SOFF + NS * 5].rearrange("p (j u) -> p j u", u=5)
                    xv = xf.rearrange("p (j t) -> p j t", t=TS)
                    self.cp("act", sv[:, :, 1:5], xv, [tXNF[s]], [tXNS[c, ti]])
                    self.cp("pool", SHO[:, c, 1:17], xv[:, :, 3], [tXNF[s]], [tSHO])
            self.rmsnorm_tile(ti, gname, out_fn, (SQ, tSQ, LN, tLN, RS, tRS, self.bank()))
            b1, b2, b3 = self.bank(), self.bank(), self.bank()
            mixproj(self.ps[b1][:, :n], W1A, W1B, slice(0, 128), t0, n, tW1, self.tps[b1])
            mixproj(self.ps[b2][:, :n], G1A, G1B, slice(0, 128), t0, n, tG1, self.tps[b2])
            mixproj(self.ps[b3][0:32, :n], G1A, G1B, slice(128, 160), t0, n, tG1, self.tps[b3])
            self.act(HWA[0:64, t0:t0 + n], self.ps[b1][0:64, :n], AF.Tanh, [self.tps[b1]], [tHWA[ti]])
            self.cp("act", HWA[64:128, t0:t0 + n], self.ps[b1][64:128, :n], [self.tps[b1]], [tHWA[ti]])
            self.act(HG1[:, t0:t0 + n], self.ps[b2][:, :n], AF.Sigmoid, [self.tps[b2]], [tHG[ti]])
            self.act(HG2[0:32, t0:t0 + n], self.ps[b3][0:32, :n], AF.Sigmoid, [self.tps[b3]], [tHG[ti]])
        if PRE:
            kb.end_defer()
            self.bank_pe = {}
        ar.release(m1)
        kb.barrier()
        kb.dma("sp", d["o_shift"], SHO, reads=[tSHO])

        WN = 256

        def f32t():
            return ar.f32(WN)

        def bf16t():
            return ar.bf16(WN)
        WA2 = [{nm: ar.bf16(NCH * 128).rearrange("p (k m) -> p k m", k=NCH) for nm in "rkv"} for _ in range(2)]
        WB2 = [{nm: ar.bf16(NCH * 128).rearrange("p (k m) -> p k m", k=NCH) for nm in "rkv"} for _ in range(2)]
        WO2 = [ar.bf16(D) for _ in range(2)]
        tWc2 = [{nm: Tok("w%s%d" % (nm, i)) for nm in "rkv"} for i in range(2)]
        tWO = [Tok("wo0"), Tok("wo1")]
        Rf, Kf, Vf, A_, EW, CUM, EM, KK, KF, Bv, T1, T2 = [f32t() for _ in range(12)]
        EQ = EW
        Vb, SQb, KTb, BTb, YG = [bf16t() for _ in range(5)]
        RKR = SQb
        S3 = []
        for _ in range(3):
            S3.append(dict(
                KR=ar.bf16(2 * WN).rearrange("p (a n) -> p a n", a=2),
                KTt=ar.bf16(4 * 128).rearrange("p (j m) -> p j m", j=4),
                BTt=ar.bf16(4 * 128).rearrange("p (j m) -> p j m", j=4),
                VTt=ar.bf16(4 * 128).rearrange("p (j m) -> p j m", j=4),
                EP=f32t(), BONUS=f32t(), Gf=f32t(), LLs=ar.bf16(4 * 2 * 4 * 64)))
        S2 = []
        for _ in range(2):
            S2.append(dict(XTs=ar.bf16(4 * 2 * 64), PW=[ar.bf16(8 * 2 * 64) for _ in range(2)]))
        PT = [ar.bf16(8 * 64) for _ in range(2)]
        Gs = ar.bf16(128)
        NU = ar.bf16(128)
        YT = ar.f32(512)
        SQ2 = ar.f32(512)
        YF = SQ2[:, 0:256]
        STAT = ar.f32(32)
        H = ar.f32(64)
        H0d = ar.f32(64)
        Hb = ar.bf16(64)
        HS = ar.f32(4 * 64).rearrange("p (j v) -> p j v", j=4)
        HSb = ar.bf16(4 * 64).rearrange("p (j v) -> p j v", j=4)
        T = TokMap()

        main_tiles = [(t0, 256, 64) for t0 in range(0, SEQ, 256)] + [(SEQ + 16 * q, 16, 4) for q in range(4)]
        import os as _os
        if "KDEBUG" in _os.environ:
            print("rwkv arena top", ar.top, "of", ar.n)
        if "RW_TILES" in _os.environ:
            main_tiles = [main_tiles[int(i)] for i in _os.environ["RW_TILES"].split(",")]
        NCc = int(_os.environ.get("RW_NC", NCH))
        units = []
        for c in range(NCc):
            for k_, (t0, n, L) in enumerate(main_tiles):
                u = len(units)
                units.append(dict(u=u, c=c, t0=t0, n=n, L=L, first=(k_ == 0), last=(k_ == len(main_tiles) - 1),
                                  lastprompt=(t0 < SEQ and (k_ + 1 == len(main_tiles) or main_tiles[k_ + 1][0] >= SEQ))))

        def load_weights(c):
            ccols = slice(c * 128, (c + 1) * 128)
            WA, WB, tWc = WA2[c % 2], WB2[c % 2], tWc2[c % 2]
            for nm, key, mi in (("r", "rw_wr", 0), ("k", "rw_wk", 2), ("v", "rw_wv", 3)):
                for K in range(NCH):
                    kb.dma("pool", WA[nm][:, K, :], d[key][0, K * 128:(K + 1) * 128, ccols], writes=[tWc[nm]])
                scale_w(WA[nm], WB[nm], 128, [(slice(0, 128), mi)], tWc[nm])

        def stageA(U):
            u, c, t0, n, L = U["u"], U["c"], U["t0"], U["n"], U["L"]
            s3, s2 = S3[u % 3], S2[u % 2]
            k3, k2 = u % 3, u % 2
            sample = t0 >= SEQ
            NCk = 4
            ti5 = min(t0 // 512, 4)
            tsl = slice(t0, t0 + n)
            cs = lambda j: slice(j * L, (j + 1) * L)
            ccols = slice(c * 128, (c + 1) * 128)
            WA, WB, tWc = WA2[c % 2], WB2[c % 2], tWc2[c % 2]
            if U["first"]:
                if c == 0:
                    load_weights(0)
                if c + 1 < NCc:
                    load_weights(c + 1)
                yield
            KR, KTt, BTt, VTt, EP, BONUS, Gf = s3["KR"], s3["KTt"], s3["BTt"], s3["VTt"], s3["EP"], s3["BONUS"], s3["Gf"]
            tKR, tKTt, tBTt, tVTt, tEP, tBONUS, tGf, tLL = (T["KR", k3], T["KTt", k3], T["BTt", k3], T["VTt", k3], T["EP", k3],
                                                          T["BONUS", k3], T["Gf", k3], T["LLs", k3])
            tXT = T["XTs", k2]
            bA, bB, bC, bD = self.bank("A"), self.bank("A"), self.bank("A"), self.bank("A")
            PR, PK = self.ps[bA][:, 0:n], self.ps[bA][:, 256:256 + n]
            PV, PGt = self.ps[bB][:, 0:n], self.ps[bB][:, 256:256 + n]
            PWL, PAL = self.ps[bC][:, 0:n], self.ps[bC][:, 256:256 + n]
            PKK, PSm = self.ps[bD][:, 0:n], self.ps[bD][:, 256:256 + n]
            for K0 in range(0, NCH, 2):
                pass
            mixproj(PR, WA["r"], WB["r"], slice(0, 128), t0, n, tWc["r"], self.tps[bA])
            yield
            mixproj(PK, WA["k"], WB["k"], slice(0, 128), t0, n, tWc["k"], self.tps[bA])
            yield
            mixproj(PV, WA["v"], WB["v"], slice(0, 128), t0, n, tWc["v"], self.tps[bB])
            self.mm(PGt, G2A[:, ccols], HG1[:, tsl], True, False, [tW2, tHG[ti5]], [self.tps[bB]])
            self.mm(PGt, G2B[:, ccols], HG2[:, tsl], False, True, [tW2, tHG[ti5], tHG[0]], [self.tps[bB]])
            self.mm(PWL, W2A2[0:64, ccols], HWA[0:64, tsl], True, True, [tW2, tHWA[ti5]], [self.tps[bC]])
            self.mm(PAL, W2A2[64:128, ccols], HWA[64:128, tsl], True, True, [tW2, tHWA[ti5]], [self.tps[bC]])
            yield
            w = lambda a: a[:, 0:n]
            tV = self.tVEC
            self.cp("act", w(Rf), PR, [self.tps[bA]], [T["Rf"]])
            self.cp("act", w(Kf), PK, [self.tps[bA]], [T["Kf"]])
            yield
            self.cp("act", w(Vf), PV, [self.tps[bB]], [T["Vf"]])
            self.cp("act", w(Gf), PGt, [self.tps[bB]], [tGf])
            self.cp("dve", w(Vb), w(Vf), [T["Vf"]], [T["Vb"]])
            yield
            self.act(w(A_), PAL, AF.Sigmoid, [self.tps[bC], tV], [T["A"]], bias=self.vcol("a0", c))
            self.act(w(T1), PWL, AF.Exp, [self.tps[bC], self.tMASK], [T["T1"]], scale=-1.0, bias=self.NEGW0[:, c:c + 1])
            self.ts("dve", w(KK), w(Kf), self.vcol("k_k", c), None, ALU.mult, None, [T["Kf"], tV], [T["KK"]])
            yield
            self.act(w(T1), w(T1), AF.Ln, [T["T1"], self.tCONST], [T["T1"]], bias=self.ONEC)
            self.act(w(SQb), w(KK), AF.Square, [T["KK"]], [T["SQb"]])
            self.mm(PKK, self.BONES, w(SQb), True, True, [T["SQb"], self.tMASK], [self.tps[bD]], signal=True)
            yield
            self.act(w(EW), w(T1), AF.Exp, [T["T1"], self.tCONST], [T["EW"]], scale=-1.0, bias=self.NHALFC)
            self.ts("dve", w(T1), w(A_), -1.0, self.vcol("k_a", c), ALU.add, ALU.mult, [T["A"], tV], [T["T1"]])
            self.stt(w(KF), w(T1), 1.0, w(Kf), ALU.add, ALU.mult, [T["T1"], T["Kf"]], [T["KF"]])
            yield
            SM = self.SM4[:, 0:n] if sample else self.SM64[:, 0:n]
            self.scan(w(CUM), SM, w(EW), 0.0, ALU.mult, ALU.subtract, [T["EW"], self.tMASK], [T["CUM"]])
            self.act(w(T2), PKK, AF.Sqrt, [self.tps[bD]], [T["T2"]])
            yield
            self.act(w(EP), w(CUM), AF.Exp, [T["CUM"]], [tEP])
            self.act(w(EM), w(CUM), AF.Exp, [T["CUM"]], [T["EM"]], scale=-1.0)
            self.ts("dve", w(T2), w(T2), 1e-12, None, ALU.max, None, [T["T2"]], [T["T2"]])
            self.recip(w(T2), w(T2), [T["T2"]], [T["T2"]])
            yield
            self.tt("dve", w(KK), w(KK), w(T2), ALU.mult, [T["KK"], T["T2"]], [T["KK"]])
            self.tt("dve", w(T2), w(CUM), w(EW), ALU.add, [T["CUM"], T["EW"], T["KK"]], [T["T2"]])
            self.act(w(EQ), w(T2), AF.Exp, [T["T2"]], [T["EW"]])
            yield
            self.stt(w(RKR), w(Rf), self.vcol("r_k", c), w(KF), ALU.mult, ALU.mult, [T["Rf"], T["KF"], tV], [T["SQb"]])
            self.mm(PSm, self.BONES, w(RKR), True, True, [T["SQb"], self.tMASK], [self.tps[bD]], signal=True)
            self.tt("dve", w(Bv), w(KK), w(A_), ALU.mult, [T["KK"], T["A"]], [T["Bv"]])
            self.tt("dve", KR[:, 1, 0:n], w(Rf), w(EP), ALU.mult, [T["Rf"], tEP], [tKR])
            yield
            self.tt("dve", KR[:, 0, 0:n], w(KK), w(EQ), ALU.mult, [T["KK"], T["EW"]], [tKR])
            self.tt("dve", w(KTb), w(KF), w(EM), ALU.mult, [T["KF"], T["EM"]], [T["KTb"]])
            self.tt("dve", w(BTb), w(Bv), w(EM), ALU.mult, [T["Bv"], T["EM"]], [T["BTb"]])
            self.tt("dve", w(BONUS), PSm, w(Vf), ALU.mult, [self.tps[bD], T["Vf"]], [tBONUS])
            yield
            bT = self.bank("A")
            PTr = self.ps[bT].bitcast(BF16)
            for (src, ts_, off) in ((KTb, "KTb", 0), (BTb, "BTb", 1)):
                for j in range(NCk):
                    self.tr(PTr[0:L, off * 512 + j * 128:off * 512 + (j + 1) * 128], src[:, cs(j)], self.IDENTB,
                            [T[ts_], self.tMASK], [self.tps[bT]])
            self.cp("act", KTt[0:L, :, :], PTr[0:L, 0:512].rearrange("p (j m) -> p j m", j=4), [self.tps[bT]], [tKTt])
            self.cp("act", BTt[0:L, :, :], PTr[0:L, 512:1024].rearrange("p (j m) -> p j m", j=4), [self.tps[bT]], [tBTt])
            yield
            bT2 = self.bank("A")
            PTr2 = self.ps[bT2].bitcast(BF16)
            for j in range(NCk):
                self.tr(PTr2[0:L, j * 128:(j + 1) * 128], Vb[:, cs(j)], self.IDENTB, [T["Vb"], self.tMASK], [self.tps[bT2]])
            self.cp("act", VTt[0:L, :, :], PTr2[0:L, 0:512].rearrange("p (j m) -> p j m", j=4), [self.tps[bT2]], [tVTt])
            yield
            LLv = s3["LLs"][:, 0:4 * 2 * 4 * L].rearrange("p (j h b t) -> p j h b t", j=4, h=2, b=4)
            XTv = s2["XTs"][:, 0:4 * 2 * L].rearrange("p (j h t) -> p j h t", j=4, h=2)
            mk = self.MASKLL[0:L, :, :, 0:L]
            for h in range(2):
                hs = slice(64 * h, 64 * h + 64)
                bX = self.bank("A")
                PXT = self.ps[bX][:, 0:4 * L].rearrange("p (j t) -> p j t", j=4)
                for g0 in (0, 2):
                    bL = self.bank("A")
                    PLL = self.ps[bL][:, 0:2 * 4 * L].rearrange("p (j b t) -> p j b t", j=2, b=4)
                    for jj in range(2):
                        j = g0 + jj
                        self.mm(PLL[0:L, jj, 0:2, :], KTb[hs, cs(j)], KR[hs, :, cs(j)], True, True, [T["KTb"], tKR], [self.tps[bL]])
                        self.mm(PLL[0:L, jj, 2:4, :], BTb[hs, cs(j)], KR[hs, :, cs(j)], True, True, [T["BTb"], tKR], [self.tps[bL]])
                        self.mm(PXT[0:L, j, :], KR[hs, 0, cs(j)], BTb[hs, cs(j)], True, True, [T["BTb"], tKR], [self.tps[bX]])
                    self.tt("dve", LLv[0:L, g0:g0 + 2, h], PLL[0:L], mk, ALU.mult, [self.tps[bL], self.tMASK], [tLL])
                    yield
                self.tt("dve", XTv[0:L, :, h, :], PXT[0:L], self.MASKXT[0:L, 0:L].unsqueeze(1).to_broadcast([L, 4, L]), ALU.mult,
                        [self.tps[bX], self.tMASK], [tXT])
                yield

        def stageB(U):
            u, L = U["u"], U["L"]
            s3, s2 = S3[u % 3], S2[u % 2]
            k3, k2 = u % 3, u % 2
            NCk, NM = 4, 8
            LLv = s3["LLs"][:, 0:4 * 2 * 4 * L].rearrange("p (j h b t) -> p j h b t", j=4, h=2, b=4)
            XTv = s2["XTs"][:, 0:4 * 2 * L].rearrange("p (j h t) -> p j h t", j=4, h=2)
            tLL, tXT = T["LLs", k3], T["XTs", k2]
            PWv = [p[:, 0:NM * 2 * L].rearrange("p (i a t) -> p i a t", i=NM, a=2) for p in s2["PW"]]
            PTv = [p[:, 0:NM * L].rearrange("p (i t) -> p i t", i=NM) for p in PT]
            tPW = [T["PW", k2, 0], T["PW", k2, 1]]
            PW4 = PWv[0].rearrange("p (j h) a t -> p j h a t", h=2)
            self.cp("dve", PW4[0:L, :, :, 0, :], LLv[0:L, :, :, 2, :], [tLL], [tPW[0]])
            self.cp("pool", PWv[0][0:L, :, 1, :], self.IDENTB[0:L, 0:L].unsqueeze(1).to_broadcast([L, NM, L]), [self.tMASK], [tPW[0]])
            self.cp("act", PTv[0][0:L].rearrange("p (j h) t -> p j h t", h=2), XTv[0:L], [tXT], [T["PT", 0]])
            yield
            nlev = 6 if L == 64 else 2
            cur = 0
            mpb = 4 if L == 64 else 8
            for lev in range(nlev):
                nxt = 1 - cur
                last = lev == nlev - 1
                for i0 in range(0, NM, mpb):
                    bI = self.bank("B")
                    PA = self.ps[bI][:, 0:mpb * 2 * L].rearrange("p (i a t) -> p i a t", i=mpb, a=2)
                    for ii in range(mpb):
                        i = i0 + ii
                        self.mm(PA[0:L, ii], PTv[cur][0:L, i, :], PWv[cur][0:L, i], True, True,
                                [T["PT", cur], tPW[cur]], [self.tps[bI]], signal=(ii == mpb - 1))
                    if not last:
                        self.cp("act", PWv[nxt][0:L, i0:i0 + mpb, 0, :], PA[0:L, :, 0, :], [self.tps[bI]], [tPW[nxt]])
                    self.tt("dve", PWv[nxt][0:L, i0:i0 + mpb, 1, :], PA[0:L, :, 1, :], PWv[cur][0:L, i0:i0 + mpb, 1, :], ALU.add,
                            [self.tps[bI], tPW[cur]], [tPW[nxt]])
                    yield
                if not last:
                    bJ = self.bank("B")
                    PB = self.ps[bJ][:, 0:NM * L].rearrange("p (i t) -> p i t", i=NM)
                    for i in range(NM):
                        self.mm(PB[0:L, i, :], PWv[cur][0:L, i, 0, :], PTv[cur][0:L, i, :], True, True,
                                [T["PT", cur], tPW[cur]], [self.tps[bJ]], signal=(i == NM - 1))
                    self.cp("act", PTv[nxt][0:L], PB[0:L], [self.tps[bJ]], [T["PT", nxt]])
                    yield
                cur = nxt
            U["Wv"] = PWv[cur]
            U["tWv"] = tPW[cur]

        def stageC(U):
            u, c, t0, n, L = U["u"], U["c"], U["t0"], U["n"], U["L"]
            s3 = S3[u % 3]
            k3 = u % 3
            sample = t0 >= SEQ
            NCk = 4
            ti5 = min(t0 // 512, 4)
            tsl = slice(t0, t0 + n)
            cs = lambda j: slice(j * L, (j + 1) * L)
            w = lambda a: a[:, 0:n]
            tV = self.tVEC
            KR, KTt, BTt, VTt, EP, BONUS, Gf = s3["KR"], s3["KTt"], s3["BTt"], s3["VTt"], s3["EP"], s3["BONUS"], s3["Gf"]
            tKR, tKTt, tBTt, tVTt, tEP, tBONUS, tGf, tLL = (T["KR", k3], T["KTt", k3], T["BTt", k3], T["VTt", k3], T["EP", k3],
                                                          T["BONUS", k3], T["Gf", k3], T["LLs", k3])
            LLv = s3["LLs"][:, 0:4 * 2 * 4 * L].rearrange("p (j h b t) -> p j h b t", j=4, h=2, b=4)
            Wv, tWv = U["Wv"], U["tWv"]
            WO = WO2[c % 2]
            if U["first"]:
                kb.dma("pool", WO, d["rw_wo"][0, c * 128:(c + 1) * 128, :], writes=[tWO[c % 2]])
                self.memset("pool", H, 0.0, [T["H"]])
                self.memset("pool", Hb, 0.0, [T["Hb"]])
            if sample:
                q = (t0 - SEQ) // 16
                for jj in range(4):
                    kb.dma("sp", HS[:, jj, :], d["rw_S0T"][4 * q + jj, 2 * c:2 * c + 2].rearrange("h k v -> (h k) v"),
                           writes=[T["HS", jj]])
                    self.cp("act", HSb[:, jj, :], HS[:, jj, :], [T["HS", jj]], [T["HSb", jj]])
                yield
            YTv = YT[:, 0:4 * 128].rearrange("p (j m) -> p j m", j=4)
            for j in range(NCk):
                if sample:
                    Hc, Hbc, Hdc = HS[:, j, :], HSb[:, j, :], H0d
                    tH, tHb, tHd = T["HS", j], T["HSb", j], T["H0d"]
                else:
                    Hc, Hbc, Hdc = H, Hb, H0d
                    tH, tHb, tHd = T["H"], T["Hb"], T["H0d"]
                DL = EP[:, (j + 1) * L - 1:(j + 1) * L]
                bS = self.bank("C")
                PG = self.ps[bS][0:L, 0:128]
                PU = self.ps[bS][0:L, 128:256]
                PY = self.ps[bS][0:L, 256:384]
                bH = self.bank("C")
                PH = self.ps[bH][:, 0:64]
                tS = self.tps[bS]
                tSH = self.tps[bH]
                for h in range(2):
                    hs = slice(64 * h, 64 * h + 64)
                    self.mm(PG[:, hs], LLv[0:L, j, h, 0, :], VTt[0:L, j, hs], True, False, [tLL, tVTt], [tS])
                    self.mm(PG[:, hs], KR[hs, 0, cs(j)], Hbc[hs, :], False, True, [tKR, tHb], [tS], signal=True)
                self.act(Hdc, Hc, AF.Identity, [tH, tEP], [tHd], scale=DL)
                yield
                self.cp("act", Gs[0:L, :], PG, [tS], [T["Gs"]])
                yield
                for h in range(2):
                    hs = slice(64 * h, 64 * h + 64)
                    self.mm(PU[:, hs], Wv[0:L, 2 * j + h, 1, :], Gs[0:L, hs], True, True, [tWv, T["Gs"]], [tS], signal=True)
                yield
                self.act(NU[0:L, :], PU, AF.Identity, [tS], [T["NU"]], scale=-1.0)
                yield
                for h in range(2):
                    hs = slice(64 * h, 64 * h + 64)
                    self.mm(PH[hs, :], KTt[0:L, j, hs], VTt[0:L, j, hs], True, False, [tKTt, tVTt], [tSH])
                    self.mm(PH[hs, :], BTt[0:L, j, hs], NU[0:L, hs], False, True, [tBTt, T["NU"]], [tSH], signal=True)
                for h in range(2):
                    hs = slice(64 * h, 64 * h + 64)
                    self.mm(PY[:, hs], LLv[0:L, j, h, 1, :], VTt[0:L, j, hs], True, False, [tLL, tVTt], [tS])
                    self.mm(PY[:, hs], LLv[0:L, j, h, 3, :], NU[0:L, hs], False, False, [tLL, T["NU"]], [tS])
                    self.mm(PY[:, hs], KR[hs, 1, cs(j)], Hbc[hs, :], False, True, [tKR, tHb], [tS], signal=True)
                yield
                self.stt(Hbc, PH, DL, Hdc, ALU.mult, ALU.add, [tSH, tEP, tHd], [tHb])
                self.stt(Hc, PH, DL, Hdc, ALU.mult, ALU.add, [tSH, tEP, tHd], [tH])
                self.cp("act", YTv[0:L, j, :], PY, [tS], [T["YT"]])
                yield
            if sample:
                for jj in range(4):
                    kb.dma("sp", d["o_rw_Ss"][4 * q + jj, 2 * c:2 * c + 2].rearrange("h k v -> (h k) v"), HS[:, jj, :],
                           reads=[T["HS", jj]])
            if U["lastprompt"]:
                kb.dma("sp", d["o_rw_Sp"][2 * c:2 * c + 2].rearrange("h k v -> (h k) v"), H, reads=[T["H"]])
            G8 = 8
            YT3 = YT[:, 0:512].rearrange("p (g v) -> p g v", g=G8)
            SQ3 = SQ2[:, 0:512].rearrange("p (g v) -> p g v", g=G8)
            SUMv, VARv, RSTv = STAT[:, 0:8], STAT[:, 8:16], STAT[:, 16:24]
            self.reduce(SUMv[0:L, :], YT3[0:L], ALU.add, [T["YT"]], [T["SUM"]])
            self.ts("dve", SUMv[0:L, :], SUMv[0:L, :], 1.0 / 64, None, ALU.mult, None, [T["SUM"]], [T["SUM"]])
            yield
            self.tt("dve", YT3[0:L], YT3[0:L], SUMv[0:L, :].unsqueeze(2).to_broadcast([L, G8, 64]), ALU.subtract,
                    [T["YT"], T["SUM"]], [T["YT"]])
            yield
            self.act(SQ2[0:L, 0:512], YT[0:L, 0:512], AF.Square, [T["YT"]], [T["SQ2"]])
            yield
            self.reduce(VARv[0:L, :], SQ3[0:L], ALU.add, [T["SQ2"]], [T["VAR"]])
            yield
            self.act(RSTv[0:L, :], VARv[0:L, :], AF.Ln, [T["VAR"], self.tCONST], [T["RST"]], scale=1.0 / 64, bias=self.GNEPSC[0:L, :])
            yield
            self.act(RSTv[0:L, :], RSTv[0:L, :], AF.Exp, [T["RST"]], [T["RST"]], scale=-0.5)
            yield
            self.tt("dve", YT3[0:L], YT3[0:L], RSTv[0:L, :].unsqueeze(2).to_broadcast([L, G8, 64]), ALU.mult,
                    [T["YT"], T["RST"]], [T["YT"]])
            yield
            bY = self.bank("C")
            PYF = self.ps[bY][:, 0:n]
            for j in range(NCk):
                self.tr(PYF[:, cs(j)], YTv[0:L, j, :], self.IDENTF[0:L, 0:L], [T["YT"], self.tMASK], [self.tps[bY]])
            yield
            self.act(w(YF), PYF, AF.Identity, [self.tps[bY], tV], [T["SQ2"]], scale=self.vcol("gn_w", c), bias=self.vcol("gn_b", c))
            yield
            self.tt("pool", w(YF), w(YF), w(BONUS), ALU.add, [T["SQ2"], tBONUS], [T["SQ2"]])
            yield
            self.tt("dve", w(YG), w(YF), w(Gf), ALU.mult, [T["SQ2"], tGf], [T["YG"]])
            yield
            for dc0 in range(0, NCH, 2):
                bO = self.bank("C")
                for k2_ in range(2):
                    dc = dc0 + k2_
                    PO = self.ps[bO][:, 256 * k2_:256 * k2_ + n]
                    self.mm(PO, WO[:, dc * 128:(dc + 1) * 128], w(YG), True, True, [tWO[c % 2], T["YG"]], [self.tps[bO]], signal=True)
                for k2_ in range(2):
                    dc = dc0 + k2_
                    PO = self.ps[bO][:, 256 * k2_:256 * k2_ + n]
                    self.tt("dve", self.X[:, dc, tsl], PO, self.X[:, dc, tsl], ALU.add, [self.tps[bO], self.tX[dc, ti5]], [self.tX[dc, ti5]])
                yield

        STEPS = [int(v) for v in _os.environ.get("RW_STEP", "1,1,1").split(",")]

        def drain(gens):
            gens = [(g, k) for g, k in zip(gens, STEPS) if g is not None]
            while gens:
                for item in list(gens):
                    g, k = item
                    try:
                        for _ in range(k):
                            next(g)
                    except StopIteration:
                        gens.remove(item)

        PIPE = int(_os.environ.get("RW_PIPE", "1"))
        NU_ = len(units)
        SCHED = int(_os.environ.get("K_SCHED", "1"))
        if SCHED:
            self.bank_pe = {}
            kb.begin_defer()
        if PIPE:
            for s in range(NU_ + 2):
                gA = stageA(units[s]) if s < NU_ else None
                gB = stageB(units[s - 1]) if 0 <= s - 1 < NU_ else None
                gC = stageC(units[s - 2]) if 0 <= s - 2 < NU_ else None
                drain([gC, gB, gA])
        else:
            for U in units:
                drain([stageA(U)])
                drain([stageB(U)])
                drain([stageC(U)])
        if SCHED:
            kb.end_defer()
            self.bank_pe = {}
        ar.release(m0)
        kb.barrier()

    def mlstm(self):
        kb, d, ar = self.kb, self.d, self.ar
        m0 = ar.mark()
        gname = "norm_mix1"
        XN, tXN = self.XN, self.tXN
        T = TokMap()
        NEG = -1.0e30
        EKA = ar.f32(NT)
        EQA = ar.f32(NT)
        EMTT = ar.f32(48 * 8).rearrange("p (j h) -> p j h", h=8)
        self.EMTP = EMTT[:, 0:32, :]
        EMTS = EMTT[:, 32:48, :]
        self.BBP = ar.f32(2)
        self.ABP = ar.f32(2)
        MLV = ar.f32(8 * 10).rearrange("p (h k) -> p h k", h=8)
        BIF = ar.f32(4)
        M0T = ar.f32(NS)
        MOUT = ar.f32(17)
        SEL = ar.f32(8 * 64).rearrange("p (h m) -> p h m", h=8)
        CONVO = ar.f32(8 * 2 * 17 * 3).rearrange("p (h w s k) -> p h w s k", h=8, w=2, s=17)
        tEK, tEQ = TokMap(), TokMap()
        kb.dma("sp", MLV[0:64], d["mlv"], writes=[T["MLV"]])
        kb.dma("sp", BIF[0:8, 0:2], d["bif"], writes=[T["BIF"]])
        kb.dma("sp", M0T[0:8, :], d["ml_m0T"], writes=[T["M0T"]])
        self.ts("dve", BIF[0:8, 2:3], BIF[0:8, 1:2], -1.0, None, ALU.mult, None, [T["BIF"]], [T["BIF"]])
        self.cp("pool", SEL[0:8], self.IDENTF[0:8, 0:8].unsqueeze(2).to_broadcast([8, 8, 64]), [self.tMASK], [T["SEL"]])

        m1 = ar.mark()
        WIF = ar.bf16(NCH * 16).rearrange("p (k m) -> p k m", k=NCH)
        for K in range(NCH):
            kb.dma("pool", WIF[:, K, :], d["ml_w_in"][0, K * 128:(K + 1) * 128, 3072:3088], writes=[T["WIF"]])
        SQ = [ar.bf16(512) for _ in range(2)]
        LN = ar.f32(512)
        RS = ar.f32(512)
        tSQ = TokMap()
        tLN, tRS = Tok("ln"), Tok("rs")
        LI, LF, BB, AA = [ar.f32(512) for _ in range(4)]
        ABX = ar.f32(513)
        D0, D1, TMPg, MTg, EMg = [ar.f32(512) for _ in range(5)]
        import os as _os
        PRE = int(_os.environ.get("K_PRE", "1"))
        if PRE:
            self.bank_pe = {}
            kb.begin_defer()
        for ti, (t0, n) in enumerate(TILES):
            sample = t0 >= SEQ
            Lc = 4 if sample else 64
            nck = n // Lc

            def out_fn(c, rs, trs, ti=ti, t0=t0, n=n):
                self.stt(XN[:, c, t0:t0 + n], self.X[:, c, t0:t0 + n], self.vcol(gname, c), rs, ALU.mult, ALU.mult,
                         [self.tX[c, ti], trs, self.tVEC], [tXN[c, ti]])
            self.rmsnorm_tile(ti, gname, out_fn, (SQ, tSQ, LN, tLN, RS, tRS, self.bank()))
            bI, bF = self.bank(), self.bank()
            PI, PF = self.ps[bI][0:8, :n], self.ps[bF][0:8, :n]
            for K in range(NCH):
                self.mm(PI, WIF[:, K, 0:8], XN[:, K, t0:t0 + n], K == 0, K == NCH - 1, [T["WIF"], tXN[K, ti]], [self.tps[bI]])
            for K in range(NCH):
                self.mm(PF, WIF[:, K, 8:16], XN[:, K, t0:t0 + n], K == 0, K == NCH - 1, [T["WIF"], tXN[K, ti]], [self.tps[bF]])
            g = lambda a: a[0:8, 0:n]
            self.act(g(LI), PI, AF.Identity, [self.tps[bI], T["BIF"]], [T["LI"]], bias=BIF[0:8, 0:1])
            self.act(g(TMPg), PF, AF.Exp, [self.tps[bF], T["BIF"]], [T["TMP"]], scale=-1.0, bias=BIF[0:8, 2:3])
            self.act(g(LF), g(TMPg), AF.Ln, [T["TMP"], self.tCONST], [T["LF"]], bias=self.ONEC[0:8, :])
            self.memset("dve", g(D0), 1.0, [T["D0"]])
            init = 0.0
            rd = []
            if sample:
                self.memset("dve", g(D0).rearrange("p (s t) -> p s t", t=TS)[:, :, 0:1], 0.0, [T["D0"]])
            elif ti > 0:
                init = self.BBP[0:8, 0:1]
                rd = [T["BBprev"]]
            self.scan(g(BB), g(D0), g(LF), init, ALU.mult, ALU.subtract, [T["D0"], T["LF"]] + rd, [T["BB"]])
            self.tt("dve", g(AA), g(LI), g(BB), ALU.subtract, [T["LI"], T["BB"]], [T["AA"]])
            ab = ABX[0:8, 1:1 + n]
            if sample:
                self.memset("dve", g(D1), 0.0, [T["D1"]])
                self.memset("dve", g(D1).rearrange("p (s t) -> p s t", t=TS)[:, :, 0:1], NEG, [T["D1"]])
                a3 = g(AA).rearrange("p (s t) -> p s t", t=TS)
                self.tt("dve", a3[:, :, 0], a3[:, :, 0], M0T[0:8, :], ALU.max, [T["AA"], T["M0T"]], [T["AA"]])
                self.scan(ab, g(D1), g(AA), 0.0, ALU.add, ALU.max, [T["D1"], T["AA"]], [T["ABX"]])
                rho = M0T[0:8, :].unsqueeze(2).to_broadcast([8, NS, TS])
                rtok = [T["M0T"]]
                a_v = g(AA).rearrange("p (s t) -> p s t", t=TS)
                ab_v = ab.rearrange("p (s t) -> p s t", t=TS)
                ek_v = g(TMPg).rearrange("p (s t) -> p s t", t=TS)
                eq_v = g(D0).rearrange("p (s t) -> p s t", t=TS)
                self.tt("dve", g(AA), g(LI), g(BB), ALU.subtract, [T["LI"], T["BB"], T["ABX"]], [T["AA"]])
            else:
                self.memset("dve", g(D1), 0.0, [T["D1"]])
                if ti == 0:
                    self.memset("dve", ABX[0:8, 0:1], 0.0, [T["ABX"]])
                    ainit = 0.0
                else:
                    self.cp("dve", ABX[0:8, 0:1], self.ABP[0:8, 0:1], [T["ABprev"]], [T["ABX"]])
                    ainit = self.ABP[0:8, 0:1]
                self.scan(ab, g(D1), g(AA), ainit, ALU.add, ALU.max, [T["D1"], T["AA"], T["ABX"]] + ([T["ABprev"]] if ti else []), [T["ABX"]])
                rho = ABX[0:8, 0:n].rearrange("p (j l) -> p j l", l=64)[:, :, 0:1].to_broadcast([8, nck, 64])
                rtok = [T["ABX"]]
                a_v = g(AA).rearrange("p (j l) -> p j l", l=64)
                ab_v = ab.rearrange("p (j l) -> p j l", l=64)
                ek_v = g(TMPg).rearrange("p (j l) -> p j l", l=64)
                eq_v = g(D0).rearrange("p (j l) -> p j l", l=64)
            self.tt("dve", ek_v, a_v, rho, ALU.subtract, [T["AA"]] + rtok, [T["TMP"]])
            self.act(EKA[0:8, t0:t0 + n], g(TMPg), AF.Exp, [T["TMP"]], [tEK[ti]])
            self.tt("dve", eq_v, rho, ab_v, ALU.subtract, [T["ABX"], T["D0"]] + rtok, [T["D0"]])
            self.act(EQA[0:8, t0:t0 + n], g(D0), AF.Exp, [T["D0"]], [tEQ[ti]])
            self.tt("dve", g(MTg), g(BB), ab, ALU.add, [T["BB"], T["ABX"]], [T["MT"]])
            self.act(g(EMg), g(MTg), AF.Exp, [T["MT"]], [T["EM"]], scale=-1.0)
            bT = self.bank()
            for j in range(nck):
                self.tr(self.ps[bT][0:Lc, j * 8:(j + 1) * 8], EMg[0:8, j * Lc:(j + 1) * Lc], self.IDENTF[0:8, 0:8],
                        [T["EM"], self.tMASK], [self.tps[bT]])
            cb0 = t0 // 64 if not sample else 32
            if sample:
                self.cp("act", EMTS[0:Lc, 0:nck, :], self.ps[bT][0:Lc, 0:nck * 8].rearrange("p (j h) -> p j h", h=8),
                        [self.tps[bT]], [T["EMTS"]])
                self.cp("pool", MOUT[0:8, 1:17], g(MTg).rearrange("p (s t) -> p s t", t=TS)[:, :, 3], [T["MT"]], [T["MOUT"]])
            else:
                self.cp("act", self.EMTP[0:Lc, cb0:cb0 + nck, :], self.ps[bT][0:Lc, 0:nck * 8].rearrange("p (j h) -> p j h", h=8),
                        [self.tps[bT]], [T["EMTP"]])
                if ti == 3:
                    self.cp("pool", MOUT[0:8, 0:1], MTg[0:8, n - 1:n], [T["MT"]], [T["MOUT"]])
                self.cp("pool", self.BBP[0:8, 0:1], BB[0:8, n - 1:n], [T["BB"]], [T["BBprev"]])
                self.cp("pool", self.ABP[0:8, 0:1], ABX[0:8, n:n + 1], [T["ABX"]], [T["ABprev"]])
        if PRE:
            kb.end_defer()
            self.bank_pe = {}
        ar.release(m1)
        kb.barrier()
        kb.dma("sp", d["o_ml_m"], MOUT[0:8, :], reads=[T["MOUT"]])

        WIN = ar.bf16(NCH * 384).rearrange("p (k m) -> p k m", k=NCH)
        WO2 = [ar.bf16(D) for _ in range(2)]
        tWO = [Tok("mwo0"), Tok("mwo1")]
        RAW = [ar.f32(520) for _ in range(2)]
        ACC = [ar.f32(512) for _ in range(2)]
        SIL = [ar.f32(512) for _ in range(2)]
        SA = [dict(QP=ar.bf16(512), KP=ar.bf16(512), VA=ar.bf16(8 * 130).rearrange("p (j m) -> p j m", j=8),
                   KTt=ar.bf16(8 * 64).rearrange("p (j m) -> p j m", j=8), LAMB=ar.f32(512)) for _ in range(2)]
        SO = [ar.f32(512) for _ in range(3)]
        SH = [ar.f32(8 * 128).rearrange("p (j m) -> p j m", j=8) for _ in range(2)]
        STall = ar.bf16(8 * 64)
        SQH = ar.f32(8 * 128)
        STATH = ar.f32(32)
        DEN = ar.f32(16)
        HG = ar.bf16(512)
        C = ar.f32(130)
        Cd = ar.f32(130)
        Cb = ar.bf16(130)
        CS = ar.f32(4 * 130).rearrange("p (s m) -> p s m", s=4)
        CSd = ar.f32(4 * 130).rearrange("p (s m) -> p s m", s=4)
        CSb = ar.bf16(4 * 130).rearrange("p (s m) -> p s m", s=4)
        PCLb = ar.f32(8 * 130).rearrange("p (j m) -> p j m", j=8)
        CbAll = ar.bf16(9 * 130).rearrange("p (j m) -> p j m", j=9)
        main_tiles = [(t0, 512, 64, 8) for t0 in range(0, SEQ, 512)] + [(SEQ + 16 * q, 16, 4, 4) for q in range(4)]
        import os as _os
        if "ML_TILES" in _os.environ:
            main_tiles = [main_tiles[int(i)] for i in _os.environ["ML_TILES"].split(",")]
        units = []
        for h in range(int(_os.environ.get("ML_NH", 8))):
            for k_, (t0, n, L, NCk) in enumerate(main_tiles):
                units.append(dict(u=len(units), h=h, t0=t0, n=n, L=L, NCk=NCk, first=(k_ == 0),
                                  lastprompt=(t0 < SEQ and (k_ + 1 == len(main_tiles) or main_tiles[k_ + 1][0] >= SEQ))))
        for sa in SA:
            self.memset("pool", sa["VA"][0:64, :, 128:129], 1.0, [T["VAinit"]])

        def stageA(U):
            u, h, t0, n, L, NCk = U["u"], U["h"], U["t0"], U["n"], U["L"], U["NCk"]
            sa, k2, k3 = SA[u % 2], u % 2, u % 3
            QP, KP, VA, KTt, LAMB, Osig = sa["QP"], sa["KP"], sa["VA"], sa["KTt"], sa["LAMB"], SO[k3]
            tQP, tKP, tVA, tKTt, tLAM, tO = T["QP", k2], T["KP", k2], T["VA", k2], T["KTt", k2], T["LAM", k2], T["O", k3]
            sample = t0 >= SEQ
            ti5 = min(t0 // 512, 4)
            tsl = slice(t0, t0 + n)
            cs = lambda j: slice(j * L, (j + 1) * L)
            xt = [tXN[K, ti5] for K in range(NCH)]
            if U["first"]:
                for K in range(NCH):
                    rows = slice(K * 128, (K + 1) * 128)
                    kb.dma("pool", WIN[:, K, 0:64], d["ml_w_in"][0, rows, h * 64:(h + 1) * 64], writes=[T["WIN"]])
                    kb.dma("pool", WIN[:, K, 64:128], d["ml_w_in"][0, rows, 512 + h * 64:512 + (h + 1) * 64], writes=[T["WIN"]])
                    kb.dma("pool", WIN[:, K, 128:256], d["ml_w_in"][0, rows, 1024 + h * 128:1024 + (h + 1) * 128], writes=[T["WIN"]])
                    kb.dma("pool", WIN[:, K, 256:384], d["ml_w_in"][0, rows, 2048 + h * 128:2048 + (h + 1) * 128], writes=[T["WIN"]])
                kb.dma("pool", WO2[h % 2], d["ml_w_out"][0, h * 128:(h + 1) * 128, :], writes=[tWO[h % 2]])
                for w_ in range(2):
                    self.memset("pool", RAW[w_][0:64, 0:3], 0.0, [T["RAW", w_]])
                yield
            if sample:
                q4 = (t0 - SEQ) // 16
                for w_ in range(2):
                    rv = RAW[w_][0:64, 0:28].rearrange("p (s u) -> p s u", u=7)
                    kb.dma("sp", rv[:, :, 0:3], d["ml_convT"][h, w_, :, 4 * q4:4 * q4 + 4, :], writes=[T["RAW", w_]])
            bQ, bK, bO = self.bank("A"), self.bank("A"), self.bank("A")
            PQ, PK, PO_ = self.ps[bQ][0:64, :n], self.ps[bK][0:64, :n], self.ps[bO][:, :n]
            for (P_, cols, bb) in ((PQ, slice(0, 64), bQ), (PK, slice(64, 128), bK), (PO_, slice(256, 384), bO)):
                for K in range(NCH):
                    self.mm(P_, WIN[:, K, cols], XN[:, K, tsl], K == 0, K == NCH - 1, [T["WIN"], xt[K]], [self.tps[bb]])
                yield
            self.act(Osig[:, :n], PO_, AF.Sigmoid, [self.tps[bO]], [tO])
            for w_, (P_, bb) in enumerate(((PQ, bQ), (PK, bK))):
                mv = lambda k_: MLV[0:64, h, 5 * w_ + k_:5 * w_ + k_ + 1]
                R_ = RAW[w_]
                if sample:
                    rv = R_[0:64, 0:28].rearrange("p (s u) -> p s u", u=7)
                    self.cp("act", rv[:, :, 3:7], P_.rearrange("p (s t) -> p s t", t=TS), [self.tps[bb]], [T["RAW", w_]])
                    taps = [rv[:, :, k_:k_ + 4] for k_ in range(4)]
                    acc = ACC[w_][0:64, 0:n].rearrange("p (s t) -> p s t", t=TS)
                    self.cp("pool", CONVO[0:64, h, w_, 1 + 4 * q4:5 + 4 * q4, :], rv[:, :, 4:7], [T["RAW", w_]], [T["CONVO"]])
                else:
                    self.cp("act", R_[0:64, 3:3 + n], P_, [self.tps[bb]], [T["RAW", w_]])
                    taps = [R_[0:64, k_:k_ + n] for k_ in range(4)]
                    acc = ACC[w_][0:64, 0:n]
                yield
                self.ts("dve", acc, taps[0], mv(0), mv(4), ALU.mult, ALU.add, [T["RAW", w_], T["MLV"]], [T["ACC", w_]])
                for k_ in range(1, 4):
                    self.stt(acc, taps[k_], mv(k_), acc, ALU.mult, ALU.add, [T["RAW", w_], T["MLV"], T["ACC", w_]], [T["ACC", w_]])
                yield
                self.act(SIL[w_][0:64, 0:n], ACC[w_][0:64, 0:n], AF.Silu, [T["ACC", w_]], [T["SIL", w_]])
                if not sample:
                    if t0 + n == SEQ:
                        self.cp("pool", CONVO[0:64, h, w_, 0, :], R_[0:64, n:n + 3], [T["RAW", w_]], [T["CONVO"]])
                    self.cp("pool", R_[0:64, 0:3], R_[0:64, n:n + 3], [T["RAW", w_], T["ACC", w_]], [T["RAW", w_]])
                yield
            for j0 in range(0, NCk, 2):
                bV = self.bank("A")
                nj = min(2, NCk - j0)
                for jj in range(nj):
                    j = j0 + jj
                    PVt = self.ps[bV][0:L, jj * 128:(jj + 1) * 128]
                    for K in range(NCH):
                        self.mm(PVt, XN[:, K, t0 + j * L:t0 + (j + 1) * L], WIN[:, K, 128:256], K == 0, K == NCH - 1,
                                [T["WIN"], xt[K]], [self.tps[bV]])
                self.cp("act", VA[0:L, j0:j0 + nj, 0:128], self.ps[bV][0:L, 0:nj * 128].rearrange("p (j m) -> p j m", m=128),
                        [self.tps[bV], T["VAinit"]], [tVA])
                yield
            bM, bM2 = self.bank("A"), self.bank("A")
            PBK, PBQ = self.ps[bM][0:64, 0:n], self.ps[bM2][0:64, 0:n]
            tBQ = self.tps[bM2]
            self.mm(PBK, SEL[0:8, h, :], EKA[0:8, tsl], True, True, [T["SEL"], tEK[ti5]], [self.tps[bM]])
            self.mm(PBQ, SEL[0:8, h, :], EQA[0:8, tsl], True, True, [T["SEL"], tEQ[ti5]], [tBQ])
            yield
            self.tt("dve", KP[0:64, 0:n], SIL[1][0:64, 0:n], PBK, ALU.mult, [T["SIL", 1], self.tps[bM]], [tKP])
            self.stt(QP[0:64, 0:n], SIL[0][0:64, 0:n], 0.125, PBQ, ALU.mult, ALU.mult, [T["SIL", 0], tBQ], [tQP])
            self.cp("act", LAMB[0:64, 0:n], PBQ, [tBQ], [tLAM])
            yield
            bT = self.bank("A")
            PTr = self.ps[bT].bitcast(BF16)
            for j in range(NCk):
                self.tr(PTr[0:L, j * 64:(j + 1) * 64], KP[0:64, cs(j)], self.IDENTB[0:64, 0:64], [tKP, self.tMASK], [self.tps[bT]])
            self.cp("act", KTt[0:L, 0:NCk, :], PTr[0:L, 0:NCk * 64].rearrange("p (j m) -> p j m", m=64), [self.tps[bT]], [tKTt])
            yield

        def stageC(U):
            u, h, t0, n, L, NCk = U["u"], U["h"], U["t0"], U["n"], U["L"], U["NCk"]
            sa, k2 = SA[u % 2], u % 2
            QP, KP, VA, KTt, LAM = sa["QP"], sa["KP"], sa["VA"], sa["KTt"], sa["LAMB"]
            tQP, tKP, tVA, tKTt, tLAM = T["QP", k2], T["KP", k2], T["VA", k2], T["KTt", k2], T["LAM", k2]
            HT, tHT = SH[k2], T["HT", k2]
            sample = t0 >= SEQ
            cs = lambda j: slice(j * L, (j + 1) * L)
            if U["first"]:
                self.memset("pool", C[0:64], 0.0, [T["C"]])
                self.memset("pool", Cb[0:64], 0.0, [T["Cb"]])
            if sample:
                q4 = (t0 - SEQ) // 16
                for s in range(4):
                    kb.dma("sp", CS[0:64, s, 0:129], d["ml_C0T"][4 * q4 + s, h], writes=[T["CS", s]])
                yield
            PCL = PCLb[0:64, 0:NCk, 0:129]
            for j0 in range(0, NCk, 2):
                bP = self.bank("C")
                for jj in range(2):
                    j = j0 + jj
                    PCS = self.ps[bP][0:64, 256 * jj:256 * jj + 129]
                    self.mm(PCS, KTt[0:L, j, :], VA[0:L, j, 0:129], True, True, [tKTt, tVA], [self.tps[bP]])
                for jj in range(2):
                    j = j0 + jj
                    PCS = self.ps[bP][0:64, 256 * jj:256 * jj + 129]
                    lam = LAM[0:64, (j + 1) * L - 1:(j + 1) * L]
                    self.act(PCL[:, j, :], PCS, AF.Identity, [self.tps[bP], tLAM], [T["PCL"]], scale=lam)
                yield
            CbA = CbAll[0:64, 0:NCk + 1, 0:129]
            for j in range(NCk):
                lam = LAM[0:64, (j + 1) * L - 1:(j + 1) * L]
                if sample:
                    Cc, tC = CS[0:64, j, 0:129], T["CS", j]
                    self.cp("act", CbA[:, j, :], Cc, [tC], [T["CbA"]])
                    self.stt(Cc, Cc, lam, PCL[:, j, :], ALU.mult, ALU.add, [tC, tLAM, T["PCL"]], [tC])
                else:
                    Cc, tC = C[0:64, 0:129], T["C"]
                    if j == 0:
                        self.cp("act", CbA[:, 0, :], Cc, [tC], [T["CbA"]])
                    self.stt(CbA[:, j + 1, :], Cc, lam, PCL[:, j, :], ALU.mult, ALU.add, [tC, tLAM, T["PCL"]], [T["CbA"]])
                    self.stt(Cc, Cc, lam, PCL[:, j, :], ALU.mult, ALU.add, [tC, tLAM, T["PCL"]], [tC])
                yield
            bS = self.bank("C")
            PSTv = self.ps[bS][0:L, 0:NCk * L].rearrange("p (j t) -> p j t", j=NCk)
            for j in range(NCk):
                self.mm(PSTv[:, j, :], KP[0:64, cs(j)], QP[0:64, cs(j)], True, True, [tKP, tQP], [self.tps[bS]])
            yield
            STv = STall[0:L, 0:NCk * L].rearrange("p (j t) -> p j t", j=NCk)
            self.tt("dve", STv, PSTv, self.MIU[0:L, 0:L].unsqueeze(1).to_broadcast([L, NCk, L]), ALU.mult,
                    [self.tps[bS], self.tMASK], [T["ST"]])
            yield
            if sample:
                emall, tem = EMTS[0:L, 4 * q4:4 * q4 + NCk, h], T["EMTS"]
            else:
                emall, tem = self.EMTP[0:L, t0 // 64:t0 // 64 + NCk, h], T["EMTP"]
            for g0 in range(0, NCk, 3):
                g = min(3, NCk - g0)
                bN = self.bank("C")
                tN = self.tps[bN]
                PNDv = self.ps[bN][0:L, 0:g * 129].rearrange("p (j m) -> p j m", j=g)
                for jj in range(g):
                    j = g0 + jj
                    self.mm(PNDv[:, jj, :], QP[0:64, cs(j)], CbA[:, j, :], True, False, [tQP, T["CbA"]], [tN])
                    self.mm(PNDv[:, jj, :], STv[:, j, :], VA[0:L, j, 0:129], False, True, [T["ST"], tVA], [tN])
                yield
                dn = DEN[0:L, g0:g0 + g]
                self.act(dn, PNDv[:, :, 128], AF.Abs, [tN], [T["DEN"]])
                yield
                self.tt("dve", dn, dn, emall[:, g0:g0 + g], ALU.max, [T["DEN"], tem], [T["DEN"]])
                self.recip(dn, dn, [T["DEN"]], [T["DEN"]])
                yield
                self.tt("dve", HT[0:L, g0:g0 + g, :], PNDv[:, :, 0:128], dn.unsqueeze(2).to_broadcast([L, g, 128]), ALU.mult,
                        [tN, T["DEN"]], [tHT])
                yield
            if sample:
                for s in range(4):
                    kb.dma("sp", d["o_ml_Cs"][4 * q4 + s, h], CS[0:64, s, 0:129], reads=[T["CS", s]])
            if U["lastprompt"]:
                kb.dma("sp", d["o_ml_Cp"][h], C[0:64, 0:129], reads=[T["C"]])

        def stageD(U):
            u, h, t0, n, L, NCk = U["u"], U["h"], U["t0"], U["n"], U["L"], U["NCk"]
            k2, k3 = u % 2, u % 3
            HT, tHT = SH[k2], T["HT", k2]
            Osig, tO = SO[k3], T["O", k3]
            WOh = WO2[h % 2]
            ti5 = min(t0 // 512, 4)
            tsl = slice(t0, t0 + n)
            cs = lambda j: slice(j * L, (j + 1) * L)
            G = NCk
            HTf = HT[0:L, 0:G, :]
            SQv = SQH[0:L, 0:G * 128].rearrange("p (j m) -> p j m", m=128)
            self.act(SQv, HTf, AF.Square, [tHT], [T["SQH"]])
            yield
            self.reduce(STATH[0:L, 0:G], SQv, ALU.add, [T["SQH"]], [T["STATH"]])
            yield
            self.act(STATH[0:L, 0:G], STATH[0:L, 0:G], AF.Ln, [T["STATH"], self.tCONST], [T["STATH"]], scale=1.0 / 128, bias=self.EPSC[0:L, :])
            yield
            self.act(STATH[0:L, 0:G], STATH[0:L, 0:G], AF.Exp, [T["STATH"]], [T["STATH"]], scale=-0.5)
            yield
            self.tt("dve", HTf, HTf, STATH[0:L, 0:G].unsqueeze(2).to_broadcast([L, G, 128]), ALU.mult, [tHT, T["STATH"]], [tHT])
            yield
            bY = self.bank("C2")
            PYF = self.ps[bY][:, 0:n]
            for j in range(NCk):
                self.tr(PYF[:, cs(j)], HT[0:L, j, :], self.IDENTF[0:L, 0:L], [tHT, self.tMASK], [self.tps[bY]])
            yield
            self.stt(HG[:, 0:n], PYF, self.vcol("ml_norm_w", h), Osig[:, 0:n], ALU.mult, ALU.mult, [self.tps[bY], tO, self.tVEC], [T["HG"]])
            yield
            for dc in range(NCH):
                bO2 = self.bank("C2")
                PO2 = self.ps[bO2][:, 0:n]
                self.mm(PO2, WOh[:, dc * 128:(dc + 1) * 128], HG[:, 0:n], True, True, [tWO[h % 2], T["HG"]], [self.tps[bO2]])
                self.tt("dve", self.X[:, dc, tsl], PO2, self.X[:, dc, tsl], ALU.add, [self.tps[bO2], self.tX[dc, ti5]], [self.tX[dc, ti5]])
                yield

        def drain(gens):
            gens = [g for g in gens if g is not None]
            while gens:
                for g in list(gens):
                    try:
                        next(g)
                    except StopIteration:
                        gens.remove(g)

        NU_ = len(units)
        SCHED = int(_os.environ.get("K_SCHED", "1"))
        if SCHED:
            self.bank_pe = {}
            kb.begin_defer()
        for s in range(NU_ + 2):
            gA = stageA(units[s]) if s < NU_ else None
            gC = stageC(units[s - 1]) if 0 <= s - 1 < NU_ else None
            gD = stageD(units[s - 2]) if 0 <= s - 2 < NU_ else None
            drain([gC, gD, gA])
        if SCHED:
            kb.end_defer()
            self.bank_pe = {}
        kb.dma("sp", d["o_ml_conv"], CONVO[0:64], reads=[T["CONVO"]])
        ar.release(m0)
        kb.barrier()

    def ffn(self, L, which):
        kb, d, ar = self.kb, self.d, self.ar
        m = ar.mark()
        wg = d["ff%s_wg" % which][L]
        wu = d["ff%s_wu" % which][L]
        wd = d["ff%s_wd" % which][L]
        gname = "norm_ff%s%d" % (which, L)
        G = 4
        groups = [(f0, min(G, NFF - f0)) for f0 in range(0, NFF, G)]
        WG = [ar.bf16(NCH * 512).rearrange("p (c f) -> p c f", c=NCH) for _ in range(2)]
        WU = [ar.bf16(NCH * 512).rearrange("p (c f) -> p c f", c=NCH) for _ in range(2)]
        WD = [ar.bf16(G * D).rearrange("p (f o) -> p f o", f=G) for _ in range(2)]
        H = [ar.bf16(G * 512).rearrange("p (f n) -> p f n", f=G) for _ in range(2)]
        SG = [ar.f32(512) for _ in range(2)]
        SQ = [ar.bf16(512) for _ in range(2)]
        LN = ar.f32(512)
        RS = ar.f32(512)
        tW = TokMap()
        tH = TokMap()
        tSG = TokMap()
        tSQ = TokMap()
        tLN, tRS = Tok("ln"), Tok("rs")

        def load_group(gi):
            f0, nf = groups[gi]
            s = gi % 2
            for c in range(NCH):
                kb.dma("pool", WG[s][:, c, 0:nf * 128], wg[c * 128:(c + 1) * 128, f0 * 128:(f0 + nf) * 128],
                       writes=[tW["g", s, c]])
                kb.dma("pool", WU[s][:, c, 0:nf * 128], wu[c * 128:(c + 1) * 128, f0 * 128:(f0 + nf) * 128],
                       writes=[tW["u", s, c]])
            for fi in range(nf):
                kb.dma("pool", WD[s][:, fi, :], wd[(f0 + fi) * 128:(f0 + fi + 1) * 128, :], writes=[tW["d", s, fi]])

        load_group(0)
        load_group(1)

        for ti, (t0, n) in enumerate(TILES):
            def out_fn(c, rs, trs, ti=ti, t0=t0, n=n):
                kb.emit("dve", lambda e: e.scalar_tensor_tensor(out=self.XN[:, c, t0:t0 + n], in0=self.X[:, c, t0:t0 + n],
                                                                scalar=self.vcol(gname, c), in1=rs,
                                                                op0=ALU.mult, op1=ALU.mult),
                        reads=[self.tX[c, ti], trs, self.tVEC], writes=[self.tXN[c, ti]])
            self.rmsnorm_tile(ti, gname, out_fn, (SQ, tSQ, LN, tLN, RS, tRS, 4 + ti % 4))

        items = [(gi, ti) for gi in range(len(groups)) for ti in range(len(TILES))]
        po_rr = [0]

        def up(idx):
            gi, ti = items[idx]
            f0, nf = groups[gi]
            s = gi % 2
            hs = idx % 2
            t0, n = TILES[ti]
            for fi in range(nf):
                b = fi % 2
                pg = self.ps[b][:, :n]
                pu = self.ps[2 + b][:, :n]
                for c in range(NCH):
                    kb.emit("pe", lambda e, c=c, fi=fi, pg=pg: e.matmul(pg, WG[s][:, c, fi * 128:(fi + 1) * 128],
                                                                     self.XN[:, c, t0:t0 + n], start=(c == 0), stop=(c == NCH - 1)),
                            reads=[tW["g", s, c], self.tXN[c, ti]], writes=[self.tps[b]], signal=(c == NCH - 1))
                for c in range(NCH):
                    kb.emit("pe", lambda e, c=c, fi=fi, pu=pu: e.matmul(pu, WU[s][:, c, fi * 128:(fi + 1) * 128],
                                                                     self.XN[:, c, t0:t0 + n], start=(c == 0), stop=(c == NCH - 1)),
                            reads=[tW["u", s, c], self.tXN[c, ti]], writes=[self.tps[2 + b]], signal=(c == NCH - 1))
                kb.emit("act", lambda e, b=b, pg=pg: e.activation(out=SG[b][:, :n], in_=pg, func=AF.Silu),
                        reads=[self.tps[b]], writes=[tSG[b]])
                kb.emit("dve", lambda e, b=b, fi=fi, pu=pu: e.tensor_tensor(out=H[hs][:, fi, :n], in0=SG[b][:, :n], in1=pu, op=ALU.mult),
                        reads=[tSG[b], self.tps[2 + b]], writes=[tH[hs, fi]])

        def down(idx):
            gi, ti = items[idx]
            f0, nf = groups[gi]
            s = gi % 2
            hs = idx % 2
            t0, n = TILES[ti]
            for dc in range(NCH):
                bank = 4 + po_rr[0] % 4
                po_rr[0] += 1
                po = self.ps[bank][:, :n]
                for fi in range(nf):
                    kb.emit("pe", lambda e, fi=fi, dc=dc, po=po: e.matmul(po, WD[s][:, fi, dc * 128:(dc + 1) * 128], H[hs][:, fi, :n],
                                                                       start=(fi == 0), stop=(fi == nf - 1)),
                            reads=[tW["d", s, fi], tH[hs, fi]], writes=[self.tps[bank]], signal=(fi == nf - 1))
                kb.emit("dve", lambda e, dc=dc, po=po: e.scalar_tensor_tensor(out=self.X[:, dc, t0:t0 + n], in0=po, scalar=0.5,
                                                                           in1=self.X[:, dc, t0:t0 + n], op0=ALU.mult, op1=ALU.add),
                        reads=[self.tps[bank], self.tX[dc, ti]], writes=[self.tX[dc, ti]])

        ntile = len(TILES)
        for idx in range(len(items)):
            up(idx)
            if idx > 0:
                down(idx - 1)
                gi_prev, ti_prev = items[idx - 1]
                if ti_prev == ntile - 1 and gi_prev + 2 < len(groups):
                    load_group(gi_prev + 2)
        down(len(items) - 1)
        ar.release(m)
        kb.barrier()

    def final_norm(self):
        kb, d, ar = self.kb, self.d, self.ar
        m = ar.mark()
        SQ = [ar.bf16(512) for _ in range(2)]
        LN = ar.f32(512)
        RS = ar.f32(512)
        Y = [ar.f32(512) for _ in range(4)]
        tSQ, tY = TokMap(), TokMap()
        tLN, tRS = Tok("ln"), Tok("rs")
        rr = [0]
        for ti, (t0, n) in enumerate(TILES):
            def out_fn(c, rs, trs, ti=ti, t0=t0, n=n):
                s = rr[0] % 4
                rr[0] += 1
                kb.emit("dve", lambda e: e.scalar_tensor_tensor(out=Y[s][:, :n], in0=self.X[:, c, t0:t0 + n],
                                                                scalar=self.vcol("norm_final", c), in1=rs,
                                                                op0=ALU.mult, op1=ALU.mult),
                        reads=[self.tX[c, ti], trs, self.tVEC], writes=[tY[s]])
                kb.dma("sp", d["yT"][c * 128:(c + 1) * 128, t0:t0 + n], Y[s][:, :n], reads=[tY[s]])
            self.rmsnorm_tile(ti, "norm_final", out_fn, (SQ, tSQ, LN, tLN, RS, tRS, 4 + ti % 4))
        ar.release(m)
        kb.barrier()

    def dump_x(self):
        kb, d = self.kb, self.d
        for c in range(NCH):
            for ti, (t0, n) in enumerate(TILES):
                kb.dma("sp", d["yT"][c * 128:(c + 1) * 128, t0:t0 + n], self.X[:, c, t0:t0 + n], reads=[self.tX[c, ti]])

    def build(self):
        st = self.stages
        self.load_inputs()
        self.consts()
        for L in range(2):
            if "ffa%d" % L in st:
                self.ffn(L, "a")
            if "mix%d" % L in st and L == 0:
                self.rwkv()
            if "mix%d" % L in st and L == 1:
                self.mlstm()
            if "ffb%d" % L in st:
                self.ffn(L, "b")
        if "final" in st:
            self.final_norm()
        else:
            self.dump_x()
        self.kb.flush()
        return self.nc


ALL_STAGES = ("ffa0", "mix0", "ffb0", "ffa1", "mix1", "ffb1", "final")


def pack_vecs(inp):
    vecs = np.zeros((NVEC, D), np.float32)

    def put(name, v):
        vecs[VID[name]] = np.asarray(v, np.float32).reshape(D)

    for L in range(2):
        put("norm_ffa%d" % L, inp["norm_ffa"][L])
        put("norm_mix%d" % L, inp["norm_mix"][L])
        put("norm_ffb%d" % L, inp["norm_ffb"][L])
    put("norm_final", inp["norm_final"])
    for i in range(6):
        put("mu%d" % i, inp["rw_mu"][0, i])
    put("w0", inp["rw_w0"][0])
    put("a0", inp["rw_a0"][0])
    put("k_k", inp["rw_k_k"][0])
    put("k_a", inp["rw_k_a"][0])
    put("r_k", inp["rw_r_k"][0])
    put("gn_w", inp["rw_gn_w"][0])
    put("gn_b", inp["rw_gn_b"][0])
    for j in range(4):
        put("cw%d" % j, inp["ml_conv_w"][0, j])
    put("cb", inp["ml_conv_b"][0])
    put("ml_norm_w", inp["ml_norm_w"][0])
    return np.ascontiguousarray(vecs.reshape(NVEC, NCH, 128).transpose(2, 0, 1).reshape(128, NVEC * NCH))


def make_in_maps(inp):
    vecs = pack_vecs(inp)
    shared = {"vecs": vecs}
    for nm in ("ffa_wg", "ffa_wu", "ffb_wg", "ffb_wu", "ffa_wd", "ffb_wd"):
        shared[nm] = np.ascontiguousarray(inp[nm], dtype=np.float32)
    cw = np.asarray(inp["ml_conv_w"][0], np.float32)
    cbv = np.asarray(inp["ml_conv_b"][0], np.float32)
    mlv = np.zeros((64, 8, 10), np.float32)
    for w in range(2):
        for k in range(4):
            mlv[:, :, 5 * w + k] = cw[k, w * 512:(w + 1) * 512].reshape(8, 64).T
        mlv[:, :, 5 * w + 4] = cbv[w * 512:(w + 1) * 512].reshape(8, 64).T
    shared["mlv"] = mlv
    shared["bif"] = np.ascontiguousarray(np.asarray(inp["ml_b_if"][0], np.float32).reshape(2, 8).T)
    for nm in ("rw_wr", "rw_wk", "rw_wv", "rw_wo", "rw_w1", "rw_w2", "rw_a1", "rw_a2", "rw_g1", "rw_g2", "ml_w_in", "ml_w_out"):
        shared[nm] = np.ascontiguousarray(inp[nm], dtype=np.float32)
    maps = []
    for core in range(NCORES):
        xs = np.concatenate([inp["x_prompt"][core], inp["x_sample"][core * NS:(core + 1) * NS].reshape(NS * TS, D)], axis=0)
        m = dict(shared)
        m["xT"] = np.ascontiguousarray(xs.T.astype(np.float32))
        sq = slice(core * NS, (core + 1) * NS)
        sh = inp["state_rwkv_shift"][0, sq]
        m["shiftT"] = np.ascontiguousarray(sh.reshape(NS, NCH, 128).transpose(2, 1, 0).astype(np.float32))
        m["rw_S0T"] = np.ascontiguousarray(inp["state_rwkv_S"][0, sq].transpose(0, 1, 3, 2).astype(np.float32))
        m["ml_m0T"] = np.ascontiguousarray(inp["state_mlstm_m"][0, sq].T.astype(np.float32))
        c0 = np.concatenate([inp["state_mlstm_C"][0, sq].transpose(0, 1, 3, 2), inp["state_mlstm_n"][0, sq][..., None]], axis=-1)
        m["ml_C0T"] = np.ascontiguousarray(c0.astype(np.float32))
        cv = inp["state_mlstm_conv"][0, sq]
        m["ml_convT"] = np.ascontiguousarray(cv.reshape(NS, 3, 2, 8, 64).transpose(3, 2, 4, 0, 1).astype(np.float32))
        maps.append(m)
    return maps


def run(inp, stages=ALL_STAGES, trace=False):
    b = Builder(stages)
    nc = b.build()
    maps = make_in_maps(inp)
    res = run_bass_kernel_spmd(nc, maps, core_ids=list(range(NCORES)), trace=trace)
    return b, res


def assemble(results):
    f = np.float32
    yp = np.zeros((8, SEQ, D), f); ys = np.zeros((128, TS, D), f)
    p_S = np.zeros((1, 8, 16, 64, 64), f); p_sh = np.zeros((1, 8, D), f)
    p_C = np.zeros((1, 8, 8, 128, 64), f); p_n = np.zeros((1, 8, 8, 64), f); p_m = np.zeros((1, 8, 8), f)
    p_cv = np.zeros((1, 8, 3, D), f)
    s_S = np.zeros((1, 128, 16, 64, 64), f); s_sh = np.zeros((1, 128, D), f)
    s_C = np.zeros((1, 128, 8, 128, 64), f); s_n = np.zeros((1, 128, 8, 64), f); s_m = np.zeros((1, 128, 8), f)
    s_cv = np.zeros((1, 128, 3, D), f)
    for core in range(NCORES):
        r = results[core]
        sq = slice(core * NS, (core + 1) * NS)
        y = r["yT"].T
        yp[core] = y[:SEQ]
        ys[sq] = y[SEQ:].reshape(NS, TS, D)
        sho = r["o_shift"]
        p_sh[0, core] = sho[:, :, 0].T.reshape(D)
        s_sh[0, sq] = sho[:, :, 1:].transpose(2, 1, 0).reshape(NS, D)
        p_S[0, core] = r["o_rw_Sp"].transpose(0, 2, 1)
        s_S[0, sq] = r["o_rw_Ss"].transpose(0, 1, 3, 2)
        cp = r["o_ml_Cp"]
        p_C[0, core] = cp[:, :, 0:128].transpose(0, 2, 1)
        p_n[0, core] = cp[:, :, 128]
        cs_ = r["o_ml_Cs"]
        s_C[0, sq] = cs_[:, :, :, 0:128].transpose(0, 1, 3, 2)
        s_n[0, sq] = cs_[:, :, :, 128]
        mo = r["o_ml_m"]
        p_m[0, core] = mo[:, 0]
        s_m[0, sq] = mo[:, 1:].T
        cv = r["o_ml_conv"]
        cvt = cv.transpose(3, 4, 2, 1, 0).reshape(17, 3, D)
        p_cv[0, core] = cvt[0]
        s_cv[0, sq] = cvt[1:]
    return (yp, ys, p_S, p_sh, p_C, p_n, p_m, p_cv, s_S, s_sh, s_C, s_n, s_m, s_cv)


def kernel(**inp):
    b, res = run(inp)
    return assemble(res.results)
```
